# Optimizing a Trainium2 kernel written in Bass

```python
import math
import jax, jax.numpy as jnp
from jax import lax
import numpy as np

D_MODEL = 1024
BATCH = 8
SEQ = 2048
DEPTH = 2

CTX_LEN = 256
GRID_W = 64
CHUNK = 64
EPS = 1e-6
MLSTM_HEAD_DIM = 128
MLSTM_WIDTH = D_MODEL // 2
MLSTM_HEADS = MLSTM_WIDTH // MLSTM_HEAD_DIM
S5_WIDTH = D_MODEL - MLSTM_WIDTH
S5_GROUP = 16
S5_GROUPS = S5_WIDTH // S5_GROUP
S5_STATE = 64
GDN_HEAD_DIM = 128
GDN_WIDTH = D_MODEL
GDN_HEADS = GDN_WIDTH // GDN_HEAD_DIM
FFN_HIDDEN = ((8 * D_MODEL + 3 * 256 - 1) // (3 * 256)) * 256
EVEN_IN = 4 * MLSTM_WIDTH + 4 * MLSTM_HEADS + S5_WIDTH
ODD_IN = 4 * GDN_WIDTH + 4 * GDN_HEADS

kernel_name = "hybrid_mlstm_s5_gdn_prefix_dit"


def rmsnorm(x, w):
    xf = x.astype(jnp.float32)
    y = xf * lax.rsqrt(jnp.mean(xf * xf, axis=-1, keepdims=True) + EPS)
    return (y * w.astype(jnp.float32)).astype(x.dtype)


def modulate(x, shift, scale):
    return x * (1.0 + scale) + shift


def swiglu(h, w1, w3, w2):
    return (jax.nn.silu(h @ w1) * (h @ w3)) @ w2


def _split_cols(p, sizes):
    idx = [int(v) for v in np.cumsum(sizes)[:-1]]
    return jnp.split(p, idx, axis=-1)


def _heads(t, n_heads):
    b, s, w = t.shape
    return t.reshape(b, s, n_heads, w // n_heads).transpose(0, 2, 1, 3)


def _merge(t):
    b, h, s, d = t.shape
    return t.transpose(0, 2, 1, 3).reshape(b, s, h * d)


def _gate(t):
    return jnp.swapaxes(t.astype(jnp.float32), 1, 2)


def _l2norm(t):
    return t * lax.rsqrt(jnp.sum(t * t, axis=-1, keepdims=True) + EPS)


def _to_dir(c_part, l_part, reverse):
    if reverse:
        c_part, l_part = jnp.flip(c_part, 2), jnp.flip(l_part, 2)
    return jnp.concatenate([c_part, l_part], axis=2)


def _from_dir(y, n_ctx, reverse):
    yc, yl = y[:, :, :n_ctx], y[:, :, n_ctx:]
    if reverse:
        yc, yl = jnp.flip(yc, 2), jnp.flip(yl, 2)
    return yc, yl


def mlstm_chunkwise(q, k, v, ig, lf):
    b, h, t, d = q.shape
    nc, L = t // CHUNK, CHUNK
    q, k, v = (a.reshape(b, h, nc, L, d) for a in (q, k, v))
    ig, lf = ig.reshape(b, h, nc, L), lf.reshape(b, h, nc, L)
    F = jnp.cumsum(lf, axis=-1)
    F_last = F[..., -1]
    w = F_last[..., None] - F + ig
    m_loc = jnp.max(w, axis=-1)
    e = jnp.exp(w - m_loc[..., None])
    C_loc = jnp.einsum('bhcl,bhcld,bhcle->bhcde', e, k, v)
    n_loc = jnp.einsum('bhcl,bhcld->bhcd', e, k)

    def step(carry, xs):
        C, n, m = carry
        fl, ml, Cl, nl = xs
        m_new = jnp.maximum(fl + m, ml)
        a_prev, a_loc = jnp.exp(fl + m - m_new), jnp.exp(ml - m_new)
        C_new = a_prev[..., None, None] * C + a_loc[..., None, None] * Cl
        n_new = a_prev[..., None] * n + a_loc[..., None] * nl
        return (C_new, n_new, m_new), (C, n, m)

    init = (jnp.zeros((b, h, d, d), q.dtype), jnp.zeros((b, h, d), q.dtype), jnp.zeros((b, h), q.dtype))
    xs = tuple(jnp.moveaxis(a, 2, 0) for a in (F_last, m_loc, C_loc, n_loc))
    _, (C_in, n_in, m_in) = lax.scan(step, init, xs)
    C_in, n_in, m_in = (jnp.moveaxis(a, 0, 2) for a in (C_in, n_in, m_in))

    causal = jnp.tril(jnp.ones((L, L), bool))
    Dm = jnp.where(causal, F[..., :, None] - F[..., None, :] + ig[..., None, :], -jnp.inf)
    inter = F + m_in[..., None]
    m_t = jnp.maximum(jnp.max(Dm, axis=-1), inter)
    P = jnp.exp(Dm - m_t[..., None]) * jnp.einsum('bhcld,bhcsd->bhcls', q, k)
    a_inter = jnp.exp(inter - m_t)
    num = jnp.einsum('bhcls,bhcse->bhcle', P, v) + a_inter[..., None] * jnp.einsum('bhcld,bhcde->bhcle', q, C_in)
    den = jnp.sum(P, axis=-1) + a_inter * jnp.einsum('bhcld,bhcd->bhcl', q, n_in)
    out = num / jnp.maximum(jnp.abs(den), jnp.exp(-m_t))[..., None]
    return out.reshape(b, h, t, d)


def s5_discretize(lam_re, lam_im, log_dt, b_re, b_im):
    f32 = jnp.float32
    lam_re, lam_im, b_re, b_im = (a.astype(f32) for a in (lam_re, lam_im, b_re, b_im))
    dt = jnp.exp(log_dt.astype(f32))[:, None]
    mag, ang = jnp.exp(lam_re * dt), lam_im * dt
    ab_re, ab_im = mag * jnp.cos(ang), mag * jnp.sin(ang)
    nr, ni = ab_re - 1.0, ab_im
    den = lam_re * lam_re + lam_im * lam_im
    co_re = (nr * lam_re + ni * lam_im) / den
    co_im = (ni * lam_re - nr * lam_im) / den
    bb_re = co_re[..., None] * b_re - co_im[..., None] * b_im
    bb_im = co_re[..., None] * b_im + co_im[..., None] * b_re
    return ab_re, ab_im, bb_re, bb_im


def complex_linear_scan(a_re, a_im, b_re, b_im):
    def combine(e1, e2):
        a1r, a1i, b1r, b1i = e1
        a2r, a2i, b2r, b2i = e2
        return (a1r * a2r - a1i * a2i, a1r * a2i + a1i * a2r,
                a2r * b1r - a2i * b1i + b2r, a2r * b1i + a2i * b1r + b2i)
    _, _, s_re, s_im = lax.associative_scan(combine, (a_re, a_im, b_re, b_im), axis=2)
    return s_re, s_im


def gated_delta_chunkwise(q, k, v, g, beta):
    b, h, t, dk = q.shape
    dv = v.shape[-1]
    nc, L = t // CHUNK, CHUNK
    q, k = q.reshape(b, h, nc, L, dk), k.reshape(b, h, nc, L, dk)
    v = v.reshape(b, h, nc, L, dv)
    g, beta = g.reshape(b, h, nc, L), beta.reshape(b, h, nc, L)
    G = jnp.cumsum(g, axis=-1)
    lower = jnp.tril(jnp.ones((L, L), bool))
    strict = jnp.tril(jnp.ones((L, L), bool), -1)
    gamma = jnp.exp(jnp.where(lower, G[..., :, None] - G[..., None, :], -jnp.inf))
    kb = k * beta[..., None]
    A = jnp.where(strict, jnp.einsum('bhcid,bhcjd->bhcij', kb, k) * gamma, 0.0)
    rhs = jnp.concatenate([v * beta[..., None], kb * jnp.exp(G)[..., None]], axis=-1)
    sol = lax.linalg.triangular_solve(A + jnp.eye(L, dtype=A.dtype), rhs,
                                      left_side=True, lower=True, unit_diagonal=True)
    U, W = sol[..., :dv], sol[..., dv:]
    attn = jnp.einsum('bhcid,bhcjd->bhcij', q, k) * gamma
    qg = q * jnp.exp(G)[..., None]
    kd = k * jnp.exp(G[..., -1:] - G)[..., None]
    gl = jnp.exp(G[..., -1])

    def step(S, xs):
        U_c, W_c, qg_c, kd_c, gl_c = xs
        v_new = U_c - jnp.einsum('bhld,bhde->bhle', W_c, S)
        o_inter = jnp.einsum('bhld,bhde->bhle', qg_c, S)
        S = gl_c[..., None, None] * S + jnp.einsum('bhld,bhle->bhde', kd_c, v_new)
        return S, (v_new, o_inter)

    xs = tuple(jnp.moveaxis(a, 2, 0) for a in (U, W, qg, kd, gl))
    _, (v_new, o_inter) = lax.scan(step, jnp.zeros((b, h, dk, dv), q.dtype), xs)
    v_new, o_inter = jnp.moveaxis(v_new, 0, 2), jnp.moveaxis(o_inter, 0, 2)
    o = o_inter + jnp.einsum('bhcij,bhcje->bhcie', attn, v_new)
    return o.reshape(b, h, t, dv)


def _depthwise_conv3x3(x, w):
    return lax.conv_general_dilated(x, w, window_strides=(1, 1), padding='SAME',
                                    dimension_numbers=('NHWC', 'HWIO', 'NHWC'),
                                    feature_group_count=x.shape[-1])


def even_mixer(hc, hl, w_in, i_bias, f_bias, head_norm_w, lam_re, lam_im, log_dt,
               b_re, b_im, c_re, c_im, s5_d, w_glu, w_out, with_ctx):
    f32 = jnp.float32
    n_ctx = hc.shape[1]
    sizes = [MLSTM_WIDTH] * 4 + [MLSTM_HEADS] * 4 + [S5_WIDTH]
    pc, pl = _split_cols(hc @ w_in, sizes), _split_cols(hl @ w_in, sizes)
    kscale = 1.0 / math.sqrt(MLSTM_HEAD_DIM)

    qc, kc, vc = _heads(pc[0].astype(f32), MLSTM_HEADS), _heads(pc[1].astype(f32), MLSTM_HEADS) * kscale, _heads(pc[2].astype(f32), MLSTM_HEADS)
    ql, kl, vl = _heads(pl[0].astype(f32), MLSTM_HEADS), _heads(pl[1].astype(f32), MLSTM_HEADS) * kscale, _heads(pl[2].astype(f32), MLSTM_HEADS)
    gc, gl = [_gate(t) for t in pc[4:8]], [_gate(t) for t in pl[4:8]]
    hA_c, hA_l = 0.0, 0.0
    for r, rev in enumerate((False, True)):
        ig = _to_dir(gc[r], gl[r], rev) + i_bias[r].astype(f32)[None, :, None]
        lf = jax.nn.log_sigmoid(_to_dir(gc[2 + r], gl[2 + r], rev) + f_bias[r].astype(f32)[None, :, None])
        hseq = mlstm_chunkwise(_to_dir(qc, ql, rev), _to_dir(kc, kl, rev), _to_dir(vc, vl, rev), ig, lf)
        h_c, h_l = _from_dir(hseq, n_ctx, rev)
        hA_c, hA_l = hA_c + h_c, hA_l + h_l

    b = hl.shape[0]
    uc = pc[8].astype(f32).reshape(b, n_ctx, S5_GROUPS, S5_GROUP)
    ul = pl[8].astype(f32).reshape(b, -1, S5_GROUPS, S5_GROUP)
    yB_c, yB_l = 0.0, 0.0
    for r, rev in enumerate((False, True)):
        ab_re, ab_im, bb_re, bb_im = s5_discretize(lam_re[r], lam_im[r], log_dt[r], b_re[r], b_im[r])
        seq_re = _to_dir(jnp.einsum('btgp,gnp->bgtn', uc, bb_re), jnp.einsum('btgp,gnp->bgtn', ul, bb_re), rev)
        seq_im = _to_dir(jnp.einsum('btgp,gnp->bgtn', uc, bb_im), jnp.einsum('btgp,gnp->bgtn', ul, bb_im), rev)
        a_re = jnp.broadcast_to(ab_re[None, :, None, :], seq_re.shape)
        a_im = jnp.broadcast_to(ab_im[None, :, None, :], seq_im.shape)
        s_re, s_im = complex_linear_scan(a_re, a_im, seq_re, seq_im)
        sc_re, sl_re = _from_dir(s_re, n_ctx, rev)
        sc_im, sl_im = _from_dir(s_im, n_ctx, rev)
        cr, ci = c_re[r].astype(f32), c_im[r].astype(f32)
        yB_c = yB_c + jnp.einsum('bgtn,gpn->btgp', sc_re, cr) - jnp.einsum('bgtn,gpn->btgp', sc_im, ci)
        yB_l = yB_l + jnp.einsum('bgtn,gpn->btgp', sl_re, cr) - jnp.einsum('bgtn,gpn->btgp', sl_im, ci)

    mh_w = head_norm_w.reshape(MLSTM_HEADS, 1, MLSTM_HEAD_DIM)
    d_skip = s5_d.astype(f32).reshape(S5_GROUPS, S5_GROUP)
    wg = w_glu.astype(f32)

    def finish(hA, o_pre, yB, u):
        a_out = _merge(rmsnorm(hA, mh_w)) * jax.nn.sigmoid(o_pre.astype(f32))
        yb = jax.nn.gelu(yB + d_skip * u)
        yb = yb.reshape(yb.shape[0], yb.shape[1], S5_WIDTH)
        ga, gg = jnp.split(yb @ wg, 2, axis=-1)
        b_out = ga * jax.nn.sigmoid(gg)
        return jnp.concatenate([a_out, b_out], axis=-1).astype(hl.dtype) @ w_out

    y_lat = finish(hA_l, pl[3], yB_l, ul)
    y_ctx = finish(hA_c, pc[3], yB_c, uc) if with_ctx else None
    return y_ctx, y_lat


def odd_mixer(hc, hl, rows, w_in, conv_w, a_log, dt_bias, head_norm_w, w_out, with_ctx):
    f32 = jnp.float32
    n_ctx = hc.shape[1]
    b = hl.shape[0]
    sizes = [3 * GDN_WIDTH, GDN_WIDTH] + [GDN_HEADS] * 4
    pc, pl = _split_cols(hc @ w_in, sizes), _split_cols(hl @ w_in, sizes)
    qkv_c = jax.nn.silu(_depthwise_conv3x3(pc[0].reshape(b, 1, n_ctx, 3 * GDN_WIDTH), conv_w)).reshape(b, n_ctx, 3 * GDN_WIDTH)
    qkv_l = jax.nn.silu(_depthwise_conv3x3(pl[0].reshape(b, rows, GRID_W, 3 * GDN_WIDTH), conv_w)).reshape(b, -1, 3 * GDN_WIDTH)
    qscale = 1.0 / math.sqrt(GDN_HEAD_DIM)

    def qkv_heads(t):
        q, k, v = jnp.split(t.astype(f32), 3, axis=-1)
        return (_l2norm(_heads(q, GDN_HEADS)) * qscale, _l2norm(_heads(k, GDN_HEADS)), _heads(v, GDN_HEADS))

    qc, kc, vc = qkv_heads(qkv_c)
    ql, kl, vl = qkv_heads(qkv_l)
    gc, gl = [_gate(t) for t in pc[2:6]], [_gate(t) for t in pl[2:6]]
    o_c, o_l = 0.0, 0.0
    for r, rev in enumerate((False, True)):
        a_pre = _to_dir(gc[r], gl[r], rev)
        g = -jnp.exp(a_log[r].astype(f32))[None, :, None] * jax.nn.softplus(a_pre + dt_bias[r].astype(f32)[None, :, None])
        beta = jax.nn.sigmoid(_to_dir(gc[2 + r], gl[2 + r], rev))
        o = gated_delta_chunkwise(_to_dir(qc, ql, rev), _to_dir(kc, kl, rev), _to_dir(vc, vl, rev), g, beta)
        oc, ol = _from_dir(o, n_ctx, rev)
        o_c, o_l = o_c + oc, o_l + ol

    hw = head_norm_w.reshape(GDN_HEADS, 1, GDN_HEAD_DIM)

    def finish(o, z):
        y = _merge(rmsnorm(o, hw)) * jax.nn.silu(z.astype(f32))
        return y.astype(hl.dtype) @ w_out

    y_lat = finish(o_l, pl[1])
    y_ctx = finish(o_c, pc[1]) if with_ctx else None
    return y_ctx, y_lat


def setup_inputs(seed: int = 0) -> dict:
    key = jax.random.key(seed)
    keys = iter(jax.random.split(key, 48))
    f32 = jnp.float32
    n_even, n_odd = (DEPTH + 1) // 2, DEPTH // 2
    d = D_MODEL

    def normal(shape, scale=1.0):
        return jax.random.normal(next(keys), shape, f32) * scale

    def uniform(shape, lo, hi):
        return jax.random.uniform(next(keys), shape, f32, lo, hi)

    def gain(shape):
        return 1.0 + normal(shape, 0.05)

    s5_shape = (n_even, 2, S5_GROUPS, S5_STATE)
    log_dt_s5 = uniform((n_even, 2, S5_GROUPS), math.log(1e-3), math.log(1e-1))
    dt_gdn = jnp.exp(uniform((n_odd, 2, GDN_HEADS), math.log(1e-3), math.log(1e-1)))
    return {
        "x": normal((BATCH, SEQ, d)),
        "c": normal((BATCH, d)),
        "ctx": normal((BATCH, CTX_LEN, d)),
        "c_ctx": normal((d,)),
        "ada_w": normal((DEPTH, d, 6 * d), 0.3 * d ** -0.5),
        "ada_b": normal((DEPTH, 6 * d), 0.01),
        "norm1_w": gain((DEPTH, d)),
        "norm2_w": gain((DEPTH, d)),
        "ffn_w1": normal((DEPTH, d, FFN_HIDDEN), d ** -0.5),
        "ffn_w3": normal((DEPTH, d, FFN_HIDDEN), d ** -0.5),
        "ffn_w2": normal((DEPTH, FFN_HIDDEN, d), FFN_HIDDEN ** -0.5),
        "final_norm_w": gain((d,)),
        "ev_w_in": normal((n_even, d, EVEN_IN), d ** -0.5),
        "ev_i_bias": normal((n_even, 2, MLSTM_HEADS), 0.1),
        "ev_f_bias": uniform((n_even, 2, MLSTM_HEADS), 3.0, 6.0),
        "ev_head_norm_w": gain((n_even, MLSTM_WIDTH)),
        "ev_lam_re": -0.5 + normal(s5_shape, 0.01),
        "ev_lam_im": jnp.pi * jnp.arange(S5_STATE, dtype=f32) + normal(s5_shape, 0.01),
        "ev_log_dt": log_dt_s5,
        "ev_b_re": normal((n_even, 2, S5_GROUPS, S5_STATE, S5_GROUP), (2 * S5_GROUP) ** -0.5),
        "ev_b_im": normal((n_even, 2, S5_GROUPS, S5_STATE, S5_GROUP), (2 * S5_GROUP) ** -0.5),
        "ev_c_re": normal((n_even, 2, S5_GROUPS, S5_GROUP, S5_STATE), S5_STATE ** -0.5),
        "ev_c_im": normal((n_even, 2, S5_GROUPS, S5_GROUP, S5_STATE), S5_STATE ** -0.5),
        "ev_d": normal((n_even, S5_WIDTH)),
        "ev_w_glu": normal((n_even, S5_WIDTH, 2 * S5_WIDTH), S5_WIDTH ** -0.5),
        "ev_w_out": normal((n_even, MLSTM_WIDTH + S5_WIDTH, d), d ** -0.5),
        "od_w_in": normal((n_odd, d, ODD_IN), d ** -0.5),
        "od_conv_w": normal((n_odd, 3, 3, 1, 3 * GDN_WIDTH), 1.0 / 3.0),
        "od_a_log": jnp.log(uniform((n_odd, 2, GDN_HEADS), 1.0, 16.0)),
        "od_dt_bias": dt_gdn + jnp.log(-jnp.expm1(-dt_gdn)),
        "od_head_norm_w": gain((n_odd, GDN_WIDTH)),
        "od_w_out": normal((n_odd, GDN_WIDTH, d), d ** -0.5),
    }


def reference(x, c, ctx, c_ctx, ada_w, ada_b, norm1_w, norm2_w, ffn_w1, ffn_w3, ffn_w2, final_norm_w,
              ev_w_in, ev_i_bias, ev_f_bias, ev_head_norm_w, ev_lam_re, ev_lam_im, ev_log_dt,
              ev_b_re, ev_b_im, ev_c_re, ev_c_im, ev_d, ev_w_glu, ev_w_out,
              od_w_in, od_conv_w, od_a_log, od_dt_bias, od_head_norm_w, od_w_out):
    rows = x.shape[1] // GRID_W
    s_lat = jax.nn.silu(c)
    s_ctx = jax.nn.silu(c_ctx)
    for i in range(DEPTH):
        with_ctx = i < DEPTH - 1
        j = i // 2
        m_l = jnp.split((s_lat @ ada_w[i] + ada_b[i])[:, None, :], 6, axis=-1)
        m_c = jnp.split((s_ctx @ ada_w[i] + ada_b[i])[None, None, :], 6, axis=-1)
        hl = modulate(rmsnorm(x, norm1_w[i]), m_l[0], m_l[1])
        hc = modulate(rmsnorm(ctx, norm1_w[i]), m_c[0], m_c[1])
        if i % 2 == 0:
            y_ctx, y_lat = even_mixer(hc, hl, ev_w_in[j], ev_i_bias[j], ev_f_bias[j], ev_head_norm_w[j],
                                      ev_lam_re[j], ev_lam_im[j], ev_log_dt[j], ev_b_re[j], ev_b_im[j],
                                      ev_c_re[j], ev_c_im[j], ev_d[j], ev_w_glu[j], ev_w_out[j], with_ctx)
        else:
            y_ctx, y_lat = odd_mixer(hc, hl, rows, od_w_in[j], od_conv_w[j], od_a_log[j], od_dt_bias[j],
                                     od_head_norm_w[j], od_w_out[j], with_ctx)
        x = x + m_l[2] * y_lat
        x = x + m_l[5] * swiglu(modulate(rmsnorm(x, norm2_w[i]), m_l[3], m_l[4]), ffn_w1[i], ffn_w3[i], ffn_w2[i])
        if with_ctx:
            ctx = ctx + m_c[2] * y_ctx
            ctx = ctx + m_c[5] * swiglu(modulate(rmsnorm(ctx, norm2_w[i]), m_c[3], m_c[4]), ffn_w1[i], ffn_w3[i], ffn_w2[i])
    return rmsnorm(x, final_norm_w)
```

```python
import math
import numpy as np
import ml_dtypes
import concourse.bass as bass
import concourse.mybir as mybir
from concourse.bass_types import AP
from concourse.bass_utils import run_bass_kernel_spmd

F32 = mybir.dt.float32
BF16 = mybir.dt.bfloat16
AF = mybir.ActivationFunctionType
ALU = mybir.AluOpType
AX = mybir.AxisListType

D = 1024
T = 2304
NCTX = 256
NLAT = 2048
NT = T // 128
EPS = 1e-6
HID = 2816
PI = math.pi


class Obj:
    def __init__(self, k, name, handle, space):
        self.k, self.name, self.h, self.space = k, name, handle, space
        self.uid = k.uid
        self.w, self.r = {}, {}
        self.sems = {}

    def __getitem__(self, idx):
        return self.h[idx]

    def ap(self):
        return self.h.ap() if self.space == "dram" else self.h[:]

    def dsem(self, kind):
        if kind not in self.sems:
            if not self.sems:
                self.k.dma_objs.append(self)
            pool = self.k.sem_pool[kind]
            if pool:
                self.sems[kind] = pool.pop()
            else:
                self.k.nsem += 1
                self.sems[kind] = [self.k.new_sem("d%s_%d" % (kind, self.k.nsem), keep=True), 0]
        return self.sems[kind]


class Eng:
    def __init__(self, k, name, e):
        self.k, self.name, self.e = k, name, e
        self.sem = k.new_sem("p_" + name)
        self.cnt = 0
        self.seen = {}

    def need(self, tok):
        s, v = tok
        if self.seen.get(id(s), 0) >= v:
            return
        self.e.wait_ge(s, v)
        self.seen[id(s)] = v


class K:
    def __init__(self, nc):
        self.nc = nc
        self._ctx = []
        self.dma_objs = []
        self.sem_pool = {"hw": [], "sw": []}
        self.nsem = 0
        self._perm = []
        self.pe = Eng(self, "pe", nc.tensor)
        self.dve = Eng(self, "dve", nc.vector)
        self.act = Eng(self, "act", nc.scalar)
        self.pool = Eng(self, "pool", nc.gpsimd)
        self.sp = Eng(self, "sp", nc.sync)
        self.engs = [self.pe, self.dve, self.act, self.pool, self.sp]
        self.n_ins = 0
        self.uid = 0

    def new_sem(self, name, keep=False):
        cm = self.nc.semaphore(name)
        s = cm.__enter__()
        self._perm.append((cm, s))
        return s

    def _alloc(self, cm, name, space):
        h = cm.__enter__()
        self._ctx.append(cm)
        return Obj(self, name, h, space)

    def sb(self, name, shape, dt=F32):
        self.uid += 1
        name = "%s_%d" % (name, self.uid)
        return self._alloc(self.nc.sbuf_tensor(name, list(shape), dt), name, "sb")

    def ps(self, name, shape, dt=F32):
        self.uid += 1
        name = "%s_%d" % (name, self.uid)
        return self._alloc(self.nc.psum_tensor(name, list(shape), dt), name, "ps")

    def dram(self, name, shape, dt=F32, kind="Internal"):
        h = self.nc.dram_tensor(name, list(shape), dt, kind=kind)
        return Obj(self, name, h, "dram")

    class _Scope:
        def __init__(self, k):
            self.k = k

        def __enter__(self):
            self.mark = len(self.k._ctx)
            self.uid0 = self.k.uid
            return self

        def __exit__(self, *a):
            k = self.k
            k.barrier()
            while len(k._ctx) > self.mark:
                k._ctx.pop().__exit__(None, None, None)
            keep = []
            for o in k.dma_objs:
                if o.space == "dram" or o.uid <= self.uid0:
                    keep.append(o)
                else:
                    for kind, sc in o.sems.items():
                        k.sem_pool[kind].append(sc)
                    o.sems = {}
            k.dma_objs = keep
            return False

    def scope(self):
        return K._Scope(self)

    def barrier(self):
        toks = [(e.sem, e.cnt) for e in self.engs if e.cnt]
        toks += [(sc[0], sc[1]) for o in self.dma_objs for sc in o.sems.values() if sc[1]]
        for e in self.engs:
            for t in toks:
                if t[0] is e.sem:
                    continue
                e.need(t)

    def _deps(self, eng, outs, ins, same_eng_raw=True):
        toks = []
        for o in ins:
            toks += list(o.w.values())
            if o.space == "ps":
                toks += [t for t in o.r.values() if t[0] is not eng.sem]
        for o in outs:
            toks += list(o.w.values())
            toks += list(o.r.values())
        for t in toks:
            if t[0] is eng.sem and (eng is self.pe or not same_eng_raw):
                continue
            eng.need(t)

    def op(self, eng, fn, outs, ins):
        self._deps(eng, outs, ins)
        ins_ = fn()
        eng.cnt += 1
        ins_.then_inc(eng.sem, 1)
        tok = (eng.sem, eng.cnt)
        eng.seen[id(eng.sem)] = max(eng.seen.get(id(eng.sem), 0), 0)
        for o in ins:
            o.r[id(tok[0])] = tok
        for o in outs:
            o.w = {id(tok[0]): tok}
            o.r = {}
        self.n_ins += 1
        return ins_

    def dma(self, out_obj, out_ap, in_obj, in_ap, q=None, **kw):
        q = q or self.sp
        self._deps(q, [out_obj], [in_obj], same_eng_raw=True)
        sc = out_obj.dsem("sw" if q is self.pool else "hw")
        s = sc[0]
        ins_ = q.e.dma_start(out=out_ap, in_=in_ap, **kw)
        sc[1] += 16
        ins_.then_inc(s, 16)
        tok = (s, sc[1])
        in_obj.r[id(s)] = tok
        out_obj.w[id(s)] = tok
        out_obj.r = {}
        self.n_ins += 1
        return ins_

    def finish(self, outs):
        self.barrier()

    def close(self):
        while self._ctx:
            self._ctx.pop().__exit__(None, None, None)
        while self._perm:
            self._perm.pop()[0].__exit__(None, None, None)

    def mm(self, out_o, out_ap, l_o, l_ap, r_o, r_ap, start=True, stop=True):
        nc = self.nc
        return self.op(self.pe, lambda: nc.tensor.matmul(out_ap, lhsT=l_ap, rhs=r_ap, start=start, stop=stop),
                       [out_o], [l_o, r_o])

    def tr(self, out_o, out_ap, in_o, in_ap, id_o, id_ap):
        nc = self.nc
        return self.op(self.pe, lambda: nc.tensor.transpose(out_ap, in_ap, id_ap), [out_o], [in_o, id_o])

    def tt(self, eng, out_o, out_ap, a_o, a_ap, b_o, b_ap, op):
        return self.op(eng, lambda: eng.e.tensor_tensor(out=out_ap, in0=a_ap, in1=b_ap, op=op), [out_o], [a_o, b_o])

    def ts(self, eng, out_o, out_ap, a_o, a_ap, s1, s2, op0, op1=None, extra=()):
        if op1 is None:
            return self.op(eng, lambda: eng.e.tensor_scalar(out=out_ap, in0=a_ap, scalar1=s1, scalar2=None, op0=op0),
                           [out_o], [a_o] + list(extra))
        return self.op(eng, lambda: eng.e.tensor_scalar(out=out_ap, in0=a_ap, scalar1=s1, scalar2=s2, op0=op0, op1=op1),
                       [out_o], [a_o] + list(extra))

    def stt(self, eng, out_o, out_ap, a_o, a_ap, sc, b_o, b_ap, op0, op1, extra=()):
        return self.op(eng, lambda: eng.e.scalar_tensor_tensor(out=out_ap, in0=a_ap, scalar=sc, in1=b_ap, op0=op0, op1=op1),
                       [out_o], [a_o, b_o] + list(extra))

    def actv(self, out_o, out_ap, a_o, a_ap, func, bias=0.0, scale=1.0, extra=()):
        nc = self.nc
        return self.op(self.act, lambda: nc.scalar.activation(out=out_ap, in_=a_ap, func=func, bias=bias, scale=scale),
                       [out_o], [a_o] + list(extra))

    def cp(self, eng, out_o, out_ap, a_o, a_ap):
        if eng is self.act:
            nc = self.nc
            return self.op(eng, lambda: nc.scalar.copy(out=out_ap, in_=a_ap), [out_o], [a_o])
        return self.op(eng, lambda: eng.e.tensor_copy(out=out_ap, in_=a_ap), [out_o], [a_o])

    def memset(self, eng, o, ap, val):
        return self.op(eng, lambda: eng.e.memset(ap, val), [o], [])


def bc(ap, shape):
    return ap.broadcast_to(list(shape))


class Ctx:
    pass


def load_w_bf16(k, c, dst, dst_ap_fn, wsrc, w_ap, ncols, kchunks=8, eng=None):
    eng = eng or k.pool
    st = c.wstage[c.wstage_i % 2]
    c.wstage_i += 1
    sv = st[:, 0:kchunks * ncols].rearrange("p (k n) -> p k n", k=kchunks)
    k.dma(st, sv, wsrc, w_ap.rearrange("(kc p) n -> p kc n", p=128))
    k.cp(eng, dst, dst_ap_fn, st, sv)


def norm_mod(k, c, xs, tiles, ab_of_tile, hT, col0=0):
    for i, t in enumerate(tiles):
        xt = c.xt[i % 2]
        k.dma(xt, xt[:], xs, xs.ap()[t * 128:(t + 1) * 128, :])
        sq = c.sq
        k.tt(k.dve, sq, sq[:], xt, xt[:], xt, xt[:], ALU.mult)
        ss = c.ss
        k.op(k.dve, lambda: k.nc.vector.reduce_sum(out=ss[:, 0:1], in_=sq[:], axis=AX.X), [ss], [sq])
        k.actv(ss, ss[:, 1:2], ss, ss[:, 0:1], AF.Ln, bias=c.epsb[:, 0:1], scale=1.0 / D, extra=[c.epsb])
        k.actv(ss, ss[:, 2:3], ss, ss[:, 1:2], AF.Exp, scale=-0.5)
        k.ts(k.dve, sq, sq[:], xt, xt[:], ss[:, 2:3], None, ALU.mult, extra=[ss])
        a, b = ab_of_tile(t)
        for half in range(2):
            pt = c.pT[half]
            for j in range(4):
                kc = half * 4 + j
                k.tr(pt, pt[:, j, :], sq, sq[:, kc * 128:(kc + 1) * 128], c.id32, c.id32[:])
            for j in range(4):
                kc = half * 4 + j
                k.actv(hT, hT[:, kc, col0 + i * 128: col0 + (i + 1) * 128], pt, pt[:, j, :], AF.Identity,
                       bias=b[0][:, b[1] + kc: b[1] + kc + 1], scale=a[0][:, a[1] + kc:a[1] + kc + 1], extra=[a[0], b[0]])


def load_T(k, c, dst_o, dst_ap, src_o, src_ap, nrows, ncols=128):
    st = c.ltst[c.lt_i % 2]; pt = c.ltps[c.lt_i % 2]; c.lt_i += 1
    k.dma(st, st[0:nrows, 0:ncols], src_o, src_ap)
    k.tr(pt, pt[0:ncols, 0:nrows], st, st[0:nrows, 0:ncols], c.id32, c.id32[0:nrows, 0:nrows])
    k.cp(k.dve, dst_o, dst_ap, pt, pt[0:ncols, 0:nrows])


def tok_blocks(tiles_n):
    out, s = [], 0
    while s < tiles_n:
        n = min(4, tiles_n - s)
        out.append((s, n))
        s += n
    return out


def build_program(stop_after=None, debug=False):
    nc = bass.Bass("TRN2", target_bir_lowering=False)
    k = K(nc)
    c = Ctx()
    c.wstage_i = 0
    c.stop = stop_after
    I = {}

    def inp(name, shape, dt=F32):
        I[name] = k.dram(name, shape, dt, kind="ExternalInput")
        return I[name]

    x_in = inp("x", [NLAT, D]); cvec = inp("c", [1, D]); ctx_in = inp("ctx", [NCTX, D]); c_ctx = inp("c_ctx", [1, D])
    ada_w = inp("ada_w", [2, D, 6 * D]); ada_b = inp("ada_b", [2, 6 * D])
    norm1_w = inp("norm1_w", [2, D]); norm2_w = inp("norm2_w", [2, D])
    ffn_w1 = inp("ffn_w1", [2, D, HID]); ffn_w3 = inp("ffn_w3", [2, D, HID]); ffn_w2 = inp("ffn_w2", [2, HID, D])
    final_w = inp("final_norm_w", [1, D])
    ev_w_in = inp("ev_w_in", [D, 2576]); ev_ib = inp("ev_i_bias", [1, 8]); ev_fb = inp("ev_f_bias", [1, 8])
    ev_hw = inp("ev_head_norm_w", [1, 512])
    ev_lre = inp("ev_lam_re", [2, 32, 64]); ev_lim = inp("ev_lam_im", [2, 32, 64]); ev_ldt = inp("ev_log_dt", [1, 64])
    ev_bre = inp("ev_b_re", [2, 32, 64, 16]); ev_bim = inp("ev_b_im", [2, 32, 64, 16])
    ev_cre = inp("ev_c_re", [1024, 64]); ev_cim = inp("ev_c_im", [1024, 64])
    ev_d = inp("ev_d", [1, 512]); ev_wglu = inp("ev_w_glu", [512, 1024]); ev_wout = inp("ev_w_out", [D, D])
    od_w_in = inp("od_w_in", [D, 4128]); od_conv = inp("od_conv_w", [9, 3072])
    od_alog = inp("od_a_log", [1, 16]); od_dtb = inp("od_dt_bias", [1, 16]); od_hw = inp("od_head_norm_w", [1, D])
    od_wout = inp("od_w_out", [D, D])
    cid32 = inp("k_id32", [128, 128]); cidb = inp("k_idb", [128, 128], BF16)
    ctri = inp("k_tri", [2, 64, 64])
    cmaskM = inp("k_maskM", [2, 128, 128])
    cstrict = inp("k_strict", [2, 64, 64])
    out_d = k.dram("out", [NLAT, D], F32, kind="ExternalOutput")

    xs = k.dram("xs", [T, D])
    modv = k.dram("modv", [2, 2, 6 * D])
    dbg = {}

    c.id32 = k.sb("id32", [128, 128]); k.dma(c.id32, c.id32[:], cid32, cid32.ap())
    c.idb = k.sb("idb", [128, 128], BF16); k.dma(c.idb, c.idb[:], cidb, cidb.ap())
    c.epsb = k.sb("epsb", [128, 2]); k.memset(k.dve, c.epsb, c.epsb[:, 0:1], EPS); k.memset(k.dve, c.epsb, c.epsb[:, 1:2], 0.5 * PI)

    k.dma(xs, xs.ap()[0:NCTX, :], ctx_in, ctx_in.ap())
    k.dma(xs, xs.ap()[NCTX:T, :], x_in, x_in.ap())
    with k.scope():
        sT = k.sb("sT", [128, 8, 2])
        c.ltst = [k.sb("ltst%d" % i, [64, 128]) for i in range(2)]
        c.ltps = [k.ps("ltps%d" % i, [128, 64]) for i in range(2)]
        c.lt_i = 0
        load_T(k, c, sT, sT[:, :, 0], cvec, cvec.ap().rearrange("o (kc p) -> (o kc) p", p=128), 8)
        load_T(k, c, sT, sT[:, :, 1], c_ctx, c_ctx.ap().rearrange("o (kc p) -> (o kc) p", p=128), 8)
        sS = k.sb("sS", [128, 8, 2])
        k.actv(sS, sS[:], sT, sT[:], AF.Silu)
        wst = [k.sb("adw%d" % i, [128, 8, 512]) for i in range(2)]
        pm = [k.ps("pm%d" % i, [128, 512]) for i in range(2)]
        brow = k.sb("brow", [2, 6 * D]); mrow = k.sb("mrow", [2, 6 * D])
        for li in range(2):
            k.dma(brow, brow[:], ada_b, ada_b.ap()[li:li + 1, :].partition_broadcast(2).rearrange("p o n -> p (o n)"))
            for j in range(12):
                w = wst[j % 2]
                k.dma(w, w[:], ada_w, ada_w.ap()[li, :, j * 512:(j + 1) * 512].rearrange("(kc p) n -> p kc n", p=128))
                p = pm[j % 2]
                for kc in range(8):
                    k.mm(p, p[0:2, :], sS, sS[:, kc, :], w, w[:, kc, :], start=(kc == 0), stop=(kc == 7))
                k.tt(k.dve, mrow, mrow[:, j * 512:(j + 1) * 512], p, p[0:2, :], brow, brow[:, j * 512:(j + 1) * 512], ALU.add)
            k.dma(modv, modv.ap()[li], mrow, mrow[:], q=k.pool)

    if stop_after == "ada":
        k.finish([])
        k.close()
        return nc, ["modv"]

    modF = k.sb("modF", [128, 2, 2, 48])
    nwF = k.sb("nwF", [128, 2, 2, 8])
    with k.scope():
        c.ltst = [k.sb("ltst%d" % i, [64, 128]) for i in range(2)]
        c.ltps = [k.ps("ltps%d" % i, [128, 64]) for i in range(2)]
        c.lt_i = 0
        for li in range(2):
            for who in range(2):
                load_T(k, c, modF, modF[:, li, who, :], modv, modv.ap()[li, who, :].rearrange("(c p) -> c p", p=128), 48)
        for wi, nw in enumerate((norm1_w, norm2_w)):
            for li in range(2):
                load_T(k, c, nwF, nwF[:, wi, li, :], nw, nw.ap()[li, :].rearrange("(c p) -> c p", p=128), 8)
    aF = k.sb("aF", [128, 2, 2, 2, 8])
    for wi in range(2):
        for li in range(2):
            for who in range(2):
                sc0 = 8 if wi == 0 else 32
                k.stt(k.dve, aF, aF[:, wi, li, who, :], modF, modF[:, li, who, sc0:sc0 + 8], 1.0, nwF, nwF[:, wi, li, :],
                      ALU.add, ALU.mult)

    def ab_fn(wi, li):
        sh0 = 0 if wi == 0 else 24

        def f(t):
            who = 1 if t < 2 else 0
            a_flat = aF.h[:].rearrange("p a b c d -> p (a b c d)")
            b_flat = modF.h[:].rearrange("p a b c -> p (a b c)")
            return ((_View(aF, a_flat), ((wi * 2 + li) * 2 + who) * 8), (_View(modF, b_flat), (li * 2 + who) * 48 + sh0))
        return f

    def load_gate(dst, li, who, part):
        k.dma(dst, dst[:], modv, modv.ap()[li, who:who + 1, part * D:(part + 1) * D].partition_broadcast(128).rearrange("p o n -> p (o n)"))

    def ffn_phase(li, tiles):
        with k.scope():
            c.xt = [k.sb("xt%d" % i, [128, D]) for i in range(2)]
            c.sq = k.sb("sq", [128, D]); c.ss = k.sb("ss", [128, 4])
            c.pT = [k.ps("pT%d" % i, [128, 4, 128]) for i in range(2)]
            c.wstage = [k.sb("wst%d" % i, [128, 2048]) for i in range(2)]
            ntl = len(tiles)
            half_n = (ntl + 1) // 2
            gate = [k.sb("gate%d" % w, [128, D]) for w in range(2)]
            load_gate(gate[0], li, 0, 5); load_gate(gate[1], li, 1, 5)
            hT = k.sb("hT", [128, 8, half_n * 128], BF16)
            gT = k.sb("gT", [128, 22, half_n * 128], BF16)
            w2b = k.sb("w2b", [128, 22, D], BF16)
            w1b = [k.sb("w1b%d" % i, [128, 8, 256], BF16) for i in range(2)]
            w3b = [k.sb("w3b%d" % i, [128, 8, 256], BF16) for i in range(2)]
            p1 = [k.ps("p1_%d" % i, [128, 512]) for i in range(2)]
            p3 = [k.ps("p3_%d" % i, [128, 512]) for i in range(2)]
            py = [k.ps("py%d" % i, [128, 512]) for i in range(2)]
            sg = [k.sb("sg%d" % i, [128, 512]) for i in range(2)]
            yo = [k.sb("yo%d" % i, [128, D]) for i in range(2)]
            for jb in range(0, 22, 4):
                n = min(4, 22 - jb)
                for cb in range(2):
                    st = c.wstage[c.wstage_i % 2]; c.wstage_i += 1
                    sv = st[:, 0:n * 512].rearrange("p (j n) -> p j n", j=n)
                    k.dma(st, sv, ffn_w2, ffn_w2.ap()[li, jb * 128:(jb + n) * 128, cb * 512:(cb + 1) * 512]
                          .rearrange("(j p) n -> p j n", p=128))
                    k.cp(k.pool, w2b, w2b[:, jb:jb + n, cb * 512:(cb + 1) * 512], st, sv)
            it = 0
            for hs in range(0, ntl, half_n):
                ht = tiles[hs:hs + half_n]
                norm_mod(k, c, xs, ht, ab_fn(1, li), hT)
                blocks = tok_blocks(len(ht))
                for jb in range(0, 22, 2):
                    n = 2
                    wa, wb = w1b[(jb // 2) % 2], w3b[(jb // 2) % 2]
                    load_w_bf16(k, c, wa, wa[:, :, 0:n * 128], ffn_w1, ffn_w1.ap()[li, :, jb * 128:(jb + n) * 128], n * 128)
                    load_w_bf16(k, c, wb, wb[:, :, 0:n * 128], ffn_w3, ffn_w3.ap()[li, :, jb * 128:(jb + n) * 128], n * 128, eng=k.dve)
                    for jj in range(n):
                        j = jb + jj
                        for (b0, bn) in blocks:
                            q1, q3, s_ = p1[it % 2], p3[it % 2], sg[it % 2]; it += 1
                            cols = slice(b0 * 128, (b0 + bn) * 128)
                            w_ = bn * 128
                            for kc in range(8):
                                k.mm(q1, q1[:, 0:w_], wa, wa[:, kc, jj * 128:(jj + 1) * 128], hT, hT[:, kc, cols], kc == 0, kc == 7)
                            for kc in range(8):
                                k.mm(q3, q3[:, 0:w_], wb, wb[:, kc, jj * 128:(jj + 1) * 128], hT, hT[:, kc, cols], kc == 0, kc == 7)
                            k.actv(s_, s_[:, 0:w_], q1, q1[:, 0:w_], AF.Silu)
                            k.tt(k.dve, gT, gT[:, j, cols], s_, s_[:, 0:w_], q3, q3[:, 0:w_], ALU.mult)
                for i, t in enumerate(ht):
                    xt = c.xt[i % 2]
                    k.dma(xt, xt[:], xs, xs.ap()[t * 128:(t + 1) * 128, :])
                    g = gate[1 if t < 2 else 0]
                    y = yo[i % 2]
                    for cb in range(2):
                        p = py[cb]
                        for j in range(22):
                            k.mm(p, p[:], gT, gT[:, j, i * 128:(i + 1) * 128], w2b, w2b[:, j, cb * 512:(cb + 1) * 512], j == 0, j == 21)
                        k.tt(k.dve, y, y[:, cb * 512:(cb + 1) * 512], p, p[:], g, g[:, cb * 512:(cb + 1) * 512], ALU.mult)
                    k.tt(k.pool, y, y[:], y, y[:], xt, xt[:], ALU.add)
                    k.dma(xs, xs.ap()[t * 128:(t + 1) * 128, :], y, y[:], q=k.pool)

    def final_phase():
        with k.scope():
            xt = [k.sb("fx%d" % i, [128, D]) for i in range(2)]
            sq = [k.sb("fs%d" % i, [128, D]) for i in range(2)]
            ss = k.sb("fss", [128, 4])
            fw = k.sb("fw", [128, D])
            k.dma(fw, fw[:], final_w, final_w.ap().partition_broadcast(128).rearrange("p o n -> p (o n)"))
            for i in range(16):
                t = i + 2
                x_, s_ = xt[i % 2], sq[i % 2]
                k.dma(x_, x_[:], xs, xs.ap()[t * 128:(t + 1) * 128, :])
                k.tt(k.dve, s_, s_[:], x_, x_[:], x_, x_[:], ALU.mult)
                k.op(k.dve, lambda: nc.vector.reduce_sum(out=ss[:, 0:1], in_=s_[:], axis=AX.X), [ss], [s_])
                k.actv(ss, ss[:, 1:2], ss, ss[:, 0:1], AF.Ln, bias=c.epsb[:, 0:1], scale=1.0 / D, extra=[c.epsb])
                k.actv(ss, ss[:, 2:3], ss, ss[:, 1:2], AF.Exp, scale=-0.5)
                k.stt(k.dve, s_, s_[:], x_, x_[:], ss[:, 2:3], fw, fw[:], ALU.mult, ALU.mult, extra=[ss])
                k.dma(out_d, out_d.ap()[i * 128:(i + 1) * 128, :], s_, s_[:], q=k.pool)

    env = dict(locals())
    layer0(k, c, env)
    if stop_after in ("proj0", "ml_0", "ml_1", "ml_2", "ml_3", "ml_4", "ml_5", "ml_5a", "ml_5b", "ml_a", "ml_b", "mlstm", "s5", "mix0"):
        k.finish([]); k.close(); return nc, ["xs"]
    ffn_phase(0, list(range(NT)))
    if stop_after == "ffn0":
        k.finish([]); k.close(); return nc, ["xs"]
    layer1(k, c, env)
    if stop_after == "mix1":
        k.finish([]); k.close(); return nc, ["xs"]
    ffn_phase(1, list(range(2, NT)))
    final_phase()
    k.finish([out_d])
    k.close()
    return nc, ["out"]


class _View:
    def __init__(self, obj, flat):
        self.obj, self.flat = obj, flat

    @property
    def space(self):
        return self.obj.space

    @property
    def w(self):
        return self.obj.w

    @property
    def r(self):
        return self.obj.r

    def __getitem__(self, idx):
        return self.flat[idx]


LAYER_FUNCS = []


class E:
    def __init__(self, d):
        self.__dict__.update(d)


ORDB = [3, 2, 1, 0] + list(range(35, 3, -1))
ORDB8 = list(range(31, -1, -1)) + list(range(287, 31, -1))


def layer0(k, c, env):
    e = E(env)
    nc = k.nc
    xs = e.xs
    qT_d = k.dram("qT_d", [512, T], BF16); kT_d = k.dram("kT_d", [512, T], BF16)
    ktok_d = k.dram("ktok_d", [T, 512], BF16); v_d = k.dram("v_d", [T, 512], BF16)
    o_d = k.dram("o_d", [T, 512]); g_d = k.dram("g_d", [T, 16]); u_d = k.dram("u_d", [T, 512])
    hA_d = k.dram("hA_d", [2, T, 512]); yS_d = k.dram("yS_d", [T, 512])
    c.l0 = dict(qT=qT_d, kT=kT_d, ktok=ktok_d, v=v_d, o=o_d, g=g_d, u=u_d, hA=hA_d, yS=yS_d)

    with k.scope():
        c.xt = [k.sb("xt%d" % i, [128, D]) for i in range(2)]
        c.sq = k.sb("sq", [128, D]); c.ss = k.sb("ss", [128, 4])
        c.pT = [k.ps("pT%d" % i, [128, 4, 128]) for i in range(2)]
        c.wstage = [k.sb("wst%d" % i, [128, 4096]) for i in range(2)]
        hT = k.sb("hT", [128, 8, T], BF16)
        norm_mod(k, c, xs, list(range(NT)), e.ab_fn(0, 0), hT)
        wb = k.sb("wb", [128, 8, 2576], BF16)
        for cb in range(0, 2576, 512):
            n = min(512, 2576 - cb)
            load_w_bf16(k, c, wb, wb[:, :, cb:cb + n], e.ev_w_in, e.ev_w_in.ap()[:, cb:cb + n], n,
                        eng=(k.pool if (cb // 512) % 2 else k.dve))
        pp = [k.ps("pp%d" % i, [128, 512]) for i in range(4)]
        fst = [k.sb("fst%d" % i, [128, T], BF16) for i in range(2)]
        blocks = [(0, 512), (512, 512), (1024, 512), (1536, 512), (2048, 256)]
        it = 0
        for which, dst in ((0, qT_d), (1, kT_d)):
            for h in range(4):
                st = fst[(which * 4 + h) % 2]
                col = which * 512 + h * 128
                for (t0, tn) in blocks:
                    p = pp[it % 4]; it += 1
                    for kc in range(8):
                        k.mm(p, p[:, 0:tn], wb, wb[:, kc, col:col + 128], hT, hT[:, kc, t0:t0 + tn], kc == 0, kc == 7)
                    k.actv(st, st[:, t0:t0 + tn], p, p[:, 0:tn], AF.Copy, scale=(1.0 if which == 0 else 1.0 / math.sqrt(128)))
                k.dma(dst, dst.ap()[h * 128:(h + 1) * 128, :], st, st[:], q=k.pool)
        tkb = [k.sb("tkb%d" % i, [128, 1024], BF16) for i in range(2)]
        tof = [k.sb("tof%d" % i, [128, 1040]) for i in range(2)]
        for t in range(NT):
            kb, of = tkb[t % 2], tof[t % 2]
            tok = slice(t * 128, (t + 1) * 128)
            for bi, (col, n) in enumerate(((512, 512), (1024, 512), (1536, 512), (2064, 512), (2048, 16))):
                p = pp[it % 4]; it += 1
                for kc in range(8):
                    k.mm(p, p[:, 0:n], hT, hT[:, kc, tok], wb, wb[:, kc, col:col + n], kc == 0, kc == 7)
                if bi == 0:
                    k.actv(kb, kb[:, 0:512], p, p[:, 0:512], AF.Copy, scale=1.0 / math.sqrt(128))
                elif bi == 1:
                    k.cp(k.dve, kb, kb[:, 512:1024], p, p[:, 0:512])
                elif bi == 2:
                    k.cp(k.act, of, of[:, 0:512], p, p[:, 0:512])
                elif bi == 3:
                    k.cp(k.dve, of, of[:, 512:1024], p, p[:, 0:512])
                else:
                    k.cp(k.act, of, of[:, 1024:1040], p, p[:, 0:16])
            k.dma(ktok_d, ktok_d.ap()[tok, :], kb, kb[:, 0:512], q=k.pool)
            k.dma(v_d, v_d.ap()[tok, :], kb, kb[:, 512:1024], q=k.pool)
            k.dma(o_d, o_d.ap()[tok, :], of, of[:, 0:512], q=k.pool)
            k.dma(u_d, u_d.ap()[tok, :], of, of[:, 512:1024], q=k.pool)
            k.dma(g_d, g_d.ap()[tok, :], of, of[:, 1024:1040], q=k.pool)
    if c.stop == "proj0":
        return
    mlstm_phase(k, c, e)
    if c.stop in ("mlstm", "ml_0", "ml_1", "ml_2", "ml_3", "ml_4", "ml_5", "ml_5a", "ml_5b", "ml_a", "ml_b"):
        return
    s5_phase(k, c, e)
    if c.stop == "s5":
        return
    finish0_phase(k, c, e)


def mlstm_phase(k, c, e):
    nc = k.nc
    L = c.l0
    with k.scope():
        qT = k.sb("qT", [128, 4, T], BF16); kT = k.sb("kT", [128, 4, T], BF16)
        for h in range(4):
            k.dma(qT, qT[:, h, :], L['qT'], L['qT'].ap()[h * 128:(h + 1) * 128, :])
            k.dma(kT, kT[:, h, :], L['kT'], L['kT'].ap()[h * 128:(h + 1) * 128, :])
        ktok = k.sb("ktok", [64, 36, 512], BF16)
        v1 = k.sb("v1", [64, 36, 4, 132], BF16)
        for c0 in range(0, 36, 6):
            k.dma(ktok, ktok[:, c0:c0 + 6, :], L['ktok'], L['ktok'].ap()[c0 * 64:(c0 + 6) * 64, :].rearrange("(c l) n -> l c n", l=64))
            for h in range(4):
                k.dma(v1, v1[:, c0:c0 + 6, h, 0:128], L['v'],
                      L['v'].ap()[c0 * 64:(c0 + 6) * 64, h * 128:(h + 1) * 128].rearrange("(c l) n -> l c n", l=64))
        if c.stop == "ml_0":
            return
        k.memset(k.dve, v1, v1[:, :, :, 128:132], 1.0)
        if c.stop == "ml_1":
            return
        g = k.sb("g", [64, 36, 16])
        for c0 in range(0, 36, 6):
            k.dma(g, g[:, c0:c0 + 6, :], L['g'], L['g'].ap()[c0 * 64:(c0 + 6) * 64, :].rearrange("(c l) n -> l c n", l=64))
        fb = k.sb("fb", [64, 8]); ib = k.sb("ib", [64, 8])
        k.dma(fb, fb[:], e.ev_fb, e.ev_fb.ap().partition_broadcast(64).rearrange("p o n -> p (o n)"))
        k.dma(ib, ib[:], e.ev_ib, e.ev_ib.ap().partition_broadcast(64).rearrange("p o n -> p (o n)"))
        tri = k.sb("tri", [64, 2, 64]); ones = k.sb("ones", [64, 128])
        k.dma(tri, tri[:], e.ctri, e.ctri.ap().rearrange("r s l -> s r l"))
        k.memset(k.dve, ones, ones[:], 1.0)
        if c.stop == "ml_2":
            return
        z = k.sb("z", [64, 2, 36, 4]); nlf = k.sb("nlf", [64, 2, 36, 4]); ig = k.sb("ig", [64, 2, 36, 4])
        A = k.sb("A", [64, 2, 36, 4]); Bk = k.sb("Bk", [64, 2, 36, 4]); gdec = k.sb("gdec", [128, 2, 36, 4])
        for d in range(2):
            k.tt(k.dve, z, z[:, d], g, g[:, :, 8 + 4 * d:12 + 4 * d], fb, bc(fb[:, None, 4 * d:4 * d + 4], [64, 36, 4]), ALU.add)
            k.tt(k.dve, ig, ig[:, d], g, g[:, :, 4 * d:4 * d + 4], ib, bc(ib[:, None, 4 * d:4 * d + 4], [64, 36, 4]), ALU.add)
        k.actv(z, z[:], z, z[:], AF.Exp, scale=-1.0)
        k.actv(nlf, nlf[:], z, z[:], AF.Ln, bias=1.0)
        if c.stop == "ml_3":
            return
        with k.scope():
            pF = k.ps("pF", [64, 2, 144]); pG = k.ps("pG", [128, 288])
            for d in range(2):
                k.mm(pF, pF[:, d, :], tri, tri[:, d, :], nlf, nlf[:, d].rearrange("p c h -> p (c h)"))
            k.mm(pG, pG[:], ones, ones[:], nlf, nlf[:].rearrange("p d c h -> p (d c h)"))
            if c.stop == "ml_4":
                k.cp(k.dve, A, A[:].rearrange("p d c h -> p d (c h)"), pF, pF[:])
                k.cp(k.dve, gdec, gdec[:].rearrange("p d c h -> p (d c h)"), pG, pG[:])
            Af = A[:].rearrange("p d c h -> p d (c h)"); Bf = Bk[:].rearrange("p d c h -> p d (c h)")
            if c.stop != "ml_4":
                if c.stop != "ml_5b":
                    k.actv(A, Af, pF, pF[:], AF.Exp, scale=-1.0)
                if c.stop != "ml_5a":
                    k.tt(k.dve, Bk, Bf, ig, ig[:].rearrange("p d c h -> p d (c h)"), pF, pF[:], ALU.add)
                if c.stop not in ("ml_5", "ml_5a", "ml_5b"):
                    k.actv(Bk, Bk[:], Bk, Bk[:], AF.Exp)
                    k.actv(gdec, gdec[:].rearrange("p d c h -> p (d c h)"), pG, pG[:], AF.Exp, scale=-1.0)
        if c.stop in ("ml_a", "ml_4", "ml_5", "ml_5a", "ml_5b"):
            return
        C32 = [k.sb("C32_%d" % d, [128, 4, 132]) for d in range(2)]
        Cb = [k.sb("Cb_%d" % d, [128, 4, 132], BF16) for d in range(2)]
        for d in range(2):
            k.memset(k.dve, C32[d], C32[d][:], 0.0)
            k.memset(k.dve, Cb[d], Cb[d][:], 0.0)
        pS = [k.ps("pS%d" % d, [64, 4, 64]) for d in range(2)]
        pN = [[k.ps("pN%d_%d" % (d, i), [64, 2, 256]) for i in range(2)] for d in range(2)]
        pC = [k.ps("pC%d" % i, [128, 2, 256]) for i in range(2)]
        kt = [k.sb("kt%d" % d, [64, 4, 128], BF16) for d in range(2)]
        MB = [k.sb("MB%d" % d, [64, 4, 64]) for d in range(2)]
        Pt = [k.sb("Pt%d" % d, [64, 4, 64], BF16) for d in range(2)]
        sm = [k.sb("sm%d" % d, [64, 4, 4]) for d in range(2)]
        ho = [k.sb("ho%d" % d, [64, 4, 128]) for d in range(2)]
        for step in range(36):
            for d in range(2):
                ch = step if d == 0 else ORDB[step]
                tok = slice(ch * 64, (ch + 1) * 64)
                Bs = Bk[:, d, ch, :]
                As = A[:, d, ch, :]
                k.tt(k.dve, kt[d], kt[d][:], ktok, ktok[:, ch, :].rearrange("p (h e) -> p h e", h=4),
                     Bk, bc(Bs[:, :, None], [64, 4, 128]), ALU.mult)
                k.tt(k.dve, MB[d], MB[d][:], tri, bc(tri[:, d:d + 1, :], [64, 4, 64]), Bk, bc(Bs[:, :, None], [64, 4, 64]), ALU.mult)
                for h in range(4):
                    k.mm(pS[d], pS[d][:, h, :], kT, kT[:, h, tok], qT, qT[:, h, tok])
                k.tt(k.dve, Pt[d], Pt[d][:], pS[d], pS[d][:], MB[d], MB[d][:], ALU.mult)
                for h in range(4):
                    pn = pN[d][h // 2]
                    k.mm(pn, pn[:, h % 2, 0:132], Pt[d], Pt[d][:, h, :], v1, v1[:, ch, h, :], True, False)
                    k.mm(pn, pn[:, h % 2, 0:132], qT, qT[:, h, tok], Cb[d], Cb[d][:, h, :], False, True)
                s_ = sm[d]
                for i in range(2):
                    pn = pN[d][i]
                    k.tt(k.dve, s_, s_[:, 2 * i:2 * i + 2, 0], pn, pn[:, :, 128], A, As[:, 2 * i:2 * i + 2], ALU.mult)
                k.stt(k.dve, s_, s_[:, :, 1], s_, s_[:, :, 0], -1.0, s_, s_[:, :, 0], ALU.mult, ALU.max)
                k.ts(k.dve, s_, s_[:, :, 1], s_, s_[:, :, 1], 1.0, None, ALU.max)
                k.op(k.dve, lambda: nc.vector.reciprocal(out=s_[:, :, 2], in_=s_[:, :, 1]), [s_], [s_])
                k.tt(k.dve, s_, s_[:, :, 3], s_, s_[:, :, 2], A, As, ALU.mult)
                for i in range(2):
                    pn = pN[d][i]
                    k.tt(k.dve, ho[d], ho[d][:, 2 * i:2 * i + 2, :], pn, pn[:, :, 0:128],
                         s_, bc(s_[:, 2 * i:2 * i + 2, 3:4], [64, 2, 128]), ALU.mult)
                k.dma(L['hA'], L['hA'].ap()[d, tok, :], ho[d], ho[d][:].rearrange("p h e -> p (h e)"), q=k.pool)
                for i in range(2):
                    pc_ = pC[i]
                    for hh in range(2):
                        h = 2 * i + hh
                        k.mm(pc_, pc_[:, hh, 0:132], kt[d], kt[d][:, h, :], v1, v1[:, ch, h, :])
                    k.tt(k.dve, C32[d], C32[d][:, 2 * i:2 * i + 2, :], pc_, pc_[:, :, 0:132], C32[d], C32[d][:, 2 * i:2 * i + 2, :], ALU.add)
                k.tt(k.dve, C32[d], C32[d][:], C32[d], C32[d][:], gdec, bc(gdec[:, d, ch, :][:, :, None], [128, 4, 132]), ALU.mult)
                k.cp(k.act, Cb[d], Cb[d][:], C32[d], C32[d][:])
            if c.stop == "ml_b" and step == 0:
                return


def s5_phase(k, c, e):
    nc = k.nc
    L = c.l0
    TWO_PI = 2 * PI
    with k.scope():
        Mw = k.sb("Mw", [128, 64, 128], BF16)
        WT = [k.sb("WT%d" % i, [128, 64, 64], BF16) for i in range(2)]
        RC = [k.sb("RC%d" % i, [64, 64, 128], BF16) for i in range(2)]
        AR2 = k.sb("AR2", [64, 2, 64]); AI2 = k.sb("AI2", [64, 2, 64])
        with k.scope():
            lre = k.sb("lre", [64, 64]); lim = k.sb("lim", [64, 64]); dt = k.sb("dt", [64, 64])
            c.ltst = [k.sb("ltst%d" % i, [64, 128]) for i in range(2)]
            c.ltps = [k.ps("ltps%d" % i, [128, 64]) for i in range(2)]
            c.lt_i = 0
            load_T(k, c, lre, lre[:], e.ev_lre, e.ev_lre.ap().rearrange("r g n -> (r g) n"), 64, ncols=64)
            load_T(k, c, lim, lim[:], e.ev_lim, e.ev_lim.ap().rearrange("r g n -> (r g) n"), 64, ncols=64)
            k.dma(dt, dt[:], e.ev_ldt, e.ev_ldt.ap().partition_broadcast(64).rearrange("p o n -> p (o n)"))
            k.actv(dt, dt[:], dt, dt[:], AF.Exp)
            ldr = k.sb("ldr", [64, 64]); ang = k.sb("ang", [64, 64])
            k.tt(k.dve, ldr, ldr[:], lre, lre[:], dt, dt[:], ALU.mult)
            k.tt(k.dve, ang, ang[:], lim, lim[:], dt, dt[:], ALU.mult)
            mg = k.sb("mg", [64, 16, 64]); sn = k.sb("sn", [64, 9, 64]); cs = k.sb("cs", [64, 9, 64])
            for ti, tau in enumerate(range(-7, 9)):
                k.actv(mg, mg[:, ti, :], ldr, ldr[:], AF.Exp, scale=float(tau))
            k.memset(k.dve, sn, sn[:, 0, :], 0.0); k.memset(k.dve, cs, cs[:, 0, :], 1.0)
            k.actv(sn, sn[:, 1, :], ang, ang[:], AF.Sin, scale=1.0 / 16)
            k.actv(cs, cs[:, 1, :], ang, ang[:], AF.Sin, bias=c.epsb[0:64, 1:2], scale=1.0 / 16, extra=[c.epsb])
            q1 = k.sb("q1", [64, 64]); q2 = k.sb("q2", [64, 64])
            for _ in range(4):
                k.tt(k.dve, q1, q1[:], sn, sn[:, 1, :], cs, cs[:, 1, :], ALU.mult)
                k.tt(k.dve, q2, q2[:], sn, sn[:, 1, :], sn, sn[:, 1, :], ALU.mult)
                k.ts(k.dve, sn, sn[:, 1, :], q1, q1[:], 2.0, None, ALU.mult)
                k.ts(k.dve, cs, cs[:, 1, :], q2, q2[:], -2.0, 1.0, ALU.mult, ALU.add)
            for tau in range(2, 9):
                k.tt(k.dve, q1, q1[:], cs, cs[:, tau - 1, :], cs, cs[:, 1, :], ALU.mult)
                k.tt(k.dve, q2, q2[:], sn, sn[:, tau - 1, :], sn, sn[:, 1, :], ALU.mult)
                k.tt(k.dve, cs, cs[:, tau, :], q1, q1[:], q2, q2[:], ALU.subtract)
                k.tt(k.dve, q1, q1[:], sn, sn[:, tau - 1, :], cs, cs[:, 1, :], ALU.mult)
                k.tt(k.dve, q2, q2[:], cs, cs[:, tau - 1, :], sn, sn[:, 1, :], ALU.mult)
                k.tt(k.dve, sn, sn[:, tau, :], q1, q1[:], q2, q2[:], ALU.add)
            pwr = k.sb("pwr", [64, 16, 64]); pwi = k.sb("pwi", [64, 16, 64])
            for ti, tau in enumerate(range(-7, 9)):
                at = abs(tau)
                k.tt(k.dve, pwr, pwr[:, ti, :], mg, mg[:, ti, :], cs, cs[:, at, :], ALU.mult)
                if tau >= 0:
                    k.tt(k.dve, pwi, pwi[:, ti, :], mg, mg[:, ti, :], sn, sn[:, at, :], ALU.mult)
                else:
                    k.stt(k.dve, pwi, pwi[:, ti, :], mg, mg[:, ti, :], -1.0, sn, sn[:, at, :], ALU.mult, ALU.mult)
            for s_ in range(2):
                k.cp(k.dve, AR2, AR2[:, s_, :], pwr, pwr[:, 15, :])
            k.ts(k.dve, AI2, AI2[:, 0, :], pwi, pwi[:, 15, :], -1.0, None, ALU.mult)
            k.cp(k.dve, AI2, AI2[:, 1, :], pwi, pwi[:, 15, :])
            nr = k.sb("nr", [64, 64]); den = k.sb("den", [64, 64]); t1 = k.sb("t1", [64, 64]); t2 = k.sb("t2", [64, 64])
            cor = k.sb("cor", [64, 64]); coi = k.sb("coi", [64, 64])
            k.ts(k.dve, nr, nr[:], pwr, pwr[:, 8, :], -1.0, None, ALU.add)
            k.tt(k.dve, den, den[:], lre, lre[:], lre, lre[:], ALU.mult)
            k.tt(k.dve, t1, t1[:], lim, lim[:], lim, lim[:], ALU.mult)
            k.tt(k.dve, den, den[:], den, den[:], t1, t1[:], ALU.add)
            k.op(k.dve, lambda: nc.vector.reciprocal(out=den[:], in_=den[:]), [den], [den])
            k.tt(k.dve, t1, t1[:], nr, nr[:], lre, lre[:], ALU.mult)
            k.tt(k.dve, t2, t2[:], pwi, pwi[:, 8, :], lim, lim[:], ALU.mult)
            k.tt(k.dve, t1, t1[:], t1, t1[:], t2, t2[:], ALU.add)
            k.tt(k.dve, cor, cor[:], t1, t1[:], den, den[:], ALU.mult)
            k.tt(k.dve, t1, t1[:], pwi, pwi[:, 8, :], lre, lre[:], ALU.mult)
            k.tt(k.dve, t2, t2[:], nr, nr[:], lim, lim[:], ALU.mult)
            k.tt(k.dve, t1, t1[:], t1, t1[:], t2, t2[:], ALU.subtract)
            k.tt(k.dve, coi, coi[:], t1, t1[:], den, den[:], ALU.mult)
            bre = k.sb("bre", [64, 64, 16]); bim = k.sb("bim", [64, 64, 16])
            for r in range(2):
                for g0 in range(0, 32, 4):
                    k.dma(bre, bre[:, r * 32 + g0:r * 32 + g0 + 4, :], e.ev_bre, e.ev_bre.ap()[r, g0:g0 + 4].rearrange("g n p -> n g p"))
                    k.dma(bim, bim[:, r * 32 + g0:r * 32 + g0 + 4, :], e.ev_bim, e.ev_bim.ap()[r, g0:g0 + 4].rearrange("g n p -> n g p"))
            bbr = k.sb("bbr", [64, 64, 16]); bbi = k.sb("bbi", [64, 64, 16])
            u1 = k.sb("u1", [64, 64, 16]); u2 = k.sb("u2", [64, 64, 16])
            corb = bc(cor[:, :, None], [64, 64, 16]); coib = bc(coi[:, :, None], [64, 64, 16])
            k.tt(k.dve, u1, u1[:], bre, bre[:], cor, corb, ALU.mult)
            k.tt(k.dve, u2, u2[:], bim, bim[:], coi, coib, ALU.mult)
            k.tt(k.dve, bbr, bbr[:], u1, u1[:], u2, u2[:], ALU.subtract)
            k.tt(k.dve, u1, u1[:], bim, bim[:], cor, corb, ALU.mult)
            k.tt(k.dve, u2, u2[:], bre, bre[:], coi, coib, ALU.mult)
            k.tt(k.dve, bbi, bbi[:], u1, u1[:], u2, u2[:], ALU.add)
            cTr = k.sb("cTr", [64, 64, 16]); cTi = k.sb("cTi", [64, 64, 16])
            cst = k.sb("cst", [128, 8, 64])
            pCt = [k.ps("pCt%d" % i, [64, 4, 128]) for i in range(2)]
            for src, dst in ((e.ev_cre, cTr), (e.ev_cim, cTi)):
                for t0 in range(0, 8, 2):
                    k.dma(cst, cst[:, t0:t0 + 2, :], src, src.ap()[t0 * 128:(t0 + 2) * 128, :].rearrange("(t p) n -> p t n", p=128))
                for t in range(8):
                    p = pCt[t // 4]
                    k.tr(p, p[:, t % 4, :], cst, cst[:, t, :], c.id32, c.id32[:])
                for hf in range(2):
                    k.cp(k.act, dst, dst[:, hf * 32:(hf + 1) * 32, :].rearrange("n g p -> n (g p)"),
                         pCt[hf], pCt[hf][:].rearrange("n t m -> n (t m)"))
            maskM = k.sb("maskM", [128, 2, 128])
            k.dma(maskM, maskM[:], e.cmaskM, e.cmaskM.ap().rearrange("r a b -> a r b"))
            w1_ = k.sb("w1_", [64, 32, 16]); w2_ = k.sb("w2_", [64, 32, 16])

            def cmul(r, powf, sr, si, dr, dr_ap, di, di_ap, neg_im=False):
                rs = slice(r * 32, (r + 1) * 32)
                for i in range(8):
                    ti = powf(i) + 7
                    pr = bc(pwr[:, ti, rs][:, :, None], [64, 32, 16]); pi_ = bc(pwi[:, ti, rs][:, :, None], [64, 32, 16])
                    k.tt(k.dve, w1_, w1_[:], sr, sr[:, rs, :], pwr, pr, ALU.mult)
                    k.tt(k.pool, w2_, w2_[:], si, si[:, rs, :], pwi, pi_, ALU.mult)
                    k.tt(k.dve, dr, dr_ap(i), w1_, w1_[:], w2_, w2_[:], ALU.subtract)
                    k.tt(k.dve, w1_, w1_[:], si, si[:, rs, :], pwr, pr, ALU.mult)
                    k.tt(k.pool, w2_, w2_[:], sr, sr[:, rs, :], pwi, pi_, ALU.mult)
                    if neg_im:
                        k.stt(k.dve, di, di_ap(i), w1_, w1_[:], -1.0, w2_, w2_[:], ALU.mult, ALU.subtract)
                    else:
                        k.tt(k.dve, di, di_ap(i), w1_, w1_[:], w2_, w2_[:], ALU.add)

            EBr = k.sb("EBr", [64, 32, 8, 16]); EBi = k.sb("EBi", [64, 32, 8, 16])
            ECr = k.sb("ECr", [64, 32, 8, 16]); ECi = k.sb("ECi", [64, 32, 8, 16])
            pM = [k.ps("pM%d" % i, [128, 4, 128]) for i in range(2)]
            pW = [k.ps("pW%d" % i, [128, 8, 64]) for i in range(2)]
            for r in range(2):
                sig = (lambda i: i) if r == 0 else (lambda i: 7 - i)
                rs = slice(r * 32, (r + 1) * 32)
                cmul(r, lambda i: -sig(i), bbr, bbi, EBr, lambda i: EBr[:, :, i, :], EBi, lambda i: EBi[:, :, i, :])
                cmul(r, lambda i: sig(i), cTr, cTi, ECr, lambda i: ECr[:, :, i, :], ECi, lambda i: ECi[:, :, i, :], neg_im=True)
                for g0 in range(0, 32, 4):
                    p = pM[(g0 // 4) % 2]
                    for gg in range(4):
                        g = g0 + gg
                        k.mm(p, p[:, gg, :], EBr, EBr[:, g].rearrange("n i p -> n (i p)"), ECr, ECr[:, g].rearrange("n i p -> n (i p)"), True, False)
                        k.mm(p, p[:, gg, :], EBi, EBi[:, g].rearrange("n i p -> n (i p)"), ECi, ECi[:, g].rearrange("n i p -> n (i p)"), False, True)
                    k.tt(k.dve, Mw, Mw[:, r * 32 + g0:r * 32 + g0 + 4, :], p, p[:], maskM, bc(maskM[:, r:r + 1, :], [128, 4, 128]), ALU.mult)
                cmul(r, lambda i: sig(i) + 1, cTr, cTi,
                     RC[0], lambda i: RC[0][:, rs, :].rearrange("n g (j p) -> n g j p", j=8)[:, :, i, :],
                     RC[1], lambda i: RC[1][:, rs, :].rearrange("n g (j p) -> n g j p", j=8)[:, :, i, :], neg_im=True)
                cmul(r, lambda i: 7 - sig(i), bbr, bbi, ECr, lambda i: ECr[:, :, i, :], ECi, lambda i: ECi[:, :, i, :])
                for comp, src in enumerate((ECr, ECi)):
                    for g0 in range(0, 32, 8):
                        p = pW[(g0 // 8) % 2]
                        for gg in range(8):
                            k.tr(p, p[:, gg, :], src, src[:, g0 + gg].rearrange("n i p -> n (i p)"), c.id32, c.id32[0:64, 0:64])
                        k.cp(k.act, WT[comp], WT[comp][:, r * 32 + g0:r * 32 + g0 + 8, :], p, p[:])
        X = k.sb("X", [128, 32, 288], BF16)
        Sa = k.sb("Sa", [64, 2, 64, 290], BF16)
        with k.scope():
            u32 = k.sb("u32", [128, 8, 512]); u16 = k.sb("u16", [128, 32, 128], BF16)
            pX = [k.ps("pX%d" % i, [128, 8, 128], BF16) for i in range(2)]
            it = 0
            for ct, (c0, n) in enumerate(((0, 128), (128, 128), (256, 32))):
                k.dma(u32, u32[0:n], L['u'], L['u'].ap()[8 * c0:8 * (c0 + n), :].rearrange("(c i) ch -> c i ch", i=8))
                k.cp(k.dve, u16, u16[0:n].rearrange("c g (i p) -> c g i p", i=8), u32, u32[0:n].rearrange("c i (g p) -> c g i p", g=32))
                for g0 in range(0, 32, 8):
                    p = pX[it % 2]; it += 1
                    for gg in range(8):
                        g = g0 + gg
                        k.tr(p, p[:, gg, 0:n], u16, u16[0:n, g, :], c.idb, c.idb[0:n, 0:n])
                    k.cp(k.act, X, X[:, g0:g0 + 8, c0:c0 + n], p, p[:, :, 0:n])
        with k.scope():
            pB = [k.ps("pB%d" % i, [64, 288]) for i in range(4)]
            it = 0
            for rg in range(64):
                for comp in range(2):
                    p = pB[it % 4]; it += 1
                    k.mm(p, p[:], WT[comp], WT[comp][:, rg, :], X, X[:, rg % 32, :])
                    off = 0 if rg < 32 else 1
                    k.cp(k.act if it % 2 else k.dve, Sa, Sa[:, comp, rg, off:off + 288], p, p[:])
        R3 = k.sb("R3", [64, 3, 64]); P1 = k.sb("P1", [64, 2, 64]); P2 = k.sb("P2", [64, 2, 64])
        k.memset(k.dve, R3, R3[:], 0.0)
        base = Sa[:, :, 0:32, 0]
        pstride = base.ap[0][0]
        R3v = R3[:, 1:3, :].rearrange("n c (r g) -> n c r g", r=2)
        for step in range(288):
            cf, cb = step, ORDB8[step]
            bv = AP(tensor=Sa.h, offset=base.offset + cf, ap=[[pstride, 64], [64 * 290, 2], [32 * 290 + (cb + 1) - cf, 2], [290, 32]])
            k.tt(k.dve, P1, P1[:], AR2, AR2[:], R3, R3[:, 1:3, :], ALU.mult)
            k.tt(k.dve, P2, P2[:], AI2, AI2[:], R3, R3[:, 0:2, :], ALU.mult)
            k.tt(k.dve, P1, P1[:], P1, P1[:], P2, P2[:], ALU.add)
            k.tt(k.dve, R3, R3v, P1, P1[:].rearrange("n c (r g) -> n c r g", r=2), Sa, bv, ALU.add)
            k.cp(k.dve, R3, R3[:, 0, :], R3, R3[:, 2, :])
            k.cp(k.dve, Sa, bv, R3, R3v)
        k.cp(k.dve, Sa, Sa[:, :, 32:64, 0:1], Sa, Sa[:, :, 32:64, 1:2])
        with k.scope():
            pY = [k.ps("pY%d" % i, [128, 288]) for i in range(2)]
            pZ = [k.ps("pZ%d" % i, [128, 4, 128]) for i in range(2)]
            Yq = k.sb("Yq", [128, 8, 288]); Y2q = [k.sb("Y2q%d" % i, [128, 8, 128]) for i in range(2)]
            it = 0; iz = 0; iy = 0
            for q in range(4):
                for gl in range(8):
                    g = q * 8 + gl
                    p = pY[it % 2]; it += 1
                    k.mm(p, p[:], Mw, Mw[:, g, :], X, X[:, g, :], True, False)
                    k.mm(p, p[:], Mw, Mw[:, 32 + g, :], X, X[:, g, :], False, False)
                    for comp in range(2):
                        k.mm(p, p[:, 1:288], RC[comp], RC[comp][:, g, :], Sa, Sa[:, comp, g, 0:287], False, False)
                    rg = 32 + g
                    for comp in range(2):
                        k.mm(p, p[:, 0:31], RC[comp], RC[comp][:, rg, :], Sa, Sa[:, comp, rg, 2:33], False, False)
                        k.mm(p, p[:, 32:287], RC[comp], RC[comp][:, rg, :], Sa, Sa[:, comp, rg, 34:289], False, False)
                        k.mm(p, p[:, 287:288], RC[comp], RC[comp][:, rg, :], Sa, Sa[:, comp, rg, 0:1], False, comp == 1)
                    k.cp(k.act, Yq, Yq[:, gl, :], p, p[:])
                for ct, (c0, n) in enumerate(((0, 128), (128, 128), (256, 32))):
                    y2 = Y2q[iy % 2]; iy += 1
                    for gl in range(8):
                        if gl % 4 == 0:
                            pz = pZ[iz % 2]; iz += 1
                        k.tr(pz, pz[0:n, gl % 4, :], Yq, Yq[:, gl, c0:c0 + n], c.id32, c.id32[:])
                        k.cp(k.dve if gl % 2 else k.act, y2, y2[0:n, :, gl * 16:(gl + 1) * 16],
                             pz, pz[0:n, gl % 4, :].rearrange("c (j p) -> c j p", j=8))
                    for jh in range(2):
                        k.dma(L['yS'], L['yS'].ap()[8 * c0:8 * (c0 + n), q * 128:(q + 1) * 128].rearrange("(c j) ch -> c j ch", j=8)[:, jh * 4:(jh + 1) * 4, :],
                              y2, y2[0:n, jh * 4:(jh + 1) * 4, :], q=k.pool)


def finish0_phase(k, c, e):
    nc = k.nc
    L = c.l0
    xs = e.xs
    with k.scope():
        c.wstage = [k.sb("wst%d" % i, [128, 4096]) for i in range(2)]
        wglu = k.sb("wglu", [128, 4, 1024], BF16); wout = k.sb("wout", [128, 8, 1024], BF16)
        for cb in range(2):
            load_w_bf16(k, c, wglu, wglu[:, :, cb * 512:(cb + 1) * 512], e.ev_wglu, e.ev_wglu.ap()[:, cb * 512:(cb + 1) * 512], 512, kchunks=4)
            load_w_bf16(k, c, wout, wout[:, :, cb * 512:(cb + 1) * 512], e.ev_wout, e.ev_wout.ap()[:, cb * 512:(cb + 1) * 512], 512)
        hwb = k.sb("hwb", [128, 512]); dsk = k.sb("dsk", [128, 512])
        k.dma(hwb, hwb[:], e.ev_hw, e.ev_hw.ap().partition_broadcast(128).rearrange("p o n -> p (o n)"))
        k.dma(dsk, dsk[:], e.ev_d, e.ev_d.ap().partition_broadcast(128).rearrange("p o n -> p (o n)"))
        gate = [k.sb("gate%d" % w, [128, D]) for w in range(2)]
        e.load_gate(gate[0], 0, 0, 2); e.load_gate(gate[1], 0, 1, 2)
        hA = [k.sb("hA%d" % i, [128, 2, 512]) for i in range(2)]
        ot = [k.sb("ot%d" % i, [128, 512]) for i in range(2)]
        ut = [k.sb("ut%d" % i, [128, 512]) for i in range(2)]
        yt = [k.sb("yt%d" % i, [128, 512]) for i in range(2)]
        xt = [k.sb("xt%d" % i, [128, D]) for i in range(2)]
        w1 = k.sb("fw1", [128, 512]); w2 = k.sb("fw2", [128, 512]); st = k.sb("fst", [128, 8])
        cat = k.sb("cat", [128, D], BF16); ybb = k.sb("ybb", [128, 512], BF16)
        ybT = k.sb("ybT", [128, 4, 128], BF16); catT = k.sb("catT", [128, 8, 128], BF16)
        pTb = k.ps("pTb", [128, 8, 128], BF16)
        pG = [k.ps("pGl%d" % i, [128, 512]) for i in range(2)]
        pO = [k.ps("pO%d" % i, [128, 512]) for i in range(2)]
        yo = k.sb("yo", [128, D])
        for t in range(NT):
            tok = slice(t * 128, (t + 1) * 128)
            h_, o_, u_, y_, x_ = hA[t % 2], ot[t % 2], ut[t % 2], yt[t % 2], xt[t % 2]
            k.dma(h_, h_[:], L['hA'], L['hA'].ap()[:, tok, :].rearrange("r t n -> t r n"))
            k.dma(o_, o_[:], L['o'], L['o'].ap()[tok, :])
            k.dma(u_, u_[:], L['u'], L['u'].ap()[tok, :])
            k.dma(y_, y_[:], L['yS'], L['yS'].ap()[tok, :])
            k.dma(x_, x_[:], xs, xs.ap()[tok, :])
            k.tt(k.dve, w1, w1[:], h_, h_[:, 0, :], h_, h_[:, 1, :], ALU.add)
            k.tt(k.dve, w2, w2[:], w1, w1[:], w1, w1[:], ALU.mult)
            k.op(k.dve, lambda: nc.vector.reduce_sum(out=st[:, 0:4], in_=w2[:].rearrange("p (h e) -> p h e", h=4), axis=AX.X), [st], [w2])
            k.actv(st, st[:, 0:4], st, st[:, 0:4], AF.Ln, bias=c.epsb[:, 0:1], scale=1.0 / 128, extra=[c.epsb])
            k.actv(st, st[:, 4:8], st, st[:, 0:4], AF.Exp, scale=-0.5)
            k.tt(k.dve, w1, w1[:].rearrange("p (h e) -> p h e", h=4), w1, w1[:].rearrange("p (h e) -> p h e", h=4),
                 st, bc(st[:, 4:8][:, :, None], [128, 4, 128]), ALU.mult)
            k.tt(k.dve, w1, w1[:], w1, w1[:], hwb, hwb[:], ALU.mult)
            k.actv(o_, o_[:], o_, o_[:], AF.Sigmoid)
            k.tt(k.dve, cat, cat[:, 0:512], w1, w1[:], o_, o_[:], ALU.mult)
            k.tt(k.dve, w2, w2[:], u_, u_[:], dsk, dsk[:], ALU.mult)
            k.tt(k.dve, w2, w2[:], w2, w2[:], y_, y_[:], ALU.add)
            k.tt(k.pool, y_, y_[:], w2, w2[:], w2, w2[:], ALU.mult)
            k.ts(k.dve, y_, y_[:], y_, y_[:], 0.044715, 1.0, ALU.mult, ALU.add)
            k.tt(k.dve, y_, y_[:], y_, y_[:], w2, w2[:], ALU.mult)
            k.actv(y_, y_[:], y_, y_[:], AF.Sigmoid, scale=2.0 * math.sqrt(2.0 / PI))
            k.tt(k.dve, ybb, ybb[:], y_, y_[:], w2, w2[:], ALU.mult)
            for j in range(4):
                k.tr(pTb, pTb[:, j, :], ybb, ybb[:, j * 128:(j + 1) * 128], c.idb, c.idb[:])
            k.cp(k.act, ybT, ybT[:], pTb, pTb[:, 0:4, :])
            for cb in range(2):
                for kc in range(4):
                    k.mm(pG[cb], pG[cb][:], ybT, ybT[:, kc, :], wglu, wglu[:, kc, cb * 512:(cb + 1) * 512], kc == 0, kc == 3)
            k.actv(w2, w2[:], pG[1], pG[1][:], AF.Sigmoid)
            k.tt(k.dve, cat, cat[:, 512:1024], pG[0], pG[0][:], w2, w2[:], ALU.mult)
            for j in range(8):
                k.tr(pTb, pTb[:, j, :], cat, cat[:, j * 128:(j + 1) * 128], c.idb, c.idb[:])
            k.cp(k.act, catT, catT[:], pTb, pTb[:])
            g = gate[1 if t < 2 else 0]
            for cb in range(2):
                for kc in range(8):
                    k.mm(pO[cb], pO[cb][:], catT, catT[:, kc, :], wout, wout[:, kc, cb * 512:(cb + 1) * 512], kc == 0, kc == 7)
                k.tt(k.dve, yo, yo[:, cb * 512:(cb + 1) * 512], pO[cb], pO[cb][:], g, g[:, cb * 512:(cb + 1) * 512], ALU.mult)
            k.tt(k.pool, yo, yo[:], yo, yo[:], x_, x_[:], ALU.add)
            k.dma(xs, xs.ap()[tok, :], yo, yo[:], q=k.pool)


def layer1(k, c, env):
    e = E(env)
    nc = k.nc
    xs = e.xs
    z_d = k.dram("z_d", [T, D]); gt_d = k.dram("gt_d", [T, 32]); o1_d = k.dram("o1_d", [2, T, D])
    with k.scope():
        qT = k.sb("gqT", [128, 8, T], BF16); kT = k.sb("gkT", [128, 8, T], BF16); vT = k.sb("gvT", [128, 8, T], BF16)
        with k.scope():
            hT = k.sb("hT", [128, 8, T], BF16)
            with k.scope():
                c.xt = [k.sb("xt%d" % i, [128, D]) for i in range(2)]
                c.sq = k.sb("sq", [128, D]); c.ss = k.sb("ss", [128, 4])
                c.pT = [k.ps("pT%d" % i, [128, 4, 128]) for i in range(2)]
                norm_mod(k, c, xs, list(range(NT)), e.ab_fn(0, 1), hT)
            c.wstage = [k.sb("wst%d" % i, [128, 2048]) for i in range(2)]
            c.ltst = [k.sb("ltst%d" % i, [64, 128]) for i in range(2)]
            c.ltps = [k.ps("ltps%d" % i, [128, 64]) for i in range(2)]
            c.lt_i = 0
            cw = k.sb("cw", [128, 24, 9])
            for ci in range(24):
                load_T(k, c, cw, cw[:, ci, :], e.od_conv, e.od_conv.ap()[:, ci * 128:(ci + 1) * 128], 9)
            ones32 = k.sb("ones32", [128, 128]); k.memset(k.dve, ones32, ones32[:], 1.0)
            wch = [k.sb("wch%d" % i, [128, 8, 256], BF16) for i in range(2)]
            P32 = k.sb("P32", [128, T]); Cv = k.sb("Cv", [128, T]); S32 = P32; Q32 = Cv
            rs = k.sb("rs", [128, 512])
            pp = [k.ps("pp%d" % i, [128, 512]) for i in range(3)]
            blocks = [(0, 512), (512, 512), (1024, 512), (1536, 512), (2048, 256)]
            it = 0
            for ci in range(24):
                if ci % 2 == 0:
                    wc = wch[(ci // 2) % 2]
                    load_w_bf16(k, c, wc, wc[:], e.od_w_in, e.od_w_in.ap()[:, ci * 128:(ci + 2) * 128], 256)
                wo = (ci % 2) * 128
                for (t0, tn) in blocks:
                    p = pp[it % 3]; it += 1
                    for kc in range(8):
                        k.mm(p, p[:, 0:tn], wc, wc[:, kc, wo:wo + 128], hT, hT[:, kc, t0:t0 + tn], kc == 0, kc == 7)
                    k.cp(k.act, P32, P32[:, t0:t0 + tn], p, p[:, 0:tn])
                w_ = lambda tap: cw[:, ci, tap:tap + 1]
                k.ts(k.dve, Cv, Cv[:], P32, P32[:], w_(4), None, ALU.mult, extra=[cw])
                k.stt(k.dve, Cv, Cv[:, 1:256], P32, P32[:, 0:255], w_(3), Cv, Cv[:, 1:256], ALU.mult, ALU.add, extra=[cw])
                k.stt(k.dve, Cv, Cv[:, 0:255], P32, P32[:, 1:256], w_(5), Cv, Cv[:, 0:255], ALU.mult, ALU.add, extra=[cw])
                Pl = P32[:, 256:T].rearrange("p (r q) -> p r q", q=64); Cl = Cv[:, 256:T].rearrange("p (r q) -> p r q", q=64)
                for a in range(3):
                    for b in range(3):
                        if a == 1 and b == 1:
                            continue
                        dr, dc = a - 1, b - 1
                        r0, r1 = max(0, -dr), 32 - max(0, dr)
                        c0, c1 = max(0, -dc), 64 - max(0, dc)
                        k.stt(k.dve, Cv, Cl[:, r0:r1, c0:c1], P32, Pl[:, r0 + dr:r1 + dr, c0 + dc:c1 + dc],
                              w_(a * 3 + b), Cv, Cl[:, r0:r1, c0:c1], ALU.mult, ALU.add, extra=[cw])
                h = ci % 8
                if ci >= 16:
                    k.actv(vT, vT[:, h, :], Cv, Cv[:], AF.Silu)
                else:
                    k.actv(S32, S32[:], Cv, Cv[:], AF.Silu)
                    k.tt(k.pool, Q32, Q32[:], S32, S32[:], S32, S32[:], ALU.mult)
                    dst = qT if ci < 8 else kT
                    for (t0, tn) in blocks:
                        p = pp[it % 3]; it += 1
                        k.mm(p, p[:, 0:tn], ones32, ones32[:], Q32, Q32[:, t0:t0 + tn])
                        k.actv(rs, rs[:, 0:tn], p, p[:, 0:tn], AF.Ln, bias=c.epsb[:, 0:1], extra=[c.epsb])
                        k.actv(rs, rs[:, 0:tn], rs, rs[:, 0:tn], AF.Exp, scale=-0.5)
                        k.stt(k.dve, dst, dst[:, h, t0:t0 + tn], S32, S32[:, t0:t0 + tn], (1.0 / math.sqrt(128) if ci < 8 else 1.0),
                              rs, rs[:, 0:tn], ALU.mult, ALU.mult)
            wz = k.sb("wz", [128, 8, 544], BF16)
            zt = [P32, Cv]
            iz = 0
            for (zc0, zn) in ((0, 512), (512, 544)):
                for cb in range(0, zn, 256):
                    n = min(256, zn - cb)
                    load_w_bf16(k, c, wz, wz[:, :, cb:cb + n], e.od_w_in, e.od_w_in.ap()[:, 3072 + zc0 + cb:3072 + zc0 + cb + n], n)
                for t in range(NT):
                    z_ = zt[iz % 2]; iz += 1
                    tok = slice(t * 128, (t + 1) * 128)
                    for bi, (col, n) in enumerate(((0, 512), (512, 32))[:(1 if zc0 == 0 else 2)]):
                        p = pp[it % 3]; it += 1
                        for kc in range(8):
                            k.mm(p, p[:, 0:n], hT, hT[:, kc, tok], wz, wz[:, kc, col:col + n], kc == 0, kc == 7)
                        k.cp(k.act if bi % 2 else k.dve, z_, z_[:, col:col + n], p, p[:, 0:n])
                    k.dma(z_d, z_d.ap()[tok, zc0:zc0 + 512], z_, z_[:, 0:512], q=k.pool)
                    if zc0:
                        k.dma(gt_d, gt_d.ap()[tok, :], z_, z_[:, 512:544], q=k.pool)
        if c.stop == "proj1":
            c.dbg_qkv = (qT, kT, vT)
            return
        gdn_phase(k, c, e, qT, kT, vT, gt_d, o1_d)
    if c.stop == "gdn":
        return
    with k.scope():
        c.wstage = [k.sb("wst%d" % i, [128, 4096]) for i in range(2)]
        wout = k.sb("wout", [128, 8, 1024], BF16)
        for cb in range(2):
            load_w_bf16(k, c, wout, wout[:, :, cb * 512:(cb + 1) * 512], e.od_wout, e.od_wout.ap()[:, cb * 512:(cb + 1) * 512], 512)
        hwb = k.sb("hwb", [128, D])
        k.dma(hwb, hwb[:], e.od_hw, e.od_hw.ap().partition_broadcast(128).rearrange("p o n -> p (o n)"))
        gate = k.sb("gate", [128, D]); e.load_gate(gate, 1, 0, 2)
        ot = [k.sb("ot%d" % i, [128, 2, D]) for i in range(2)]
        zt = [k.sb("zt%d" % i, [128, D]) for i in range(2)]
        xt = [k.sb("xt%d" % i, [128, D]) for i in range(2)]
        w1 = k.sb("w1", [128, D]); w2 = k.sb("w2", [128, D]); st = k.sb("st", [128, 16])
        cat = k.sb("cat", [128, D], BF16); catT = k.sb("catT", [128, 8, 128], BF16)
        pTb = k.ps("pTb", [128, 8, 128], BF16)
        pO = [k.ps("pO%d" % i, [128, 512]) for i in range(2)]
        yo = k.sb("yo", [128, D])
        for i, t in enumerate(range(2, NT)):
            tok = slice(t * 128, (t + 1) * 128)
            o_, z_, x_ = ot[i % 2], zt[i % 2], xt[i % 2]
            k.dma(o_, o_[:], o1_d, o1_d.ap()[:, tok, :].rearrange("r t n -> t r n"))
            k.dma(z_, z_[:], z_d, z_d.ap()[tok, :])
            k.dma(x_, x_[:], xs, xs.ap()[tok, :])
            k.tt(k.dve, w1, w1[:], o_, o_[:, 0, :], o_, o_[:, 1, :], ALU.add)
            k.tt(k.pool, w2, w2[:], w1, w1[:], w1, w1[:], ALU.mult)
            k.op(k.dve, lambda: nc.vector.reduce_sum(out=st[:, 0:8], in_=w2[:].rearrange("p (h e) -> p h e", h=8), axis=AX.X), [st], [w2])
            k.actv(st, st[:, 0:8], st, st[:, 0:8], AF.Ln, bias=c.epsb[:, 0:1], scale=1.0 / 128, extra=[c.epsb])
            k.actv(st, st[:, 8:16], st, st[:, 0:8], AF.Exp, scale=-0.5)
            k.tt(k.dve, w1, w1[:].rearrange("p (h e) -> p h e", h=8), w1, w1[:].rearrange("p (h e) -> p h e", h=8),
                 st, bc(st[:, 8:16][:, :, None], [128, 8, 128]), ALU.mult)
            k.tt(k.dve, w1, w1[:], w1, w1[:], hwb, hwb[:], ALU.mult)
            k.actv(z_, z_[:], z_, z_[:], AF.Silu)
            k.tt(k.dve, cat, cat[:], w1, w1[:], z_, z_[:], ALU.mult)
            for j in range(8):
                k.tr(pTb, pTb[:, j, :], cat, cat[:, j * 128:(j + 1) * 128], c.idb, c.idb[:])
            k.cp(k.act, catT, catT[:], pTb, pTb[:])
            for cb in range(2):
                for kc in range(8):
                    k.mm(pO[cb], pO[cb][:], catT, catT[:, kc, :], wout, wout[:, kc, cb * 512:(cb + 1) * 512], kc == 0, kc == 7)
                k.tt(k.dve, yo, yo[:, cb * 512:(cb + 1) * 512], pO[cb], pO[cb][:], gate, gate[:, cb * 512:(cb + 1) * 512], ALU.mult)
            k.tt(k.pool, yo, yo[:], yo, yo[:], x_, x_[:], ALU.add)
            k.dma(xs, xs.ap()[tok, :], yo, yo[:], q=k.pool)


def gdn_phase(k, c, e, qT, kT, vT, gt_d, o1_d):
    nc = k.nc
    with k.scope():
        gt = k.sb("gt", [64, 36, 32])
        for c0 in range(0, 36, 6):
            k.dma(gt, gt[:, c0:c0 + 6, :], gt_d, gt_d.ap()[c0 * 64:(c0 + 6) * 64, :].rearrange("(c l) n -> l c n", l=64))
        ga = k.sb("ga", [64, 16]); dtb = k.sb("dtb", [64, 16])
        k.dma(ga, ga[:], e.od_alog, e.od_alog.ap().partition_broadcast(64).rearrange("p o n -> p (o n)"))
        k.dma(dtb, dtb[:], e.od_dtb, e.od_dtb.ap().partition_broadcast(64).rearrange("p o n -> p (o n)"))
        k.actv(ga, ga[:], ga, ga[:], AF.Exp)
        tri = k.sb("tri", [64, 2, 64]); strict = k.sb("strict", [64, 2, 64])
        k.dma(tri, tri[:], e.ctri, e.ctri.ap().rearrange("r s l -> s r l"))
        k.dma(strict, strict[:], e.cstrict, e.cstrict.ap().rearrange("r s l -> s r l"))
        ones = k.sb("ones", [64, 128]); k.memset(k.dve, ones, ones[:], 1.0)
        ng = k.sb("ng", [64, 2, 36, 8]); beta = k.sb("beta", [64, 2, 36, 8])
        eG = k.sb("eG", [64, 2, 36, 8]); kds = k.sb("kds", [64, 2, 36, 8]); bg = k.sb("bg", [64, 2, 36, 8])
        gl = k.sb("gl", [128, 2, 36, 8])
        for d in range(2):
            k.tt(k.dve, ng, ng[:, d], gt, gt[:, :, 8 * d:8 * d + 8], dtb, bc(dtb[:, None, 8 * d:8 * d + 8], [64, 36, 8]), ALU.add)
            k.actv(beta, beta[:, d], gt, gt[:, :, 16 + 8 * d:24 + 8 * d], AF.Sigmoid)
        k.actv(ng, ng[:], ng, ng[:], AF.Exp)
        k.actv(ng, ng[:], ng, ng[:], AF.Ln, bias=1.0)
        for d in range(2):
            k.tt(k.dve, ng, ng[:, d], ng, ng[:, d], ga, bc(ga[:, None, 8 * d:8 * d + 8], [64, 36, 8]), ALU.mult)
        with k.scope():
            pF = k.ps("pF", [64, 2, 512]); pT_ = k.ps("pTt", [64, 2, 512]); pG = k.ps("pG", [128, 2, 512])
            for d in range(2):
                ngd = ng[:, d].rearrange("p c h -> p (c h)")
                k.mm(pF, pF[:, d, 0:288], tri, tri[:, d, :], ng, ngd)
                k.mm(pT_, pT_[:, d, 0:288], ones, ones[:, 0:64], ng, ngd)
                k.mm(pG, pG[:, d, 0:288], ones, ones[:], ng, ngd)
            fl = lambda t_: t_[:].rearrange("p d c h -> p d (c h)")
            k.actv(eG, fl(eG), pF, pF[:, :, 0:288], AF.Exp, scale=-1.0)
            k.cp(k.dve, kds, fl(kds), pF, pF[:, :, 0:288])
            k.tt(k.dve, kds, fl(kds), kds, fl(kds), pT_, pT_[:, :, 0:288], ALU.subtract)
            k.actv(kds, kds[:], kds, kds[:], AF.Exp)
            k.tt(k.dve, bg, bg[:], beta, beta[:], eG, eG[:], ALU.mult)
            k.actv(gl, fl(gl), pG, pG[:, :, 0:288], AF.Exp, scale=-1.0)
        S32 = [k.sb("S32_%d" % d, [128, 8, 128]) for d in range(2)]
        Sb = [k.sb("Sb_%d" % d, [128, 8, 128], BF16) for d in range(2)]
        for d in range(2):
            k.memset(k.dve, S32[d], S32[d][:], 0.0); k.memset(k.dve, Sb[d], Sb[d][:], 0.0)
        id64 = c.id32[0:64, 0:64]
        pKD = k.ps("pKD", [64, 4, 64]); pQD = k.ps("pQD", [64, 4, 64]); pN = k.ps("pN", [64, 4, 64]); pWT = k.ps("pWT", [128, 4, 64])
        pTk = k.ps("pTk", [64, 8, 128], BF16); pTv = k.ps("pTv", [64, 8, 128], BF16)
        pVO = k.ps("pVO", [64, 4, 128]); pSO = k.ps("pSO", [128, 4, 128])
        kbg = k.sb("kbg", [64, 8, 128], BF16); kd = k.sb("kd", [64, 8, 128], BF16); bv = k.sb("bv", [64, 8, 128], BF16)
        gm = k.sb("gm", [64, 8, 64]); MBs = k.sb("MBs", [64, 8, 64])
        gam = k.sb("gam", [64, 2, 64]); A32 = k.sb("A32", [64, 2, 64])
        Mb = k.sb("Mb", [64, 2, 64], BF16); nAb = k.sb("nAb", [64, 2, 64], BF16)
        Y = k.sb("Y", [64, 2, 64], BF16); Rt = k.sb("Rt", [64, 2, 64], BF16)
        nWT = k.sb("nWT", [128, 2, 64], BF16); vn = k.sb("vn", [64, 2, 128], BF16)
        gT_ = k.sb("gT_", [64, 2, 64]); attT = k.sb("attT", [64, 2, 64], BF16)
        t2 = k.sb("t2", [64, 2, 128]); otl = [k.sb("otl%d" % d, [64, 8, 128]) for d in range(2)]
        idb2 = bc(c.id32[0:64, None, 0:64], [64, 2, 64])
        for step in range(36):
            for d in range(2):
                ch = step if d == 0 else ORDB[step]
                tok = slice(ch * 64, (ch + 1) * 64)
                for h in range(8):
                    k.tr(pTk, pTk[:, h, :], kT, kT[:, h, tok], c.idb, c.idb[:])
                    k.tr(pTv, pTv[:, h, :], vT, vT[:, h, tok], c.idb, c.idb[:])
                sc = lambda t_: bc(t_[:, d, ch, :][:, :, None], [64, 8, 128])
                k.tt(k.dve, kbg, kbg[:], pTk, pTk[:], bg, sc(bg), ALU.mult)
                k.tt(k.dve, kd, kd[:], pTk, pTk[:], kds, sc(kds), ALU.mult)
                k.tt(k.dve, bv, bv[:], pTv, pTv[:], beta, sc(beta), ALU.mult)
                k.tt(k.pool, gm, gm[:], tri, bc(tri[:, d:d + 1, :], [64, 8, 64]), ng, bc(ng[:, d, ch, :][:, :, None], [64, 8, 64]), ALU.mult)
                k.tt(k.pool, MBs, MBs[:], strict, bc(strict[:, d:d + 1, :], [64, 8, 64]), beta, bc(beta[:, d, ch, :][:, :, None], [64, 8, 64]), ALU.mult)
                ot_ = otl[d]
                for hp in range(4):
                    h0 = 2 * hp
                    for hh in range(2):
                        h = h0 + hh
                        k.mm(pKD, pKD[:, hh, :], kT, kT[:, h, tok], kT, kT[:, h, tok])
                        k.mm(pKD, pKD[:, 2 + hh, :], gm, gm[:, h, :], strict, strict[:, d, :])
                        k.mm(pQD, pQD[:, hh, :], kT, kT[:, h, tok], qT, qT[:, h, tok])
                        k.mm(pQD, pQD[:, 2 + hh, :], strict, strict[:, d, :], gm, gm[:, h, :])
                    k.actv(gam, gam[:], pKD, pKD[:, 2:4, :], AF.Exp, scale=-1.0)
                    k.tt(k.dve, gam, gam[:], gam, gam[:], MBs, MBs[:, h0:h0 + 2, :], ALU.mult)
                    k.tt(k.dve, A32, A32[:], pKD, pKD[:, 0:2, :], gam, gam[:], ALU.mult)
                    k.tt(k.dve, Mb, Mb[:], A32, A32[:], c.id32, idb2, ALU.add)
                    k.ts(k.dve, nAb, nAb[:], A32, A32[:], -1.0, None, ALU.mult)
                    for hh in range(2):
                        k.mm(pN, pN[:, hh, :], nAb, nAb[:, hh, :], c.idb, c.idb[0:64, 0:64])
                    k.tt(k.dve, Y, Y[:], pN, pN[:, 0:2, :], c.id32, idb2, ALU.add)
                    for itn in range(5):
                        for hh in range(2):
                            k.mm(pN, pN[:, hh, :], Y, Y[:, hh, :], Mb, Mb[:, hh, :])
                        k.tt(k.dve, Rt, Rt[:], c.id32, idb2, pN, pN[:, 0:2, :], ALU.subtract)
                        for hh in range(2):
                            k.mm(pN, pN[:, 2 + hh, :], Rt, Rt[:, hh, :], Y, Y[:, hh, :])
                        k.tt(k.dve, Y, Y[:], Y, Y[:], pN, pN[:, 2:4, :], ALU.add)
                    for hh in range(2):
                        h = h0 + hh
                        k.mm(pWT, pWT[:, hh, :], kbg, kbg[:, h, :], Y, Y[:, hh, :])
                    k.actv(nWT, nWT[:], pWT, pWT[:, 0:2, :], AF.Copy, scale=-1.0)
                    for hh in range(2):
                        h = h0 + hh
                        k.mm(pVO, pVO[:, hh, :], Y, Y[:, hh, :], bv, bv[:, h, :], True, False)
                        k.mm(pVO, pVO[:, hh, :], nWT, nWT[:, hh, :], Sb[d], Sb[d][:, h, :], False, True)
                    k.cp(k.act, vn, vn[:], pVO, pVO[:, 0:2, :])
                    k.actv(gT_, gT_[:], pQD, pQD[:, 2:4, :], AF.Exp, scale=-1.0)
                    k.tt(k.pool, gT_, gT_[:], gT_, gT_[:], tri, bc(tri[:, d:d + 1, :], [64, 2, 64]), ALU.mult)
                    k.tt(k.dve, attT, attT[:], pQD, pQD[:, 0:2, :], gT_, gT_[:], ALU.mult)
                    for hh in range(2):
                        h = h0 + hh
                        k.mm(pVO, pVO[:, 2 + hh, :], qT, qT[:, h, tok], Sb[d], Sb[d][:, h, :])
                        k.mm(pSO, pSO[0:64, 2 + hh, :], attT, attT[:, hh, :], vn, vn[:, hh, :])
                        k.mm(pSO, pSO[:, hh, :], kd, kd[:, h, :], vn, vn[:, hh, :])
                    k.cp(k.act, t2, t2[:], pSO, pSO[0:64, 2:4, :])
                    k.tt(k.dve, ot_, ot_[:, h0:h0 + 2, :], pVO, pVO[:, 2:4, :], eG, bc(eG[:, d, ch, h0:h0 + 2][:, :, None], [64, 2, 128]), ALU.mult)
                    k.tt(k.pool, ot_, ot_[:, h0:h0 + 2, :], ot_, ot_[:, h0:h0 + 2, :], t2, t2[:], ALU.add)
                    Sv = S32[d][:, h0:h0 + 2, :]
                    k.tt(k.dve, S32[d], Sv, S32[d], Sv, gl, bc(gl[:, d, ch, h0:h0 + 2][:, :, None], [128, 2, 128]), ALU.mult)
                    k.tt(k.dve, S32[d], Sv, S32[d], Sv, pSO, pSO[:, 0:2, :], ALU.add)
                    k.cp(k.act, Sb[d], Sb[d][:, h0:h0 + 2, :], S32[d], Sv)
                k.dma(o1_d, o1_d.ap()[d, tok, :], ot_, ot_[:].rearrange("p h e -> p (h e)"), q=k.pool)


def host_consts():
    s = np.arange(64)
    tri = np.stack([(s[:, None] <= s[None, :]), (s[:, None] >= s[None, :])]).astype(np.float32)
    ip = np.arange(128) // 16
    maskM = np.stack([(ip[None, :] >= ip[:, None]), (ip[None, :] <= ip[:, None])]).astype(np.float32)
    return {
        "k_id32": np.eye(128, dtype=np.float32),
        "k_idb": np.eye(128, dtype=np.float32).astype(ml_dtypes.bfloat16),
        "k_tri": tri, "k_maskM": maskM,
        "k_strict": np.stack([(s[:, None] > s[None, :]), (s[:, None] < s[None, :])]).astype(np.float32),
    }


def make_in_maps(inputs, cores):
    f = lambda a: np.ascontiguousarray(np.asarray(a, dtype=np.float32))
    sh = {
        "c_ctx": f(inputs["c_ctx"]).reshape(1, D), "ada_w": f(inputs["ada_w"]), "ada_b": f(inputs["ada_b"]),
        "norm1_w": f(inputs["norm1_w"]), "norm2_w": f(inputs["norm2_w"]),
        "ffn_w1": f(inputs["ffn_w1"]), "ffn_w3": f(inputs["ffn_w3"]), "ffn_w2": f(inputs["ffn_w2"]),
        "final_norm_w": f(inputs["final_norm_w"]).reshape(1, D),
        "ev_w_in": f(inputs["ev_w_in"])[0], "ev_i_bias": f(inputs["ev_i_bias"]).reshape(1, 8),
        "ev_f_bias": f(inputs["ev_f_bias"]).reshape(1, 8), "ev_head_norm_w": f(inputs["ev_head_norm_w"]).reshape(1, 512),
        "ev_lam_re": f(inputs["ev_lam_re"])[0], "ev_lam_im": f(inputs["ev_lam_im"])[0],
        "ev_log_dt": f(inputs["ev_log_dt"]).reshape(1, 64),
        "ev_b_re": f(inputs["ev_b_re"])[0], "ev_b_im": f(inputs["ev_b_im"])[0],
        "ev_c_re": f(inputs["ev_c_re"]).reshape(1024, 64), "ev_c_im": f(inputs["ev_c_im"]).reshape(1024, 64),
        "ev_d": f(inputs["ev_d"]).reshape(1, 512), "ev_w_glu": f(inputs["ev_w_glu"])[0], "ev_w_out": f(inputs["ev_w_out"])[0],
        "od_w_in": f(inputs["od_w_in"])[0], "od_conv_w": f(inputs["od_conv_w"]).reshape(9, 3072),
        "od_a_log": f(inputs["od_a_log"]).reshape(1, 16), "od_dt_bias": f(inputs["od_dt_bias"]).reshape(1, 16),
        "od_head_norm_w": f(inputs["od_head_norm_w"]).reshape(1, D), "od_w_out": f(inputs["od_w_out"])[0],
    }
    sh.update(host_consts())
    x, cc, ctx = f(inputs["x"]), f(inputs["c"]), f(inputs["ctx"])
    maps = []
    for b in cores:
        m = dict(sh)
        m["x"] = x[b]; m["c"] = cc[b:b + 1]; m["ctx"] = ctx[b]
        maps.append(m)
    return maps


def kernel(**inputs):
    nc, _ = build_program()
    maps = make_in_maps(inputs, list(range(8)))
    res = run_bass_kernel_spmd(nc, maps, core_ids=list(range(8)))
    return np.stack([np.asarray(r["out"], dtype=np.float32) for r in res.results], axis=0)
```

```python
import math
import numpy as np
import ml_dtypes
import concourse.bass as bass
import concourse.mybir as mybir
from concourse.bass_types import AP
from concourse.bass_utils import run_bass_kernel_spmd

F32 = mybir.dt.float32
BF16 = mybir.dt.bfloat16
AF = mybir.ActivationFunctionType
ALU = mybir.AluOpType
AX = mybir.AxisListType

D = 1024
T = 2304
NCTX = 256
NLAT = 2048
NT = T // 128
EPS = 1e-6
HID = 2816
PI = math.pi


class Obj:
    def __init__(self, k, name, handle, space):
        self.k, self.name, self.h, self.space = k, name, handle, space
        self.uid = k.uid
        self.w, self.r = {}, {}
        self.sems = {}

    def __getitem__(self, idx):
        return self.h[idx]

    def ap(self):
        return self.h.ap() if self.space == "dram" else self.h[:]

    def dsem(self, kind):
        if kind not in self.sems:
            if not self.sems:
                self.k.dma_objs.append(self)
            pool = self.k.sem_pool[kind]
            if pool:
                self.sems[kind] = pool.pop()
            else:
                self.k.nsem += 1
                self.sems[kind] = [self.k.new_sem("d%s_%d" % (kind, self.k.nsem), keep=True), 0]
        return self.sems[kind]


class Eng:
    def __init__(self, k, name, e):
        self.k, self.name, self.e = k, name, e
        self.sem = k.new_sem("p_" + name)
        self.cnt = 0
        self.seen = {}

    def need(self, tok):
        s, v = tok
        if self.seen.get(id(s), 0) >= v:
            return
        self.e.wait_ge(s, v)
        self.seen[id(s)] = v


class K:
    def __init__(self, nc):
        self.nc = nc
        self._ctx = []
        self.dma_objs = []
        self.sem_pool = {"hw": [], "sw": []}
        self.nsem = 0
        self._perm = []
        self.pe = Eng(self, "pe", nc.tensor)
        self.dve = Eng(self, "dve", nc.vector)
        self.act = Eng(self, "act", nc.scalar)
        self.pool = Eng(self, "pool", nc.gpsimd)
        self.sp = Eng(self, "sp", nc.sync)
        self.engs = [self.pe, self.dve, self.act, self.pool, self.sp]
        self.n_ins = 0
        self.uid = 0

    def new_sem(self, name, keep=False):
        cm = self.nc.semaphore(name)
        s = cm.__enter__()
        self._perm.append((cm, s))
        return s

    def _alloc(self, cm, name, space):
        h = cm.__enter__()
        self._ctx.append(cm)
        return Obj(self, name, h, space)

    def sb(self, name, shape, dt=F32):
        self.uid += 1
        name = "%s_%d" % (name, self.uid)
        return self._alloc(self.nc.sbuf_tensor(name, list(shape), dt), name, "sb")

    def ps(self, name, shape, dt=F32):
        self.uid += 1
        name = "%s_%d" % (name, self.uid)
        return self._alloc(self.nc.psum_tensor(name, list(shape), dt), name, "ps")

    def dram(self, name, shape, dt=F32, kind="Internal"):
        h = self.nc.dram_tensor(name, list(shape), dt, kind=kind)
        return Obj(self, name, h, "dram")

    class _Scope:
        def __init__(self, k):
            self.k = k

        def __enter__(self):
            self.mark = len(self.k._ctx)
            self.uid0 = self.k.uid
            return self

        def __exit__(self, *a):
            k = self.k
            k.barrier()
            while len(k._ctx) > self.mark:
                k._ctx.pop().__exit__(None, None, None)
            keep = []
            for o in k.dma_objs:
                if o.space == "dram" or o.uid <= self.uid0:
                    keep.append(o)
                else:
                    for kind, sc in o.sems.items():
                        k.sem_pool[kind].append(sc)
                    o.sems = {}
            k.dma_objs = keep
            return False

    def scope(self):
        return K._Scope(self)

    def barrier(self):
        toks = [(e.sem, e.cnt) for e in self.engs if e.cnt]
        toks += [(sc[0], sc[1]) for o in self.dma_objs for sc in o.sems.values() if sc[1]]
        for e in self.engs:
            for t in toks:
                if t[0] is e.sem:
                    continue
                e.need(t)

    def _deps(self, eng, outs, ins, same_eng_raw=True):
        toks = []
        for o in ins:
            toks += list(o.w.values())
            if o.space == "ps":
                toks += [t for t in o.r.values() if t[0] is not eng.sem]
        for o in outs:
            toks += list(o.w.values())
            toks += list(o.r.values())
        for t in toks:
            if t[0] is eng.sem and (eng is self.pe or not same_eng_raw):
                continue
            eng.need(t)

    def op(self, eng, fn, outs, ins):
        self._deps(eng, outs, ins)
        ins_ = fn()
        eng.cnt += 1
        ins_.then_inc(eng.sem, 1)
        tok = (eng.sem, eng.cnt)
        eng.seen[id(eng.sem)] = max(eng.seen.get(id(eng.sem), 0), 0)
        for o in ins:
            o.r[id(tok[0])] = tok
        for o in outs:
            o.w = {id(tok[0]): tok}
            o.r = {}
        self.n_ins += 1
        return ins_

    def dma(self, out_obj, out_ap, in_obj, in_ap, q=None, **kw):
        q = q or self.sp
        self._deps(q, [out_obj], [in_obj], same_eng_raw=True)
        sc = out_obj.dsem("sw" if q is self.pool else "hw")
        s = sc[0]
        ins_ = q.e.dma_start(out=out_ap, in_=in_ap, **kw)
        sc[1] += 16
        ins_.then_inc(s, 16)
        tok = (s, sc[1])
        in_obj.r[id(s)] = tok
        out_obj.w[id(s)] = tok
        out_obj.r = {}
        self.n_ins += 1
        return ins_

    def finish(self, outs):
        self.barrier()

    def close(self):
        while self._ctx:
            self._ctx.pop().__exit__(None, None, None)
        while self._perm:
            self._perm.pop()[0].__exit__(None, None, None)

    def mm(self, out_o, out_ap, l_o, l_ap, r_o, r_ap, start=True, stop=True):
        nc = self.nc
        return self.op(self.pe, lambda: nc.tensor.matmul(out_ap, lhsT=l_ap, rhs=r_ap, start=start, stop=stop),
                       [out_o], [l_o, r_o])

    def tr(self, out_o, out_ap, in_o, in_ap, id_o, id_ap):
        nc = self.nc
        return self.op(self.pe, lambda: nc.tensor.transpose(out_ap, in_ap, id_ap), [out_o], [in_o, id_o])

    def tt(self, eng, out_o, out_ap, a_o, a_ap, b_o, b_ap, op):
        return self.op(eng, lambda: eng.e.tensor_tensor(out=out_ap, in0=a_ap, in1=b_ap, op=op), [out_o], [a_o, b_o])

    def ts(self, eng, out_o, out_ap, a_o, a_ap, s1, s2, op0, op1=None, extra=()):
        if op1 is None:
            return self.op(eng, lambda: eng.e.tensor_scalar(out=out_ap, in0=a_ap, scalar1=s1, scalar2=None, op0=op0),
                           [out_o], [a_o] + list(extra))
        return self.op(eng, lambda: eng.e.tensor_scalar(out=out_ap, in0=a_ap, scalar1=s1, scalar2=s2, op0=op0, op1=op1),
                       [out_o], [a_o] + list(extra))

    def stt(self, eng, out_o, out_ap, a_o, a_ap, sc, b_o, b_ap, op0, op1, extra=()):
        return self.op(eng, lambda: eng.e.scalar_tensor_tensor(out=out_ap, in0=a_ap, scalar=sc, in1=b_ap, op0=op0, op1=op1),
                       [out_o], [a_o, b_o] + list(extra))

    def actv(self, out_o, out_ap, a_o, a_ap, func, bias=0.0, scale=1.0, extra=()):
        nc = self.nc
        return self.op(self.act, lambda: nc.scalar.activation(out=out_ap, in_=a_ap, func=func, bias=bias, scale=scale),
                       [out_o], [a_o] + list(extra))

    def cp(self, eng, out_o, out_ap, a_o, a_ap):
        if eng is self.act:
            nc = self.nc
            return self.op(eng, lambda: nc.scalar.copy(out=out_ap, in_=a_ap), [out_o], [a_o])
        return self.op(eng, lambda: eng.e.tensor_copy(out=out_ap, in_=a_ap), [out_o], [a_o])

    def memset(self, eng, o, ap, val):
        return self.op(eng, lambda: eng.e.memset(ap, val), [o], [])


def bc(ap, shape):
    return ap.broadcast_to(list(shape))


class Ctx:
    pass


def load_w_bf16(k, c, dst, dst_ap_fn, wsrc, w_ap, ncols, kchunks=8, eng=None):
    eng = eng or k.pool
    st = c.wstage[c.wstage_i % 2]
    c.wstage_i += 1
    sv = st[:, 0:kchunks * ncols].rearrange("p (k n) -> p k n", k=kchunks)
    k.dma(st, sv, wsrc, w_ap.rearrange("(kc p) n -> p kc n", p=128))
    k.cp(eng, dst, dst_ap_fn, st, sv)


def norm_mod(k, c, xs, tiles, ab_of_tile, hT, col0=0):
    for i, t in enumerate(tiles):
        xt = c.xt[i % 2]
        k.dma(xt, xt[:], xs, xs.ap()[t * 128:(t + 1) * 128, :])
        sq = c.sq
        k.tt(k.dve, sq, sq[:], xt, xt[:], xt, xt[:], ALU.mult)
        ss = c.ss
        k.op(k.dve, lambda: k.nc.vector.reduce_sum(out=ss[:, 0:1], in_=sq[:], axis=AX.X), [ss], [sq])
        k.actv(ss, ss[:, 1:2], ss, ss[:, 0:1], AF.Ln, bias=c.epsb[:, 0:1], scale=1.0 / D, extra=[c.epsb])
        k.actv(ss, ss[:, 2:3], ss, ss[:, 1:2], AF.Exp, scale=-0.5)
        k.ts(k.dve, sq, sq[:], xt, xt[:], ss[:, 2:3], None, ALU.mult, extra=[ss])
        a, b = ab_of_tile(t)
        for half in range(2):
            pt = c.pT[half]
            for j in range(4):
                kc = half * 4 + j
                k.tr(pt, pt[:, j, :], sq, sq[:, kc * 128:(kc + 1) * 128], c.id32, c.id32[:])
            for j in range(4):
                kc = half * 4 + j
                k.actv(hT, hT[:, kc, col0 + i * 128: col0 + (i + 1) * 128], pt, pt[:, j, :], AF.Identity,
                       bias=b[0][:, b[1] + kc: b[1] + kc + 1], scale=a[0][:, a[1] + kc:a[1] + kc + 1], extra=[a[0], b[0]])


def load_T(k, c, dst_o, dst_ap, src_o, src_ap, nrows, ncols=128):
    st = c.ltst[c.lt_i % 2]; pt = c.ltps[c.lt_i % 2]; c.lt_i += 1
    k.dma(st, st[0:nrows, 0:ncols], src_o, src_ap)
    k.tr(pt, pt[0:ncols, 0:nrows], st, st[0:nrows, 0:ncols], c.id32, c.id32[0:nrows, 0:nrows])
    k.cp(k.dve, dst_o, dst_ap, pt, pt[0:ncols, 0:nrows])


def tok_blocks(tiles_n):
    out, s = [], 0
    while s < tiles_n:
        n = min(4, tiles_n - s)
        out.append((s, n))
        s += n
    return out


def build_program(stop_after=None, debug=False):
    nc = bass.Bass("TRN2", target_bir_lowering=False)
    k = K(nc)
    c = Ctx()
    c.wstage_i = 0
    c.stop = stop_after
    I = {}

    def inp(name, shape, dt=F32):
        I[name] = k.dram(name, shape, dt, kind="ExternalInput")
        return I[name]

    x_in = inp("x", [NLAT, D]); cvec = inp("c", [1, D]); ctx_in = inp("ctx", [NCTX, D]); c_ctx = inp("c_ctx", [1, D])
    ada_w = inp("ada_w", [2, D, 6 * D]); ada_b = inp("ada_b", [2, 6 * D])
    norm1_w = inp("norm1_w", [2, D]); norm2_w = inp("norm2_w", [2, D])
    ffn_w1 = inp("ffn_w1", [2, D, HID]); ffn_w3 = inp("ffn_w3", [2, D, HID]); ffn_w2 = inp("ffn_w2", [2, HID, D])
    final_w = inp("final_norm_w", [1, D])
    ev_w_in = inp("ev_w_in", [D, 2576]); ev_ib = inp("ev_i_bias", [1, 8]); ev_fb = inp("ev_f_bias", [1, 8])
    ev_hw = inp("ev_head_norm_w", [1, 512])
    ev_lre = inp("ev_lam_re", [2, 32, 64]); ev_lim = inp("ev_lam_im", [2, 32, 64]); ev_ldt = inp("ev_log_dt", [1, 64])
    ev_bre = inp("ev_b_re", [2, 32, 64, 16]); ev_bim = inp("ev_b_im", [2, 32, 64, 16])
    ev_cre = inp("ev_c_re", [1024, 64]); ev_cim = inp("ev_c_im", [1024, 64])
    ev_d = inp("ev_d", [1, 512]); ev_wglu = inp("ev_w_glu", [512, 1024]); ev_wout = inp("ev_w_out", [D, D])
    od_w_in = inp("od_w_in", [D, 4128]); od_conv = inp("od_conv_w", [9, 3072])
    od_alog = inp("od_a_log", [1, 16]); od_dtb = inp("od_dt_bias", [1, 16]); od_hw = inp("od_head_norm_w", [1, D])
    od_wout = inp("od_w_out", [D, D])
    cid32 = inp("k_id32", [128, 128]); cidb = inp("k_idb", [128, 128], BF16)
    ctri = inp("k_tri", [2, 64, 64])
    cmaskM = inp("k_maskM", [2, 128, 128])
    cstrict = inp("k_strict", [2, 64, 64])
    out_d = k.dram("out", [NLAT, D], F32, kind="ExternalOutput")

    xs = k.dram("xs", [T, D])
    modv = k.dram("modv", [2, 2, 6 * D])
    dbg = {}

    c.id32 = k.sb("id32", [128, 128]); k.dma(c.id32, c.id32[:], cid32, cid32.ap())
    c.idb = k.sb("idb", [128, 128], BF16); k.dma(c.idb, c.idb[:], cidb, cidb.ap())
    c.epsb = k.sb("epsb", [128, 2]); k.memset(k.dve, c.epsb, c.epsb[:, 0:1], EPS); k.memset(k.dve, c.epsb, c.epsb[:, 1:2], 0.5 * PI)

    k.dma(xs, xs.ap()[0:NCTX, :], ctx_in, ctx_in.ap())
    k.dma(xs, xs.ap()[NCTX:T, :], x_in, x_in.ap())
    with k.scope():
        sT = k.sb("sT", [128, 8, 2])
        c.ltst = [k.sb("ltst%d" % i, [64, 128]) for i in range(2)]
        c.ltps = [k.ps("ltps%d" % i, [128, 64]) for i in range(2)]
        c.lt_i = 0
        load_T(k, c, sT, sT[:, :, 0], cvec, cvec.ap().rearrange("o (kc p) -> (o kc) p", p=128), 8)
        load_T(k, c, sT, sT[:, :, 1], c_ctx, c_ctx.ap().rearrange("o (kc p) -> (o kc) p", p=128), 8)
        sS = k.sb("sS", [128, 8, 2])
        k.actv(sS, sS[:], sT, sT[:], AF.Silu)
        wst = [k.sb("adw%d" % i, [128, 8, 512]) for i in range(2)]
        pm = [k.ps("pm%d" % i, [128, 512]) for i in range(2)]
        brow = k.sb("brow", [2, 6 * D]); mrow = k.sb("mrow", [2, 6 * D])
        for li in range(2):
            k.dma(brow, brow[:], ada_b, ada_b.ap()[li:li + 1, :].partition_broadcast(2).rearrange("p o n -> p (o n)"))
            for j in range(12):
                w = wst[j % 2]
                k.dma(w, w[:], ada_w, ada_w.ap()[li, :, j * 512:(j + 1) * 512].rearrange("(kc p) n -> p kc n", p=128))
                p = pm[j % 2]
                for kc in range(8):
                    k.mm(p, p[0:2, :], sS, sS[:, kc, :], w, w[:, kc, :], start=(kc == 0), stop=(kc == 7))
                k.tt(k.dve, mrow, mrow[:, j * 512:(j + 1) * 512], p, p[0:2, :], brow, brow[:, j * 512:(j + 1) * 512], ALU.add)
            k.dma(modv, modv.ap()[li], mrow, mrow[:], q=k.pool)

    if stop_after == "ada":
        k.finish([])
        k.close()
        return nc, ["modv"]

    modF = k.sb("modF", [128, 2, 2, 48])
    nwF = k.sb("nwF", [128, 2, 2, 8])
    with k.scope():
        c.ltst = [k.sb("ltst%d" % i, [64, 128]) for i in range(2)]
        c.ltps = [k.ps("ltps%d" % i, [128, 64]) for i in range(2)]
        c.lt_i = 0
        for li in range(2):
            for who in range(2):
                load_T(k, c, modF, modF[:, li, who, :], modv, modv.ap()[li, who, :].rearrange("(c p) -> c p", p=128), 48)
        for wi, nw in enumerate((norm1_w, norm2_w)):
            for li in range(2):
                load_T(k, c, nwF, nwF[:, wi, li, :], nw, nw.ap()[li, :].rearrange("(c p) -> c p", p=128), 8)
    aF = k.sb("aF", [128, 2, 2, 2, 8])
    for wi in range(2):
        for li in range(2):
            for who in range(2):
                sc0 = 8 if wi == 0 else 32
                k.stt(k.dve, aF, aF[:, wi, li, who, :], modF, modF[:, li, who, sc0:sc0 + 8], 1.0, nwF, nwF[:, wi, li, :],
                      ALU.add, ALU.mult)

    def ab_fn(wi, li):
        sh0 = 0 if wi == 0 else 24

        def f(t):
            who = 1 if t < 2 else 0
            a_flat = aF.h[:].rearrange("p a b c d -> p (a b c d)")
            b_flat = modF.h[:].rearrange("p a b c -> p (a b c)")
            return ((_View(aF, a_flat), ((wi * 2 + li) * 2 + who) * 8), (_View(modF, b_flat), (li * 2 + who) * 48 + sh0))
        return f

    def load_gate(dst, li, who, part):
        k.dma(dst, dst[:], modv, modv.ap()[li, who:who + 1, part * D:(part + 1) * D].partition_broadcast(128).rearrange("p o n -> p (o n)"))

    def ffn_phase(li, tiles):
        with k.scope():
            c.xt = [k.sb("xt%d" % i, [128, D]) for i in range(2)]
            c.sq = k.sb("sq", [128, D]); c.ss = k.sb("ss", [128, 4])
            c.pT = [k.ps("pT%d" % i, [128, 4, 128]) for i in range(2)]
            c.wstage = [k.sb("wst%d" % i, [128, 2048]) for i in range(2)]
            ntl = len(tiles)
            half_n = (ntl + 1) // 2
            gate = [k.sb("gate%d" % w, [128, D]) for w in range(2)]
            load_gate(gate[0], li, 0, 5); load_gate(gate[1], li, 1, 5)
            hT = k.sb("hT", [128, 8, half_n * 128], BF16)
            gT = k.sb("gT", [128, 22, half_n * 128], BF16)
            w2b = k.sb("w2b", [128, 22, D], BF16)
            w1b = [k.sb("w1b%d" % i, [128, 8, 256], BF16) for i in range(2)]
            w3b = [k.sb("w3b%d" % i, [128, 8, 256], BF16) for i in range(2)]
            p1 = [k.ps("p1_%d" % i, [128, 512]) for i in range(2)]
            p3 = [k.ps("p3_%d" % i, [128, 512]) for i in range(2)]
            py = [k.ps("py%d" % i, [128, 512]) for i in range(2)]
            sg = [k.sb("sg%d" % i, [128, 512]) for i in range(2)]
            yo = [k.sb("yo%d" % i, [128, D]) for i in range(2)]
            for jb in range(0, 22, 4):
                n = min(4, 22 - jb)
                for cb in range(2):
                    st = c.wstage[c.wstage_i % 2]; c.wstage_i += 1
                    sv = st[:, 0:n * 512].rearrange("p (j n) -> p j n", j=n)
                    k.dma(st, sv, ffn_w2, ffn_w2.ap()[li, jb * 128:(jb + n) * 128, cb * 512:(cb + 1) * 512]
                          .rearrange("(j p) n -> p j n", p=128))
                    k.cp(k.pool, w2b, w2b[:, jb:jb + n, cb * 512:(cb + 1) * 512], st, sv)
            it = 0
            for hs in range(0, ntl, half_n):
                ht = tiles[hs:hs + half_n]
                norm_mod(k, c, xs, ht, ab_fn(1, li), hT)
                blocks = tok_blocks(len(ht))
                for jb in range(0, 22, 2):
                    n = 2
                    wa, wb = w1b[(jb // 2) % 2], w3b[(jb // 2) % 2]
                    load_w_bf16(k, c, wa, wa[:, :, 0:n * 128], ffn_w1, ffn_w1.ap()[li, :, jb * 128:(jb + n) * 128], n * 128)
                    load_w_bf16(k, c, wb, wb[:, :, 0:n * 128], ffn_w3, ffn_w3.ap()[li, :, jb * 128:(jb + n) * 128], n * 128, eng=k.dve)
                    for jj in range(n):
                        j = jb + jj
                        for (b0, bn) in blocks:
                            q1, q3, s_ = p1[it % 2], p3[it % 2], sg[it % 2]; it += 1
                            cols = slice(b0 * 128, (b0 + bn) * 128)
                            w_ = bn * 128
                            for kc in range(8):
                                k.mm(q1, q1[:, 0:w_], wa, wa[:, kc, jj * 128:(jj + 1) * 128], hT, hT[:, kc, cols], kc == 0, kc == 7)
                            for kc in range(8):
                                k.mm(q3, q3[:, 0:w_], wb, wb[:, kc, jj * 128:(jj + 1) * 128], hT, hT[:, kc, cols], kc == 0, kc == 7)
                            k.actv(s_, s_[:, 0:w_], q1, q1[:, 0:w_], AF.Silu)
                            k.tt(k.dve, gT, gT[:, j, cols], s_, s_[:, 0:w_], q3, q3[:, 0:w_], ALU.mult)
                for i, t in enumerate(ht):
                    xt = c.xt[i % 2]
                    k.dma(xt, xt[:], xs, xs.ap()[t * 128:(t + 1) * 128, :])
                    g = gate[1 if t < 2 else 0]
                    y = yo[i % 2]
                    for cb in range(2):
                        p = py[cb]
                        for j in range(22):
                            k.mm(p, p[:], gT, gT[:, j, i * 128:(i + 1) * 128], w2b, w2b[:, j, cb * 512:(cb + 1) * 512], j == 0, j == 21)
                        k.tt(k.dve, y, y[:, cb * 512:(cb + 1) * 512], p, p[:], g, g[:, cb * 512:(cb + 1) * 512], ALU.mult)
                    k.tt(k.pool, y, y[:], y, y[:], xt, xt[:], ALU.add)
                    k.dma(xs, xs.ap()[t * 128:(t + 1) * 128, :], y, y[:], q=k.pool)

    def final_phase():
        with k.scope():
            xt = [k.sb("fx%d" % i, [128, D]) for i in range(2)]
            sq = [k.sb("fs%d" % i, [128, D]) for i in range(2)]
            ss = k.sb("fss", [128, 4])
            fw = k.sb("fw", [128, D])
            k.dma(fw, fw[:], final_w, final_w.ap().partition_broadcast(128).rearrange("p o n -> p (o n)"))
            for i in range(16):
                t = i + 2
                x_, s_ = xt[i % 2], sq[i % 2]
                k.dma(x_, x_[:], xs, xs.ap()[t * 128:(t + 1) * 128, :])
                k.tt(k.dve, s_, s_[:], x_, x_[:], x_, x_[:], ALU.mult)
                k.op(k.dve, lambda: nc.vector.reduce_sum(out=ss[:, 0:1], in_=s_[:], axis=AX.X), [ss], [s_])
                k.actv(ss, ss[:, 1:2], ss, ss[:, 0:1], AF.Ln, bias=c.epsb[:, 0:1], scale=1.0 / D, extra=[c.epsb])
                k.actv(ss, ss[:, 2:3], ss, ss[:, 1:2], AF.Exp, scale=-0.5)
                k.stt(k.dve, s_, s_[:], x_, x_[:], ss[:, 2:3], fw, fw[:], ALU.mult, ALU.mult, extra=[ss])
                k.dma(out_d, out_d.ap()[i * 128:(i + 1) * 128, :], s_, s_[:], q=k.pool)

    env = dict(locals())
    layer0(k, c, env)
    if stop_after in ("proj0", "ml_0", "ml_1", "ml_2", "ml_3", "ml_4", "ml_5", "ml_5a", "ml_5b", "ml_a", "ml_b", "mlstm", "s5", "mix0"):
        k.finish([]); k.close(); return nc, ["xs"]
    ffn_phase(0, list(range(NT)))
    if stop_after == "ffn0":
        k.finish([]); k.close(); return nc, ["xs"]
    layer1(k, c, env)
    if stop_after == "mix1":
        k.finish([]); k.close(); return nc, ["xs"]
    ffn_phase(1, list(range(2, NT)))
    final_phase()
    k.finish([out_d])
    k.close()
    return nc, ["out"]


class _View:
    def __init__(self, obj, flat):
        self.obj, self.flat = obj, flat

    @property
    def space(self):
        return self.obj.space

    @property
    def w(self):
        return self.obj.w

    @property
    def r(self):
        return self.obj.r

    def __getitem__(self, idx):
        return self.flat[idx]


LAYER_FUNCS = []


class E:
    def __init__(self, d):
        self.__dict__.update(d)


ORDB = [3, 2, 1, 0] + list(range(35, 3, -1))
ORDB8 = list(range(31, -1, -1)) + list(range(287, 31, -1))


def layer0(k, c, env):
    e = E(env)
    nc = k.nc
    xs = e.xs
    qT_d = k.dram("qT_d", [512, T], BF16); kT_d = k.dram("kT_d", [512, T], BF16)
    ktok_d = k.dram("ktok_d", [T, 512], BF16); v_d = k.dram("v_d", [T, 512], BF16)
    o_d = k.dram("o_d", [T, 512]); g_d = k.dram("g_d", [T, 16]); u_d = k.dram("u_d", [T, 512])
    hA_d = k.dram("hA_d", [2, T, 512]); yS_d = k.dram("yS_d", [T, 512])
    c.l0 = dict(qT=qT_d, kT=kT_d, ktok=ktok_d, v=v_d, o=o_d, g=g_d, u=u_d, hA=hA_d, yS=yS_d)

    with k.scope():
        c.xt = [k.sb("xt%d" % i, [128, D]) for i in range(2)]
        c.sq = k.sb("sq", [128, D]); c.ss = k.sb("ss", [128, 4])
        c.pT = [k.ps("pT%d" % i, [128, 4, 128]) for i in range(2)]
        c.wstage = [k.sb("wst%d" % i, [128, 4096]) for i in range(2)]
        hT = k.sb("hT", [128, 8, T], BF16)
        norm_mod(k, c, xs, list(range(NT)), e.ab_fn(0, 0), hT)
        wb = k.sb("wb", [128, 8, 2576], BF16)
        for cb in range(0, 2576, 512):
            n = min(512, 2576 - cb)
            load_w_bf16(k, c, wb, wb[:, :, cb:cb + n], e.ev_w_in, e.ev_w_in.ap()[:, cb:cb + n], n,
                        eng=(k.pool if (cb // 512) % 2 else k.dve))
        pp = [k.ps("pp%d" % i, [128, 512]) for i in range(4)]
        fst = [k.sb("fst%d" % i, [128, T], BF16) for i in range(2)]
        blocks = [(0, 512), (512, 512), (1024, 512), (1536, 512), (2048, 256)]
        it = 0
        for which, dst in ((0, qT_d), (1, kT_d)):
            for h in range(4):
                st = fst[(which * 4 + h) % 2]
                col = which * 512 + h * 128
                for (t0, tn) in blocks:
                    p = pp[it % 4]; it += 1
                    for kc in range(8):
                        k.mm(p, p[:, 0:tn], wb, wb[:, kc, col:col + 128], hT, hT[:, kc, t0:t0 + tn], kc == 0, kc == 7)
                    k.actv(st, st[:, t0:t0 + tn], p, p[:, 0:tn], AF.Copy, scale=(1.0 if which == 0 else 1.0 / math.sqrt(128)))
                k.dma(dst, dst.ap()[h * 128:(h + 1) * 128, :], st, st[:], q=k.pool)
        tkb = [k.sb("tkb%d" % i, [128, 1024], BF16) for i in range(2)]
        tof = [k.sb("tof%d" % i, [128, 1040]) for i in range(2)]
        for t in range(NT):
            kb, of = tkb[t % 2], tof[t % 2]
            tok = slice(t * 128, (t + 1) * 128)
            for bi, (col, n) in enumerate(((512, 512), (1024, 512), (1536, 512), (2064, 512), (2048, 16))):
                p = pp[it % 4]; it += 1
                for kc in range(8):
                    k.mm(p, p[:, 0:n], hT, hT[:, kc, tok], wb, wb[:, kc, col:col + n], kc == 0, kc == 7)
                if bi == 0:
                    k.actv(kb, kb[:, 0:512], p, p[:, 0:512], AF.Copy, scale=1.0 / math.sqrt(128))
                elif bi == 1:
                    k.cp(k.dve, kb, kb[:, 512:1024], p, p[:, 0:512])
                elif bi == 2:
                    k.cp(k.act, of, of[:, 0:512], p, p[:, 0:512])
                elif bi == 3:
                    k.cp(k.dve, of, of[:, 512:1024], p, p[:, 0:512])
                else:
                    k.cp(k.act, of, of[:, 1024:1040], p, p[:, 0:16])
            k.dma(ktok_d, ktok_d.ap()[tok, :], kb, kb[:, 0:512], q=k.pool)
            k.dma(v_d, v_d.ap()[tok, :], kb, kb[:, 512:1024], q=k.pool)
            k.dma(o_d, o_d.ap()[tok, :], of, of[:, 0:512], q=k.pool)
            k.dma(u_d, u_d.ap()[tok, :], of, of[:, 512:1024], q=k.pool)
            k.dma(g_d, g_d.ap()[tok, :], of, of[:, 1024:1040], q=k.pool)
    if c.stop == "proj0":
        return
    mlstm_phase(k, c, e)
    if c.stop in ("mlstm", "ml_0", "ml_1", "ml_2", "ml_3", "ml_4", "ml_5", "ml_5a", "ml_5b", "ml_a", "ml_b"):
        return
    s5_phase(k, c, e)
    if c.stop == "s5":
        return
    finish0_phase(k, c, e)


def mlstm_phase(k, c, e):
    nc = k.nc
    L = c.l0
    with k.scope():
        qT = k.sb("qT", [128, 4, T], BF16); kT = k.sb("kT", [128, 4, T], BF16)
        for h in range(4):
            k.dma(qT, qT[:, h, :], L['qT'], L['qT'].ap()[h * 128:(h + 1) * 128, :])
            k.dma(kT, kT[:, h, :], L['kT'], L['kT'].ap()[h * 128:(h + 1) * 128, :])
        ktok = k.sb("ktok", [64, 36, 512], BF16)
        v1 = k.sb("v1", [64, 36, 4, 132], BF16)
        for c0 in range(0, 36, 6):
            k.dma(ktok, ktok[:, c0:c0 + 6, :], L['ktok'], L['ktok'].ap()[c0 * 64:(c0 + 6) * 64, :].rearrange("(c l) n -> l c n", l=64))
            for h in range(4):
                k.dma(v1, v1[:, c0:c0 + 6, h, 0:128], L['v'],
                      L['v'].ap()[c0 * 64:(c0 + 6) * 64, h * 128:(h + 1) * 128].rearrange("(c l) n -> l c n", l=64))
        if c.stop == "ml_0":
            return
        k.memset(k.dve, v1, v1[:, :, :, 128:132], 1.0)
        if c.stop == "ml_1":
            return
        g = k.sb("g", [64, 36, 16])
        for c0 in range(0, 36, 6):
            k.dma(g, g[:, c0:c0 + 6, :], L['g'], L['g'].ap()[c0 * 64:(c0 + 6) * 64, :].rearrange("(c l) n -> l c n", l=64))
        fb = k.sb("fb", [64, 8]); ib = k.sb("ib", [64, 8])
        k.dma(fb, fb[:], e.ev_fb, e.ev_fb.ap().partition_broadcast(64).rearrange("p o n -> p (o n)"))
        k.dma(ib, ib[:], e.ev_ib, e.ev_ib.ap().partition_broadcast(64).rearrange("p o n -> p (o n)"))
        tri = k.sb("tri", [64, 2, 64]); ones = k.sb("ones", [64, 128])
        k.dma(tri, tri[:], e.ctri, e.ctri.ap().rearrange("r s l -> s r l"))
        k.memset(k.dve, ones, ones[:], 1.0)
        if c.stop == "ml_2":
            return
        z = k.sb("z", [64, 2, 36, 4]); nlf = k.sb("nlf", [64, 2, 36, 4]); ig = k.sb("ig", [64, 2, 36, 4])
        A = k.sb("A", [64, 2, 36, 4]); Bk = k.sb("Bk", [64, 2, 36, 4]); gdec = k.sb("gdec", [128, 2, 36, 4])
        for d in range(2):
            k.tt(k.dve, z, z[:, d], g, g[:, :, 8 + 4 * d:12 + 4 * d], fb, bc(fb[:, None, 4 * d:4 * d + 4], [64, 36, 4]), ALU.add)
            k.tt(k.dve, ig, ig[:, d], g, g[:, :, 4 * d:4 * d + 4], ib, bc(ib[:, None, 4 * d:4 * d + 4], [64, 36, 4]), ALU.add)
        k.actv(z, z[:], z, z[:], AF.Exp, scale=-1.0)
        k.actv(nlf, nlf[:], z, z[:], AF.Ln, bias=1.0)
        if c.stop == "ml_3":
            return
        with k.scope():
            pF = k.ps("pF", [64, 2, 144]); pG = k.ps("pG", [128, 288])
            for d in range(2):
                k.mm(pF, pF[:, d, :], tri, tri[:, d, :], nlf, nlf[:, d].rearrange("p c h -> p (c h)"))
            k.mm(pG, pG[:], ones, ones[:], nlf, nlf[:].rearrange("p d c h -> p (d c h)"))
            if c.stop == "ml_4":
                k.cp(k.dve, A, A[:].rearrange("p d c h -> p d (c h)"), pF, pF[:])
                k.cp(k.dve, gdec, gdec[:].rearrange("p d c h -> p (d c h)"), pG, pG[:])
            Af = A[:].rearrange("p d c h -> p d (c h)"); Bf = Bk[:].rearrange("p d c h -> p d (c h)")
            if c.stop != "ml_4":
                if c.stop != "ml_5b":
                    k.actv(A, Af, pF, pF[:], AF.Exp, scale=-1.0)
                if c.stop != "ml_5a":
                    k.tt(k.dve, Bk, Bf, ig, ig[:].rearrange("p d c h -> p d (c h)"), pF, pF[:], ALU.add)
                if c.stop not in ("ml_5", "ml_5a", "ml_5b"):
                    k.actv(Bk, Bk[:], Bk, Bk[:], AF.Exp)
                    k.actv(gdec, gdec[:].rearrange("p d c h -> p (d c h)"), pG, pG[:], AF.Exp, scale=-1.0)
        if c.stop in ("ml_a", "ml_4", "ml_5", "ml_5a", "ml_5b"):
            return
        C32 = [k.sb("C32_%d" % d, [128, 4, 132]) for d in range(2)]
        Cb = [k.sb("Cb_%d" % d, [128, 4, 132], BF16) for d in range(2)]
        for d in range(2):
            k.memset(k.dve, C32[d], C32[d][:], 0.0)
            k.memset(k.dve, Cb[d], Cb[d][:], 0.0)
        pS = [k.ps("pS%d" % d, [64, 4, 64]) for d in range(2)]
        pN = [[k.ps("pN%d_%d" % (d, i), [64, 2, 256]) for i in range(2)] for d in range(2)]
        pC = [k.ps("pC%d" % i, [128, 2, 256]) for i in range(2)]
        kt = [k.sb("kt%d" % d, [64, 4, 128], BF16) for d in range(2)]
        MB = [k.sb("MB%d" % d, [64, 4, 64]) for d in range(2)]
        Pt = [k.sb("Pt%d" % d, [64, 4, 64], BF16) for d in range(2)]
        sm = [k.sb("sm%d" % d, [64, 4, 4]) for d in range(2)]
        ho = [k.sb("ho%d" % d, [64, 4, 128]) for d in range(2)]
        for step in range(36):
            for d in range(2):
                ch = step if d == 0 else ORDB[step]
                tok = slice(ch * 64, (ch + 1) * 64)
                Bs = Bk[:, d, ch, :]
                As = A[:, d, ch, :]
                k.tt(k.dve, kt[d], kt[d][:], ktok, ktok[:, ch, :].rearrange("p (h e) -> p h e", h=4),
                     Bk, bc(Bs[:, :, None], [64, 4, 128]), ALU.mult)
                k.tt(k.dve, MB[d], MB[d][:], tri, bc(tri[:, d:d + 1, :], [64, 4, 64]), Bk, bc(Bs[:, :, None], [64, 4, 64]), ALU.mult)
                for h in range(4):
                    k.mm(pS[d], pS[d][:, h, :], kT, kT[:, h, tok], qT, qT[:, h, tok])
                k.tt(k.dve, Pt[d], Pt[d][:], pS[d], pS[d][:], MB[d], MB[d][:], ALU.mult)
                for h in range(4):
                    pn = pN[d][h // 2]
                    k.mm(pn, pn[:, h % 2, 0:132], Pt[d], Pt[d][:, h, :], v1, v1[:, ch, h, :], True, False)
                    k.mm(pn, pn[:, h % 2, 0:132], qT, qT[:, h, tok], Cb[d], Cb[d][:, h, :], False, True)
                s_ = sm[d]
                for i in range(2):
                    pn = pN[d][i]
                    k.tt(k.dve, s_, s_[:, 2 * i:2 * i + 2, 0], pn, pn[:, :, 128], A, As[:, 2 * i:2 * i + 2], ALU.mult)
                k.stt(k.dve, s_, s_[:, :, 1], s_, s_[:, :, 0], -1.0, s_, s_[:, :, 0], ALU.mult, ALU.max)
                k.ts(k.dve, s_, s_[:, :, 1], s_, s_[:, :, 1], 1.0, None, ALU.max)
                k.op(k.dve, lambda: nc.vector.reciprocal(out=s_[:, :, 2], in_=s_[:, :, 1]), [s_], [s_])
                k.tt(k.dve, s_, s_[:, :, 3], s_, s_[:, :, 2], A, As, ALU.mult)
                for i in range(2):
                    pn = pN[d][i]
                    k.tt(k.dve, ho[d], ho[d][:, 2 * i:2 * i + 2, :], pn, pn[:, :, 0:128],
                         s_, bc(s_[:, 2 * i:2 * i + 2, 3:4], [64, 2, 128]), ALU.mult)
                k.dma(L['hA'], L['hA'].ap()[d, tok, :], ho[d], ho[d][:].rearrange("p h e -> p (h e)"), q=k.pool)
                for i in range(2):
                    pc_ = pC[i]
                    for hh in range(2):
                        h = 2 * i + hh
                        k.mm(pc_, pc_[:, hh, 0:132], kt[d], kt[d][:, h, :], v1, v1[:, ch, h, :])
                    k.tt(k.dve, C32[d], C32[d][:, 2 * i:2 * i + 2, :], pc_, pc_[:, :, 0:132], C32[d], C32[d][:, 2 * i:2 * i + 2, :], ALU.add)
                k.tt(k.dve, C32[d], C32[d][:], C32[d], C32[d][:], gdec, bc(gdec[:, d, ch, :][:, :, None], [128, 4, 132]), ALU.mult)
                k.cp(k.act, Cb[d], Cb[d][:], C32[d], C32[d][:])
            if c.stop == "ml_b" and step == 0:
                return


def s5_phase(k, c, e):
    nc = k.nc
    L = c.l0
    TWO_PI = 2 * PI
    with k.scope():
        Mw = k.sb("Mw", [128, 64, 128], BF16)
        WT = [k.sb("WT%d" % i, [128, 64, 64], BF16) for i in range(2)]
        RC = [k.sb("RC%d" % i, [64, 64, 128], BF16) for i in range(2)]
        AR2 = k.sb("AR2", [64, 2, 64]); AI2 = k.sb("AI2", [64, 2, 64])
        with k.scope():
            lre = k.sb("lre", [64, 64]); lim = k.sb("lim", [64, 64]); dt = k.sb("dt", [64, 64])
            c.ltst = [k.sb("ltst%d" % i, [64, 128]) for i in range(2)]
            c.ltps = [k.ps("ltps%d" % i, [128, 64]) for i in range(2)]
            c.lt_i = 0
            load_T(k, c, lre, lre[:], e.ev_lre, e.ev_lre.ap().rearrange("r g n -> (r g) n"), 64, ncols=64)
            load_T(k, c, lim, lim[:], e.ev_lim, e.ev_lim.ap().rearrange("r g n -> (r g) n"), 64, ncols=64)
            k.dma(dt, dt[:], e.ev_ldt, e.ev_ldt.ap().partition_broadcast(64).rearrange("p o n -> p (o n)"))
            k.actv(dt, dt[:], dt, dt[:], AF.Exp)
            ldr = k.sb("ldr", [64, 64]); ang = k.sb("ang", [64, 64])
            k.tt(k.dve, ldr, ldr[:], lre, lre[:], dt, dt[:], ALU.mult)
            k.tt(k.dve, ang, ang[:], lim, lim[:], dt, dt[:], ALU.mult)
            mg = k.sb("mg", [64, 16, 64]); sn = k.sb("sn", [64, 9, 64]); cs = k.sb("cs", [64, 9, 64])
            for ti, tau in enumerate(range(-7, 9)):
                k.actv(mg, mg[:, ti, :], ldr, ldr[:], AF.Exp, scale=float(tau))
            k.memset(k.dve, sn, sn[:, 0, :], 0.0); k.memset(k.dve, cs, cs[:, 0, :], 1.0)
            k.actv(sn, sn[:, 1, :], ang, ang[:], AF.Sin, scale=1.0 / 16)
            k.actv(cs, cs[:, 1, :], ang, ang[:], AF.Sin, bias=c.epsb[0:64, 1:2], scale=1.0 / 16, extra=[c.epsb])
            q1 = k.sb("q1", [64, 64]); q2 = k.sb("q2", [64, 64])
            for _ in range(4):
                k.tt(k.dve, q1, q1[:], sn, sn[:, 1, :], cs, cs[:, 1, :], ALU.mult)
                k.tt(k.dve, q2, q2[:], sn, sn[:, 1, :], sn, sn[:, 1, :], ALU.mult)
                k.ts(k.dve, sn, sn[:, 1, :], q1, q1[:], 2.0, None, ALU.mult)
                k.ts(k.dve, cs, cs[:, 1, :], q2, q2[:], -2.0, 1.0, ALU.mult, ALU.add)
            for tau in range(2, 9):
                k.tt(k.dve, q1, q1[:], cs, cs[:, tau - 1, :], cs, cs[:, 1, :], ALU.mult)
                k.tt(k.dve, q2, q2[:], sn, sn[:, tau - 1, :], sn, sn[:, 1, :], ALU.mult)
                k.tt(k.dve, cs, cs[:, tau, :], q1, q1[:], q2, q2[:], ALU.subtract)
                k.tt(k.dve, q1, q1[:], sn, sn[:, tau - 1, :], cs, cs[:, 1, :], ALU.mult)
                k.tt(k.dve, q2, q2[:], cs, cs[:, tau - 1, :], sn, sn[:, 1, :], ALU.mult)
                k.tt(k.dve, sn, sn[:, tau, :], q1, q1[:], q2, q2[:], ALU.add)
            pwr = k.sb("pwr", [64, 16, 64]); pwi = k.sb("pwi", [64, 16, 64])
            for ti, tau in enumerate(range(-7, 9)):
                at = abs(tau)
                k.tt(k.dve, pwr, pwr[:, ti, :], mg, mg[:, ti, :], cs, cs[:, at, :], ALU.mult)
                if tau >= 0:
                    k.tt(k.dve, pwi, pwi[:, ti, :], mg, mg[:, ti, :], sn, sn[:, at, :], ALU.mult)
                else:
                    k.stt(k.dve, pwi, pwi[:, ti, :], mg, mg[:, ti, :], -1.0, sn, sn[:, at, :], ALU.mult, ALU.mult)
            for s_ in range(2):
                k.cp(k.dve, AR2, AR2[:, s_, :], pwr, pwr[:, 15, :])
            k.ts(k.dve, AI2, AI2[:, 0, :], pwi, pwi[:, 15, :], -1.0, None, ALU.mult)
            k.cp(k.dve, AI2, AI2[:, 1, :], pwi, pwi[:, 15, :])
            nr = k.sb("nr", [64, 64]); den = k.sb("den", [64, 64]); t1 = k.sb("t1", [64, 64]); t2 = k.sb("t2", [64, 64])
            cor = k.sb("cor", [64, 64]); coi = k.sb("coi", [64, 64])
            k.ts(k.dve, nr, nr[:], pwr, pwr[:, 8, :], -1.0, None, ALU.add)
            k.tt(k.dve, den, den[:], lre, lre[:], lre, lre[:], ALU.mult)
            k.tt(k.dve, t1, t1[:], lim, lim[:], lim, lim[:], ALU.mult)
            k.tt(k.dve, den, den[:], den, den[:], t1, t1[:], ALU.add)
            k.op(k.dve, lambda: nc.vector.reciprocal(out=den[:], in_=den[:]), [den], [den])
            k.tt(k.dve, t1, t1[:], nr, nr[:], lre, lre[:], ALU.mult)
            k.tt(k.dve, t2, t2[:], pwi, pwi[:, 8, :], lim, lim[:], ALU.mult)
            k.tt(k.dve, t1, t1[:], t1, t1[:], t2, t2[:], ALU.add)
            k.tt(k.dve, cor, cor[:], t1, t1[:], den, den[:], ALU.mult)
            k.tt(k.dve, t1, t1[:], pwi, pwi[:, 8, :], lre, lre[:], ALU.mult)
            k.tt(k.dve, t2, t2[:], nr, nr[:], lim, lim[:], ALU.mult)
            k.tt(k.dve, t1, t1[:], t1, t1[:], t2, t2[:], ALU.subtract)
            k.tt(k.dve, coi, coi[:], t1, t1[:], den, den[:], ALU.mult)
            bre = k.sb("bre", [64, 64, 16]); bim = k.sb("bim", [64, 64, 16])
            for r in range(2):
                for g0 in range(0, 32, 4):
                    k.dma(bre, bre[:, r * 32 + g0:r * 32 + g0 + 4, :], e.ev_bre, e.ev_bre.ap()[r, g0:g0 + 4].rearrange("g n p -> n g p"))
                    k.dma(bim, bim[:, r * 32 + g0:r * 32 + g0 + 4, :], e.ev_bim, e.ev_bim.ap()[r, g0:g0 + 4].rearrange("g n p -> n g p"))
            bbr = k.sb("bbr", [64, 64, 16]); bbi = k.sb("bbi", [64, 64, 16])
            u1 = k.sb("u1", [64, 64, 16]); u2 = k.sb("u2", [64, 64, 16])
            corb = bc(cor[:, :, None], [64, 64, 16]); coib = bc(coi[:, :, None], [64, 64, 16])
            k.tt(k.dve, u1, u1[:], bre, bre[:], cor, corb, ALU.mult)
            k.tt(k.dve, u2, u2[:], bim, bim[:], coi, coib, ALU.mult)
            k.tt(k.dve, bbr, bbr[:], u1, u1[:], u2, u2[:], ALU.subtract)
            k.tt(k.dve, u1, u1[:], bim, bim[:], cor, corb, ALU.mult)
            k.tt(k.dve, u2, u2[:], bre, bre[:], coi, coib, ALU.mult)
            k.tt(k.dve, bbi, bbi[:], u1, u1[:], u2, u2[:], ALU.add)
            cTr = k.sb("cTr", [64, 64, 16]); cTi = k.sb("cTi", [64, 64, 16])
            cst = k.sb("cst", [128, 8, 64])
            pCt = [k.ps("pCt%d" % i, [64, 4, 128]) for i in range(2)]
            for src, dst in ((e.ev_cre, cTr), (e.ev_cim, cTi)):
                for t0 in range(0, 8, 2):
                    k.dma(cst, cst[:, t0:t0 + 2, :], src, src.ap()[t0 * 128:(t0 + 2) * 128, :].rearrange("(t p) n -> p t n", p=128))
                for t in range(8):
                    p = pCt[t // 4]
                    k.tr(p, p[:, t % 4, :], cst, cst[:, t, :], c.id32, c.id32[:])
                for hf in range(2):
                    k.cp(k.act, dst, dst[:, hf * 32:(hf + 1) * 32, :].rearrange("n g p -> n (g p)"),
                         pCt[hf], pCt[hf][:].rearrange("n t m -> n (t m)"))
            maskM = k.sb("maskM", [128, 2, 128])
            k.dma(maskM, maskM[:], e.cmaskM, e.cmaskM.ap().rearrange("r a b -> a r b"))
            w1_ = k.sb("w1_", [64, 32, 16]); w2_ = k.sb("w2_", [64, 32, 16])

            def cmul(r, powf, sr, si, dr, dr_ap, di, di_ap, neg_im=False):
                rs = slice(r * 32, (r + 1) * 32)
                for i in range(8):
                    ti = powf(i) + 7
                    pr = bc(pwr[:, ti, rs][:, :, None], [64, 32, 16]); pi_ = bc(pwi[:, ti, rs][:, :, None], [64, 32, 16])
                    k.tt(k.dve, w1_, w1_[:], sr, sr[:, rs, :], pwr, pr, ALU.mult)
                    k.tt(k.pool, w2_, w2_[:], si, si[:, rs, :], pwi, pi_, ALU.mult)
                    k.tt(k.dve, dr, dr_ap(i), w1_, w1_[:], w2_, w2_[:], ALU.subtract)
                    k.tt(k.dve, w1_, w1_[:], si, si[:, rs, :], pwr, pr, ALU.mult)
                    k.tt(k.pool, w2_, w2_[:], sr, sr[:, rs, :], pwi, pi_, ALU.mult)
                    if neg_im:
                        k.stt(k.dve, di, di_ap(i), w1_, w1_[:], -1.0, w2_, w2_[:], ALU.mult, ALU.subtract)
                    else:
                        k.tt(k.dve, di, di_ap(i), w1_, w1_[:], w2_, w2_[:], ALU.add)

            EBr = k.sb("EBr", [64, 32, 8, 16]); EBi = k.sb("EBi", [64, 32, 8, 16])
            ECr = k.sb("ECr", [64, 32, 8, 16]); ECi = k.sb("ECi", [64, 32, 8, 16])
            pM = [k.ps("pM%d" % i, [128, 4, 128]) for i in range(2)]
            pW = [k.ps("pW%d" % i, [128, 8, 64]) for i in range(2)]
            for r in range(2):
                sig = (lambda i: i) if r == 0 else (lambda i: 7 - i)
                rs = slice(r * 32, (r + 1) * 32)
                cmul(r, lambda i: -sig(i), bbr, bbi, EBr, lambda i: EBr[:, :, i, :], EBi, lambda i: EBi[:, :, i, :])
                cmul(r, lambda i: sig(i), cTr, cTi, ECr, lambda i: ECr[:, :, i, :], ECi, lambda i: ECi[:, :, i, :], neg_im=True)
                for g0 in range(0, 32, 4):
                    p = pM[(g0 // 4) % 2]
                    for gg in range(4):
                        g = g0 + gg
                        k.mm(p, p[:, gg, :], EBr, EBr[:, g].rearrange("n i p -> n (i p)"), ECr, ECr[:, g].rearrange("n i p -> n (i p)"), True, False)
                        k.mm(p, p[:, gg, :], EBi, EBi[:, g].rearrange("n i p -> n (i p)"), ECi, ECi[:, g].rearrange("n i p -> n (i p)"), False, True)
                    k.tt(k.dve, Mw, Mw[:, r * 32 + g0:r * 32 + g0 + 4, :], p, p[:], maskM, bc(maskM[:, r:r + 1, :], [128, 4, 128]), ALU.mult)
                cmul(r, lambda i: sig(i) + 1, cTr, cTi,
                     RC[0], lambda i: RC[0][:, rs, :].rearrange("n g (j p) -> n g j p", j=8)[:, :, i, :],
                     RC[1], lambda i: RC[1][:, rs, :].rearrange("n g (j p) -> n g j p", j=8)[:, :, i, :], neg_im=True)
                cmul(r, lambda i: 7 - sig(i), bbr, bbi, ECr, lambda i: ECr[:, :, i, :], ECi, lambda i: ECi[:, :, i, :])
                for comp, src in enumerate((ECr, ECi)):
                    for g0 in range(0, 32, 8):
                        p = pW[(g0 // 8) % 2]
                        for gg in range(8):
                            k.tr(p, p[:, gg, :], src, src[:, g0 + gg].rearrange("n i p -> n (i p)"), c.id32, c.id32[0:64, 0:64])
                        k.cp(k.act, WT[comp], WT[comp][:, r * 32 + g0:r * 32 + g0 + 8, :], p, p[:])
        X = k.sb("X", [128, 32, 288], BF16)
        Sa = k.sb("Sa", [64, 2, 64, 290], BF16)
        with k.scope():
            u32 = k.sb("u32", [128, 8, 512]); u16 = k.sb("u16", [128, 32, 128], BF16)
            pX = [k.ps("pX%d" % i, [128, 8, 128], BF16) for i in range(2)]
            it = 0
            for ct, (c0, n) in enumerate(((0, 128), (128, 128), (256, 32))):
                k.dma(u32, u32[0:n], L['u'], L['u'].ap()[8 * c0:8 * (c0 + n), :].rearrange("(c i) ch -> c i ch", i=8))
                k.cp(k.dve, u16, u16[0:n].rearrange("c g (i p) -> c g i p", i=8), u32, u32[0:n].rearrange("c i (g p) -> c g i p", g=32))
                for g0 in range(0, 32, 8):
                    p = pX[it % 2]; it += 1
                    for gg in range(8):
                        g = g0 + gg
                        k.tr(p, p[:, gg, 0:n], u16, u16[0:n, g, :], c.idb, c.idb[0:n, 0:n])
                    k.cp(k.act, X, X[:, g0:g0 + 8, c0:c0 + n], p, p[:, :, 0:n])
        with k.scope():
            pB = [k.ps("pB%d" % i, [64, 288]) for i in range(4)]
            it = 0
            for rg in range(64):
                for comp in range(2):
                    p = pB[it % 4]; it += 1
                    k.mm(p, p[:], WT[comp], WT[comp][:, rg, :], X, X[:, rg % 32, :])
                    off = 0 if rg < 32 else 1
                    k.cp(k.act if it % 2 else k.dve, Sa, Sa[:, comp, rg, off:off + 288], p, p[:])
        R3 = k.sb("R3", [64, 3, 64]); P1 = k.sb("P1", [64, 2, 64]); P2 = k.sb("P2", [64, 2, 64])
        k.memset(k.dve, R3, R3[:], 0.0)
        base = Sa[:, :, 0:32, 0]
        pstride = base.ap[0][0]
        R3v = R3[:, 1:3, :].rearrange("n c (r g) -> n c r g", r=2)
        for step in range(288):
            cf, cb = step, ORDB8[step]
            bv = AP(tensor=Sa.h, offset=base.offset + cf, ap=[[pstride, 64], [64 * 290, 2], [32 * 290 + (cb + 1) - cf, 2], [290, 32]])
            k.tt(k.dve, P1, P1[:], AR2, AR2[:], R3, R3[:, 1:3, :], ALU.mult)
            k.tt(k.dve, P2, P2[:], AI2, AI2[:], R3, R3[:, 0:2, :], ALU.mult)
            k.tt(k.dve, P1, P1[:], P1, P1[:], P2, P2[:], ALU.add)
            k.tt(k.dve, R3, R3v, P1, P1[:].rearrange("n c (r g) -> n c r g", r=2), Sa, bv, ALU.add)
            k.cp(k.dve, R3, R3[:, 0, :], R3, R3[:, 2, :])
            k.cp(k.dve, Sa, bv, R3, R3v)
        k.cp(k.dve, Sa, Sa[:, :, 32:64, 0:1], Sa, Sa[:, :, 32:64, 1:2])
        with k.scope():
            pY = [k.ps("pY%d" % i, [128, 288]) for i in range(2)]
            pZ = [k.ps("pZ%d" % i, [128, 4, 128]) for i in range(2)]
            Yq = k.sb("Yq", [128, 8, 288]); Y2q = [k.sb("Y2q%d" % i, [128, 8, 128]) for i in range(2)]
            it = 0; iz = 0; iy = 0
            for q in range(4):
                for gl in range(8):
                    g = q * 8 + gl
                    p = pY[it % 2]; it += 1
                    k.mm(p, p[:], Mw, Mw[:, g, :], X, X[:, g, :], True, False)
                    k.mm(p, p[:], Mw, Mw[:, 32 + g, :], X, X[:, g, :], False, False)
                    for comp in range(2):
                        k.mm(p, p[:, 1:288], RC[comp], RC[comp][:, g, :], Sa, Sa[:, comp, g, 0:287], False, False)
                    rg = 32 + g
                    for comp in range(2):
                        k.mm(p, p[:, 0:31], RC[comp], RC[comp][:, rg, :], Sa, Sa[:, comp, rg, 2:33], False, False)
                        k.mm(p, p[:, 32:287], RC[comp], RC[comp][:, rg, :], Sa, Sa[:, comp, rg, 34:289], False, False)
                        k.mm(p, p[:, 287:288], RC[comp], RC[comp][:, rg, :], Sa, Sa[:, comp, rg, 0:1], False, comp == 1)
                    k.cp(k.act, Yq, Yq[:, gl, :], p, p[:])
                for ct, (c0, n) in enumerate(((0, 128), (128, 128), (256, 32))):
                    y2 = Y2q[iy % 2]; iy += 1
                    for gl in range(8):
                        if gl % 4 == 0:
                            pz = pZ[iz % 2]; iz += 1
                        k.tr(pz, pz[0:n, gl % 4, :], Yq, Yq[:, gl, c0:c0 + n], c.id32, c.id32[:])
                        k.cp(k.dve if gl % 2 else k.act, y2, y2[0:n, :, gl * 16:(gl + 1) * 16],
                             pz, pz[0:n, gl % 4, :].rearrange("c (j p) -> c j p", j=8))
                    for jh in range(2):
                        k.dma(L['yS'], L['yS'].ap()[8 * c0:8 * (c0 + n), q * 128:(q + 1) * 128].rearrange("(c j) ch -> c j ch", j=8)[:, jh * 4:(jh + 1) * 4, :],
                              y2, y2[0:n, jh * 4:(jh + 1) * 4, :], q=k.pool)


def finish0_phase(k, c, e):
    nc = k.nc
    L = c.l0
    xs = e.xs
    with k.scope():
        c.wstage = [k.sb("wst%d" % i, [128, 4096]) for i in range(2)]
        wglu = k.sb("wglu", [128, 4, 1024], BF16); wout = k.sb("wout", [128, 8, 1024], BF16)
        for cb in range(2):
            load_w_bf16(k, c, wglu, wglu[:, :, cb * 512:(cb + 1) * 512], e.ev_wglu, e.ev_wglu.ap()[:, cb * 512:(cb + 1) * 512], 512, kchunks=4)
            load_w_bf16(k, c, wout, wout[:, :, cb * 512:(cb + 1) * 512], e.ev_wout, e.ev_wout.ap()[:, cb * 512:(cb + 1) * 512], 512)
        hwb = k.sb("hwb", [128, 512]); dsk = k.sb("dsk", [128, 512])
        k.dma(hwb, hwb[:], e.ev_hw, e.ev_hw.ap().partition_broadcast(128).rearrange("p o n -> p (o n)"))
        k.dma(dsk, dsk[:], e.ev_d, e.ev_d.ap().partition_broadcast(128).rearrange("p o n -> p (o n)"))
        gate = [k.sb("gate%d" % w, [128, D]) for w in range(2)]
        e.load_gate(gate[0], 0, 0, 2); e.load_gate(gate[1], 0, 1, 2)
        hA = [k.sb("hA%d" % i, [128, 2, 512]) for i in range(2)]
        ot = [k.sb("ot%d" % i, [128, 512]) for i in range(2)]
        ut = [k.sb("ut%d" % i, [128, 512]) for i in range(2)]
        yt = [k.sb("yt%d" % i, [128, 512]) for i in range(2)]
        xt = [k.sb("xt%d" % i, [128, D]) for i in range(2)]
        w1 = k.sb("fw1", [128, 512]); w2 = k.sb("fw2", [128, 512]); st = k.sb("fst", [128, 8])
        cat = k.sb("cat", [128, D], BF16); ybb = k.sb("ybb", [128, 512], BF16)
        ybT = k.sb("ybT", [128, 4, 128], BF16); catT = k.sb("catT", [128, 8, 128], BF16)
        pTb = k.ps("pTb", [128, 8, 128], BF16)
        pG = [k.ps("pGl%d" % i, [128, 512]) for i in range(2)]
        pO = [k.ps("pO%d" % i, [128, 512]) for i in range(2)]
        yo = k.sb("yo", [128, D])
        for t in range(NT):
            tok = slice(t * 128, (t + 1) * 128)
            h_, o_, u_, y_, x_ = hA[t % 2], ot[t % 2], ut[t % 2], yt[t % 2], xt[t % 2]
            k.dma(h_, h_[:], L['hA'], L['hA'].ap()[:, tok, :].rearrange("r t n -> t r n"))
            k.dma(o_, o_[:], L['o'], L['o'].ap()[tok, :])
            k.dma(u_, u_[:], L['u'], L['u'].ap()[tok, :])
            k.dma(y_, y_[:], L['yS'], L['yS'].ap()[tok, :])
            k.dma(x_, x_[:], xs, xs.ap()[tok, :])
            k.tt(k.dve, w1, w1[:], h_, h_[:, 0, :], h_, h_[:, 1, :], ALU.add)
            k.tt(k.dve, w2, w2[:], w1, w1[:], w1, w1[:], ALU.mult)
            k.op(k.dve, lambda: nc.vector.reduce_sum(out=st[:, 0:4], in_=w2[:].rearrange("p (h e) -> p h e", h=4), axis=AX.X), [st], [w2])
            k.actv(st, st[:, 0:4], st, st[:, 0:4], AF.Ln, bias=c.epsb[:, 0:1], scale=1.0 / 128, extra=[c.epsb])
            k.actv(st, st[:, 4:8], st, st[:, 0:4], AF.Exp, scale=-0.5)
            k.tt(k.dve, w1, w1[:].rearrange("p (h e) -> p h e", h=4), w1, w1[:].rearrange("p (h e) -> p h e", h=4),
                 st, bc(st[:, 4:8][:, :, None], [128, 4, 128]), ALU.mult)
            k.tt(k.dve, w1, w1[:], w1, w1[:], hwb, hwb[:], ALU.mult)
            k.actv(o_, o_[:], o_, o_[:], AF.Sigmoid)
            k.tt(k.dve, cat, cat[:, 0:512], w1, w1[:], o_, o_[:], ALU.mult)
            k.tt(k.dve, w2, w2[:], u_, u_[:], dsk, dsk[:], ALU.mult)
            k.tt(k.dve, w2, w2[:], w2, w2[:], y_, y_[:], ALU.add)
            k.tt(k.pool, y_, y_[:], w2, w2[:], w2, w2[:], ALU.mult)
            k.ts(k.dve, y_, y_[:], y_, y_[:], 0.044715, 1.0, ALU.mult, ALU.add)
            k.tt(k.dve, y_, y_[:], y_, y_[:], w2, w2[:], ALU.mult)
            k.actv(y_, y_[:], y_, y_[:], AF.Sigmoid, scale=2.0 * math.sqrt(2.0 / PI))
            k.tt(k.dve, ybb, ybb[:], y_, y_[:], w2, w2[:], ALU.mult)
            for j in range(4):
                k.tr(pTb, pTb[:, j, :], ybb, ybb[:, j * 128:(j + 1) * 128], c.idb, c.idb[:])
            k.cp(k.act, ybT, ybT[:], pTb, pTb[:, 0:4, :])
            for cb in range(2):
                for kc in range(4):
                    k.mm(pG[cb], pG[cb][:], ybT, ybT[:, kc, :], wglu, wglu[:, kc, cb * 512:(cb + 1) * 512], kc == 0, kc == 3)
            k.actv(w2, w2[:], pG[1], pG[1][:], AF.Sigmoid)
            k.tt(k.dve, cat, cat[:, 512:1024], pG[0], pG[0][:], w2, w2[:], ALU.mult)
            for j in range(8):
                k.tr(pTb, pTb[:, j, :], cat, cat[:, j * 128:(j + 1) * 128], c.idb, c.idb[:])
            k.cp(k.act, catT, catT[:], pTb, pTb[:])
            g = gate[1 if t < 2 else 0]
            for cb in range(2):
                for kc in range(8):
                    k.mm(pO[cb], pO[cb][:], catT, catT[:, kc, :], wout, wout[:, kc, cb * 512:(cb + 1) * 512], kc == 0, kc == 7)
                k.tt(k.dve, yo, yo[:, cb * 512:(cb + 1) * 512], pO[cb], pO[cb][:], g, g[:, cb * 512:(cb + 1) * 512], ALU.mult)
            k.tt(k.pool, yo, yo[:], yo, yo[:], x_, x_[:], ALU.add)
            k.dma(xs, xs.ap()[tok, :], yo, yo[:], q=k.pool)


def layer1(k, c, env):
    e = E(env)
    nc = k.nc
    xs = e.xs
    z_d = k.dram("z_d", [T, D]); gt_d = k.dram("gt_d", [T, 32]); o1_d = k.dram("o1_d", [2, T, D])
    with k.scope():
        qT = k.sb("gqT", [128, 8, T], BF16); kT = k.sb("gkT", [128, 8, T], BF16); vT = k.sb("gvT", [128, 8, T], BF16)
        with k.scope():
            hT = k.sb("hT", [128, 8, T], BF16)
            with k.scope():
                c.xt = [k.sb("xt%d" % i, [128, D]) for i in range(2)]
                c.sq = k.sb("sq", [128, D]); c.ss = k.sb("ss", [128, 4])
                c.pT = [k.ps("pT%d" % i, [128, 4, 128]) for i in range(2)]
                norm_mod(k, c, xs, list(range(NT)), e.ab_fn(0, 1), hT)
            c.wstage = [k.sb("wst%d" % i, [128, 2048]) for i in range(2)]
            c.ltst = [k.sb("ltst%d" % i, [64, 128]) for i in range(2)]
            c.ltps = [k.ps("ltps%d" % i, [128, 64]) for i in range(2)]
            c.lt_i = 0
            cw = k.sb("cw", [128, 24, 9])
            for ci in range(24):
                load_T(k, c, cw, cw[:, ci, :], e.od_conv, e.od_conv.ap()[:, ci * 128:(ci + 1) * 128], 9)
            ones32 = k.sb("ones32", [128, 128]); k.memset(k.dve, ones32, ones32[:], 1.0)
            wch = [k.sb("wch%d" % i, [128, 8, 256], BF16) for i in range(2)]
            P32 = k.sb("P32", [128, T]); Cv = k.sb("Cv", [128, T]); S32 = P32; Q32 = Cv
            rs = k.sb("rs", [128, 512])
            pp = [k.ps("pp%d" % i, [128, 512]) for i in range(3)]
            blocks = [(0, 512), (512, 512), (1024, 512), (1536, 512), (2048, 256)]
            it = 0
            for ci in range(24):
                if ci % 2 == 0:
                    wc = wch[(ci // 2) % 2]
                    load_w_bf16(k, c, wc, wc[:], e.od_w_in, e.od_w_in.ap()[:, ci * 128:(ci + 2) * 128], 256)
                wo = (ci % 2) * 128
                for (t0, tn) in blocks:
                    p = pp[it % 3]; it += 1
                    for kc in range(8):
                        k.mm(p, p[:, 0:tn], wc, wc[:, kc, wo:wo + 128], hT, hT[:, kc, t0:t0 + tn], kc == 0, kc == 7)
                    k.cp(k.act, P32, P32[:, t0:t0 + tn], p, p[:, 0:tn])
                w_ = lambda tap: cw[:, ci, tap:tap + 1]
                k.ts(k.dve, Cv, Cv[:], P32, P32[:], w_(4), None, ALU.mult, extra=[cw])
                k.stt(k.dve, Cv, Cv[:, 1:256], P32, P32[:, 0:255], w_(3), Cv, Cv[:, 1:256], ALU.mult, ALU.add, extra=[cw])
                k.stt(k.dve, Cv, Cv[:, 0:255], P32, P32[:, 1:256], w_(5), Cv, Cv[:, 0:255], ALU.mult, ALU.add, extra=[cw])
                Pl = P32[:, 256:T].rearrange("p (r q) -> p r q", q=64); Cl = Cv[:, 256:T].rearrange("p (r q) -> p r q", q=64)
                for a in range(3):
                    for b in range(3):
                        if a == 1 and b == 1:
                            continue
                        dr, dc = a - 1, b - 1
                        r0, r1 = max(0, -dr), 32 - max(0, dr)
                        c0, c1 = max(0, -dc), 64 - max(0, dc)
                        k.stt(k.dve, Cv, Cl[:, r0:r1, c0:c1], P32, Pl[:, r0 + dr:r1 + dr, c0 + dc:c1 + dc],
                              w_(a * 3 + b), Cv, Cl[:, r0:r1, c0:c1], ALU.mult, ALU.add, extra=[cw])
                h = ci % 8
                if ci >= 16:
                    k.actv(vT, vT[:, h, :], Cv, Cv[:], AF.Silu)
                else:
                    k.actv(S32, S32[:], Cv, Cv[:], AF.Silu)
                    k.tt(k.pool, Q32, Q32[:], S32, S32[:], S32, S32[:], ALU.mult)
                    dst = qT if ci < 8 else kT
                    for (t0, tn) in blocks:
                        p = pp[it % 3]; it += 1
                        k.mm(p, p[:, 0:tn], ones32, ones32[:], Q32, Q32[:, t0:t0 + tn])
                        k.actv(rs, rs[:, 0:tn], p, p[:, 0:tn], AF.Ln, bias=c.epsb[:, 0:1], extra=[c.epsb])
                        k.actv(rs, rs[:, 0:tn], rs, rs[:, 0:tn], AF.Exp, scale=-0.5)
                        k.stt(k.dve, dst, dst[:, h, t0:t0 + tn], S32, S32[:, t0:t0 + tn], (1.0 / math.sqrt(128) if ci < 8 else 1.0),
                              rs, rs[:, 0:tn], ALU.mult, ALU.mult)
            wz = k.sb("wz", [128, 8, 544], BF16)
            zt = [P32, Cv]
            iz = 0
            for (zc0, zn) in ((0, 512), (512, 544)):
                for cb in range(0, zn, 256):
                    n = min(256, zn - cb)
                    load_w_bf16(k, c, wz, wz[:, :, cb:cb + n], e.od_w_in, e.od_w_in.ap()[:, 3072 + zc0 + cb:3072 + zc0 + cb + n], n)
                for t in range(NT):
                    z_ = zt[iz % 2]; iz += 1
                    tok = slice(t * 128, (t + 1) * 128)
                    for bi, (col, n) in enumerate(((0, 512), (512, 32))[:(1 if zc0 == 0 else 2)]):
                        p = pp[it % 3]; it += 1
                        for kc in range(8):
                            k.mm(p, p[:, 0:n], hT, hT[:, kc, tok], wz, wz[:, kc, col:col + n], kc == 0, kc == 7)
                        k.cp(k.act if bi % 2 else k.dve, z_, z_[:, col:col + n], p, p[:, 0:n])
                    k.dma(z_d, z_d.ap()[tok, zc0:zc0 + 512], z_, z_[:, 0:512], q=k.pool)
                    if zc0:
                        k.dma(gt_d, gt_d.ap()[tok, :], z_, z_[:, 512:544], q=k.pool)
        if c.stop == "proj1":
            c.dbg_qkv = (qT, kT, vT)
            return
        gdn_phase(k, c, e, qT, kT, vT, gt_d, o1_d)
    if c.stop == "gdn":
        return
    with k.scope():
        c.wstage = [k.sb("wst%d" % i, [128, 4096]) for i in range(2)]
        wout = k.sb("wout", [128, 8, 1024], BF16)
        for cb in range(2):
            load_w_bf16(k, c, wout, wout[:, :, cb * 512:(cb + 1) * 512], e.od_wout, e.od_wout.ap()[:, cb * 512:(cb + 1) * 512], 512)
        hwb = k.sb("hwb", [128, D])
        k.dma(hwb, hwb[:], e.od_hw, e.od_hw.ap().partition_broadcast(128).rearrange("p o n -> p (o n)"))
        gate = k.sb("gate", [128, D]); e.load_gate(gate, 1, 0, 2)
        ot = [k.sb("ot%d" % i, [128, 2, D]) for i in range(2)]
        zt = [k.sb("zt%d" % i, [128, D]) for i in range(2)]
        xt = [k.sb("xt%d" % i, [128, D]) for i in range(2)]
        w1 = k.sb("w1", [128, D]); w2 = k.sb("w2", [128, D]); st = k.sb("st", [128, 16])
        cat = k.sb("cat", [128, D], BF16); catT = k.sb("catT", [128, 8, 128], BF16)
        pTb = k.ps("pTb", [128, 8, 128], BF16)
        pO = [k.ps("pO%d" % i, [128, 512]) for i in range(2)]
        yo = k.sb("yo", [128, D])
        for i, t in enumerate(range(2, NT)):
            tok = slice(t * 128, (t + 1) * 128)
            o_, z_, x_ = ot[i % 2], zt[i % 2], xt[i % 2]
            k.dma(o_, o_[:], o1_d, o1_d.ap()[:, tok, :].rearrange("r t n -> t r n"))
            k.dma(z_, z_[:], z_d, z_d.ap()[tok, :])
            k.dma(x_, x_[:], xs, xs.ap()[tok, :])
            k.tt(k.dve, w1, w1[:], o_, o_[:, 0, :], o_, o_[:, 1, :], ALU.add)
            k.tt(k.pool, w2, w2[:], w1, w1[:], w1, w1[:], ALU.mult)
            k.op(k.dve, lambda: nc.vector.reduce_sum(out=st[:, 0:8], in_=w2[:].rearrange("p (h e) -> p h e", h=8), axis=AX.X), [st], [w2])
            k.actv(st, st[:, 0:8], st, st[:, 0:8], AF.Ln, bias=c.epsb[:, 0:1], scale=1.0 / 128, extra=[c.epsb])
            k.actv(st, st[:, 8:16], st, st[:, 0:8], AF.Exp, scale=-0.5)
            k.tt(k.dve, w1, w1[:].rearrange("p (h e) -> p h e", h=8), w1, w1[:].rearrange("p (h e) -> p h e", h=8),
                 st, bc(st[:, 8:16][:, :, None], [128, 8, 128]), ALU.mult)
            k.tt(k.dve, w1, w1[:], w1, w1[:], hwb, hwb[:], ALU.mult)
            k.actv(z_, z_[:], z_, z_[:], AF.Silu)
            k.tt(k.dve, cat, cat[:], w1, w1[:], z_, z_[:], ALU.mult)
            for j in range(8):
                k.tr(pTb, pTb[:, j, :], cat, cat[:, j * 128:(j + 1) * 128], c.idb, c.idb[:])
            k.cp(k.act, catT, catT[:], pTb, pTb[:])
            for cb in range(2):
                for kc in range(8):
                    k.mm(pO[cb], pO[cb][:], catT, catT[:, kc, :], wout, wout[:, kc, cb * 512:(cb + 1) * 512], kc == 0, kc == 7)
                k.tt(k.dve, yo, yo[:, cb * 512:(cb + 1) * 512], pO[cb], pO[cb][:], gate, gate[:, cb * 512:(cb + 1) * 512], ALU.mult)
            k.tt(k.pool, yo, yo[:], yo, yo[:], x_, x_[:], ALU.add)
            k.dma(xs, xs.ap()[tok, :], yo, yo[:], q=k.pool)


def gdn_phase(k, c, e, qT, kT, vT, gt_d, o1_d):
    nc = k.nc
    with k.scope():
        gt = k.sb("gt", [64, 36, 32])
        for c0 in range(0, 36, 6):
            k.dma(gt, gt[:, c0:c0 + 6, :], gt_d, gt_d.ap()[c0 * 64:(c0 + 6) * 64, :].rearrange("(c l) n -> l c n", l=64))
        ga = k.sb("ga", [64, 16]); dtb = k.sb("dtb", [64, 16])
        k.dma(ga, ga[:], e.od_alog, e.od_alog.ap().partition_broadcast(64).rearrange("p o n -> p (o n)"))
        k.dma(dtb, dtb[:], e.od_dtb, e.od_dtb.ap().partition_broadcast(64).rearrange("p o n -> p (o n)"))
        k.actv(ga, ga[:], ga, ga[:], AF.Exp)
        tri = k.sb("tri", [64, 2, 64]); strict = k.sb("strict", [64, 2, 64])
        k.dma(tri, tri[:], e.ctri, e.ctri.ap().rearrange("r s l -> s r l"))
        k.dma(strict, strict[:], e.cstrict, e.cstrict.ap().rearrange("r s l -> s r l"))
        ones = k.sb("ones", [64, 128]); k.memset(k.dve, ones, ones[:], 1.0)
        ng = k.sb("ng", [64, 2, 36, 8]); beta = k.sb("beta", [64, 2, 36, 8])
        eG = k.sb("eG", [64, 2, 36, 8]); kds = k.sb("kds", [64, 2, 36, 8]); bg = k.sb("bg", [64, 2, 36, 8])
        gl = k.sb("gl", [128, 2, 36, 8])
        for d in range(2):
            k.tt(k.dve, ng, ng[:, d], gt, gt[:, :, 8 * d:8 * d + 8], dtb, bc(dtb[:, None, 8 * d:8 * d + 8], [64, 36, 8]), ALU.add)
            k.actv(beta, beta[:, d], gt, gt[:, :, 16 + 8 * d:24 + 8 * d], AF.Sigmoid)
        k.actv(ng, ng[:], ng, ng[:], AF.Exp)
        k.actv(ng, ng[:], ng, ng[:], AF.Ln, bias=1.0)
        for d in range(2):
            k.tt(k.dve, ng, ng[:, d], ng, ng[:, d], ga, bc(ga[:, None, 8 * d:8 * d + 8], [64, 36, 8]), ALU.mult)
        with k.scope():
            pF = k.ps("pF", [64, 2, 512]); pT_ = k.ps("pTt", [64, 2, 512]); pG = k.ps("pG", [128, 2, 512])
            for d in range(2):
                ngd = ng[:, d].rearrange("p c h -> p (c h)")
                k.mm(pF, pF[:, d, 0:288], tri, tri[:, d, :], ng, ngd)
                k.mm(pT_, pT_[:, d, 0:288], ones, ones[:, 0:64], ng, ngd)
                k.mm(pG, pG[:, d, 0:288], ones, ones[:], ng, ngd)
            fl = lambda t_: t_[:].rearrange("p d c h -> p d (c h)")
            k.actv(eG, fl(eG), pF, pF[:, :, 0:288], AF.Exp, scale=-1.0)
            k.cp(k.dve, kds, fl(kds), pF, pF[:, :, 0:288])
            k.tt(k.dve, kds, fl(kds), kds, fl(kds), pT_, pT_[:, :, 0:288], ALU.subtract)
            k.actv(kds, kds[:], kds, kds[:], AF.Exp)
            k.tt(k.dve, bg, bg[:], beta, beta[:], eG, eG[:], ALU.mult)
            k.actv(gl, fl(gl), pG, pG[:, :, 0:288], AF.Exp, scale=-1.0)
        S32 = [k.sb("S32_%d" % d, [128, 8, 128]) for d in range(2)]
        Sb = [k.sb("Sb_%d" % d, [128, 8, 128], BF16) for d in range(2)]
        for d in range(2):
            k.memset(k.dve, S32[d], S32[d][:], 0.0); k.memset(k.dve, Sb[d], Sb[d][:], 0.0)
        banks = [[k.ps("gb%d_%d" % (d, i), [128, 512]) for i in range(4)] for d in range(2)]

        def sub(parent, nm):
            return parent

        idb2 = bc(c.id32[0:64, None, 0:64], [64, 2, 64])

        class G:
            pass
        GS = []
        for d in range(2):
            g = G()
            b0, b1, b2, b3 = banks[d]
            g.KD = b0; g.QD = sub(b0, "QD%d" % d); g.N = b1; g.WT = sub(b1, "WT%d" % d)
            g.T = b2; g.V = sub(b2, "V%d" % d); g.O1 = b3; g.S = sub(b3, "S%d" % d)
            f3 = lambda h_, p0, p1, c0, a_: h_[p0:p1, c0:c0 + 256].rearrange("p (a b) -> p a b", a=a_)
            g.KDv = f3(b0.h, 0, 64, 0, 4); g.QDv = f3(b0.h, 0, 64, 256, 4)
            g.Nv = f3(b1.h, 0, 64, 0, 4); g.WTv = b1.h[:, 256:384].rearrange("p (a b) -> p a b", a=2)
            g.Tv = b2.h[0:64, 0:256].bitcast(BF16).rearrange("p (a b) -> p a b", a=4)
            g.O2v = f3(b2.h, 0, 64, 0, 2); g.Vv = f3(b2.h, 0, 64, 256, 2)
            g.O1v = f3(b3.h, 0, 64, 0, 2); g.Sv = f3(b3.h, 0, 128, 256, 2)
            sfx = "_%d" % d
            g.kbg = k.sb("kbg" + sfx, [64, 2, 128], BF16); g.kd = k.sb("kd" + sfx, [64, 2, 128], BF16); g.bv = k.sb("bv" + sfx, [64, 2, 128], BF16)
            g.gm = k.sb("gm" + sfx, [64, 8, 64]); g.MBs = k.sb("MBs" + sfx, [64, 8, 64])
            g.gam = k.sb("gam" + sfx, [64, 2, 64]); g.A32 = k.sb("A32" + sfx, [64, 2, 64])
            g.Mb = k.sb("Mb" + sfx, [64, 2, 64], BF16); g.nAb = k.sb("nAb" + sfx, [64, 2, 64], BF16)
            g.Y = k.sb("Y" + sfx, [64, 2, 64], BF16); g.Rt = k.sb("Rt" + sfx, [64, 2, 64], BF16)
            g.nWT = k.sb("nWT" + sfx, [128, 2, 64], BF16); g.vn = k.sb("vn" + sfx, [64, 2, 128], BF16)
            g.gT_ = k.sb("gT_" + sfx, [64, 2, 64]); g.attT = k.sb("attT" + sfx, [64, 2, 64], BF16)
            g.t2 = k.sb("t2" + sfx, [64, 2, 128]); g.otl = k.sb("otl" + sfx, [64, 8, 128])
            GS.append(g)

        def chunk_gen(d, ch):
            g = GS[d]
            tok = slice(ch * 64, (ch + 1) * 64)
            k.tt(k.pool, g.gm, g.gm[:], tri, bc(tri[:, d:d + 1, :], [64, 8, 64]), ng, bc(ng[:, d, ch, :][:, :, None], [64, 8, 64]), ALU.mult)
            k.tt(k.pool, g.MBs, g.MBs[:], strict, bc(strict[:, d:d + 1, :], [64, 8, 64]), beta, bc(beta[:, d, ch, :][:, :, None], [64, 8, 64]), ALU.mult)
            for hp in range(4):
                h0 = 2 * hp
                for hh in range(2):
                    h = h0 + hh
                    k.tr(g.T, g.Tv[:, hh, :], kT, kT[:, h, tok], c.idb, c.idb[:])
                    k.tr(g.T, g.Tv[:, 2 + hh, :], vT, vT[:, h, tok], c.idb, c.idb[:])
                    k.mm(g.KD, g.KDv[:, hh, :], kT, kT[:, h, tok], kT, kT[:, h, tok])
                    k.mm(g.KD, g.KDv[:, 2 + hh, :], g.gm, g.gm[:, h, :], strict, strict[:, d, :])
                    k.mm(g.QD, g.QDv[:, hh, :], kT, kT[:, h, tok], qT, qT[:, h, tok])
                    k.mm(g.QD, g.QDv[:, 2 + hh, :], strict, strict[:, d, :], g.gm, g.gm[:, h, :])
                sc = lambda t_: bc(t_[:, d, ch, h0:h0 + 2][:, :, None], [64, 2, 128])
                k.tt(k.dve, g.kbg, g.kbg[:], g.T, g.Tv[:, 0:2, :], bg, sc(bg), ALU.mult)
                k.tt(k.dve, g.kd, g.kd[:], g.T, g.Tv[:, 0:2, :], kds, sc(kds), ALU.mult)
                k.tt(k.dve, g.bv, g.bv[:], g.T, g.Tv[:, 2:4, :], beta, sc(beta), ALU.mult)
                k.actv(g.gam, g.gam[:], g.KD, g.KDv[:, 2:4, :], AF.Exp, scale=-1.0)
                k.tt(k.pool, g.gam, g.gam[:], g.gam, g.gam[:], g.MBs, g.MBs[:, h0:h0 + 2, :], ALU.mult)
                k.tt(k.dve, g.A32, g.A32[:], g.KD, g.KDv[:, 0:2, :], g.gam, g.gam[:], ALU.mult)
                k.tt(k.pool, g.Mb, g.Mb[:], g.A32, g.A32[:], c.id32, idb2, ALU.add)
                k.ts(k.dve, g.nAb, g.nAb[:], g.A32, g.A32[:], -1.0, None, ALU.mult)
                for hh in range(2):
                    k.mm(g.N, g.Nv[:, hh, :], g.nAb, g.nAb[:, hh, :], c.idb, c.idb[0:64, 0:64])
                yield
                k.tt(k.dve, g.Y, g.Y[:], g.N, g.Nv[:, 0:2, :], c.id32, idb2, ALU.add)
                for itn in range(5):
                    for hh in range(2):
                        k.mm(g.N, g.Nv[:, hh, :], g.Y, g.Y[:, hh, :], g.Mb, g.Mb[:, hh, :])
                    yield
                    k.tt(k.dve, g.Rt, g.Rt[:], c.id32, idb2, g.N, g.Nv[:, 0:2, :], ALU.subtract)
                    for hh in range(2):
                        k.mm(g.N, g.Nv[:, 2 + hh, :], g.Rt, g.Rt[:, hh, :], g.Y, g.Y[:, hh, :])
                    yield
                    k.tt(k.dve, g.Y, g.Y[:], g.Y, g.Y[:], g.N, g.Nv[:, 2:4, :], ALU.add)
                for hh in range(2):
                    k.mm(g.WT, g.WTv[:, hh, :], g.kbg, g.kbg[:, hh, :], g.Y, g.Y[:, hh, :])
                yield
                k.actv(g.nWT, g.nWT[:], g.WT, g.WTv, AF.Copy, scale=-1.0)
                for hh in range(2):
                    h = h0 + hh
                    k.mm(g.V, g.Vv[:, hh, :], g.Y, g.Y[:, hh, :], g.bv, g.bv[:, hh, :], True, False)
                    k.mm(g.V, g.Vv[:, hh, :], g.nWT, g.nWT[:, hh, :], Sb[d], Sb[d][:, h, :], False, True)
                k.actv(g.gT_, g.gT_[:], g.QD, g.QDv[:, 2:4, :], AF.Exp, scale=-1.0)
                k.tt(k.pool, g.gT_, g.gT_[:], g.gT_, g.gT_[:], tri, bc(tri[:, d:d + 1, :], [64, 2, 64]), ALU.mult)
                k.tt(k.dve, g.attT, g.attT[:], g.QD, g.QDv[:, 0:2, :], g.gT_, g.gT_[:], ALU.mult)
                yield
                k.cp(k.act, g.vn, g.vn[:], g.V, g.Vv)
                for hh in range(2):
                    h = h0 + hh
                    k.mm(g.O1, g.O1v[:, hh, :], qT, qT[:, h, tok], Sb[d], Sb[d][:, h, :])
                    k.mm(g.T, g.O2v[:, hh, :], g.attT, g.attT[:, hh, :], g.vn, g.vn[:, hh, :])
                    k.mm(g.S, g.Sv[:, hh, :], g.kd, g.kd[:, hh, :], g.vn, g.vn[:, hh, :])
                yield
                k.cp(k.act, g.t2, g.t2[:], g.T, g.O2v)
                k.tt(k.dve, g.otl, g.otl[:, h0:h0 + 2, :], g.O1, g.O1v, eG, bc(eG[:, d, ch, h0:h0 + 2][:, :, None], [64, 2, 128]), ALU.mult)
                k.tt(k.pool, g.otl, g.otl[:, h0:h0 + 2, :], g.otl, g.otl[:, h0:h0 + 2, :], g.t2, g.t2[:], ALU.add)
                Sv = S32[d][:, h0:h0 + 2, :]
                k.tt(k.pool, S32[d], Sv, S32[d], Sv, gl, bc(gl[:, d, ch, h0:h0 + 2][:, :, None], [128, 2, 128]), ALU.mult)
                k.tt(k.dve, S32[d], Sv, S32[d], Sv, g.S, g.Sv, ALU.add)
                k.cp(k.act, Sb[d], Sb[d][:, h0:h0 + 2, :], S32[d], Sv)
            k.dma(o1_d, o1_d.ap()[d, tok, :], g.otl, g.otl[:].rearrange("p h e -> p (h e)"), q=k.pool)

        for step in range(36):
            gens = [chunk_gen(0, step), chunk_gen(1, ORDB[step])]
            alive = [True, True]
            while any(alive):
                for i in range(2):
                    if alive[i]:
                        try:
                            next(gens[i])
                        except StopIteration:
                            alive[i] = False


def host_consts():
    s = np.arange(64)
    tri = np.stack([(s[:, None] <= s[None, :]), (s[:, None] >= s[None, :])]).astype(np.float32)
    ip = np.arange(128) // 16
    maskM = np.stack([(ip[None, :] >= ip[:, None]), (ip[None, :] <= ip[:, None])]).astype(np.float32)
    return {
        "k_id32": np.eye(128, dtype=np.float32),
        "k_idb": np.eye(128, dtype=np.float32).astype(ml_dtypes.bfloat16),
        "k_tri": tri, "k_maskM": maskM,
        "k_strict": np.stack([(s[:, None] > s[None, :]), (s[:, None] < s[None, :])]).astype(np.float32),
    }


def make_in_maps(inputs, cores):
    f = lambda a: np.ascontiguousarray(np.asarray(a, dtype=np.float32))
    sh = {
        "c_ctx": f(inputs["c_ctx"]).reshape(1, D), "ada_w": f(inputs["ada_w"]), "ada_b": f(inputs["ada_b"]),
        "norm1_w": f(inputs["norm1_w"]), "norm2_w": f(inputs["norm2_w"]),
        "ffn_w1": f(inputs["ffn_w1"]), "ffn_w3": f(inputs["ffn_w3"]), "ffn_w2": f(inputs["ffn_w2"]),
        "final_norm_w": f(inputs["final_norm_w"]).reshape(1, D),
        "ev_w_in": f(inputs["ev_w_in"])[0], "ev_i_bias": f(inputs["ev_i_bias"]).reshape(1, 8),
        "ev_f_bias": f(inputs["ev_f_bias"]).reshape(1, 8), "ev_head_norm_w": f(inputs["ev_head_norm_w"]).reshape(1, 512),
        "ev_lam_re": f(inputs["ev_lam_re"])[0], "ev_lam_im": f(inputs["ev_lam_im"])[0],
        "ev_log_dt": f(inputs["ev_log_dt"]).reshape(1, 64),
        "ev_b_re": f(inputs["ev_b_re"])[0], "ev_b_im": f(inputs["ev_b_im"])[0],
        "ev_c_re": f(inputs["ev_c_re"]).reshape(1024, 64), "ev_c_im": f(inputs["ev_c_im"]).reshape(1024, 64),
        "ev_d": f(inputs["ev_d"]).reshape(1, 512), "ev_w_glu": f(inputs["ev_w_glu"])[0], "ev_w_out": f(inputs["ev_w_out"])[0],
        "od_w_in": f(inputs["od_w_in"])[0], "od_conv_w": f(inputs["od_conv_w"]).reshape(9, 3072),
        "od_a_log": f(inputs["od_a_log"]).reshape(1, 16), "od_dt_bias": f(inputs["od_dt_bias"]).reshape(1, 16),
        "od_head_norm_w": f(inputs["od_head_norm_w"]).reshape(1, D), "od_w_out": f(inputs["od_w_out"])[0],
    }
    sh.update(host_consts())
    x, cc, ctx = f(inputs["x"]), f(inputs["c"]), f(inputs["ctx"])
    maps = []
    for b in cores:
        m = dict(sh)
        m["x"] = x[b]; m["c"] = cc[b:b + 1]; m["ctx"] = ctx[b]
        maps.append(m)
    return maps


def kernel(**inputs):
    nc, _ = build_program()
    maps = make_in_maps(inputs, list(range(8)))
    res = run_bass_kernel_spmd(nc, maps, core_ids=list(range(8)))
    return np.stack([np.asarray(r["out"], dtype=np.float32) for r in res.results], axis=0)
```

```python
import math
import numpy as np
import ml_dtypes
import concourse.bass as bass
import concourse.mybir as mybir
from concourse.bass_types import AP
from concourse.bass_utils import run_bass_kernel_spmd

F32 = mybir.dt.float32
BF16 = mybir.dt.bfloat16
AF = mybir.ActivationFunctionType
ALU = mybir.AluOpType
AX = mybir.AxisListType

D = 1024
T = 2304
NCTX = 256
NLAT = 2048
NT = T // 128
EPS = 1e-6
HID = 2816
PI = math.pi


class Obj:
    def __init__(self, k, name, handle, space):
        self.k, self.name, self.h, self.space = k, name, handle, space
        self.uid = k.uid
        self.w, self.r = {}, {}
        self.sems = {}

    def __getitem__(self, idx):
        return self.h[idx]

    def ap(self):
        return self.h.ap() if self.space == "dram" else self.h[:]

    def dsem(self, kind):
        if kind not in self.sems:
            if not self.sems:
                self.k.dma_objs.append(self)
            pool = self.k.sem_pool[kind]
            if pool:
                self.sems[kind] = pool.pop()
            else:
                self.k.nsem += 1
                self.sems[kind] = [self.k.new_sem("d%s_%d" % (kind, self.k.nsem), keep=True), 0]
        return self.sems[kind]


class Eng:
    def __init__(self, k, name, e):
        self.k, self.name, self.e = k, name, e
        self.sem = k.new_sem("p_" + name)
        self.cnt = 0
        self.seen = {}

    def need(self, tok):
        s, v = tok
        if self.seen.get(id(s), 0) >= v:
            return
        self.e.wait_ge(s, v)
        self.seen[id(s)] = v


class K:
    def __init__(self, nc):
        self.nc = nc
        self._ctx = []
        self.dma_objs = []
        self.sem_pool = {"hw": [], "sw": []}
        self.nsem = 0
        self._perm = []
        self.pe = Eng(self, "pe", nc.tensor)
        self.dve = Eng(self, "dve", nc.vector)
        self.act = Eng(self, "act", nc.scalar)
        self.pool = Eng(self, "pool", nc.gpsimd)
        self.sp = Eng(self, "sp", nc.sync)
        self.engs = [self.pe, self.dve, self.act, self.pool, self.sp]
        self.n_ins = 0
        self.uid = 0

    def new_sem(self, name, keep=False):
        cm = self.nc.semaphore(name)
        s = cm.__enter__()
        self._perm.append((cm, s))
        return s

    def _alloc(self, cm, name, space):
        h = cm.__enter__()
        self._ctx.append(cm)
        return Obj(self, name, h, space)

    def sb(self, name, shape, dt=F32):
        self.uid += 1
        name = "%s_%d" % (name, self.uid)
        return self._alloc(self.nc.sbuf_tensor(name, list(shape), dt), name, "sb")

    def ps(self, name, shape, dt=F32):
        self.uid += 1
        name = "%s_%d" % (name, self.uid)
        return self._alloc(self.nc.psum_tensor(name, list(shape), dt), name, "ps")

    def dram(self, name, shape, dt=F32, kind="Internal"):
        h = self.nc.dram_tensor(name, list(shape), dt, kind=kind)
        return Obj(self, name, h, "dram")

    class _Scope:
        def __init__(self, k):
            self.k = k

        def __enter__(self):
            self.mark = len(self.k._ctx)
            self.uid0 = self.k.uid
            return self

        def __exit__(self, *a):
            k = self.k
            k.barrier()
            while len(k._ctx) > self.mark:
                k._ctx.pop().__exit__(None, None, None)
            keep = []
            for o in k.dma_objs:
                if o.space == "dram" or o.uid <= self.uid0:
                    keep.append(o)
                else:
                    for kind, sc in o.sems.items():
                        k.sem_pool[kind].append(sc)
                    o.sems = {}
            k.dma_objs = keep
            return False

    def scope(self):
        return K._Scope(self)

    def barrier(self):
        toks = [(e.sem, e.cnt) for e in self.engs if e.cnt]
        toks += [(sc[0], sc[1]) for o in self.dma_objs for sc in o.sems.values() if sc[1]]
        for e in self.engs:
            for t in toks:
                if t[0] is e.sem:
                    continue
                e.need(t)

    def _deps(self, eng, outs, ins, same_eng_raw=True):
        toks = []
        for o in ins:
            toks += list(o.w.values())
            if o.space == "ps":
                toks += [t for t in o.r.values() if t[0] is not eng.sem]
        for o in outs:
            toks += list(o.w.values())
            toks += list(o.r.values())
        for t in toks:
            if t[0] is eng.sem and (eng is self.pe or not same_eng_raw):
                continue
            eng.need(t)

    SAME_ENG_WAIT = True

    SKIP_SELF = ()

    def op(self, eng, fn, outs, ins):
        self._deps(eng, outs, ins, same_eng_raw=(K.SAME_ENG_WAIT and eng.name not in K.SKIP_SELF))
        ins_ = fn()
        eng.cnt += 1
        ins_.then_inc(eng.sem, 1)
        tok = (eng.sem, eng.cnt)
        eng.seen[id(eng.sem)] = max(eng.seen.get(id(eng.sem), 0), 0)
        for o in ins:
            o.r[id(tok[0])] = tok
        for o in outs:
            o.w = {id(tok[0]): tok}
            o.r = {}
        self.n_ins += 1
        return ins_

    def dma(self, out_obj, out_ap, in_obj, in_ap, q=None, **kw):
        q = q or self.sp
        self._deps(q, [out_obj], [in_obj], same_eng_raw=True)
        sc = out_obj.dsem("sw" if q is self.pool else "hw")
        s = sc[0]
        ins_ = q.e.dma_start(out=out_ap, in_=in_ap, **kw)
        sc[1] += 16
        ins_.then_inc(s, 16)
        tok = (s, sc[1])
        in_obj.r[id(s)] = tok
        out_obj.w[id(s)] = tok
        out_obj.r = {}
        self.n_ins += 1
        return ins_

    def finish(self, outs):
        self.barrier()

    def close(self):
        while self._ctx:
            self._ctx.pop().__exit__(None, None, None)
        while self._perm:
            self._perm.pop()[0].__exit__(None, None, None)

    def mm(self, out_o, out_ap, l_o, l_ap, r_o, r_ap, start=True, stop=True):
        nc = self.nc
        return self.op(self.pe, lambda: nc.tensor.matmul(out_ap, lhsT=l_ap, rhs=r_ap, start=start, stop=stop),
                       [out_o], [l_o, r_o])

    def tr(self, out_o, out_ap, in_o, in_ap, id_o, id_ap):
        nc = self.nc
        return self.op(self.pe, lambda: nc.tensor.transpose(out_ap, in_ap, id_ap), [out_o], [in_o, id_o])

    def tt(self, eng, out_o, out_ap, a_o, a_ap, b_o, b_ap, op):
        return self.op(eng, lambda: eng.e.tensor_tensor(out=out_ap, in0=a_ap, in1=b_ap, op=op), [out_o], [a_o, b_o])

    def ts(self, eng, out_o, out_ap, a_o, a_ap, s1, s2, op0, op1=None, extra=()):
        if op1 is None:
            return self.op(eng, lambda: eng.e.tensor_scalar(out=out_ap, in0=a_ap, scalar1=s1, scalar2=None, op0=op0),
                           [out_o], [a_o] + list(extra))
        return self.op(eng, lambda: eng.e.tensor_scalar(out=out_ap, in0=a_ap, scalar1=s1, scalar2=s2, op0=op0, op1=op1),
                       [out_o], [a_o] + list(extra))

    def stt(self, eng, out_o, out_ap, a_o, a_ap, sc, b_o, b_ap, op0, op1, extra=()):
        return self.op(eng, lambda: eng.e.scalar_tensor_tensor(out=out_ap, in0=a_ap, scalar=sc, in1=b_ap, op0=op0, op1=op1),
                       [out_o], [a_o, b_o] + list(extra))

    def actv(self, out_o, out_ap, a_o, a_ap, func, bias=0.0, scale=1.0, extra=()):
        nc = self.nc
        return self.op(self.act, lambda: nc.scalar.activation(out=out_ap, in_=a_ap, func=func, bias=bias, scale=scale),
                       [out_o], [a_o] + list(extra))

    def cp(self, eng, out_o, out_ap, a_o, a_ap):
        if eng is self.act:
            nc = self.nc
            return self.op(eng, lambda: nc.scalar.copy(out=out_ap, in_=a_ap), [out_o], [a_o])
        return self.op(eng, lambda: eng.e.tensor_copy(out=out_ap, in_=a_ap), [out_o], [a_o])

    def memset(self, eng, o, ap, val):
        return self.op(eng, lambda: eng.e.memset(ap, val), [o], [])


def bc(ap, shape):
    return ap.broadcast_to(list(shape))


class Ctx:
    pass


def load_w_bf16(k, c, dst, dst_ap_fn, wsrc, w_ap, ncols, kchunks=8, eng=None):
    eng = eng or k.pool
    st = c.wstage[c.wstage_i % 2]
    c.wstage_i += 1
    sv = st[:, 0:kchunks * ncols].rearrange("p (k n) -> p k n", k=kchunks)
    k.dma(st, sv, wsrc, w_ap.rearrange("(kc p) n -> p kc n", p=128))
    k.cp(eng, dst, dst_ap_fn, st, sv)


def norm_mod(k, c, xs, tiles, ab_of_tile, hT, col0=0):
    for i, t in enumerate(tiles):
        xt = c.xt[i % 2]
        k.dma(xt, xt[:], xs, xs.ap()[t * 128:(t + 1) * 128, :])
        sq = c.sq
        k.tt(k.dve, sq, sq[:], xt, xt[:], xt, xt[:], ALU.mult)
        ss = c.ss
        k.op(k.dve, lambda: k.nc.vector.reduce_sum(out=ss[:, 0:1], in_=sq[:], axis=AX.X), [ss], [sq])
        k.actv(ss, ss[:, 1:2], ss, ss[:, 0:1], AF.Ln, bias=c.epsb[:, 0:1], scale=1.0 / D, extra=[c.epsb])
        k.actv(ss, ss[:, 2:3], ss, ss[:, 1:2], AF.Exp, scale=-0.5)
        k.ts(k.dve, sq, sq[:], xt, xt[:], ss[:, 2:3], None, ALU.mult, extra=[ss])
        a, b = ab_of_tile(t)
        for half in range(2):
            pt = c.pT[half]
            for j in range(4):
                kc = half * 4 + j
                k.tr(pt, pt[:, j, :], sq, sq[:, kc * 128:(kc + 1) * 128], c.id32, c.id32[:])
            for j in range(4):
                kc = half * 4 + j
                k.actv(hT, hT[:, kc, col0 + i * 128: col0 + (i + 1) * 128], pt, pt[:, j, :], AF.Identity,
                       bias=b[0][:, b[1] + kc: b[1] + kc + 1], scale=a[0][:, a[1] + kc:a[1] + kc + 1], extra=[a[0], b[0]])


def load_T(k, c, dst_o, dst_ap, src_o, src_ap, nrows, ncols=128):
    st = c.ltst[c.lt_i % 2]; pt = c.ltps[c.lt_i % 2]; c.lt_i += 1
    k.dma(st, st[0:nrows, 0:ncols], src_o, src_ap)
    k.tr(pt, pt[0:ncols, 0:nrows], st, st[0:nrows, 0:ncols], c.id32, c.id32[0:nrows, 0:nrows])
    k.cp(k.dve, dst_o, dst_ap, pt, pt[0:ncols, 0:nrows])


def tok_blocks(tiles_n):
    out, s = [], 0
    while s < tiles_n:
        n = min(4, tiles_n - s)
        out.append((s, n))
        s += n
    return out


def build_program(stop_after=None, debug=False):
    nc = bass.Bass("TRN2", target_bir_lowering=False)
    k = K(nc)
    c = Ctx()
    c.wstage_i = 0
    c.stop = stop_after
    I = {}

    def inp(name, shape, dt=F32):
        I[name] = k.dram(name, shape, dt, kind="ExternalInput")
        return I[name]

    x_in = inp("x", [NLAT, D]); cvec = inp("c", [1, D]); ctx_in = inp("ctx", [NCTX, D]); c_ctx = inp("c_ctx", [1, D])
    ada_w = inp("ada_w", [2, D, 6 * D]); ada_b = inp("ada_b", [2, 6 * D])
    norm1_w = inp("norm1_w", [2, D]); norm2_w = inp("norm2_w", [2, D])
    ffn_w1 = inp("ffn_w1", [2, D, HID]); ffn_w3 = inp("ffn_w3", [2, D, HID]); ffn_w2 = inp("ffn_w2", [2, HID, D])
    final_w = inp("final_norm_w", [1, D])
    ev_w_in = inp("ev_w_in", [D, 2576]); ev_ib = inp("ev_i_bias", [1, 8]); ev_fb = inp("ev_f_bias", [1, 8])
    ev_hw = inp("ev_head_norm_w", [1, 512])
    ev_lre = inp("ev_lam_re", [2, 32, 64]); ev_lim = inp("ev_lam_im", [2, 32, 64]); ev_ldt = inp("ev_log_dt", [1, 64])
    ev_bre = inp("ev_b_re", [2, 32, 64, 16]); ev_bim = inp("ev_b_im", [2, 32, 64, 16])
    ev_cre = inp("ev_c_re", [1024, 64]); ev_cim = inp("ev_c_im", [1024, 64])
    ev_d = inp("ev_d", [1, 512]); ev_wglu = inp("ev_w_glu", [512, 1024]); ev_wout = inp("ev_w_out", [D, D])
    od_w_in = inp("od_w_in", [D, 4128]); od_conv = inp("od_conv_w", [9, 3072])
    od_alog = inp("od_a_log", [1, 16]); od_dtb = inp("od_dt_bias", [1, 16]); od_hw = inp("od_head_norm_w", [1, D])
    od_wout = inp("od_w_out", [D, D])
    cid32 = inp("k_id32", [128, 128]); cidb = inp("k_idb", [128, 128], BF16)
    ctri = inp("k_tri", [2, 64, 64])
    cmaskM = inp("k_maskM", [2, 128, 128])
    cstrict = inp("k_strict", [2, 64, 64])
    out_d = k.dram("out", [NLAT, D], F32, kind="ExternalOutput")

    xs = k.dram("xs", [T, D])
    modv = k.dram("modv", [2, 2, 6 * D])
    dbg = {}

    c.id32 = k.sb("id32", [128, 128]); k.dma(c.id32, c.id32[:], cid32, cid32.ap())
    c.idb = k.sb("idb", [128, 128], BF16); k.dma(c.idb, c.idb[:], cidb, cidb.ap())
    c.epsb = k.sb("epsb", [128, 2]); k.memset(k.dve, c.epsb, c.epsb[:, 0:1], EPS); k.memset(k.dve, c.epsb, c.epsb[:, 1:2], 0.5 * PI)

    k.dma(xs, xs.ap()[0:NCTX, :], ctx_in, ctx_in.ap())
    k.dma(xs, xs.ap()[NCTX:T, :], x_in, x_in.ap())
    with k.scope():
        sT = k.sb("sT", [128, 8, 2])
        c.ltst = [k.sb("ltst%d" % i, [64, 128]) for i in range(2)]
        c.ltps = [k.ps("ltps%d" % i, [128, 64]) for i in range(2)]
        c.lt_i = 0
        load_T(k, c, sT, sT[:, :, 0], cvec, cvec.ap().rearrange("o (kc p) -> (o kc) p", p=128), 8)
        load_T(k, c, sT, sT[:, :, 1], c_ctx, c_ctx.ap().rearrange("o (kc p) -> (o kc) p", p=128), 8)
        sS = k.sb("sS", [128, 8, 2])
        k.actv(sS, sS[:], sT, sT[:], AF.Silu)
        wst = [k.sb("adw%d" % i, [128, 8, 512]) for i in range(2)]
        pm = [k.ps("pm%d" % i, [128, 512]) for i in range(2)]
        brow = k.sb("brow", [2, 6 * D]); mrow = k.sb("mrow", [2, 6 * D])
        for li in range(2):
            k.dma(brow, brow[:], ada_b, ada_b.ap()[li:li + 1, :].partition_broadcast(2).rearrange("p o n -> p (o n)"))
            for j in range(12):
                w = wst[j % 2]
                k.dma(w, w[:], ada_w, ada_w.ap()[li, :, j * 512:(j + 1) * 512].rearrange("(kc p) n -> p kc n", p=128))
                p = pm[j % 2]
                for kc in range(8):
                    k.mm(p, p[0:2, :], sS, sS[:, kc, :], w, w[:, kc, :], start=(kc == 0), stop=(kc == 7))
                k.tt(k.dve, mrow, mrow[:, j * 512:(j + 1) * 512], p, p[0:2, :], brow, brow[:, j * 512:(j + 1) * 512], ALU.add)
            k.dma(modv, modv.ap()[li], mrow, mrow[:], q=k.pool)

    if stop_after == "ada":
        k.finish([])
        k.close()
        return nc, ["modv"]

    modF = k.sb("modF", [128, 2, 2, 48])
    nwF = k.sb("nwF", [128, 2, 2, 8])
    with k.scope():
        c.ltst = [k.sb("ltst%d" % i, [64, 128]) for i in range(2)]
        c.ltps = [k.ps("ltps%d" % i, [128, 64]) for i in range(2)]
        c.lt_i = 0
        for li in range(2):
            for who in range(2):
                load_T(k, c, modF, modF[:, li, who, :], modv, modv.ap()[li, who, :].rearrange("(c p) -> c p", p=128), 48)
        for wi, nw in enumerate((norm1_w, norm2_w)):
            for li in range(2):
                load_T(k, c, nwF, nwF[:, wi, li, :], nw, nw.ap()[li, :].rearrange("(c p) -> c p", p=128), 8)
    aF = k.sb("aF", [128, 2, 2, 2, 8])
    for wi in range(2):
        for li in range(2):
            for who in range(2):
                sc0 = 8 if wi == 0 else 32
                k.stt(k.dve, aF, aF[:, wi, li, who, :], modF, modF[:, li, who, sc0:sc0 + 8], 1.0, nwF, nwF[:, wi, li, :],
                      ALU.add, ALU.mult)

    def ab_fn(wi, li):
        sh0 = 0 if wi == 0 else 24

        def f(t):
            who = 1 if t < 2 else 0
            a_flat = aF.h[:].rearrange("p a b c d -> p (a b c d)")
            b_flat = modF.h[:].rearrange("p a b c -> p (a b c)")
            return ((_View(aF, a_flat), ((wi * 2 + li) * 2 + who) * 8), (_View(modF, b_flat), (li * 2 + who) * 48 + sh0))
        return f

    def load_gate(dst, li, who, part):
        k.dma(dst, dst[:], modv, modv.ap()[li, who:who + 1, part * D:(part + 1) * D].partition_broadcast(128).rearrange("p o n -> p (o n)"))

    def ffn_phase(li, tiles):
        with k.scope():
            c.xt = [k.sb("xt%d" % i, [128, D]) for i in range(2)]
            c.sq = k.sb("sq", [128, D]); c.ss = k.sb("ss", [128, 4])
            c.pT = [k.ps("pT%d" % i, [128, 4, 128]) for i in range(2)]
            c.wstage = [k.sb("wst%d" % i, [128, 2048]) for i in range(2)]
            ntl = len(tiles)
            half_n = (ntl + 1) // 2
            gate = [k.sb("gate%d" % w, [128, D]) for w in range(2)]
            load_gate(gate[0], li, 0, 5); load_gate(gate[1], li, 1, 5)
            hT = k.sb("hT", [128, 8, half_n * 128], BF16)
            gT = k.sb("gT", [128, 22, half_n * 128], BF16)
            w2b = k.sb("w2b", [128, 22, D], BF16)
            w1b = [k.sb("w1b%d" % i, [128, 8, 256], BF16) for i in range(2)]
            w3b = [k.sb("w3b%d" % i, [128, 8, 256], BF16) for i in range(2)]
            p1 = [k.ps("p1_%d" % i, [128, 512]) for i in range(2)]
            p3 = [k.ps("p3_%d" % i, [128, 512]) for i in range(2)]
            py = [k.ps("py%d" % i, [128, 512]) for i in range(2)]
            sg = [k.sb("sg%d" % i, [128, 512]) for i in range(2)]
            yo = [k.sb("yo%d" % i, [128, D]) for i in range(2)]
            for jb in range(0, 22, 4):
                n = min(4, 22 - jb)
                for cb in range(2):
                    st = c.wstage[c.wstage_i % 2]; c.wstage_i += 1
                    sv = st[:, 0:n * 512].rearrange("p (j n) -> p j n", j=n)
                    k.dma(st, sv, ffn_w2, ffn_w2.ap()[li, jb * 128:(jb + n) * 128, cb * 512:(cb + 1) * 512]
                          .rearrange("(j p) n -> p j n", p=128))
                    k.cp(k.pool, w2b, w2b[:, jb:jb + n, cb * 512:(cb + 1) * 512], st, sv)
            it = 0
            for hs in range(0, ntl, half_n):
                ht = tiles[hs:hs + half_n]
                norm_mod(k, c, xs, ht, ab_fn(1, li), hT)
                blocks = tok_blocks(len(ht))
                for jb in range(0, 22, 2):
                    n = 2
                    wa, wb = w1b[(jb // 2) % 2], w3b[(jb // 2) % 2]
                    load_w_bf16(k, c, wa, wa[:, :, 0:n * 128], ffn_w1, ffn_w1.ap()[li, :, jb * 128:(jb + n) * 128], n * 128)
                    load_w_bf16(k, c, wb, wb[:, :, 0:n * 128], ffn_w3, ffn_w3.ap()[li, :, jb * 128:(jb + n) * 128], n * 128, eng=k.dve)
                    for jj in range(n):
                        j = jb + jj
                        for (b0, bn) in blocks:
                            q1, q3, s_ = p1[it % 2], p3[it % 2], sg[it % 2]; it += 1
                            cols = slice(b0 * 128, (b0 + bn) * 128)
                            w_ = bn * 128
                            for kc in range(8):
                                k.mm(q1, q1[:, 0:w_], wa, wa[:, kc, jj * 128:(jj + 1) * 128], hT, hT[:, kc, cols], kc == 0, kc == 7)
                            for kc in range(8):
                                k.mm(q3, q3[:, 0:w_], wb, wb[:, kc, jj * 128:(jj + 1) * 128], hT, hT[:, kc, cols], kc == 0, kc == 7)
                            k.actv(s_, s_[:, 0:w_], q1, q1[:, 0:w_], AF.Silu)
                            k.tt(k.dve, gT, gT[:, j, cols], s_, s_[:, 0:w_], q3, q3[:, 0:w_], ALU.mult)
                for i, t in enumerate(ht):
                    xt = c.xt[i % 2]
                    k.dma(xt, xt[:], xs, xs.ap()[t * 128:(t + 1) * 128, :])
                    g = gate[1 if t < 2 else 0]
                    y = yo[i % 2]
                    for cb in range(2):
                        p = py[cb]
                        for j in range(22):
                            k.mm(p, p[:], gT, gT[:, j, i * 128:(i + 1) * 128], w2b, w2b[:, j, cb * 512:(cb + 1) * 512], j == 0, j == 21)
                        k.tt(k.dve, y, y[:, cb * 512:(cb + 1) * 512], p, p[:], g, g[:, cb * 512:(cb + 1) * 512], ALU.mult)
                    k.tt(k.pool, y, y[:], y, y[:], xt, xt[:], ALU.add)
                    k.dma(xs, xs.ap()[t * 128:(t + 1) * 128, :], y, y[:], q=k.pool)

    def final_phase():
        with k.scope():
            xt = [k.sb("fx%d" % i, [128, D]) for i in range(2)]
            sq = [k.sb("fs%d" % i, [128, D]) for i in range(2)]
            ss = k.sb("fss", [128, 4])
            fw = k.sb("fw", [128, D])
            k.dma(fw, fw[:], final_w, final_w.ap().partition_broadcast(128).rearrange("p o n -> p (o n)"))
            for i in range(16):
                t = i + 2
                x_, s_ = xt[i % 2], sq[i % 2]
                k.dma(x_, x_[:], xs, xs.ap()[t * 128:(t + 1) * 128, :])
                k.tt(k.dve, s_, s_[:], x_, x_[:], x_, x_[:], ALU.mult)
                k.op(k.dve, lambda: nc.vector.reduce_sum(out=ss[:, 0:1], in_=s_[:], axis=AX.X), [ss], [s_])
                k.actv(ss, ss[:, 1:2], ss, ss[:, 0:1], AF.Ln, bias=c.epsb[:, 0:1], scale=1.0 / D, extra=[c.epsb])
                k.actv(ss, ss[:, 2:3], ss, ss[:, 1:2], AF.Exp, scale=-0.5)
                k.stt(k.dve, s_, s_[:], x_, x_[:], ss[:, 2:3], fw, fw[:], ALU.mult, ALU.mult, extra=[ss])
                k.dma(out_d, out_d.ap()[i * 128:(i + 1) * 128, :], s_, s_[:], q=k.pool)

    env = dict(locals())
    layer0(k, c, env)
    if stop_after in ("proj0", "ml_0", "ml_1", "ml_2", "ml_3", "ml_4", "ml_5", "ml_5a", "ml_5b", "ml_a", "ml_b", "mlstm", "s5", "mix0"):
        k.finish([]); k.close(); return nc, ["xs"]
    ffn_phase(0, list(range(NT)))
    if stop_after == "ffn0":
        k.finish([]); k.close(); return nc, ["xs"]
    layer1(k, c, env)
    if stop_after == "mix1":
        k.finish([]); k.close(); return nc, ["xs"]
    ffn_phase(1, list(range(2, NT)))
    final_phase()
    k.finish([out_d])
    k.close()
    return nc, ["out"]


class _View:
    def __init__(self, obj, flat):
        self.obj, self.flat = obj, flat

    @property
    def space(self):
        return self.obj.space

    @property
    def w(self):
        return self.obj.w

    @property
    def r(self):
        return self.obj.r

    def __getitem__(self, idx):
        return self.flat[idx]


LAYER_FUNCS = []


class E:
    def __init__(self, d):
        self.__dict__.update(d)


ORDB = [3, 2, 1, 0] + list(range(35, 3, -1))
ORDB8 = list(range(31, -1, -1)) + list(range(287, 31, -1))


def layer0(k, c, env):
    e = E(env)
    nc = k.nc
    xs = e.xs
    qT_d = k.dram("qT_d", [512, T], BF16); kT_d = k.dram("kT_d", [512, T], BF16)
    ktok_d = k.dram("ktok_d", [T, 512], BF16); v_d = k.dram("v_d", [T, 512], BF16)
    o_d = k.dram("o_d", [T, 512]); g_d = k.dram("g_d", [T, 16]); u_d = k.dram("u_d", [T, 512])
    hA_d = k.dram("hA_d", [2, T, 512]); yS_d = k.dram("yS_d", [T, 512])
    c.l0 = dict(qT=qT_d, kT=kT_d, ktok=ktok_d, v=v_d, o=o_d, g=g_d, u=u_d, hA=hA_d, yS=yS_d)

    with k.scope():
        c.xt = [k.sb("xt%d" % i, [128, D]) for i in range(2)]
        c.sq = k.sb("sq", [128, D]); c.ss = k.sb("ss", [128, 4])
        c.pT = [k.ps("pT%d" % i, [128, 4, 128]) for i in range(2)]
        c.wstage = [k.sb("wst%d" % i, [128, 4096]) for i in range(2)]
        hT = k.sb("hT", [128, 8, T], BF16)
        norm_mod(k, c, xs, list(range(NT)), e.ab_fn(0, 0), hT)
        wb = k.sb("wb", [128, 8, 2576], BF16)
        for cb in range(0, 2576, 512):
            n = min(512, 2576 - cb)
            load_w_bf16(k, c, wb, wb[:, :, cb:cb + n], e.ev_w_in, e.ev_w_in.ap()[:, cb:cb + n], n,
                        eng=(k.pool if (cb // 512) % 2 else k.dve))
        pp = [k.ps("pp%d" % i, [128, 512]) for i in range(4)]
        fst = [k.sb("fst%d" % i, [128, T], BF16) for i in range(2)]
        blocks = [(0, 512), (512, 512), (1024, 512), (1536, 512), (2048, 256)]
        it = 0
        for which, dst in ((0, qT_d), (1, kT_d)):
            for h in range(4):
                st = fst[(which * 4 + h) % 2]
                col = which * 512 + h * 128
                for (t0, tn) in blocks:
                    p = pp[it % 4]; it += 1
                    for kc in range(8):
                        k.mm(p, p[:, 0:tn], wb, wb[:, kc, col:col + 128], hT, hT[:, kc, t0:t0 + tn], kc == 0, kc == 7)
                    k.actv(st, st[:, t0:t0 + tn], p, p[:, 0:tn], AF.Copy, scale=(1.0 if which == 0 else 1.0 / math.sqrt(128)))
                k.dma(dst, dst.ap()[h * 128:(h + 1) * 128, :], st, st[:], q=k.pool)
        tkb = [k.sb("tkb%d" % i, [128, 1024], BF16) for i in range(2)]
        tof = [k.sb("tof%d" % i, [128, 1040]) for i in range(2)]
        for t in range(NT):
            kb, of = tkb[t % 2], tof[t % 2]
            tok = slice(t * 128, (t + 1) * 128)
            for bi, (col, n) in enumerate(((512, 512), (1024, 512), (1536, 512), (2064, 512), (2048, 16))):
                p = pp[it % 4]; it += 1
                for kc in range(8):
                    k.mm(p, p[:, 0:n], hT, hT[:, kc, tok], wb, wb[:, kc, col:col + n], kc == 0, kc == 7)
                if bi == 0:
                    k.actv(kb, kb[:, 0:512], p, p[:, 0:512], AF.Copy, scale=1.0 / math.sqrt(128))
                elif bi == 1:
                    k.cp(k.dve, kb, kb[:, 512:1024], p, p[:, 0:512])
                elif bi == 2:
                    k.cp(k.act, of, of[:, 0:512], p, p[:, 0:512])
                elif bi == 3:
                    k.cp(k.dve, of, of[:, 512:1024], p, p[:, 0:512])
                else:
                    k.cp(k.act, of, of[:, 1024:1040], p, p[:, 0:16])
            k.dma(ktok_d, ktok_d.ap()[tok, :], kb, kb[:, 0:512], q=k.pool)
            k.dma(v_d, v_d.ap()[tok, :], kb, kb[:, 512:1024], q=k.pool)
            k.dma(o_d, o_d.ap()[tok, :], of, of[:, 0:512], q=k.pool)
            k.dma(u_d, u_d.ap()[tok, :], of, of[:, 512:1024], q=k.pool)
            k.dma(g_d, g_d.ap()[tok, :], of, of[:, 1024:1040], q=k.pool)
    if c.stop == "proj0":
        return
    mlstm_phase(k, c, e)
    if c.stop in ("mlstm", "ml_0", "ml_1", "ml_2", "ml_3", "ml_4", "ml_5", "ml_5a", "ml_5b", "ml_a", "ml_b"):
        return
    s5_phase(k, c, e)
    if c.stop == "s5":
        return
    finish0_phase(k, c, e)


def mlstm_phase(k, c, e):
    nc = k.nc
    L = c.l0
    with k.scope():
        qT = k.sb("qT", [128, 4, T], BF16); kT = k.sb("kT", [128, 4, T], BF16)
        for h in range(4):
            k.dma(qT, qT[:, h, :], L['qT'], L['qT'].ap()[h * 128:(h + 1) * 128, :])
            k.dma(kT, kT[:, h, :], L['kT'], L['kT'].ap()[h * 128:(h + 1) * 128, :])
        ktok = k.sb("ktok", [64, 36, 512], BF16)
        v1 = k.sb("v1", [64, 36, 4, 132], BF16)
        for c0 in range(0, 36, 6):
            k.dma(ktok, ktok[:, c0:c0 + 6, :], L['ktok'], L['ktok'].ap()[c0 * 64:(c0 + 6) * 64, :].rearrange("(c l) n -> l c n", l=64))
            for h in range(4):
                k.dma(v1, v1[:, c0:c0 + 6, h, 0:128], L['v'],
                      L['v'].ap()[c0 * 64:(c0 + 6) * 64, h * 128:(h + 1) * 128].rearrange("(c l) n -> l c n", l=64))
        if c.stop == "ml_0":
            return
        k.memset(k.dve, v1, v1[:, :, :, 128:132], 1.0)
        if c.stop == "ml_1":
            return
        g = k.sb("g", [64, 36, 16])
        for c0 in range(0, 36, 6):
            k.dma(g, g[:, c0:c0 + 6, :], L['g'], L['g'].ap()[c0 * 64:(c0 + 6) * 64, :].rearrange("(c l) n -> l c n", l=64))
        fb = k.sb("fb", [64, 8]); ib = k.sb("ib", [64, 8])
        k.dma(fb, fb[:], e.ev_fb, e.ev_fb.ap().partition_broadcast(64).rearrange("p o n -> p (o n)"))
        k.dma(ib, ib[:], e.ev_ib, e.ev_ib.ap().partition_broadcast(64).rearrange("p o n -> p (o n)"))
        tri = k.sb("tri", [64, 2, 64]); ones = k.sb("ones", [64, 128])
        k.dma(tri, tri[:], e.ctri, e.ctri.ap().rearrange("r s l -> s r l"))
        k.memset(k.dve, ones, ones[:], 1.0)
        if c.stop == "ml_2":
            return
        z = k.sb("z", [64, 2, 36, 4]); nlf = k.sb("nlf", [64, 2, 36, 4]); ig = k.sb("ig", [64, 2, 36, 4])
        A = k.sb("A", [64, 2, 36, 4]); Bk = k.sb("Bk", [64, 2, 36, 4]); gdec = k.sb("gdec", [128, 2, 36, 4])
        for d in range(2):
            k.tt(k.dve, z, z[:, d], g, g[:, :, 8 + 4 * d:12 + 4 * d], fb, bc(fb[:, None, 4 * d:4 * d + 4], [64, 36, 4]), ALU.add)
            k.tt(k.dve, ig, ig[:, d], g, g[:, :, 4 * d:4 * d + 4], ib, bc(ib[:, None, 4 * d:4 * d + 4], [64, 36, 4]), ALU.add)
        k.actv(z, z[:], z, z[:], AF.Exp, scale=-1.0)
        k.actv(nlf, nlf[:], z, z[:], AF.Ln, bias=1.0)
        if c.stop == "ml_3":
            return
        with k.scope():
            pF = k.ps("pF", [64, 2, 144]); pG = k.ps("pG", [128, 288])
            for d in range(2):
                k.mm(pF, pF[:, d, :], tri, tri[:, d, :], nlf, nlf[:, d].rearrange("p c h -> p (c h)"))
            k.mm(pG, pG[:], ones, ones[:], nlf, nlf[:].rearrange("p d c h -> p (d c h)"))
            if c.stop == "ml_4":
                k.cp(k.dve, A, A[:].rearrange("p d c h -> p d (c h)"), pF, pF[:])
                k.cp(k.dve, gdec, gdec[:].rearrange("p d c h -> p (d c h)"), pG, pG[:])
            Af = A[:].rearrange("p d c h -> p d (c h)"); Bf = Bk[:].rearrange("p d c h -> p d (c h)")
            if c.stop != "ml_4":
                if c.stop != "ml_5b":
                    k.actv(A, Af, pF, pF[:], AF.Exp, scale=-1.0)
                if c.stop != "ml_5a":
                    k.tt(k.dve, Bk, Bf, ig, ig[:].rearrange("p d c h -> p d (c h)"), pF, pF[:], ALU.add)
                if c.stop not in ("ml_5", "ml_5a", "ml_5b"):
                    k.actv(Bk, Bk[:], Bk, Bk[:], AF.Exp)
                    k.actv(gdec, gdec[:].rearrange("p d c h -> p (d c h)"), pG, pG[:], AF.Exp, scale=-1.0)
        if c.stop in ("ml_a", "ml_4", "ml_5", "ml_5a", "ml_5b"):
            return
        C32 = [k.sb("C32_%d" % d, [128, 4, 132]) for d in range(2)]
        Cb = [k.sb("Cb_%d" % d, [128, 4, 132], BF16) for d in range(2)]
        for d in range(2):
            k.memset(k.dve, C32[d], C32[d][:], 0.0)
            k.memset(k.dve, Cb[d], Cb[d][:], 0.0)
        pS = [k.ps("pS%d" % d, [64, 4, 64]) for d in range(2)]
        pN = [[k.ps("pN%d_%d" % (d, i), [64, 2, 256]) for i in range(2)] for d in range(2)]
        pC = [k.ps("pC%d" % i, [128, 2, 256]) for i in range(2)]
        kt = [k.sb("kt%d" % d, [64, 4, 128], BF16) for d in range(2)]
        MB = [k.sb("MB%d" % d, [64, 4, 64]) for d in range(2)]
        Pt = [k.sb("Pt%d" % d, [64, 4, 64], BF16) for d in range(2)]
        sm = [k.sb("sm%d" % d, [64, 4, 4]) for d in range(2)]
        ho = [k.sb("ho%d" % d, [64, 4, 128]) for d in range(2)]
        for step in range(36):
            for d in range(2):
                ch = step if d == 0 else ORDB[step]
                tok = slice(ch * 64, (ch + 1) * 64)
                Bs = Bk[:, d, ch, :]
                As = A[:, d, ch, :]
                k.tt(k.dve, kt[d], kt[d][:], ktok, ktok[:, ch, :].rearrange("p (h e) -> p h e", h=4),
                     Bk, bc(Bs[:, :, None], [64, 4, 128]), ALU.mult)
                k.tt(k.dve, MB[d], MB[d][:], tri, bc(tri[:, d:d + 1, :], [64, 4, 64]), Bk, bc(Bs[:, :, None], [64, 4, 64]), ALU.mult)
                for h in range(4):
                    k.mm(pS[d], pS[d][:, h, :], kT, kT[:, h, tok], qT, qT[:, h, tok])
                k.tt(k.dve, Pt[d], Pt[d][:], pS[d], pS[d][:], MB[d], MB[d][:], ALU.mult)
                for h in range(4):
                    pn = pN[d][h // 2]
                    k.mm(pn, pn[:, h % 2, 0:132], Pt[d], Pt[d][:, h, :], v1, v1[:, ch, h, :], True, False)
                    k.mm(pn, pn[:, h % 2, 0:132], qT, qT[:, h, tok], Cb[d], Cb[d][:, h, :], False, True)
                s_ = sm[d]
                for i in range(2):
                    pn = pN[d][i]
                    k.tt(k.dve, s_, s_[:, 2 * i:2 * i + 2, 0], pn, pn[:, :, 128], A, As[:, 2 * i:2 * i + 2], ALU.mult)
                k.stt(k.dve, s_, s_[:, :, 1], s_, s_[:, :, 0], -1.0, s_, s_[:, :, 0], ALU.mult, ALU.max)
                k.ts(k.dve, s_, s_[:, :, 1], s_, s_[:, :, 1], 1.0, None, ALU.max)
                k.op(k.dve, lambda: nc.vector.reciprocal(out=s_[:, :, 2], in_=s_[:, :, 1]), [s_], [s_])
                k.tt(k.dve, s_, s_[:, :, 3], s_, s_[:, :, 2], A, As, ALU.mult)
                for i in range(2):
                    pn = pN[d][i]
                    k.tt(k.dve, ho[d], ho[d][:, 2 * i:2 * i + 2, :], pn, pn[:, :, 0:128],
                         s_, bc(s_[:, 2 * i:2 * i + 2, 3:4], [64, 2, 128]), ALU.mult)
                k.dma(L['hA'], L['hA'].ap()[d, tok, :], ho[d], ho[d][:].rearrange("p h e -> p (h e)"), q=k.pool)
                for i in range(2):
                    pc_ = pC[i]
                    for hh in range(2):
                        h = 2 * i + hh
                        k.mm(pc_, pc_[:, hh, 0:132], kt[d], kt[d][:, h, :], v1, v1[:, ch, h, :])
                    k.tt(k.dve, C32[d], C32[d][:, 2 * i:2 * i + 2, :], pc_, pc_[:, :, 0:132], C32[d], C32[d][:, 2 * i:2 * i + 2, :], ALU.add)
                k.tt(k.dve, C32[d], C32[d][:], C32[d], C32[d][:], gdec, bc(gdec[:, d, ch, :][:, :, None], [128, 4, 132]), ALU.mult)
                k.cp(k.act, Cb[d], Cb[d][:], C32[d], C32[d][:])
            if c.stop == "ml_b" and step == 0:
                return


def s5_phase(k, c, e):
    nc = k.nc
    L = c.l0
    TWO_PI = 2 * PI
    with k.scope():
        Mw = k.sb("Mw", [128, 64, 128], BF16)
        WT = [k.sb("WT%d" % i, [128, 64, 64], BF16) for i in range(2)]
        RC = [k.sb("RC%d" % i, [64, 64, 128], BF16) for i in range(2)]
        AR2 = k.sb("AR2", [64, 2, 64]); AI2 = k.sb("AI2", [64, 2, 64])
        with k.scope():
            lre = k.sb("lre", [64, 64]); lim = k.sb("lim", [64, 64]); dt = k.sb("dt", [64, 64])
            c.ltst = [k.sb("ltst%d" % i, [64, 128]) for i in range(2)]
            c.ltps = [k.ps("ltps%d" % i, [128, 64]) for i in range(2)]
            c.lt_i = 0
            load_T(k, c, lre, lre[:], e.ev_lre, e.ev_lre.ap().rearrange("r g n -> (r g) n"), 64, ncols=64)
            load_T(k, c, lim, lim[:], e.ev_lim, e.ev_lim.ap().rearrange("r g n -> (r g) n"), 64, ncols=64)
            k.dma(dt, dt[:], e.ev_ldt, e.ev_ldt.ap().partition_broadcast(64).rearrange("p o n -> p (o n)"))
            k.actv(dt, dt[:], dt, dt[:], AF.Exp)
            ldr = k.sb("ldr", [64, 64]); ang = k.sb("ang", [64, 64])
            k.tt(k.dve, ldr, ldr[:], lre, lre[:], dt, dt[:], ALU.mult)
            k.tt(k.dve, ang, ang[:], lim, lim[:], dt, dt[:], ALU.mult)
            mg = k.sb("mg", [64, 16, 64]); sn = k.sb("sn", [64, 9, 64]); cs = k.sb("cs", [64, 9, 64])
            for ti, tau in enumerate(range(-7, 9)):
                k.actv(mg, mg[:, ti, :], ldr, ldr[:], AF.Exp, scale=float(tau))
            k.memset(k.dve, sn, sn[:, 0, :], 0.0); k.memset(k.dve, cs, cs[:, 0, :], 1.0)
            k.actv(sn, sn[:, 1, :], ang, ang[:], AF.Sin, scale=1.0 / 16)
            k.actv(cs, cs[:, 1, :], ang, ang[:], AF.Sin, bias=c.epsb[0:64, 1:2], scale=1.0 / 16, extra=[c.epsb])
            q1 = k.sb("q1", [64, 64]); q2 = k.sb("q2", [64, 64])
            for _ in range(4):
                k.tt(k.dve, q1, q1[:], sn, sn[:, 1, :], cs, cs[:, 1, :], ALU.mult)
                k.tt(k.dve, q2, q2[:], sn, sn[:, 1, :], sn, sn[:, 1, :], ALU.mult)
                k.ts(k.dve, sn, sn[:, 1, :], q1, q1[:], 2.0, None, ALU.mult)
                k.ts(k.dve, cs, cs[:, 1, :], q2, q2[:], -2.0, 1.0, ALU.mult, ALU.add)
            for tau in range(2, 9):
                k.tt(k.dve, q1, q1[:], cs, cs[:, tau - 1, :], cs, cs[:, 1, :], ALU.mult)
                k.tt(k.dve, q2, q2[:], sn, sn[:, tau - 1, :], sn, sn[:, 1, :], ALU.mult)
                k.tt(k.dve, cs, cs[:, tau, :], q1, q1[:], q2, q2[:], ALU.subtract)
                k.tt(k.dve, q1, q1[:], sn, sn[:, tau - 1, :], cs, cs[:, 1, :], ALU.mult)
                k.tt(k.dve, q2, q2[:], cs, cs[:, tau - 1, :], sn, sn[:, 1, :], ALU.mult)
                k.tt(k.dve, sn, sn[:, tau, :], q1, q1[:], q2, q2[:], ALU.add)
            pwr = k.sb("pwr", [64, 16, 64]); pwi = k.sb("pwi", [64, 16, 64])
            for ti, tau in enumerate(range(-7, 9)):
                at = abs(tau)
                k.tt(k.dve, pwr, pwr[:, ti, :], mg, mg[:, ti, :], cs, cs[:, at, :], ALU.mult)
                if tau >= 0:
                    k.tt(k.dve, pwi, pwi[:, ti, :], mg, mg[:, ti, :], sn, sn[:, at, :], ALU.mult)
                else:
                    k.stt(k.dve, pwi, pwi[:, ti, :], mg, mg[:, ti, :], -1.0, sn, sn[:, at, :], ALU.mult, ALU.mult)
            for s_ in range(2):
                k.cp(k.dve, AR2, AR2[:, s_, :], pwr, pwr[:, 15, :])
            k.ts(k.dve, AI2, AI2[:, 0, :], pwi, pwi[:, 15, :], -1.0, None, ALU.mult)
            k.cp(k.dve, AI2, AI2[:, 1, :], pwi, pwi[:, 15, :])
            nr = k.sb("nr", [64, 64]); den = k.sb("den", [64, 64]); t1 = k.sb("t1", [64, 64]); t2 = k.sb("t2", [64, 64])
            cor = k.sb("cor", [64, 64]); coi = k.sb("coi", [64, 64])
            k.ts(k.dve, nr, nr[:], pwr, pwr[:, 8, :], -1.0, None, ALU.add)
            k.tt(k.dve, den, den[:], lre, lre[:], lre, lre[:], ALU.mult)
            k.tt(k.dve, t1, t1[:], lim, lim[:], lim, lim[:], ALU.mult)
            k.tt(k.dve, den, den[:], den, den[:], t1, t1[:], ALU.add)
            k.op(k.dve, lambda: nc.vector.reciprocal(out=den[:], in_=den[:]), [den], [den])
            k.tt(k.dve, t1, t1[:], nr, nr[:], lre, lre[:], ALU.mult)
            k.tt(k.dve, t2, t2[:], pwi, pwi[:, 8, :], lim, lim[:], ALU.mult)
            k.tt(k.dve, t1, t1[:], t1, t1[:], t2, t2[:], ALU.add)
            k.tt(k.dve, cor, cor[:], t1, t1[:], den, den[:], ALU.mult)
            k.tt(k.dve, t1, t1[:], pwi, pwi[:, 8, :], lre, lre[:], ALU.mult)
            k.tt(k.dve, t2, t2[:], nr, nr[:], lim, lim[:], ALU.mult)
            k.tt(k.dve, t1, t1[:], t1, t1[:], t2, t2[:], ALU.subtract)
            k.tt(k.dve, coi, coi[:], t1, t1[:], den, den[:], ALU.mult)
            bre = k.sb("bre", [64, 64, 16]); bim = k.sb("bim", [64, 64, 16])
            for r in range(2):
                for g0 in range(0, 32, 4):
                    k.dma(bre, bre[:, r * 32 + g0:r * 32 + g0 + 4, :], e.ev_bre, e.ev_bre.ap()[r, g0:g0 + 4].rearrange("g n p -> n g p"))
                    k.dma(bim, bim[:, r * 32 + g0:r * 32 + g0 + 4, :], e.ev_bim, e.ev_bim.ap()[r, g0:g0 + 4].rearrange("g n p -> n g p"))
            bbr = k.sb("bbr", [64, 64, 16]); bbi = k.sb("bbi", [64, 64, 16])
            u1 = k.sb("u1", [64, 64, 16]); u2 = k.sb("u2", [64, 64, 16])
            corb = bc(cor[:, :, None], [64, 64, 16]); coib = bc(coi[:, :, None], [64, 64, 16])
            k.tt(k.dve, u1, u1[:], bre, bre[:], cor, corb, ALU.mult)
            k.tt(k.dve, u2, u2[:], bim, bim[:], coi, coib, ALU.mult)
            k.tt(k.dve, bbr, bbr[:], u1, u1[:], u2, u2[:], ALU.subtract)
            k.tt(k.dve, u1, u1[:], bim, bim[:], cor, corb, ALU.mult)
            k.tt(k.dve, u2, u2[:], bre, bre[:], coi, coib, ALU.mult)
            k.tt(k.dve, bbi, bbi[:], u1, u1[:], u2, u2[:], ALU.add)
            cTr = k.sb("cTr", [64, 64, 16]); cTi = k.sb("cTi", [64, 64, 16])
            cst = k.sb("cst", [128, 8, 64])
            pCt = [k.ps("pCt%d" % i, [64, 4, 128]) for i in range(2)]
            for src, dst in ((e.ev_cre, cTr), (e.ev_cim, cTi)):
                for t0 in range(0, 8, 2):
                    k.dma(cst, cst[:, t0:t0 + 2, :], src, src.ap()[t0 * 128:(t0 + 2) * 128, :].rearrange("(t p) n -> p t n", p=128))
                for t in range(8):
                    p = pCt[t // 4]
                    k.tr(p, p[:, t % 4, :], cst, cst[:, t, :], c.id32, c.id32[:])
                for hf in range(2):
                    k.cp(k.act, dst, dst[:, hf * 32:(hf + 1) * 32, :].rearrange("n g p -> n (g p)"),
                         pCt[hf], pCt[hf][:].rearrange("n t m -> n (t m)"))
            maskM = k.sb("maskM", [128, 2, 128])
            k.dma(maskM, maskM[:], e.cmaskM, e.cmaskM.ap().rearrange("r a b -> a r b"))
            w1_ = k.sb("w1_", [64, 32, 16]); w2_ = k.sb("w2_", [64, 32, 16])

            def cmul(r, powf, sr, si, dr, dr_ap, di, di_ap, neg_im=False):
                rs = slice(r * 32, (r + 1) * 32)
                for i in range(8):
                    ti = powf(i) + 7
                    pr = bc(pwr[:, ti, rs][:, :, None], [64, 32, 16]); pi_ = bc(pwi[:, ti, rs][:, :, None], [64, 32, 16])
                    k.tt(k.dve, w1_, w1_[:], sr, sr[:, rs, :], pwr, pr, ALU.mult)
                    k.tt(k.pool, w2_, w2_[:], si, si[:, rs, :], pwi, pi_, ALU.mult)
                    k.tt(k.dve, dr, dr_ap(i), w1_, w1_[:], w2_, w2_[:], ALU.subtract)
                    k.tt(k.dve, w1_, w1_[:], si, si[:, rs, :], pwr, pr, ALU.mult)
                    k.tt(k.pool, w2_, w2_[:], sr, sr[:, rs, :], pwi, pi_, ALU.mult)
                    if neg_im:
                        k.stt(k.dve, di, di_ap(i), w1_, w1_[:], -1.0, w2_, w2_[:], ALU.mult, ALU.subtract)
                    else:
                        k.tt(k.dve, di, di_ap(i), w1_, w1_[:], w2_, w2_[:], ALU.add)

            EBr = k.sb("EBr", [64, 32, 8, 16]); EBi = k.sb("EBi", [64, 32, 8, 16])
            ECr = k.sb("ECr", [64, 32, 8, 16]); ECi = k.sb("ECi", [64, 32, 8, 16])
            pM = [k.ps("pM%d" % i, [128, 4, 128]) for i in range(2)]
            pW = [k.ps("pW%d" % i, [128, 8, 64]) for i in range(2)]
            for r in range(2):
                sig = (lambda i: i) if r == 0 else (lambda i: 7 - i)
                rs = slice(r * 32, (r + 1) * 32)
                cmul(r, lambda i: -sig(i), bbr, bbi, EBr, lambda i: EBr[:, :, i, :], EBi, lambda i: EBi[:, :, i, :])
                cmul(r, lambda i: sig(i), cTr, cTi, ECr, lambda i: ECr[:, :, i, :], ECi, lambda i: ECi[:, :, i, :], neg_im=True)
                for g0 in range(0, 32, 4):
                    p = pM[(g0 // 4) % 2]
                    for gg in range(4):
                        g = g0 + gg
                        k.mm(p, p[:, gg, :], EBr, EBr[:, g].rearrange("n i p -> n (i p)"), ECr, ECr[:, g].rearrange("n i p -> n (i p)"), True, False)
                        k.mm(p, p[:, gg, :], EBi, EBi[:, g].rearrange("n i p -> n (i p)"), ECi, ECi[:, g].rearrange("n i p -> n (i p)"), False, True)
                    k.tt(k.dve, Mw, Mw[:, r * 32 + g0:r * 32 + g0 + 4, :], p, p[:], maskM, bc(maskM[:, r:r + 1, :], [128, 4, 128]), ALU.mult)
                cmul(r, lambda i: sig(i) + 1, cTr, cTi,
                     RC[0], lambda i: RC[0][:, rs, :].rearrange("n g (j p) -> n g j p", j=8)[:, :, i, :],
                     RC[1], lambda i: RC[1][:, rs, :].rearrange("n g (j p) -> n g j p", j=8)[:, :, i, :], neg_im=True)
                cmul(r, lambda i: 7 - sig(i), bbr, bbi, ECr, lambda i: ECr[:, :, i, :], ECi, lambda i: ECi[:, :, i, :])
                for comp, src in enumerate((ECr, ECi)):
                    for g0 in range(0, 32, 8):
                        p = pW[(g0 // 8) % 2]
                        for gg in range(8):
                            k.tr(p, p[:, gg, :], src, src[:, g0 + gg].rearrange("n i p -> n (i p)"), c.id32, c.id32[0:64, 0:64])
                        k.cp(k.act, WT[comp], WT[comp][:, r * 32 + g0:r * 32 + g0 + 8, :], p, p[:])
        X = k.sb("X", [128, 32, 288], BF16)
        Sa = k.sb("Sa", [64, 2, 64, 290], BF16)
        with k.scope():
            u32 = k.sb("u32", [128, 8, 512]); u16 = k.sb("u16", [128, 32, 128], BF16)
            pX = [k.ps("pX%d" % i, [128, 8, 128], BF16) for i in range(2)]
            it = 0
            for ct, (c0, n) in enumerate(((0, 128), (128, 128), (256, 32))):
                k.dma(u32, u32[0:n], L['u'], L['u'].ap()[8 * c0:8 * (c0 + n), :].rearrange("(c i) ch -> c i ch", i=8))
                k.cp(k.dve, u16, u16[0:n].rearrange("c g (i p) -> c g i p", i=8), u32, u32[0:n].rearrange("c i (g p) -> c g i p", g=32))
                for g0 in range(0, 32, 8):
                    p = pX[it % 2]; it += 1
                    for gg in range(8):
                        g = g0 + gg
                        k.tr(p, p[:, gg, 0:n], u16, u16[0:n, g, :], c.idb, c.idb[0:n, 0:n])
                    k.cp(k.act, X, X[:, g0:g0 + 8, c0:c0 + n], p, p[:, :, 0:n])
        with k.scope():
            pB = [k.ps("pB%d" % i, [64, 288]) for i in range(4)]
            it = 0
            for rg in range(64):
                for comp in range(2):
                    p = pB[it % 4]; it += 1
                    k.mm(p, p[:], WT[comp], WT[comp][:, rg, :], X, X[:, rg % 32, :])
                    off = 0 if rg < 32 else 1
                    k.cp(k.act if it % 2 else k.dve, Sa, Sa[:, comp, rg, off:off + 288], p, p[:])
        R3 = k.sb("R3", [64, 3, 64]); P1 = k.sb("P1", [64, 2, 64]); P2 = k.sb("P2", [64, 2, 64])
        k.memset(k.dve, R3, R3[:], 0.0)
        base = Sa[:, :, 0:32, 0]
        pstride = base.ap[0][0]
        R3v = R3[:, 1:3, :].rearrange("n c (r g) -> n c r g", r=2)
        for step in range(288):
            cf, cb = step, ORDB8[step]
            bv = AP(tensor=Sa.h, offset=base.offset + cf, ap=[[pstride, 64], [64 * 290, 2], [32 * 290 + (cb + 1) - cf, 2], [290, 32]])
            k.tt(k.dve, P1, P1[:], AR2, AR2[:], R3, R3[:, 1:3, :], ALU.mult)
            k.tt(k.dve, P2, P2[:], AI2, AI2[:], R3, R3[:, 0:2, :], ALU.mult)
            k.tt(k.dve, P1, P1[:], P1, P1[:], P2, P2[:], ALU.add)
            k.tt(k.dve, R3, R3v, P1, P1[:].rearrange("n c (r g) -> n c r g", r=2), Sa, bv, ALU.add)
            k.cp(k.dve, R3, R3[:, 0, :], R3, R3[:, 2, :])
            k.cp(k.dve, Sa, bv, R3, R3v)
        k.cp(k.dve, Sa, Sa[:, :, 32:64, 0:1], Sa, Sa[:, :, 32:64, 1:2])
        with k.scope():
            pY = [k.ps("pY%d" % i, [128, 288]) for i in range(2)]
            pZ = [k.ps("pZ%d" % i, [128, 4, 128]) for i in range(2)]
            Yq = k.sb("Yq", [128, 8, 288]); Y2q = [k.sb("Y2q%d" % i, [128, 8, 128]) for i in range(2)]
            it = 0; iz = 0; iy = 0
            for q in range(4):
                for gl in range(8):
                    g = q * 8 + gl
                    p = pY[it % 2]; it += 1
                    k.mm(p, p[:], Mw, Mw[:, g, :], X, X[:, g, :], True, False)
                    k.mm(p, p[:], Mw, Mw[:, 32 + g, :], X, X[:, g, :], False, False)
                    for comp in range(2):
                        k.mm(p, p[:, 1:288], RC[comp], RC[comp][:, g, :], Sa, Sa[:, comp, g, 0:287], False, False)
                    rg = 32 + g
                    for comp in range(2):
                        k.mm(p, p[:, 0:31], RC[comp], RC[comp][:, rg, :], Sa, Sa[:, comp, rg, 2:33], False, False)
                        k.mm(p, p[:, 32:287], RC[comp], RC[comp][:, rg, :], Sa, Sa[:, comp, rg, 34:289], False, False)
                        k.mm(p, p[:, 287:288], RC[comp], RC[comp][:, rg, :], Sa, Sa[:, comp, rg, 0:1], False, comp == 1)
                    k.cp(k.act, Yq, Yq[:, gl, :], p, p[:])
                for ct, (c0, n) in enumerate(((0, 128), (128, 128), (256, 32))):
                    y2 = Y2q[iy % 2]; iy += 1
                    for gl in range(8):
                        if gl % 4 == 0:
                            pz = pZ[iz % 2]; iz += 1
                        k.tr(pz, pz[0:n, gl % 4, :], Yq, Yq[:, gl, c0:c0 + n], c.id32, c.id32[:])
                        k.cp(k.dve if gl % 2 else k.act, y2, y2[0:n, :, gl * 16:(gl + 1) * 16],
                             pz, pz[0:n, gl % 4, :].rearrange("c (j p) -> c j p", j=8))
                    for jh in range(2):
                        k.dma(L['yS'], L['yS'].ap()[8 * c0:8 * (c0 + n), q * 128:(q + 1) * 128].rearrange("(c j) ch -> c j ch", j=8)[:, jh * 4:(jh + 1) * 4, :],
                              y2, y2[0:n, jh * 4:(jh + 1) * 4, :], q=k.pool)


def finish0_phase(k, c, e):
    nc = k.nc
    L = c.l0
    xs = e.xs
    with k.scope():
        c.wstage = [k.sb("wst%d" % i, [128, 4096]) for i in range(2)]
        wglu = k.sb("wglu", [128, 4, 1024], BF16); wout = k.sb("wout", [128, 8, 1024], BF16)
        for cb in range(2):
            load_w_bf16(k, c, wglu, wglu[:, :, cb * 512:(cb + 1) * 512], e.ev_wglu, e.ev_wglu.ap()[:, cb * 512:(cb + 1) * 512], 512, kchunks=4)
            load_w_bf16(k, c, wout, wout[:, :, cb * 512:(cb + 1) * 512], e.ev_wout, e.ev_wout.ap()[:, cb * 512:(cb + 1) * 512], 512)
        hwb = k.sb("hwb", [128, 512]); dsk = k.sb("dsk", [128, 512])
        k.dma(hwb, hwb[:], e.ev_hw, e.ev_hw.ap().partition_broadcast(128).rearrange("p o n -> p (o n)"))
        k.dma(dsk, dsk[:], e.ev_d, e.ev_d.ap().partition_broadcast(128).rearrange("p o n -> p (o n)"))
        gate = [k.sb("gate%d" % w, [128, D]) for w in range(2)]
        e.load_gate(gate[0], 0, 0, 2); e.load_gate(gate[1], 0, 1, 2)
        hA = [k.sb("hA%d" % i, [128, 2, 512]) for i in range(2)]
        ot = [k.sb("ot%d" % i, [128, 512]) for i in range(2)]
        ut = [k.sb("ut%d" % i, [128, 512]) for i in range(2)]
        yt = [k.sb("yt%d" % i, [128, 512]) for i in range(2)]
        xt = [k.sb("xt%d" % i, [128, D]) for i in range(2)]
        w1 = k.sb("fw1", [128, 512]); w2 = k.sb("fw2", [128, 512]); st = k.sb("fst", [128, 8])
        cat = k.sb("cat", [128, D], BF16); ybb = k.sb("ybb", [128, 512], BF16)
        ybT = k.sb("ybT", [128, 4, 128], BF16); catT = k.sb("catT", [128, 8, 128], BF16)
        pTb = k.ps("pTb", [128, 8, 128], BF16)
        pG = [k.ps("pGl%d" % i, [128, 512]) for i in range(2)]
        pO = [k.ps("pO%d" % i, [128, 512]) for i in range(2)]
        yo = k.sb("yo", [128, D])
        for t in range(NT):
            tok = slice(t * 128, (t + 1) * 128)
            h_, o_, u_, y_, x_ = hA[t % 2], ot[t % 2], ut[t % 2], yt[t % 2], xt[t % 2]
            k.dma(h_, h_[:], L['hA'], L['hA'].ap()[:, tok, :].rearrange("r t n -> t r n"))
            k.dma(o_, o_[:], L['o'], L['o'].ap()[tok, :])
            k.dma(u_, u_[:], L['u'], L['u'].ap()[tok, :])
            k.dma(y_, y_[:], L['yS'], L['yS'].ap()[tok, :])
            k.dma(x_, x_[:], xs, xs.ap()[tok, :])
            k.tt(k.dve, w1, w1[:], h_, h_[:, 0, :], h_, h_[:, 1, :], ALU.add)
            k.tt(k.dve, w2, w2[:], w1, w1[:], w1, w1[:], ALU.mult)
            k.op(k.dve, lambda: nc.vector.reduce_sum(out=st[:, 0:4], in_=w2[:].rearrange("p (h e) -> p h e", h=4), axis=AX.X), [st], [w2])
            k.actv(st, st[:, 0:4], st, st[:, 0:4], AF.Ln, bias=c.epsb[:, 0:1], scale=1.0 / 128, extra=[c.epsb])
            k.actv(st, st[:, 4:8], st, st[:, 0:4], AF.Exp, scale=-0.5)
            k.tt(k.dve, w1, w1[:].rearrange("p (h e) -> p h e", h=4), w1, w1[:].rearrange("p (h e) -> p h e", h=4),
                 st, bc(st[:, 4:8][:, :, None], [128, 4, 128]), ALU.mult)
            k.tt(k.dve, w1, w1[:], w1, w1[:], hwb, hwb[:], ALU.mult)
            k.actv(o_, o_[:], o_, o_[:], AF.Sigmoid)
            k.tt(k.dve, cat, cat[:, 0:512], w1, w1[:], o_, o_[:], ALU.mult)
            k.tt(k.dve, w2, w2[:], u_, u_[:], dsk, dsk[:], ALU.mult)
            k.tt(k.dve, w2, w2[:], w2, w2[:], y_, y_[:], ALU.add)
            k.tt(k.pool, y_, y_[:], w2, w2[:], w2, w2[:], ALU.mult)
            k.ts(k.dve, y_, y_[:], y_, y_[:], 0.044715, 1.0, ALU.mult, ALU.add)
            k.tt(k.dve, y_, y_[:], y_, y_[:], w2, w2[:], ALU.mult)
            k.actv(y_, y_[:], y_, y_[:], AF.Sigmoid, scale=2.0 * math.sqrt(2.0 / PI))
            k.tt(k.dve, ybb, ybb[:], y_, y_[:], w2, w2[:], ALU.mult)
            for j in range(4):
                k.tr(pTb, pTb[:, j, :], ybb, ybb[:, j * 128:(j + 1) * 128], c.idb, c.idb[:])
            k.cp(k.act, ybT, ybT[:], pTb, pTb[:, 0:4, :])
            for cb in range(2):
                for kc in range(4):
                    k.mm(pG[cb], pG[cb][:], ybT, ybT[:, kc, :], wglu, wglu[:, kc, cb * 512:(cb + 1) * 512], kc == 0, kc == 3)
            k.actv(w2, w2[:], pG[1], pG[1][:], AF.Sigmoid)
            k.tt(k.dve, cat, cat[:, 512:1024], pG[0], pG[0][:], w2, w2[:], ALU.mult)
            for j in range(8):
                k.tr(pTb, pTb[:, j, :], cat, cat[:, j * 128:(j + 1) * 128], c.idb, c.idb[:])
            k.cp(k.act, catT, catT[:], pTb, pTb[:])
            g = gate[1 if t < 2 else 0]
            for cb in range(2):
                for kc in range(8):
                    k.mm(pO[cb], pO[cb][:], catT, catT[:, kc, :], wout, wout[:, kc, cb * 512:(cb + 1) * 512], kc == 0, kc == 7)
                k.tt(k.dve, yo, yo[:, cb * 512:(cb + 1) * 512], pO[cb], pO[cb][:], g, g[:, cb * 512:(cb + 1) * 512], ALU.mult)
            k.tt(k.pool, yo, yo[:], yo, yo[:], x_, x_[:], ALU.add)
            k.dma(xs, xs.ap()[tok, :], yo, yo[:], q=k.pool)


def layer1(k, c, env):
    e = E(env)
    nc = k.nc
    xs = e.xs
    z_d = k.dram("z_d", [T, D]); gt_d = k.dram("gt_d", [T, 32]); o1_d = k.dram("o1_d", [2, T, D])
    with k.scope():
        qT = k.sb("gqT", [128, 8, T], BF16); kT = k.sb("gkT", [128, 8, T], BF16); vT = k.sb("gvT", [128, 8, T], BF16)
        with k.scope():
            hT = k.sb("hT", [128, 8, T], BF16)
            with k.scope():
                c.xt = [k.sb("xt%d" % i, [128, D]) for i in range(2)]
                c.sq = k.sb("sq", [128, D]); c.ss = k.sb("ss", [128, 4])
                c.pT = [k.ps("pT%d" % i, [128, 4, 128]) for i in range(2)]
                norm_mod(k, c, xs, list(range(NT)), e.ab_fn(0, 1), hT)
            c.wstage = [k.sb("wst%d" % i, [128, 2048]) for i in range(2)]
            c.ltst = [k.sb("ltst%d" % i, [64, 128]) for i in range(2)]
            c.ltps = [k.ps("ltps%d" % i, [128, 64]) for i in range(2)]
            c.lt_i = 0
            cw = k.sb("cw", [128, 24, 9])
            for ci in range(24):
                load_T(k, c, cw, cw[:, ci, :], e.od_conv, e.od_conv.ap()[:, ci * 128:(ci + 1) * 128], 9)
            ones32 = k.sb("ones32", [128, 128]); k.memset(k.dve, ones32, ones32[:], 1.0)
            wch = [k.sb("wch%d" % i, [128, 8, 256], BF16) for i in range(2)]
            P32 = k.sb("P32", [128, T]); Cv = k.sb("Cv", [128, T]); S32 = P32; Q32 = Cv
            rs = k.sb("rs", [128, 512])
            pp = [k.ps("pp%d" % i, [128, 512]) for i in range(3)]
            blocks = [(0, 512), (512, 512), (1024, 512), (1536, 512), (2048, 256)]
            it = 0
            for ci in range(24):
                if ci % 2 == 0:
                    wc = wch[(ci // 2) % 2]
                    load_w_bf16(k, c, wc, wc[:], e.od_w_in, e.od_w_in.ap()[:, ci * 128:(ci + 2) * 128], 256)
                wo = (ci % 2) * 128
                for (t0, tn) in blocks:
                    p = pp[it % 3]; it += 1
                    for kc in range(8):
                        k.mm(p, p[:, 0:tn], wc, wc[:, kc, wo:wo + 128], hT, hT[:, kc, t0:t0 + tn], kc == 0, kc == 7)
                    k.cp(k.act, P32, P32[:, t0:t0 + tn], p, p[:, 0:tn])
                w_ = lambda tap: cw[:, ci, tap:tap + 1]
                k.ts(k.dve, Cv, Cv[:], P32, P32[:], w_(4), None, ALU.mult, extra=[cw])
                k.stt(k.dve, Cv, Cv[:, 1:256], P32, P32[:, 0:255], w_(3), Cv, Cv[:, 1:256], ALU.mult, ALU.add, extra=[cw])
                k.stt(k.dve, Cv, Cv[:, 0:255], P32, P32[:, 1:256], w_(5), Cv, Cv[:, 0:255], ALU.mult, ALU.add, extra=[cw])
                Pl = P32[:, 256:T].rearrange("p (r q) -> p r q", q=64); Cl = Cv[:, 256:T].rearrange("p (r q) -> p r q", q=64)
                for a in range(3):
                    for b in range(3):
                        if a == 1 and b == 1:
                            continue
                        dr, dc = a - 1, b - 1
                        r0, r1 = max(0, -dr), 32 - max(0, dr)
                        c0, c1 = max(0, -dc), 64 - max(0, dc)
                        k.stt(k.dve, Cv, Cl[:, r0:r1, c0:c1], P32, Pl[:, r0 + dr:r1 + dr, c0 + dc:c1 + dc],
                              w_(a * 3 + b), Cv, Cl[:, r0:r1, c0:c1], ALU.mult, ALU.add, extra=[cw])
                h = ci % 8
                if ci >= 16:
                    k.actv(vT, vT[:, h, :], Cv, Cv[:], AF.Silu)
                else:
                    k.actv(S32, S32[:], Cv, Cv[:], AF.Silu)
                    k.tt(k.pool, Q32, Q32[:], S32, S32[:], S32, S32[:], ALU.mult)
                    dst = qT if ci < 8 else kT
                    for (t0, tn) in blocks:
                        p = pp[it % 3]; it += 1
                        k.mm(p, p[:, 0:tn], ones32, ones32[:], Q32, Q32[:, t0:t0 + tn])
                        k.actv(rs, rs[:, 0:tn], p, p[:, 0:tn], AF.Ln, bias=c.epsb[:, 0:1], extra=[c.epsb])
                        k.actv(rs, rs[:, 0:tn], rs, rs[:, 0:tn], AF.Exp, scale=-0.5)
                        k.stt(k.dve, dst, dst[:, h, t0:t0 + tn], S32, S32[:, t0:t0 + tn], (1.0 / math.sqrt(128) if ci < 8 else 1.0),
                              rs, rs[:, 0:tn], ALU.mult, ALU.mult)
            wz = k.sb("wz", [128, 8, 544], BF16)
            zt = [P32, Cv]
            iz = 0
            for (zc0, zn) in ((0, 512), (512, 544)):
                for cb in range(0, zn, 256):
                    n = min(256, zn - cb)
                    load_w_bf16(k, c, wz, wz[:, :, cb:cb + n], e.od_w_in, e.od_w_in.ap()[:, 3072 + zc0 + cb:3072 + zc0 + cb + n], n)
                for t in range(NT):
                    z_ = zt[iz % 2]; iz += 1
                    tok = slice(t * 128, (t + 1) * 128)
                    for bi, (col, n) in enumerate(((0, 512), (512, 32))[:(1 if zc0 == 0 else 2)]):
                        p = pp[it % 3]; it += 1
                        for kc in range(8):
                            k.mm(p, p[:, 0:n], hT, hT[:, kc, tok], wz, wz[:, kc, col:col + n], kc == 0, kc == 7)
                        k.cp(k.act if bi % 2 else k.dve, z_, z_[:, col:col + n], p, p[:, 0:n])
                    k.dma(z_d, z_d.ap()[tok, zc0:zc0 + 512], z_, z_[:, 0:512], q=k.pool)
                    if zc0:
                        k.dma(gt_d, gt_d.ap()[tok, :], z_, z_[:, 512:544], q=k.pool)
        if c.stop == "proj1":
            c.dbg_qkv = (qT, kT, vT)
            return
        gdn_phase(k, c, e, qT, kT, vT, gt_d, o1_d)
    if c.stop == "gdn":
        return
    with k.scope():
        c.wstage = [k.sb("wst%d" % i, [128, 4096]) for i in range(2)]
        wout = k.sb("wout", [128, 8, 1024], BF16)
        for cb in range(2):
            load_w_bf16(k, c, wout, wout[:, :, cb * 512:(cb + 1) * 512], e.od_wout, e.od_wout.ap()[:, cb * 512:(cb + 1) * 512], 512)
        hwb = k.sb("hwb", [128, D])
        k.dma(hwb, hwb[:], e.od_hw, e.od_hw.ap().partition_broadcast(128).rearrange("p o n -> p (o n)"))
        gate = k.sb("gate", [128, D]); e.load_gate(gate, 1, 0, 2)
        ot = [k.sb("ot%d" % i, [128, 2, D]) for i in range(2)]
        zt = [k.sb("zt%d" % i, [128, D]) for i in range(2)]
        xt = [k.sb("xt%d" % i, [128, D]) for i in range(2)]
        w1 = k.sb("w1", [128, D]); w2 = k.sb("w2", [128, D]); st = k.sb("st", [128, 16])
        cat = k.sb("cat", [128, D], BF16); catT = k.sb("catT", [128, 8, 128], BF16)
        pTb = k.ps("pTb", [128, 8, 128], BF16)
        pO = [k.ps("pO%d" % i, [128, 512]) for i in range(2)]
        yo = k.sb("yo", [128, D])
        for i, t in enumerate(range(2, NT)):
            tok = slice(t * 128, (t + 1) * 128)
            o_, z_, x_ = ot[i % 2], zt[i % 2], xt[i % 2]
            k.dma(o_, o_[:], o1_d, o1_d.ap()[:, tok, :].rearrange("r t n -> t r n"))
            k.dma(z_, z_[:], z_d, z_d.ap()[tok, :])
            k.dma(x_, x_[:], xs, xs.ap()[tok, :])
            k.tt(k.dve, w1, w1[:], o_, o_[:, 0, :], o_, o_[:, 1, :], ALU.add)
            k.tt(k.pool, w2, w2[:], w1, w1[:], w1, w1[:], ALU.mult)
            k.op(k.dve, lambda: nc.vector.reduce_sum(out=st[:, 0:8], in_=w2[:].rearrange("p (h e) -> p h e", h=8), axis=AX.X), [st], [w2])
            k.actv(st, st[:, 0:8], st, st[:, 0:8], AF.Ln, bias=c.epsb[:, 0:1], scale=1.0 / 128, extra=[c.epsb])
            k.actv(st, st[:, 8:16], st, st[:, 0:8], AF.Exp, scale=-0.5)
            k.tt(k.dve, w1, w1[:].rearrange("p (h e) -> p h e", h=8), w1, w1[:].rearrange("p (h e) -> p h e", h=8),
                 st, bc(st[:, 8:16][:, :, None], [128, 8, 128]), ALU.mult)
            k.tt(k.dve, w1, w1[:], w1, w1[:], hwb, hwb[:], ALU.mult)
            k.actv(z_, z_[:], z_, z_[:], AF.Silu)
            k.tt(k.dve, cat, cat[:], w1, w1[:], z_, z_[:], ALU.mult)
            for j in range(8):
                k.tr(pTb, pTb[:, j, :], cat, cat[:, j * 128:(j + 1) * 128], c.idb, c.idb[:])
            k.cp(k.act, catT, catT[:], pTb, pTb[:])
            for cb in range(2):
                for kc in range(8):
                    k.mm(pO[cb], pO[cb][:], catT, catT[:, kc, :], wout, wout[:, kc, cb * 512:(cb + 1) * 512], kc == 0, kc == 7)
                k.tt(k.dve, yo, yo[:, cb * 512:(cb + 1) * 512], pO[cb], pO[cb][:], gate, gate[:, cb * 512:(cb + 1) * 512], ALU.mult)
            k.tt(k.pool, yo, yo[:], yo, yo[:], x_, x_[:], ALU.add)
            k.dma(xs, xs.ap()[tok, :], yo, yo[:], q=k.pool)


def gdn_phase(k, c, e, qT, kT, vT, gt_d, o1_d):
    nc = k.nc
    with k.scope():
        gt = k.sb("gt", [64, 36, 32])
        for c0 in range(0, 36, 6):
            k.dma(gt, gt[:, c0:c0 + 6, :], gt_d, gt_d.ap()[c0 * 64:(c0 + 6) * 64, :].rearrange("(c l) n -> l c n", l=64))
        ga = k.sb("ga", [64, 16]); dtb = k.sb("dtb", [64, 16])
        k.dma(ga, ga[:], e.od_alog, e.od_alog.ap().partition_broadcast(64).rearrange("p o n -> p (o n)"))
        k.dma(dtb, dtb[:], e.od_dtb, e.od_dtb.ap().partition_broadcast(64).rearrange("p o n -> p (o n)"))
        k.actv(ga, ga[:], ga, ga[:], AF.Exp)
        tri = k.sb("tri", [64, 2, 64]); strict = k.sb("strict", [64, 2, 64])
        k.dma(tri, tri[:], e.ctri, e.ctri.ap().rearrange("r s l -> s r l"))
        k.dma(strict, strict[:], e.cstrict, e.cstrict.ap().rearrange("r s l -> s r l"))
        ones = k.sb("ones", [64, 128]); k.memset(k.dve, ones, ones[:], 1.0)
        ng = k.sb("ng", [64, 2, 36, 8]); beta = k.sb("beta", [64, 2, 36, 8])
        eG = k.sb("eG", [64, 2, 36, 8]); kds = k.sb("kds", [64, 2, 36, 8]); bg = k.sb("bg", [64, 2, 36, 8])
        gl = k.sb("gl", [128, 2, 36, 8])
        for d in range(2):
            k.tt(k.dve, ng, ng[:, d], gt, gt[:, :, 8 * d:8 * d + 8], dtb, bc(dtb[:, None, 8 * d:8 * d + 8], [64, 36, 8]), ALU.add)
            k.actv(beta, beta[:, d], gt, gt[:, :, 16 + 8 * d:24 + 8 * d], AF.Sigmoid)
        k.actv(ng, ng[:], ng, ng[:], AF.Exp)
        k.actv(ng, ng[:], ng, ng[:], AF.Ln, bias=1.0)
        for d in range(2):
            k.tt(k.dve, ng, ng[:, d], ng, ng[:, d], ga, bc(ga[:, None, 8 * d:8 * d + 8], [64, 36, 8]), ALU.mult)
        with k.scope():
            pF = k.ps("pF", [64, 2, 512]); pT_ = k.ps("pTt", [64, 2, 512]); pG = k.ps("pG", [128, 2, 512])
            for d in range(2):
                ngd = ng[:, d].rearrange("p c h -> p (c h)")
                k.mm(pF, pF[:, d, 0:288], tri, tri[:, d, :], ng, ngd)
                k.mm(pT_, pT_[:, d, 0:288], ones, ones[:, 0:64], ng, ngd)
                k.mm(pG, pG[:, d, 0:288], ones, ones[:], ng, ngd)
            fl = lambda t_: t_[:].rearrange("p d c h -> p d (c h)")
            k.actv(eG, fl(eG), pF, pF[:, :, 0:288], AF.Exp, scale=-1.0)
            k.cp(k.dve, kds, fl(kds), pF, pF[:, :, 0:288])
            k.tt(k.dve, kds, fl(kds), kds, fl(kds), pT_, pT_[:, :, 0:288], ALU.subtract)
            k.actv(kds, kds[:], kds, kds[:], AF.Exp)
            k.tt(k.dve, bg, bg[:], beta, beta[:], eG, eG[:], ALU.mult)
            k.actv(gl, fl(gl), pG, pG[:, :, 0:288], AF.Exp, scale=-1.0)
        S32 = [[k.sb("S32_%d_%d" % (d, hp), [128, 2, 128]) for hp in range(4)] for d in range(2)]
        Sb = [[k.sb("Sb_%d_%d" % (d, hp), [128, 2, 128], BF16) for hp in range(4)] for d in range(2)]
        for d in range(2):
            for hp in range(4):
                k.memset(k.dve, S32[d][hp], S32[d][hp][:], 0.0); k.memset(k.pool, Sb[d][hp], Sb[d][hp][:], 0.0)
        idb2 = bc(c.id32[0:64, None, 0:64], [64, 2, 64])

        class G:
            pass
        GS = {}
        for d in range(2):
            for par in range(2):
                g = G()
                sfx = "_%d%d" % (d, par)
                g.X = k.ps("gX" + sfx, [128, 512]); g.Y_ = k.ps("gY" + sfx, [128, 512])
                f3 = lambda h_, p1, c0, a_: h_[0:p1, c0:c0 + 256].rearrange("p (a b) -> p a b", a=a_)
                g.KDv = f3(g.X.h, 64, 0, 4); g.QDv = f3(g.X.h, 64, 256, 4)
                g.Nv = f3(g.X.h, 64, 0, 4); g.Vv = f3(g.X.h, 64, 256, 2); g.O1v = f3(g.X.h, 64, 0, 2)
                g.Tv = g.Y_.h[0:64, 0:256].bitcast(BF16).rearrange("p (a b) -> p a b", a=4)
                g.WTv = g.Y_.h[:, 256:384].rearrange("p (a b) -> p a b", a=2)
                g.O2v = f3(g.Y_.h, 64, 0, 2); g.Sv = f3(g.Y_.h, 128, 256, 2)
                g.kbg = k.sb("kbg" + sfx, [64, 2, 128], BF16); g.kd = k.sb("kd" + sfx, [64, 2, 128], BF16); g.bv = k.sb("bv" + sfx, [64, 2, 128], BF16)
                g.gm = k.sb("gm" + sfx, [64, 4, 64]); g.MBs = k.sb("MBs" + sfx, [64, 4, 64])
                g.gam = k.sb("gam" + sfx, [64, 2, 64]); g.A32 = k.sb("A32" + sfx, [64, 2, 64])
                g.Mb = k.sb("Mb" + sfx, [64, 2, 64], BF16); g.nAb = k.sb("nAb" + sfx, [64, 2, 64], BF16)
                g.Y = k.sb("Y" + sfx, [64, 2, 64], BF16); g.Rt = k.sb("Rt" + sfx, [64, 2, 64], BF16)
                g.nWT = k.sb("nWT" + sfx, [128, 2, 64], BF16); g.vn = k.sb("vn" + sfx, [64, 2, 128], BF16)
                g.gT_ = k.sb("gT_" + sfx, [64, 2, 64]); g.attT = k.sb("attT" + sfx, [64, 2, 64], BF16)
                g.t2 = k.sb("t2" + sfx, [64, 2, 128]); g.otl = [k.sb("otl%d" % i + sfx, [64, 2, 128]) for i in range(2)]
                GS[(d, par)] = g

        def chunk_gen(d, par, ch):
            g = GS[(d, par)]
            tok = slice(ch * 64, (ch + 1) * 64)
            hsel = lambda t_: bc(t_[:, d, ch, :].rearrange("p (a b) -> p a b", b=2)[:, par::2, :].rearrange("p a b -> p (a b)")[:, :, None], [64, 4, 64]) if False else None
            for qi, hp in enumerate((par, par + 2)):
                h0 = 2 * hp
                k.tt(k.pool, g.gm, g.gm[:, 2 * qi:2 * qi + 2, :], tri, bc(tri[:, d:d + 1, :], [64, 2, 64]),
                     ng, bc(ng[:, d, ch, h0:h0 + 2][:, :, None], [64, 2, 64]), ALU.mult)
                k.tt(k.pool, g.MBs, g.MBs[:, 2 * qi:2 * qi + 2, :], strict, bc(strict[:, d:d + 1, :], [64, 2, 64]),
                     beta, bc(beta[:, d, ch, h0:h0 + 2][:, :, None], [64, 2, 64]), ALU.mult)
            for qi, hp in enumerate((par, par + 2)):
                h0 = 2 * hp
                S3, Sb_ = S32[d][hp], Sb[d][hp]
                for hh in range(2):
                    h = h0 + hh
                    k.tr(g.Y_, g.Tv[:, hh, :], kT, kT[:, h, tok], c.idb, c.idb[:])
                    k.tr(g.Y_, g.Tv[:, 2 + hh, :], vT, vT[:, h, tok], c.idb, c.idb[:])
                    k.mm(g.X, g.KDv[:, hh, :], kT, kT[:, h, tok], kT, kT[:, h, tok])
                    k.mm(g.X, g.KDv[:, 2 + hh, :], g.gm, g.gm[:, 2 * qi + hh, :], strict, strict[:, d, :])
                    k.mm(g.X, g.QDv[:, hh, :], kT, kT[:, h, tok], qT, qT[:, h, tok])
                    k.mm(g.X, g.QDv[:, 2 + hh, :], strict, strict[:, d, :], g.gm, g.gm[:, 2 * qi + hh, :])
                yield
                sc = lambda t_: bc(t_[:, d, ch, h0:h0 + 2][:, :, None], [64, 2, 128])
                k.tt(k.dve, g.kbg, g.kbg[:], g.Y_, g.Tv[:, 0:2, :], bg, sc(bg), ALU.mult)
                k.tt(k.dve, g.kd, g.kd[:], g.Y_, g.Tv[:, 0:2, :], kds, sc(kds), ALU.mult)
                k.tt(k.dve, g.bv, g.bv[:], g.Y_, g.Tv[:, 2:4, :], beta, sc(beta), ALU.mult)
                k.actv(g.gam, g.gam[:], g.X, g.KDv[:, 2:4, :], AF.Exp, scale=-1.0)
                k.actv(g.gT_, g.gT_[:], g.X, g.QDv[:, 2:4, :], AF.Exp, scale=-1.0)
                k.tt(k.pool, g.gam, g.gam[:], g.gam, g.gam[:], g.MBs, g.MBs[:, 2 * qi:2 * qi + 2, :], ALU.mult)
                k.tt(k.pool, g.gT_, g.gT_[:], g.gT_, g.gT_[:], tri, bc(tri[:, d:d + 1, :], [64, 2, 64]), ALU.mult)
                k.tt(k.dve, g.A32, g.A32[:], g.X, g.KDv[:, 0:2, :], g.gam, g.gam[:], ALU.mult)
                k.tt(k.dve, g.attT, g.attT[:], g.X, g.QDv[:, 0:2, :], g.gT_, g.gT_[:], ALU.mult)
                k.tt(k.pool, g.Mb, g.Mb[:], g.A32, g.A32[:], c.id32, idb2, ALU.add)
                k.ts(k.dve, g.nAb, g.nAb[:], g.A32, g.A32[:], -1.0, None, ALU.mult)
                for hh in range(2):
                    k.mm(g.X, g.Nv[:, hh, :], g.nAb, g.nAb[:, hh, :], c.idb, c.idb[0:64, 0:64])
                yield
                k.tt(k.dve, g.Y, g.Y[:], g.X, g.Nv[:, 0:2, :], c.id32, idb2, ALU.add)
                for itn in range(5):
                    for hh in range(2):
                        k.mm(g.X, g.Nv[:, hh, :], g.Y, g.Y[:, hh, :], g.Mb, g.Mb[:, hh, :])
                    yield
                    k.tt(k.dve, g.Rt, g.Rt[:], c.id32, idb2, g.X, g.Nv[:, 0:2, :], ALU.subtract)
                    for hh in range(2):
                        k.mm(g.X, g.Nv[:, 2 + hh, :], g.Rt, g.Rt[:, hh, :], g.Y, g.Y[:, hh, :])
                    yield
                    k.tt(k.dve, g.Y, g.Y[:], g.Y, g.Y[:], g.X, g.Nv[:, 2:4, :], ALU.add)
                for hh in range(2):
                    k.mm(g.Y_, g.WTv[:, hh, :], g.kbg, g.kbg[:, hh, :], g.Y, g.Y[:, hh, :])
                yield
                k.actv(g.nWT, g.nWT[:], g.Y_, g.WTv, AF.Copy, scale=-1.0)
                for hh in range(2):
                    k.mm(g.X, g.Vv[:, hh, :], g.Y, g.Y[:, hh, :], g.bv, g.bv[:, hh, :], True, False)
                    k.mm(g.X, g.Vv[:, hh, :], g.nWT, g.nWT[:, hh, :], Sb_, Sb_[:, hh, :], False, True)
                yield
                k.cp(k.act, g.vn, g.vn[:], g.X, g.Vv)
                for hh in range(2):
                    h = h0 + hh
                    k.mm(g.X, g.O1v[:, hh, :], qT, qT[:, h, tok], Sb_, Sb_[:, hh, :])
                    k.mm(g.Y_, g.O2v[:, hh, :], g.attT, g.attT[:, hh, :], g.vn, g.vn[:, hh, :])
                    k.mm(g.Y_, g.Sv[:, hh, :], g.kd, g.kd[:, hh, :], g.vn, g.vn[:, hh, :])
                yield
                ot_ = g.otl[qi]
                k.cp(k.act, g.t2, g.t2[:], g.Y_, g.O2v)
                k.tt(k.dve, ot_, ot_[:], g.X, g.O1v, eG, bc(eG[:, d, ch, h0:h0 + 2][:, :, None], [64, 2, 128]), ALU.mult)
                k.tt(k.pool, ot_, ot_[:], ot_, ot_[:], g.t2, g.t2[:], ALU.add)
                k.tt(k.pool, S3, S3[:], S3, S3[:], gl, bc(gl[:, d, ch, h0:h0 + 2][:, :, None], [128, 2, 128]), ALU.mult)
                k.tt(k.dve, S3, S3[:], S3, S3[:], g.Y_, g.Sv, ALU.add)
                k.cp(k.act, Sb_, Sb_[:], S3, S3[:])
                k.dma(o1_d, o1_d.ap()[d, tok, h0 * 128:(h0 + 2) * 128], ot_, ot_[:].rearrange("p h e -> p (h e)"), q=k.pool)
                yield

        for step in range(36):
            gens = [chunk_gen(0, 0, step), chunk_gen(1, 0, ORDB[step]), chunk_gen(0, 1, step), chunk_gen(1, 1, ORDB[step])]
            alive = [True] * 4
            while any(alive):
                for i in range(4):
                    if alive[i]:
                        try:
                            next(gens[i])
                        except StopIteration:
                            alive[i] = False


def host_consts():
    s = np.arange(64)
    tri = np.stack([(s[:, None] <= s[None, :]), (s[:, None] >= s[None, :])]).astype(np.float32)
    ip = np.arange(128) // 16
    maskM = np.stack([(ip[None, :] >= ip[:, None]), (ip[None, :] <= ip[:, None])]).astype(np.float32)
    return {
        "k_id32": np.eye(128, dtype=np.float32),
        "k_idb": np.eye(128, dtype=np.float32).astype(ml_dtypes.bfloat16),
        "k_tri": tri, "k_maskM": maskM,
        "k_strict": np.stack([(s[:, None] > s[None, :]), (s[:, None] < s[None, :])]).astype(np.float32),
    }


def make_in_maps(inputs, cores):
    f = lambda a: np.ascontiguousarray(np.asarray(a, dtype=np.float32))
    sh = {
        "c_ctx": f(inputs["c_ctx"]).reshape(1, D), "ada_w": f(inputs["ada_w"]), "ada_b": f(inputs["ada_b"]),
        "norm1_w": f(inputs["norm1_w"]), "norm2_w": f(inputs["norm2_w"]),
        "ffn_w1": f(inputs["ffn_w1"]), "ffn_w3": f(inputs["ffn_w3"]), "ffn_w2": f(inputs["ffn_w2"]),
        "final_norm_w": f(inputs["final_norm_w"]).reshape(1, D),
        "ev_w_in": f(inputs["ev_w_in"])[0], "ev_i_bias": f(inputs["ev_i_bias"]).reshape(1, 8),
        "ev_f_bias": f(inputs["ev_f_bias"]).reshape(1, 8), "ev_head_norm_w": f(inputs["ev_head_norm_w"]).reshape(1, 512),
        "ev_lam_re": f(inputs["ev_lam_re"])[0], "ev_lam_im": f(inputs["ev_lam_im"])[0],
        "ev_log_dt": f(inputs["ev_log_dt"]).reshape(1, 64),
        "ev_b_re": f(inputs["ev_b_re"])[0], "ev_b_im": f(inputs["ev_b_im"])[0],
        "ev_c_re": f(inputs["ev_c_re"]).reshape(1024, 64), "ev_c_im": f(inputs["ev_c_im"]).reshape(1024, 64),
        "ev_d": f(inputs["ev_d"]).reshape(1, 512), "ev_w_glu": f(inputs["ev_w_glu"])[0], "ev_w_out": f(inputs["ev_w_out"])[0],
        "od_w_in": f(inputs["od_w_in"])[0], "od_conv_w": f(inputs["od_conv_w"]).reshape(9, 3072),
        "od_a_log": f(inputs["od_a_log"]).reshape(1, 16), "od_dt_bias": f(inputs["od_dt_bias"]).reshape(1, 16),
        "od_head_norm_w": f(inputs["od_head_norm_w"]).reshape(1, D), "od_w_out": f(inputs["od_w_out"])[0],
    }
    sh.update(host_consts())
    x, cc, ctx = f(inputs["x"]), f(inputs["c"]), f(inputs["ctx"])
    maps = []
    for b in cores:
        m = dict(sh)
        m["x"] = x[b]; m["c"] = cc[b:b + 1]; m["ctx"] = ctx[b]
        maps.append(m)
    return maps


def kernel(**inputs):
    nc, _ = build_program()
    maps = make_in_maps(inputs, list(range(8)))
    res = run_bass_kernel_spmd(nc, maps, core_ids=list(range(8)))
    return np.stack([np.asarray(r["out"], dtype=np.float32) for r in res.results], axis=0)
```

```python
import math
import numpy as np
import ml_dtypes
import concourse.bass as bass
import concourse.mybir as mybir
from concourse.bass_types import AP
from concourse.bass_utils import run_bass_kernel_spmd

F32 = mybir.dt.float32
BF16 = mybir.dt.bfloat16
AF = mybir.ActivationFunctionType
ALU = mybir.AluOpType
AX = mybir.AxisListType

D = 1024
T = 2304
NCTX = 256
NLAT = 2048
NT = T // 128
EPS = 1e-6
HID = 2816
PI = math.pi


class Obj:
    def __init__(self, k, name, handle, space):
        self.k, self.name, self.h, self.space = k, name, handle, space
        self.uid = k.uid
        self.w, self.r = {}, {}
        self.sems = {}

    def __getitem__(self, idx):
        return self.h[idx]

    def ap(self):
        return self.h.ap() if self.space == "dram" else self.h[:]

    def dsem(self, kind):
        if kind not in self.sems:
            if not self.sems:
                self.k.dma_objs.append(self)
            pool = self.k.sem_pool[kind]
            if pool:
                self.sems[kind] = pool.pop()
            else:
                self.k.nsem += 1
                self.sems[kind] = [self.k.new_sem("d%s_%d" % (kind, self.k.nsem), keep=True), 0]
        return self.sems[kind]


class Eng:
    def __init__(self, k, name, e):
        self.k, self.name, self.e = k, name, e
        self.sem = k.new_sem("p_" + name)
        self.cnt = 0
        self.seen = {}

    def need(self, tok):
        s, v = tok
        if self.seen.get(id(s), 0) >= v:
            return
        self.e.wait_ge(s, v)
        self.seen[id(s)] = v


class K:
    def __init__(self, nc):
        self.nc = nc
        self._ctx = []
        self.dma_objs = []
        self.sem_pool = {"hw": [], "sw": []}
        self.nsem = 0
        self._perm = []
        self.pe = Eng(self, "pe", nc.tensor)
        self.dve = Eng(self, "dve", nc.vector)
        self.act = Eng(self, "act", nc.scalar)
        self.pool = Eng(self, "pool", nc.gpsimd)
        self.sp = Eng(self, "sp", nc.sync)
        self.engs = [self.pe, self.dve, self.act, self.pool, self.sp]
        self.n_ins = 0
        self.uid = 0

    def new_sem(self, name, keep=False):
        cm = self.nc.semaphore(name)
        s = cm.__enter__()
        self._perm.append((cm, s))
        return s

    def _alloc(self, cm, name, space):
        h = cm.__enter__()
        self._ctx.append(cm)
        return Obj(self, name, h, space)

    def sb(self, name, shape, dt=F32):
        self.uid += 1
        name = "%s_%d" % (name, self.uid)
        return self._alloc(self.nc.sbuf_tensor(name, list(shape), dt), name, "sb")

    def ps(self, name, shape, dt=F32):
        self.uid += 1
        name = "%s_%d" % (name, self.uid)
        return self._alloc(self.nc.psum_tensor(name, list(shape), dt), name, "ps")

    def dram(self, name, shape, dt=F32, kind="Internal"):
        h = self.nc.dram_tensor(name, list(shape), dt, kind=kind)
        return Obj(self, name, h, "dram")

    class _Scope:
        def __init__(self, k):
            self.k = k

        def __enter__(self):
            self.mark = len(self.k._ctx)
            self.uid0 = self.k.uid
            return self

        def __exit__(self, *a):
            k = self.k
            k.barrier()
            while len(k._ctx) > self.mark:
                k._ctx.pop().__exit__(None, None, None)
            keep = []
            for o in k.dma_objs:
                if o.space == "dram" or o.uid <= self.uid0:
                    keep.append(o)
                else:
                    for kind, sc in o.sems.items():
                        k.sem_pool[kind].append(sc)
                    o.sems = {}
            k.dma_objs = keep
            return False

    def scope(self):
        return K._Scope(self)

    def barrier(self):
        toks = [(e.sem, e.cnt) for e in self.engs if e.cnt]
        toks += [(sc[0], sc[1]) for o in self.dma_objs for sc in o.sems.values() if sc[1]]
        for e in self.engs:
            for t in toks:
                if t[0] is e.sem:
                    continue
                e.need(t)

    def _deps(self, eng, outs, ins, same_eng_raw=True):
        toks = []
        for o in ins:
            toks += list(o.w.values())
            if o.space == "ps":
                toks += [t for t in o.r.values() if t[0] is not eng.sem]
        for o in outs:
            toks += list(o.w.values())
            toks += list(o.r.values())
        for t in toks:
            if t[0] is eng.sem and (eng is self.pe or not same_eng_raw):
                continue
            eng.need(t)

    SAME_ENG_WAIT = True

    SKIP_SELF = ()

    def op(self, eng, fn, outs, ins):
        self._deps(eng, outs, ins, same_eng_raw=(K.SAME_ENG_WAIT and eng.name not in K.SKIP_SELF))
        ins_ = fn()
        eng.cnt += 1
        ins_.then_inc(eng.sem, 1)
        tok = (eng.sem, eng.cnt)
        eng.seen[id(eng.sem)] = max(eng.seen.get(id(eng.sem), 0), 0)
        for o in ins:
            o.r[id(tok[0])] = tok
        for o in outs:
            o.w = {id(tok[0]): tok}
            o.r = {}
        self.n_ins += 1
        return ins_

    def dma(self, out_obj, out_ap, in_obj, in_ap, q=None, **kw):
        q = q or self.sp
        self._deps(q, [out_obj], [in_obj], same_eng_raw=True)
        sc = out_obj.dsem("sw" if q is self.pool else "hw")
        s = sc[0]
        ins_ = q.e.dma_start(out=out_ap, in_=in_ap, **kw)
        sc[1] += 16
        ins_.then_inc(s, 16)
        tok = (s, sc[1])
        in_obj.r[id(s)] = tok
        out_obj.w[id(s)] = tok
        out_obj.r = {}
        self.n_ins += 1
        return ins_

    def finish(self, outs):
        self.barrier()

    def close(self):
        while self._ctx:
            self._ctx.pop().__exit__(None, None, None)
        while self._perm:
            self._perm.pop()[0].__exit__(None, None, None)

    def mm(self, out_o, out_ap, l_o, l_ap, r_o, r_ap, start=True, stop=True):
        nc = self.nc
        return self.op(self.pe, lambda: nc.tensor.matmul(out_ap, lhsT=l_ap, rhs=r_ap, start=start, stop=stop),
                       [out_o], [l_o, r_o])

    def tr(self, out_o, out_ap, in_o, in_ap, id_o, id_ap):
        nc = self.nc
        return self.op(self.pe, lambda: nc.tensor.transpose(out_ap, in_ap, id_ap), [out_o], [in_o, id_o])

    def tt(self, eng, out_o, out_ap, a_o, a_ap, b_o, b_ap, op):
        return self.op(eng, lambda: eng.e.tensor_tensor(out=out_ap, in0=a_ap, in1=b_ap, op=op), [out_o], [a_o, b_o])

    def ts(self, eng, out_o, out_ap, a_o, a_ap, s1, s2, op0, op1=None, extra=()):
        if op1 is None:
            return self.op(eng, lambda: eng.e.tensor_scalar(out=out_ap, in0=a_ap, scalar1=s1, scalar2=None, op0=op0),
                           [out_o], [a_o] + list(extra))
        return self.op(eng, lambda: eng.e.tensor_scalar(out=out_ap, in0=a_ap, scalar1=s1, scalar2=s2, op0=op0, op1=op1),
                       [out_o], [a_o] + list(extra))

    def stt(self, eng, out_o, out_ap, a_o, a_ap, sc, b_o, b_ap, op0, op1, extra=()):
        return self.op(eng, lambda: eng.e.scalar_tensor_tensor(out=out_ap, in0=a_ap, scalar=sc, in1=b_ap, op0=op0, op1=op1),
                       [out_o], [a_o, b_o] + list(extra))

    def actv(self, out_o, out_ap, a_o, a_ap, func, bias=0.0, scale=1.0, extra=()):
        nc = self.nc
        return self.op(self.act, lambda: nc.scalar.activation(out=out_ap, in_=a_ap, func=func, bias=bias, scale=scale),
                       [out_o], [a_o] + list(extra))

    def cp(self, eng, out_o, out_ap, a_o, a_ap):
        if eng is self.act:
            nc = self.nc
            return self.op(eng, lambda: nc.scalar.copy(out=out_ap, in_=a_ap), [out_o], [a_o])
        return self.op(eng, lambda: eng.e.tensor_copy(out=out_ap, in_=a_ap), [out_o], [a_o])

    def memset(self, eng, o, ap, val):
        return self.op(eng, lambda: eng.e.memset(ap, val), [o], [])


def bc(ap, shape):
    return ap.broadcast_to(list(shape))


class Ctx:
    pass


def load_w_bf16(k, c, dst, dst_ap_fn, wsrc, w_ap, ncols, kchunks=8, eng=None):
    eng = eng or k.pool
    st = c.wstage[c.wstage_i % 2]
    c.wstage_i += 1
    sv = st[:, 0:kchunks * ncols].rearrange("p (k n) -> p k n", k=kchunks)
    k.dma(st, sv, wsrc, w_ap.rearrange("(kc p) n -> p kc n", p=128))
    k.cp(eng, dst, dst_ap_fn, st, sv)


def norm_mod(k, c, xs, tiles, ab_of_tile, hT, col0=0):
    for i, t in enumerate(tiles):
        xt = c.xt[i % 2]
        k.dma(xt, xt[:], xs, xs.ap()[t * 128:(t + 1) * 128, :])
        sq = c.sq[i % 2] if isinstance(c.sq, list) else c.sq
        k.tt(k.dve, sq, sq[:], xt, xt[:], xt, xt[:], ALU.mult)
        ss = c.ss[i % 2] if isinstance(c.ss, list) else c.ss
        k.op(k.dve, lambda: k.nc.vector.reduce_sum(out=ss[:, 0:1], in_=sq[:], axis=AX.X), [ss], [sq])
        k.actv(ss, ss[:, 1:2], ss, ss[:, 0:1], AF.Ln, bias=c.epsb[:, 0:1], scale=1.0 / D, extra=[c.epsb])
        k.actv(ss, ss[:, 2:3], ss, ss[:, 1:2], AF.Exp, scale=-0.5)
        k.ts(k.dve, sq, sq[:], xt, xt[:], ss[:, 2:3], None, ALU.mult, extra=[ss])
        a, b = ab_of_tile(t)
        for half in range(2):
            pt = c.pT[half]
            for j in range(4):
                kc = half * 4 + j
                k.tr(pt, pt[:, j, :], sq, sq[:, kc * 128:(kc + 1) * 128], c.id32, c.id32[:])
            for j in range(4):
                kc = half * 4 + j
                k.actv(hT, hT[:, kc, col0 + i * 128: col0 + (i + 1) * 128], pt, pt[:, j, :], AF.Identity,
                       bias=b[0][:, b[1] + kc: b[1] + kc + 1], scale=a[0][:, a[1] + kc:a[1] + kc + 1], extra=[a[0], b[0]])


def load_T(k, c, dst_o, dst_ap, src_o, src_ap, nrows, ncols=128):
    st = c.ltst[c.lt_i % 2]; pt = c.ltps[c.lt_i % 2]; c.lt_i += 1
    k.dma(st, st[0:nrows, 0:ncols], src_o, src_ap)
    k.tr(pt, pt[0:ncols, 0:nrows], st, st[0:nrows, 0:ncols], c.id32, c.id32[0:nrows, 0:nrows])
    k.cp(k.dve, dst_o, dst_ap, pt, pt[0:ncols, 0:nrows])


def tok_blocks(tiles_n):
    out, s = [], 0
    while s < tiles_n:
        n = min(4, tiles_n - s)
        out.append((s, n))
        s += n
    return out


def build_program(stop_after=None, debug=False):
    nc = bass.Bass("TRN2", target_bir_lowering=False)
    k = K(nc)
    c = Ctx()
    c.wstage_i = 0
    c.stop = stop_after
    I = {}

    def inp(name, shape, dt=F32):
        I[name] = k.dram(name, shape, dt, kind="ExternalInput")
        return I[name]

    x_in = inp("x", [NLAT, D]); cvec = inp("c", [1, D]); ctx_in = inp("ctx", [NCTX, D]); c_ctx = inp("c_ctx", [1, D])
    ada_w = inp("ada_w", [2, D, 6 * D]); ada_b = inp("ada_b", [2, 6 * D])
    norm1_w = inp("norm1_w", [2, D]); norm2_w = inp("norm2_w", [2, D])
    ffn_w1 = inp("ffn_w1", [2, D, HID]); ffn_w3 = inp("ffn_w3", [2, D, HID]); ffn_w2 = inp("ffn_w2", [2, HID, D])
    final_w = inp("final_norm_w", [1, D])
    ev_w_in = inp("ev_w_in", [D, 2576]); ev_ib = inp("ev_i_bias", [1, 8]); ev_fb = inp("ev_f_bias", [1, 8])
    ev_hw = inp("ev_head_norm_w", [1, 512])
    ev_lre = inp("ev_lam_re", [2, 32, 64]); ev_lim = inp("ev_lam_im", [2, 32, 64]); ev_ldt = inp("ev_log_dt", [1, 64])
    ev_bre = inp("ev_b_re", [2, 32, 64, 16]); ev_bim = inp("ev_b_im", [2, 32, 64, 16])
    ev_cre = inp("ev_c_re", [1024, 64]); ev_cim = inp("ev_c_im", [1024, 64])
    ev_d = inp("ev_d", [1, 512]); ev_wglu = inp("ev_w_glu", [512, 1024]); ev_wout = inp("ev_w_out", [D, D])
    od_w_in = inp("od_w_in", [D, 4128]); od_conv = inp("od_conv_w", [9, 3072])
    od_alog = inp("od_a_log", [1, 16]); od_dtb = inp("od_dt_bias", [1, 16]); od_hw = inp("od_head_norm_w", [1, D])
    od_wout = inp("od_w_out", [D, D])
    cid32 = inp("k_id32", [128, 128]); cidb = inp("k_idb", [128, 128], BF16)
    ctri = inp("k_tri", [2, 64, 64])
    cmaskM = inp("k_maskM", [2, 128, 128])
    cstrict = inp("k_strict", [2, 64, 64])
    out_d = k.dram("out", [NLAT, D], F32, kind="ExternalOutput")

    xs = k.dram("xs", [T, D])
    modv = k.dram("modv", [2, 2, 6 * D])
    dbg = {}

    c.id32 = k.sb("id32", [128, 128]); k.dma(c.id32, c.id32[:], cid32, cid32.ap())
    c.idb = k.sb("idb", [128, 128], BF16); k.dma(c.idb, c.idb[:], cidb, cidb.ap())
    c.epsb = k.sb("epsb", [128, 2]); k.memset(k.dve, c.epsb, c.epsb[:, 0:1], EPS); k.memset(k.dve, c.epsb, c.epsb[:, 1:2], 0.5 * PI)

    k.dma(xs, xs.ap()[0:NCTX, :], ctx_in, ctx_in.ap())
    k.dma(xs, xs.ap()[NCTX:T, :], x_in, x_in.ap())
    with k.scope():
        sT = k.sb("sT", [128, 8, 2])
        c.ltst = [k.sb("ltst%d" % i, [64, 128]) for i in range(2)]
        c.ltps = [k.ps("ltps%d" % i, [128, 64]) for i in range(2)]
        c.lt_i = 0
        load_T(k, c, sT, sT[:, :, 0], cvec, cvec.ap().rearrange("o (kc p) -> (o kc) p", p=128), 8)
        load_T(k, c, sT, sT[:, :, 1], c_ctx, c_ctx.ap().rearrange("o (kc p) -> (o kc) p", p=128), 8)
        sS = k.sb("sS", [128, 8, 2])
        k.actv(sS, sS[:], sT, sT[:], AF.Silu)
        wst = [k.sb("adw%d" % i, [128, 8, 512]) for i in range(2)]
        pm = [k.ps("pm%d" % i, [128, 512]) for i in range(2)]
        brow = k.sb("brow", [2, 6 * D]); mrow = k.sb("mrow", [2, 6 * D])
        for li in range(2):
            k.dma(brow, brow[:], ada_b, ada_b.ap()[li:li + 1, :].partition_broadcast(2).rearrange("p o n -> p (o n)"))
            for j in range(12):
                w = wst[j % 2]
                k.dma(w, w[:], ada_w, ada_w.ap()[li, :, j * 512:(j + 1) * 512].rearrange("(kc p) n -> p kc n", p=128))
                p = pm[j % 2]
                for kc in range(8):
                    k.mm(p, p[0:2, :], sS, sS[:, kc, :], w, w[:, kc, :], start=(kc == 0), stop=(kc == 7))
                k.tt(k.dve, mrow, mrow[:, j * 512:(j + 1) * 512], p, p[0:2, :], brow, brow[:, j * 512:(j + 1) * 512], ALU.add)
            k.dma(modv, modv.ap()[li], mrow, mrow[:], q=k.pool)

    if stop_after == "ada":
        k.finish([])
        k.close()
        return nc, ["modv"]

    modF = k.sb("modF", [128, 2, 2, 48])
    nwF = k.sb("nwF", [128, 2, 2, 8])
    with k.scope():
        c.ltst = [k.sb("ltst%d" % i, [64, 128]) for i in range(2)]
        c.ltps = [k.ps("ltps%d" % i, [128, 64]) for i in range(2)]
        c.lt_i = 0
        for li in range(2):
            for who in range(2):
                load_T(k, c, modF, modF[:, li, who, :], modv, modv.ap()[li, who, :].rearrange("(c p) -> c p", p=128), 48)
        for wi, nw in enumerate((norm1_w, norm2_w)):
            for li in range(2):
                load_T(k, c, nwF, nwF[:, wi, li, :], nw, nw.ap()[li, :].rearrange("(c p) -> c p", p=128), 8)
    aF = k.sb("aF", [128, 2, 2, 2, 8])
    for wi in range(2):
        for li in range(2):
            for who in range(2):
                sc0 = 8 if wi == 0 else 32
                k.stt(k.dve, aF, aF[:, wi, li, who, :], modF, modF[:, li, who, sc0:sc0 + 8], 1.0, nwF, nwF[:, wi, li, :],
                      ALU.add, ALU.mult)

    def ab_fn(wi, li):
        sh0 = 0 if wi == 0 else 24

        def f(t):
            who = 1 if t < 2 else 0
            a_flat = aF.h[:].rearrange("p a b c d -> p (a b c d)")
            b_flat = modF.h[:].rearrange("p a b c -> p (a b c)")
            return ((_View(aF, a_flat), ((wi * 2 + li) * 2 + who) * 8), (_View(modF, b_flat), (li * 2 + who) * 48 + sh0))
        return f

    def load_gate(dst, li, who, part):
        k.dma(dst, dst[:], modv, modv.ap()[li, who:who + 1, part * D:(part + 1) * D].partition_broadcast(128).rearrange("p o n -> p (o n)"))

    def ffn_phase(li, tiles):
        with k.scope():
            c.xt = [k.sb("xt%d" % i, [128, D]) for i in range(2)]
            c.sq = [k.sb("sq%d" % i, [128, D]) for i in range(2)]; c.ss = [k.sb("ss%d" % i, [128, 4]) for i in range(2)]
            c.pT = [k.ps("pT%d" % i, [128, 4, 128]) for i in range(2)]
            c.wstage = [k.sb("wst%d" % i, [128, 2048]) for i in range(2)]
            ntl = len(tiles)
            half_n = (ntl + 1) // 2
            gate = [k.sb("gate%d" % w, [128, D]) for w in range(2)]
            load_gate(gate[0], li, 0, 5); load_gate(gate[1], li, 1, 5)
            hT = k.sb("hT", [128, 8, half_n * 128], BF16)
            gT = k.sb("gT", [128, 22, half_n * 128], BF16)
            w2b = k.sb("w2b", [128, 22, D], BF16)
            w1b = [k.sb("w1b%d" % i, [128, 8, 256], BF16) for i in range(2)]
            w3b = [k.sb("w3b%d" % i, [128, 8, 256], BF16) for i in range(2)]
            p1 = [k.ps("p1_%d" % i, [128, 512]) for i in range(2)]
            p3 = [k.ps("p3_%d" % i, [128, 512]) for i in range(2)]
            py = [k.ps("py%d" % i, [128, 512]) for i in range(2)]
            sg = [k.sb("sg%d" % i, [128, 512]) for i in range(2)]
            yo = [k.sb("yo%d" % i, [128, D]) for i in range(2)]
            for jb in range(0, 22, 4):
                n = min(4, 22 - jb)
                for cb in range(2):
                    st = c.wstage[c.wstage_i % 2]; c.wstage_i += 1
                    sv = st[:, 0:n * 512].rearrange("p (j n) -> p j n", j=n)
                    k.dma(st, sv, ffn_w2, ffn_w2.ap()[li, jb * 128:(jb + n) * 128, cb * 512:(cb + 1) * 512]
                          .rearrange("(j p) n -> p j n", p=128))
                    k.cp(k.pool, w2b, w2b[:, jb:jb + n, cb * 512:(cb + 1) * 512], st, sv)
            it = 0
            for hs in range(0, ntl, half_n):
                ht = tiles[hs:hs + half_n]
                norm_mod(k, c, xs, ht, ab_fn(1, li), hT)
                blocks = tok_blocks(len(ht))
                for jb in range(0, 22, 2):
                    n = 2
                    wa, wb = w1b[(jb // 2) % 2], w3b[(jb // 2) % 2]
                    load_w_bf16(k, c, wa, wa[:, :, 0:n * 128], ffn_w1, ffn_w1.ap()[li, :, jb * 128:(jb + n) * 128], n * 128)
                    load_w_bf16(k, c, wb, wb[:, :, 0:n * 128], ffn_w3, ffn_w3.ap()[li, :, jb * 128:(jb + n) * 128], n * 128, eng=k.dve)
                    for jj in range(n):
                        j = jb + jj
                        for (b0, bn) in blocks:
                            q1, q3, s_ = p1[it % 2], p3[it % 2], sg[it % 2]; it += 1
                            cols = slice(b0 * 128, (b0 + bn) * 128)
                            w_ = bn * 128
                            for kc in range(8):
                                k.mm(q1, q1[:, 0:w_], wa, wa[:, kc, jj * 128:(jj + 1) * 128], hT, hT[:, kc, cols], kc == 0, kc == 7)
                            for kc in range(8):
                                k.mm(q3, q3[:, 0:w_], wb, wb[:, kc, jj * 128:(jj + 1) * 128], hT, hT[:, kc, cols], kc == 0, kc == 7)
                            k.actv(s_, s_[:, 0:w_], q1, q1[:, 0:w_], AF.Silu)
                            k.tt(k.dve, gT, gT[:, j, cols], s_, s_[:, 0:w_], q3, q3[:, 0:w_], ALU.mult)
                for i, t in enumerate(ht):
                    xt = c.xt[i % 2]
                    k.dma(xt, xt[:], xs, xs.ap()[t * 128:(t + 1) * 128, :])
                    g = gate[1 if t < 2 else 0]
                    y = yo[i % 2]
                    for cb in range(2):
                        p = py[cb]
                        for j in range(22):
                            k.mm(p, p[:], gT, gT[:, j, i * 128:(i + 1) * 128], w2b, w2b[:, j, cb * 512:(cb + 1) * 512], j == 0, j == 21)
                        k.tt(k.dve, y, y[:, cb * 512:(cb + 1) * 512], p, p[:], g, g[:, cb * 512:(cb + 1) * 512], ALU.mult)
                    k.tt(k.pool, y, y[:], y, y[:], xt, xt[:], ALU.add)
                    k.dma(xs, xs.ap()[t * 128:(t + 1) * 128, :], y, y[:], q=k.pool)

    def final_phase():
        with k.scope():
            xt = [k.sb("fx%d" % i, [128, D]) for i in range(2)]
            sq = [k.sb("fs%d" % i, [128, D]) for i in range(2)]
            ss = k.sb("fss", [128, 4])
            fw = k.sb("fw", [128, D])
            k.dma(fw, fw[:], final_w, final_w.ap().partition_broadcast(128).rearrange("p o n -> p (o n)"))
            for i in range(16):
                t = i + 2
                x_, s_ = xt[i % 2], sq[i % 2]
                k.dma(x_, x_[:], xs, xs.ap()[t * 128:(t + 1) * 128, :])
                k.tt(k.dve, s_, s_[:], x_, x_[:], x_, x_[:], ALU.mult)
                k.op(k.dve, lambda: nc.vector.reduce_sum(out=ss[:, 0:1], in_=s_[:], axis=AX.X), [ss], [s_])
                k.actv(ss, ss[:, 1:2], ss, ss[:, 0:1], AF.Ln, bias=c.epsb[:, 0:1], scale=1.0 / D, extra=[c.epsb])
                k.actv(ss, ss[:, 2:3], ss, ss[:, 1:2], AF.Exp, scale=-0.5)
                k.stt(k.dve, s_, s_[:], x_, x_[:], ss[:, 2:3], fw, fw[:], ALU.mult, ALU.mult, extra=[ss])
                k.dma(out_d, out_d.ap()[i * 128:(i + 1) * 128, :], s_, s_[:], q=k.pool)

    env = dict(locals())
    layer0(k, c, env)
    if stop_after in ("proj0", "ml_0", "ml_1", "ml_2", "ml_3", "ml_4", "ml_5", "ml_5a", "ml_5b", "ml_a", "ml_b", "mlstm", "s5", "mix0"):
        k.finish([]); k.close(); return nc, ["xs"]
    ffn_phase(0, list(range(NT)))
    if stop_after == "ffn0":
        k.finish([]); k.close(); return nc, ["xs"]
    layer1(k, c, env)
    if stop_after == "mix1":
        k.finish([]); k.close(); return nc, ["xs"]
    ffn_phase(1, list(range(2, NT)))
    final_phase()
    k.finish([out_d])
    k.close()
    return nc, ["out"]


class _View:
    def __init__(self, obj, flat):
        self.obj, self.flat = obj, flat

    @property
    def space(self):
        return self.obj.space

    @property
    def w(self):
        return self.obj.w

    @property
    def r(self):
        return self.obj.r

    def __getitem__(self, idx):
        return self.flat[idx]


LAYER_FUNCS = []


class E:
    def __init__(self, d):
        self.__dict__.update(d)


ORDB = [3, 2, 1, 0] + list(range(35, 3, -1))
ORDB8 = list(range(31, -1, -1)) + list(range(287, 31, -1))


def layer0(k, c, env):
    e = E(env)
    nc = k.nc
    xs = e.xs
    qT_d = k.dram("qT_d", [512, T], BF16); kT_d = k.dram("kT_d", [512, T], BF16)
    ktok_d = k.dram("ktok_d", [T, 512], BF16); v_d = k.dram("v_d", [T, 512], BF16)
    o_d = k.dram("o_d", [T, 512]); g_d = k.dram("g_d", [T, 16]); u_d = k.dram("u_d", [T, 512])
    hA_d = k.dram("hA_d", [2, T, 512]); yS_d = k.dram("yS_d", [T, 512])
    c.l0 = dict(qT=qT_d, kT=kT_d, ktok=ktok_d, v=v_d, o=o_d, g=g_d, u=u_d, hA=hA_d, yS=yS_d)

    with k.scope():
        c.xt = [k.sb("xt%d" % i, [128, D]) for i in range(2)]
        c.sq = [k.sb("sq%d" % i, [128, D]) for i in range(2)]; c.ss = [k.sb("ss%d" % i, [128, 4]) for i in range(2)]
        c.pT = [k.ps("pT%d" % i, [128, 4, 128]) for i in range(2)]
        c.wstage = [k.sb("wst%d" % i, [128, 4096]) for i in range(2)]
        hT = k.sb("hT", [128, 8, T], BF16)
        norm_mod(k, c, xs, list(range(NT)), e.ab_fn(0, 0), hT)
        wb = k.sb("wb", [128, 8, 2576], BF16)
        for cb in range(0, 2576, 512):
            n = min(512, 2576 - cb)
            load_w_bf16(k, c, wb, wb[:, :, cb:cb + n], e.ev_w_in, e.ev_w_in.ap()[:, cb:cb + n], n,
                        eng=(k.pool if (cb // 512) % 2 else k.dve))
        pp = [k.ps("pp%d" % i, [128, 512]) for i in range(4)]
        fst = [k.sb("fst%d" % i, [128, T], BF16) for i in range(2)]
        blocks = [(0, 512), (512, 512), (1024, 512), (1536, 512), (2048, 256)]
        it = 0
        for which, dst in ((0, qT_d), (1, kT_d)):
            for h in range(4):
                st = fst[(which * 4 + h) % 2]
                col = which * 512 + h * 128
                for (t0, tn) in blocks:
                    p = pp[it % 4]; it += 1
                    for kc in range(8):
                        k.mm(p, p[:, 0:tn], wb, wb[:, kc, col:col + 128], hT, hT[:, kc, t0:t0 + tn], kc == 0, kc == 7)
                    k.actv(st, st[:, t0:t0 + tn], p, p[:, 0:tn], AF.Copy, scale=(1.0 if which == 0 else 1.0 / math.sqrt(128)))
                k.dma(dst, dst.ap()[h * 128:(h + 1) * 128, :], st, st[:], q=k.pool)
        tkb = [k.sb("tkb%d" % i, [128, 1024], BF16) for i in range(2)]
        tof = [k.sb("tof%d" % i, [128, 1040]) for i in range(2)]
        for t in range(NT):
            kb, of = tkb[t % 2], tof[t % 2]
            tok = slice(t * 128, (t + 1) * 128)
            for bi, (col, n) in enumerate(((512, 512), (1024, 512), (1536, 512), (2064, 512), (2048, 16))):
                p = pp[it % 4]; it += 1
                for kc in range(8):
                    k.mm(p, p[:, 0:n], hT, hT[:, kc, tok], wb, wb[:, kc, col:col + n], kc == 0, kc == 7)
                if bi == 0:
                    k.actv(kb, kb[:, 0:512], p, p[:, 0:512], AF.Copy, scale=1.0 / math.sqrt(128))
                elif bi == 1:
                    k.cp(k.dve, kb, kb[:, 512:1024], p, p[:, 0:512])
                elif bi == 2:
                    k.cp(k.act, of, of[:, 0:512], p, p[:, 0:512])
                elif bi == 3:
                    k.cp(k.dve, of, of[:, 512:1024], p, p[:, 0:512])
                else:
                    k.cp(k.act, of, of[:, 1024:1040], p, p[:, 0:16])
            k.dma(ktok_d, ktok_d.ap()[tok, :], kb, kb[:, 0:512], q=k.pool)
            k.dma(v_d, v_d.ap()[tok, :], kb, kb[:, 512:1024], q=k.pool)
            k.dma(o_d, o_d.ap()[tok, :], of, of[:, 0:512], q=k.pool)
            k.dma(u_d, u_d.ap()[tok, :], of, of[:, 512:1024], q=k.pool)
            k.dma(g_d, g_d.ap()[tok, :], of, of[:, 1024:1040], q=k.pool)
    if c.stop == "proj0":
        return
    mlstm_phase(k, c, e)
    if c.stop in ("mlstm", "ml_0", "ml_1", "ml_2", "ml_3", "ml_4", "ml_5", "ml_5a", "ml_5b", "ml_a", "ml_b"):
        return
    s5_phase(k, c, e)
    if c.stop == "s5":
        return
    finish0_phase(k, c, e)


def mlstm_phase(k, c, e):
    nc = k.nc
    L = c.l0
    with k.scope():
        qT = k.sb("qT", [128, 4, T], BF16); kT = k.sb("kT", [128, 4, T], BF16)
        for h in range(4):
            k.dma(qT, qT[:, h, :], L['qT'], L['qT'].ap()[h * 128:(h + 1) * 128, :])
            k.dma(kT, kT[:, h, :], L['kT'], L['kT'].ap()[h * 128:(h + 1) * 128, :])
        ktok = k.sb("ktok", [64, 36, 512], BF16)
        v1 = k.sb("v1", [64, 36, 4, 132], BF16)
        for c0 in range(0, 36, 6):
            k.dma(ktok, ktok[:, c0:c0 + 6, :], L['ktok'], L['ktok'].ap()[c0 * 64:(c0 + 6) * 64, :].rearrange("(c l) n -> l c n", l=64))
            for h in range(4):
                k.dma(v1, v1[:, c0:c0 + 6, h, 0:128], L['v'],
                      L['v'].ap()[c0 * 64:(c0 + 6) * 64, h * 128:(h + 1) * 128].rearrange("(c l) n -> l c n", l=64))
        if c.stop == "ml_0":
            return
        k.memset(k.dve, v1, v1[:, :, :, 128:132], 1.0)
        if c.stop == "ml_1":
            return
        g = k.sb("g", [64, 36, 16])
        for c0 in range(0, 36, 6):
            k.dma(g, g[:, c0:c0 + 6, :], L['g'], L['g'].ap()[c0 * 64:(c0 + 6) * 64, :].rearrange("(c l) n -> l c n", l=64))
        fb = k.sb("fb", [64, 8]); ib = k.sb("ib", [64, 8])
        k.dma(fb, fb[:], e.ev_fb, e.ev_fb.ap().partition_broadcast(64).rearrange("p o n -> p (o n)"))
        k.dma(ib, ib[:], e.ev_ib, e.ev_ib.ap().partition_broadcast(64).rearrange("p o n -> p (o n)"))
        tri = k.sb("tri", [64, 2, 64]); ones = k.sb("ones", [64, 128])
        k.dma(tri, tri[:], e.ctri, e.ctri.ap().rearrange("r s l -> s r l"))
        k.memset(k.dve, ones, ones[:], 1.0)
        if c.stop == "ml_2":
            return
        z = k.sb("z", [64, 2, 36, 4]); nlf = k.sb("nlf", [64, 2, 36, 4]); ig = k.sb("ig", [64, 2, 36, 4])
        A = k.sb("A", [64, 2, 36, 4]); Bk = k.sb("Bk", [64, 2, 36, 4]); gdec = k.sb("gdec", [128, 2, 36, 4])
        for d in range(2):
            k.tt(k.dve, z, z[:, d], g, g[:, :, 8 + 4 * d:12 + 4 * d], fb, bc(fb[:, None, 4 * d:4 * d + 4], [64, 36, 4]), ALU.add)
            k.tt(k.dve, ig, ig[:, d], g, g[:, :, 4 * d:4 * d + 4], ib, bc(ib[:, None, 4 * d:4 * d + 4], [64, 36, 4]), ALU.add)
        k.actv(z, z[:], z, z[:], AF.Exp, scale=-1.0)
        k.actv(nlf, nlf[:], z, z[:], AF.Ln, bias=1.0)
        if c.stop == "ml_3":
            return
        with k.scope():
            pF = k.ps("pF", [64, 2, 144]); pG = k.ps("pG", [128, 288])
            for d in range(2):
                k.mm(pF, pF[:, d, :], tri, tri[:, d, :], nlf, nlf[:, d].rearrange("p c h -> p (c h)"))
            k.mm(pG, pG[:], ones, ones[:], nlf, nlf[:].rearrange("p d c h -> p (d c h)"))
            if c.stop == "ml_4":
                k.cp(k.dve, A, A[:].rearrange("p d c h -> p d (c h)"), pF, pF[:])
                k.cp(k.dve, gdec, gdec[:].rearrange("p d c h -> p (d c h)"), pG, pG[:])
            Af = A[:].rearrange("p d c h -> p d (c h)"); Bf = Bk[:].rearrange("p d c h -> p d (c h)")
            if c.stop != "ml_4":
                if c.stop != "ml_5b":
                    k.actv(A, Af, pF, pF[:], AF.Exp, scale=-1.0)
                if c.stop != "ml_5a":
                    k.tt(k.dve, Bk, Bf, ig, ig[:].rearrange("p d c h -> p d (c h)"), pF, pF[:], ALU.add)
                if c.stop not in ("ml_5", "ml_5a", "ml_5b"):
                    k.actv(Bk, Bk[:], Bk, Bk[:], AF.Exp)
                    k.actv(gdec, gdec[:].rearrange("p d c h -> p (d c h)"), pG, pG[:], AF.Exp, scale=-1.0)
        if c.stop in ("ml_a", "ml_4", "ml_5", "ml_5a", "ml_5b"):
            return
        C32 = [k.sb("C32_%d" % d, [128, 4, 132]) for d in range(2)]
        Cb = [k.sb("Cb_%d" % d, [128, 4, 132], BF16) for d in range(2)]
        for d in range(2):
            k.memset(k.dve, C32[d], C32[d][:], 0.0)
            k.memset(k.dve, Cb[d], Cb[d][:], 0.0)
        pS = [k.ps("pS%d" % d, [64, 4, 64]) for d in range(2)]
        pN = [[k.ps("pN%d_%d" % (d, i), [64, 2, 256]) for i in range(2)] for d in range(2)]
        pC = [k.ps("pC%d" % i, [128, 2, 256]) for i in range(2)]
        kt = [k.sb("kt%d" % d, [64, 4, 128], BF16) for d in range(2)]
        MB = [k.sb("MB%d" % d, [64, 4, 64]) for d in range(2)]
        Pt = [k.sb("Pt%d" % d, [64, 4, 64], BF16) for d in range(2)]
        sm = [k.sb("sm%d" % d, [64, 4, 4]) for d in range(2)]
        ho = [k.sb("ho%d" % d, [64, 4, 128]) for d in range(2)]
        for step in range(36):
            for d in range(2):
                ch = step if d == 0 else ORDB[step]
                tok = slice(ch * 64, (ch + 1) * 64)
                Bs = Bk[:, d, ch, :]
                As = A[:, d, ch, :]
                k.tt(k.dve, kt[d], kt[d][:], ktok, ktok[:, ch, :].rearrange("p (h e) -> p h e", h=4),
                     Bk, bc(Bs[:, :, None], [64, 4, 128]), ALU.mult)
                k.tt(k.dve, MB[d], MB[d][:], tri, bc(tri[:, d:d + 1, :], [64, 4, 64]), Bk, bc(Bs[:, :, None], [64, 4, 64]), ALU.mult)
                for h in range(4):
                    k.mm(pS[d], pS[d][:, h, :], kT, kT[:, h, tok], qT, qT[:, h, tok])
                k.tt(k.dve, Pt[d], Pt[d][:], pS[d], pS[d][:], MB[d], MB[d][:], ALU.mult)
                for h in range(4):
                    pn = pN[d][h // 2]
                    k.mm(pn, pn[:, h % 2, 0:132], Pt[d], Pt[d][:, h, :], v1, v1[:, ch, h, :], True, False)
                    k.mm(pn, pn[:, h % 2, 0:132], qT, qT[:, h, tok], Cb[d], Cb[d][:, h, :], False, True)
                s_ = sm[d]
                for i in range(2):
                    pn = pN[d][i]
                    k.tt(k.dve, s_, s_[:, 2 * i:2 * i + 2, 0], pn, pn[:, :, 128], A, As[:, 2 * i:2 * i + 2], ALU.mult)
                k.stt(k.dve, s_, s_[:, :, 1], s_, s_[:, :, 0], -1.0, s_, s_[:, :, 0], ALU.mult, ALU.max)
                k.ts(k.dve, s_, s_[:, :, 1], s_, s_[:, :, 1], 1.0, None, ALU.max)
                k.op(k.dve, lambda: nc.vector.reciprocal(out=s_[:, :, 2], in_=s_[:, :, 1]), [s_], [s_])
                k.tt(k.dve, s_, s_[:, :, 3], s_, s_[:, :, 2], A, As, ALU.mult)
                for i in range(2):
                    pn = pN[d][i]
                    k.tt(k.dve, ho[d], ho[d][:, 2 * i:2 * i + 2, :], pn, pn[:, :, 0:128],
                         s_, bc(s_[:, 2 * i:2 * i + 2, 3:4], [64, 2, 128]), ALU.mult)
                k.dma(L['hA'], L['hA'].ap()[d, tok, :], ho[d], ho[d][:].rearrange("p h e -> p (h e)"), q=k.pool)
                for i in range(2):
                    pc_ = pC[i]
                    for hh in range(2):
                        h = 2 * i + hh
                        k.mm(pc_, pc_[:, hh, 0:132], kt[d], kt[d][:, h, :], v1, v1[:, ch, h, :])
                    k.tt(k.dve, C32[d], C32[d][:, 2 * i:2 * i + 2, :], pc_, pc_[:, :, 0:132], C32[d], C32[d][:, 2 * i:2 * i + 2, :], ALU.add)
                k.tt(k.dve, C32[d], C32[d][:], C32[d], C32[d][:], gdec, bc(gdec[:, d, ch, :][:, :, None], [128, 4, 132]), ALU.mult)
                k.cp(k.act, Cb[d], Cb[d][:], C32[d], C32[d][:])
            if c.stop == "ml_b" and step == 0:
                return


def s5_phase(k, c, e):
    nc = k.nc
    L = c.l0
    TWO_PI = 2 * PI
    with k.scope():
        Mw = k.sb("Mw", [128, 64, 128], BF16)
        WT = [k.sb("WT%d" % i, [128, 64, 64], BF16) for i in range(2)]
        RC = [k.sb("RC%d" % i, [64, 64, 128], BF16) for i in range(2)]
        AR2 = k.sb("AR2", [64, 2, 64]); AI2 = k.sb("AI2", [64, 2, 64])
        with k.scope():
            lre = k.sb("lre", [64, 64]); lim = k.sb("lim", [64, 64]); dt = k.sb("dt", [64, 64])
            c.ltst = [k.sb("ltst%d" % i, [64, 128]) for i in range(2)]
            c.ltps = [k.ps("ltps%d" % i, [128, 64]) for i in range(2)]
            c.lt_i = 0
            load_T(k, c, lre, lre[:], e.ev_lre, e.ev_lre.ap().rearrange("r g n -> (r g) n"), 64, ncols=64)
            load_T(k, c, lim, lim[:], e.ev_lim, e.ev_lim.ap().rearrange("r g n -> (r g) n"), 64, ncols=64)
            k.dma(dt, dt[:], e.ev_ldt, e.ev_ldt.ap().partition_broadcast(64).rearrange("p o n -> p (o n)"))
            k.actv(dt, dt[:], dt, dt[:], AF.Exp)
            ldr = k.sb("ldr", [64, 64]); ang = k.sb("ang", [64, 64])
            k.tt(k.dve, ldr, ldr[:], lre, lre[:], dt, dt[:], ALU.mult)
            k.tt(k.dve, ang, ang[:], lim, lim[:], dt, dt[:], ALU.mult)
            mg = k.sb("mg", [64, 16, 64]); sn = k.sb("sn", [64, 9, 64]); cs = k.sb("cs", [64, 9, 64])
            for ti, tau in enumerate(range(-7, 9)):
                k.actv(mg, mg[:, ti, :], ldr, ldr[:], AF.Exp, scale=float(tau))
            k.memset(k.dve, sn, sn[:, 0, :], 0.0); k.memset(k.dve, cs, cs[:, 0, :], 1.0)
            k.actv(sn, sn[:, 1, :], ang, ang[:], AF.Sin, scale=1.0 / 16)
            k.actv(cs, cs[:, 1, :], ang, ang[:], AF.Sin, bias=c.epsb[0:64, 1:2], scale=1.0 / 16, extra=[c.epsb])
            q1 = k.sb("q1", [64, 64]); q2 = k.sb("q2", [64, 64])
            for _ in range(4):
                k.tt(k.dve, q1, q1[:], sn, sn[:, 1, :], cs, cs[:, 1, :], ALU.mult)
                k.tt(k.dve, q2, q2[:], sn, sn[:, 1, :], sn, sn[:, 1, :], ALU.mult)
                k.ts(k.dve, sn, sn[:, 1, :], q1, q1[:], 2.0, None, ALU.mult)
                k.ts(k.dve, cs, cs[:, 1, :], q2, q2[:], -2.0, 1.0, ALU.mult, ALU.add)
            for tau in range(2, 9):
                k.tt(k.dve, q1, q1[:], cs, cs[:, tau - 1, :], cs, cs[:, 1, :], ALU.mult)
                k.tt(k.dve, q2, q2[:], sn, sn[:, tau - 1, :], sn, sn[:, 1, :], ALU.mult)
                k.tt(k.dve, cs, cs[:, tau, :], q1, q1[:], q2, q2[:], ALU.subtract)
                k.tt(k.dve, q1, q1[:], sn, sn[:, tau - 1, :], cs, cs[:, 1, :], ALU.mult)
                k.tt(k.dve, q2, q2[:], cs, cs[:, tau - 1, :], sn, sn[:, 1, :], ALU.mult)
                k.tt(k.dve, sn, sn[:, tau, :], q1, q1[:], q2, q2[:], ALU.add)
            pwr = k.sb("pwr", [64, 16, 64]); pwi = k.sb("pwi", [64, 16, 64])
            for ti, tau in enumerate(range(-7, 9)):
                at = abs(tau)
                k.tt(k.dve, pwr, pwr[:, ti, :], mg, mg[:, ti, :], cs, cs[:, at, :], ALU.mult)
                if tau >= 0:
                    k.tt(k.dve, pwi, pwi[:, ti, :], mg, mg[:, ti, :], sn, sn[:, at, :], ALU.mult)
                else:
                    k.stt(k.dve, pwi, pwi[:, ti, :], mg, mg[:, ti, :], -1.0, sn, sn[:, at, :], ALU.mult, ALU.mult)
            for s_ in range(2):
                k.cp(k.dve, AR2, AR2[:, s_, :], pwr, pwr[:, 15, :])
            k.ts(k.dve, AI2, AI2[:, 0, :], pwi, pwi[:, 15, :], -1.0, None, ALU.mult)
            k.cp(k.dve, AI2, AI2[:, 1, :], pwi, pwi[:, 15, :])
            nr = k.sb("nr", [64, 64]); den = k.sb("den", [64, 64]); t1 = k.sb("t1", [64, 64]); t2 = k.sb("t2", [64, 64])
            cor = k.sb("cor", [64, 64]); coi = k.sb("coi", [64, 64])
            k.ts(k.dve, nr, nr[:], pwr, pwr[:, 8, :], -1.0, None, ALU.add)
            k.tt(k.dve, den, den[:], lre, lre[:], lre, lre[:], ALU.mult)
            k.tt(k.dve, t1, t1[:], lim, lim[:], lim, lim[:], ALU.mult)
            k.tt(k.dve, den, den[:], den, den[:], t1, t1[:], ALU.add)
            k.op(k.dve, lambda: nc.vector.reciprocal(out=den[:], in_=den[:]), [den], [den])
            k.tt(k.dve, t1, t1[:], nr, nr[:], lre, lre[:], ALU.mult)
            k.tt(k.dve, t2, t2[:], pwi, pwi[:, 8, :], lim, lim[:], ALU.mult)
            k.tt(k.dve, t1, t1[:], t1, t1[:], t2, t2[:], ALU.add)
            k.tt(k.dve, cor, cor[:], t1, t1[:], den, den[:], ALU.mult)
            k.tt(k.dve, t1, t1[:], pwi, pwi[:, 8, :], lre, lre[:], ALU.mult)
            k.tt(k.dve, t2, t2[:], nr, nr[:], lim, lim[:], ALU.mult)
            k.tt(k.dve, t1, t1[:], t1, t1[:], t2, t2[:], ALU.subtract)
            k.tt(k.dve, coi, coi[:], t1, t1[:], den, den[:], ALU.mult)
            bre = k.sb("bre", [64, 64, 16]); bim = k.sb("bim", [64, 64, 16])
            for r in range(2):
                for g0 in range(0, 32, 4):
                    k.dma(bre, bre[:, r * 32 + g0:r * 32 + g0 + 4, :], e.ev_bre, e.ev_bre.ap()[r, g0:g0 + 4].rearrange("g n p -> n g p"))
                    k.dma(bim, bim[:, r * 32 + g0:r * 32 + g0 + 4, :], e.ev_bim, e.ev_bim.ap()[r, g0:g0 + 4].rearrange("g n p -> n g p"))
            bbr = k.sb("bbr", [64, 64, 16]); bbi = k.sb("bbi", [64, 64, 16])
            u1 = k.sb("u1", [64, 64, 16]); u2 = k.sb("u2", [64, 64, 16])
            corb = bc(cor[:, :, None], [64, 64, 16]); coib = bc(coi[:, :, None], [64, 64, 16])
            k.tt(k.dve, u1, u1[:], bre, bre[:], cor, corb, ALU.mult)
            k.tt(k.dve, u2, u2[:], bim, bim[:], coi, coib, ALU.mult)
            k.tt(k.dve, bbr, bbr[:], u1, u1[:], u2, u2[:], ALU.subtract)
            k.tt(k.dve, u1, u1[:], bim, bim[:], cor, corb, ALU.mult)
            k.tt(k.dve, u2, u2[:], bre, bre[:], coi, coib, ALU.mult)
            k.tt(k.dve, bbi, bbi[:], u1, u1[:], u2, u2[:], ALU.add)
            cTr = k.sb("cTr", [64, 64, 16]); cTi = k.sb("cTi", [64, 64, 16])
            cst = k.sb("cst", [128, 8, 64])
            pCt = [k.ps("pCt%d" % i, [64, 4, 128]) for i in range(2)]
            for src, dst in ((e.ev_cre, cTr), (e.ev_cim, cTi)):
                for t0 in range(0, 8, 2):
                    k.dma(cst, cst[:, t0:t0 + 2, :], src, src.ap()[t0 * 128:(t0 + 2) * 128, :].rearrange("(t p) n -> p t n", p=128))
                for t in range(8):
                    p = pCt[t // 4]
                    k.tr(p, p[:, t % 4, :], cst, cst[:, t, :], c.id32, c.id32[:])
                for hf in range(2):
                    k.cp(k.act, dst, dst[:, hf * 32:(hf + 1) * 32, :].rearrange("n g p -> n (g p)"),
                         pCt[hf], pCt[hf][:].rearrange("n t m -> n (t m)"))
            maskM = k.sb("maskM", [128, 2, 128])
            k.dma(maskM, maskM[:], e.cmaskM, e.cmaskM.ap().rearrange("r a b -> a r b"))
            w1_ = k.sb("w1_", [64, 32, 16]); w2_ = k.sb("w2_", [64, 32, 16])

            def cmul(r, powf, sr, si, dr, dr_ap, di, di_ap, neg_im=False):
                rs = slice(r * 32, (r + 1) * 32)
                for i in range(8):
                    ti = powf(i) + 7
                    pr = bc(pwr[:, ti, rs][:, :, None], [64, 32, 16]); pi_ = bc(pwi[:, ti, rs][:, :, None], [64, 32, 16])
                    k.tt(k.dve, w1_, w1_[:], sr, sr[:, rs, :], pwr, pr, ALU.mult)
                    k.tt(k.pool, w2_, w2_[:], si, si[:, rs, :], pwi, pi_, ALU.mult)
                    k.tt(k.dve, dr, dr_ap(i), w1_, w1_[:], w2_, w2_[:], ALU.subtract)
                    k.tt(k.dve, w1_, w1_[:], si, si[:, rs, :], pwr, pr, ALU.mult)
                    k.tt(k.pool, w2_, w2_[:], sr, sr[:, rs, :], pwi, pi_, ALU.mult)
                    if neg_im:
                        k.stt(k.dve, di, di_ap(i), w1_, w1_[:], -1.0, w2_, w2_[:], ALU.mult, ALU.subtract)
                    else:
                        k.tt(k.dve, di, di_ap(i), w1_, w1_[:], w2_, w2_[:], ALU.add)

            EBr = k.sb("EBr", [64, 32, 8, 16]); EBi = k.sb("EBi", [64, 32, 8, 16])
            ECr = k.sb("ECr", [64, 32, 8, 16]); ECi = k.sb("ECi", [64, 32, 8, 16])
            pM = [k.ps("pM%d" % i, [128, 4, 128]) for i in range(2)]
            pW = [k.ps("pW%d" % i, [128, 8, 64]) for i in range(2)]
            for r in range(2):
                sig = (lambda i: i) if r == 0 else (lambda i: 7 - i)
                rs = slice(r * 32, (r + 1) * 32)
                cmul(r, lambda i: -sig(i), bbr, bbi, EBr, lambda i: EBr[:, :, i, :], EBi, lambda i: EBi[:, :, i, :])
                cmul(r, lambda i: sig(i), cTr, cTi, ECr, lambda i: ECr[:, :, i, :], ECi, lambda i: ECi[:, :, i, :], neg_im=True)
                for g0 in range(0, 32, 4):
                    p = pM[(g0 // 4) % 2]
                    for gg in range(4):
                        g = g0 + gg
                        k.mm(p, p[:, gg, :], EBr, EBr[:, g].rearrange("n i p -> n (i p)"), ECr, ECr[:, g].rearrange("n i p -> n (i p)"), True, False)
                        k.mm(p, p[:, gg, :], EBi, EBi[:, g].rearrange("n i p -> n (i p)"), ECi, ECi[:, g].rearrange("n i p -> n (i p)"), False, True)
                    k.tt(k.dve, Mw, Mw[:, r * 32 + g0:r * 32 + g0 + 4, :], p, p[:], maskM, bc(maskM[:, r:r + 1, :], [128, 4, 128]), ALU.mult)
                cmul(r, lambda i: sig(i) + 1, cTr, cTi,
                     RC[0], lambda i: RC[0][:, rs, :].rearrange("n g (j p) -> n g j p", j=8)[:, :, i, :],
                     RC[1], lambda i: RC[1][:, rs, :].rearrange("n g (j p) -> n g j p", j=8)[:, :, i, :], neg_im=True)
                cmul(r, lambda i: 7 - sig(i), bbr, bbi, ECr, lambda i: ECr[:, :, i, :], ECi, lambda i: ECi[:, :, i, :])
                for comp, src in enumerate((ECr, ECi)):
                    for g0 in range(0, 32, 8):
                        p = pW[(g0 // 8) % 2]
                        for gg in range(8):
                            k.tr(p, p[:, gg, :], src, src[:, g0 + gg].rearrange("n i p -> n (i p)"), c.id32, c.id32[0:64, 0:64])
                        k.cp(k.act, WT[comp], WT[comp][:, r * 32 + g0:r * 32 + g0 + 8, :], p, p[:])
        X = k.sb("X", [128, 32, 288], BF16)
        SaL = [k.sb("Sa%d" % r_, [64, 2, 32, 290], BF16) for r_ in range(2)]
        with k.scope():
            u32 = k.sb("u32", [128, 8, 512]); u16 = k.sb("u16", [128, 32, 128], BF16)
            pX = [k.ps("pX%d" % i, [128, 8, 128], BF16) for i in range(2)]
            it = 0
            for ct, (c0, n) in enumerate(((0, 128), (128, 128), (256, 32))):
                k.dma(u32, u32[0:n], L['u'], L['u'].ap()[8 * c0:8 * (c0 + n), :].rearrange("(c i) ch -> c i ch", i=8))
                k.cp(k.dve, u16, u16[0:n].rearrange("c g (i p) -> c g i p", i=8), u32, u32[0:n].rearrange("c i (g p) -> c g i p", g=32))
                for g0 in range(0, 32, 8):
                    p = pX[it % 2]; it += 1
                    for gg in range(8):
                        g = g0 + gg
                        k.tr(p, p[:, gg, 0:n], u16, u16[0:n, g, :], c.idb, c.idb[0:n, 0:n])
                    k.cp(k.act, X, X[:, g0:g0 + 8, c0:c0 + n], p, p[:, :, 0:n])
        with k.scope():
            pB = [k.ps("pB%d" % i, [64, 288]) for i in range(4)]
            it = 0
            for rg in range(64):
                for comp in range(2):
                    p = pB[it % 4]; it += 1
                    k.mm(p, p[:], WT[comp], WT[comp][:, rg, :], X, X[:, rg % 32, :])
                    off = 0 if rg < 32 else 1
                    Sa = SaL[rg // 32]
                    k.cp(k.act if it % 2 else k.dve, Sa, Sa[:, comp, rg % 32, off:off + 288], p, p[:])
        R3 = [k.sb("R3_%d" % r_, [64, 3, 32]) for r_ in range(2)]
        P1 = [k.sb("P1_%d" % r_, [64, 2, 32]) for r_ in range(2)]; P2 = [k.sb("P2_%d" % r_, [64, 2, 32]) for r_ in range(2)]
        for r_ in range(2):
            k.memset(k.dve, R3[r_], R3[r_][:], 0.0)
        for step in range(288):
            cols = (step, ORDB8[step] + 1)
            bvs = [SaL[r_][:, :, :, cols[r_]] for r_ in range(2)]
            rsl = [slice(0, 32), slice(32, 64)]
            for r_ in range(2):
                k.tt(k.dve, P1[r_], P1[r_][:], AR2, AR2[:, :, rsl[r_]], R3[r_], R3[r_][:, 1:3, :], ALU.mult)
            for r_ in range(2):
                k.tt(k.dve, P2[r_], P2[r_][:], AI2, AI2[:, :, rsl[r_]], R3[r_], R3[r_][:, 0:2, :], ALU.mult)
            for r_ in range(2):
                k.tt(k.dve, P1[r_], P1[r_][:], P1[r_], P1[r_][:], P2[r_], P2[r_][:], ALU.add)
            for r_ in range(2):
                k.tt(k.dve, R3[r_], R3[r_][:, 1:3, :], P1[r_], P1[r_][:], SaL[r_], bvs[r_], ALU.add)
            for r_ in range(2):
                k.cp(k.dve, R3[r_], R3[r_][:, 0, :], R3[r_], R3[r_][:, 2, :])
            for r_ in range(2):
                k.cp(k.dve, SaL[r_], bvs[r_], R3[r_], R3[r_][:, 1:3, :])
        k.cp(k.dve, SaL[1], SaL[1][:, :, :, 0:1], SaL[1], SaL[1][:, :, :, 1:2])
        with k.scope():
            pY = [k.ps("pY%d" % i, [128, 288]) for i in range(2)]
            pZ = [k.ps("pZ%d" % i, [128, 4, 128]) for i in range(2)]
            Yq = k.sb("Yq", [128, 8, 288]); Y2q = [k.sb("Y2q%d" % i, [128, 8, 128]) for i in range(2)]
            it = 0; iz = 0; iy = 0
            for q in range(4):
                for gl in range(8):
                    g = q * 8 + gl
                    p = pY[it % 2]; it += 1
                    k.mm(p, p[:], Mw, Mw[:, g, :], X, X[:, g, :], True, False)
                    k.mm(p, p[:], Mw, Mw[:, 32 + g, :], X, X[:, g, :], False, False)
                    for comp in range(2):
                        k.mm(p, p[:, 1:288], RC[comp], RC[comp][:, g, :], SaL[0], SaL[0][:, comp, g, 0:287], False, False)
                    rg = 32 + g
                    for comp in range(2):
                        k.mm(p, p[:, 0:31], RC[comp], RC[comp][:, rg, :], SaL[1], SaL[1][:, comp, g, 2:33], False, False)
                        k.mm(p, p[:, 32:287], RC[comp], RC[comp][:, rg, :], SaL[1], SaL[1][:, comp, g, 34:289], False, False)
                        k.mm(p, p[:, 287:288], RC[comp], RC[comp][:, rg, :], SaL[1], SaL[1][:, comp, g, 0:1], False, comp == 1)
                    k.cp(k.act, Yq, Yq[:, gl, :], p, p[:])
                for ct, (c0, n) in enumerate(((0, 128), (128, 128), (256, 32))):
                    y2 = Y2q[iy % 2]; iy += 1
                    for gl in range(8):
                        if gl % 4 == 0:
                            pz = pZ[iz % 2]; iz += 1
                        k.tr(pz, pz[0:n, gl % 4, :], Yq, Yq[:, gl, c0:c0 + n], c.id32, c.id32[:])
                        k.cp(k.dve if gl % 2 else k.act, y2, y2[0:n, :, gl * 16:(gl + 1) * 16],
                             pz, pz[0:n, gl % 4, :].rearrange("c (j p) -> c j p", j=8))
                    for jh in range(2):
                        k.dma(L['yS'], L['yS'].ap()[8 * c0:8 * (c0 + n), q * 128:(q + 1) * 128].rearrange("(c j) ch -> c j ch", j=8)[:, jh * 4:(jh + 1) * 4, :],
                              y2, y2[0:n, jh * 4:(jh + 1) * 4, :], q=k.pool)


def finish0_phase(k, c, e):
    nc = k.nc
    L = c.l0
    xs = e.xs
    with k.scope():
        c.wstage = [k.sb("wst%d" % i, [128, 4096]) for i in range(2)]
        wglu = k.sb("wglu", [128, 4, 1024], BF16); wout = k.sb("wout", [128, 8, 1024], BF16)
        for cb in range(2):
            load_w_bf16(k, c, wglu, wglu[:, :, cb * 512:(cb + 1) * 512], e.ev_wglu, e.ev_wglu.ap()[:, cb * 512:(cb + 1) * 512], 512, kchunks=4)
            load_w_bf16(k, c, wout, wout[:, :, cb * 512:(cb + 1) * 512], e.ev_wout, e.ev_wout.ap()[:, cb * 512:(cb + 1) * 512], 512)
        hwb = k.sb("hwb", [128, 512]); dsk = k.sb("dsk", [128, 512])
        k.dma(hwb, hwb[:], e.ev_hw, e.ev_hw.ap().partition_broadcast(128).rearrange("p o n -> p (o n)"))
        k.dma(dsk, dsk[:], e.ev_d, e.ev_d.ap().partition_broadcast(128).rearrange("p o n -> p (o n)"))
        gate = [k.sb("gate%d" % w, [128, D]) for w in range(2)]
        e.load_gate(gate[0], 0, 0, 2); e.load_gate(gate[1], 0, 1, 2)
        hA = [k.sb("hA%d" % i, [128, 2, 512]) for i in range(2)]
        ot = [k.sb("ot%d" % i, [128, 512]) for i in range(2)]
        ut = [k.sb("ut%d" % i, [128, 512]) for i in range(2)]
        yt = [k.sb("yt%d" % i, [128, 512]) for i in range(2)]
        xt = [k.sb("xt%d" % i, [128, D]) for i in range(2)]
        w1s = [k.sb("fw1_%d" % i, [128, 512]) for i in range(2)]; w2s = [k.sb("fw2_%d" % i, [128, 512]) for i in range(2)]
        sts = [k.sb("fst_%d" % i, [128, 8]) for i in range(2)]
        cats = [k.sb("cat%d" % i, [128, D], BF16) for i in range(2)]; ybbs = [k.sb("ybb%d" % i, [128, 512], BF16) for i in range(2)]
        ybTs = [k.sb("ybT%d" % i, [128, 4, 128], BF16) for i in range(2)]; catTs = [k.sb("catT%d" % i, [128, 8, 128], BF16) for i in range(2)]
        pTbs = [k.ps("pTb%d" % i, [128, 8, 128], BF16) for i in range(2)]
        pG = [k.ps("pGl%d" % i, [128, 512]) for i in range(2)]
        pO = [k.ps("pO%d" % i, [128, 512]) for i in range(2)]
        yos = [k.sb("yo%d" % i, [128, D]) for i in range(2)]
        for t in range(NT):
            tok = slice(t * 128, (t + 1) * 128)
            h_, o_, u_, y_, x_ = hA[t % 2], ot[t % 2], ut[t % 2], yt[t % 2], xt[t % 2]
            w1, w2, st, cat, ybb, ybT, catT, pTb, yo = w1s[t % 2], w2s[t % 2], sts[t % 2], cats[t % 2], ybbs[t % 2], ybTs[t % 2], catTs[t % 2], pTbs[t % 2], yos[t % 2]
            k.dma(h_, h_[:], L['hA'], L['hA'].ap()[:, tok, :].rearrange("r t n -> t r n"))
            k.dma(o_, o_[:], L['o'], L['o'].ap()[tok, :])
            k.dma(u_, u_[:], L['u'], L['u'].ap()[tok, :])
            k.dma(y_, y_[:], L['yS'], L['yS'].ap()[tok, :])
            k.dma(x_, x_[:], xs, xs.ap()[tok, :])
            k.tt(k.dve, w1, w1[:], h_, h_[:, 0, :], h_, h_[:, 1, :], ALU.add)
            k.tt(k.dve, w2, w2[:], w1, w1[:], w1, w1[:], ALU.mult)
            k.op(k.dve, lambda: nc.vector.reduce_sum(out=st[:, 0:4], in_=w2[:].rearrange("p (h e) -> p h e", h=4), axis=AX.X), [st], [w2])
            k.actv(st, st[:, 0:4], st, st[:, 0:4], AF.Ln, bias=c.epsb[:, 0:1], scale=1.0 / 128, extra=[c.epsb])
            k.actv(st, st[:, 4:8], st, st[:, 0:4], AF.Exp, scale=-0.5)
            k.tt(k.dve, w1, w1[:].rearrange("p (h e) -> p h e", h=4), w1, w1[:].rearrange("p (h e) -> p h e", h=4),
                 st, bc(st[:, 4:8][:, :, None], [128, 4, 128]), ALU.mult)
            k.tt(k.dve, w1, w1[:], w1, w1[:], hwb, hwb[:], ALU.mult)
            k.actv(o_, o_[:], o_, o_[:], AF.Sigmoid)
            k.tt(k.dve, cat, cat[:, 0:512], w1, w1[:], o_, o_[:], ALU.mult)
            k.tt(k.dve, w2, w2[:], u_, u_[:], dsk, dsk[:], ALU.mult)
            k.tt(k.dve, w2, w2[:], w2, w2[:], y_, y_[:], ALU.add)
            k.tt(k.pool, y_, y_[:], w2, w2[:], w2, w2[:], ALU.mult)
            k.ts(k.dve, y_, y_[:], y_, y_[:], 0.044715, 1.0, ALU.mult, ALU.add)
            k.tt(k.dve, y_, y_[:], y_, y_[:], w2, w2[:], ALU.mult)
            k.actv(y_, y_[:], y_, y_[:], AF.Sigmoid, scale=2.0 * math.sqrt(2.0 / PI))
            k.tt(k.dve, ybb, ybb[:], y_, y_[:], w2, w2[:], ALU.mult)
            for j in range(4):
                k.tr(pTb, pTb[:, j, :], ybb, ybb[:, j * 128:(j + 1) * 128], c.idb, c.idb[:])
            k.cp(k.act, ybT, ybT[:], pTb, pTb[:, 0:4, :])
            for cb in range(2):
                for kc in range(4):
                    k.mm(pG[cb], pG[cb][:], ybT, ybT[:, kc, :], wglu, wglu[:, kc, cb * 512:(cb + 1) * 512], kc == 0, kc == 3)
            k.actv(w2, w2[:], pG[1], pG[1][:], AF.Sigmoid)
            k.tt(k.dve, cat, cat[:, 512:1024], pG[0], pG[0][:], w2, w2[:], ALU.mult)
            for j in range(8):
                k.tr(pTb, pTb[:, j, :], cat, cat[:, j * 128:(j + 1) * 128], c.idb, c.idb[:])
            k.cp(k.act, catT, catT[:], pTb, pTb[:])
            g = gate[1 if t < 2 else 0]
            for cb in range(2):
                for kc in range(8):
                    k.mm(pO[cb], pO[cb][:], catT, catT[:, kc, :], wout, wout[:, kc, cb * 512:(cb + 1) * 512], kc == 0, kc == 7)
                k.tt(k.dve, yo, yo[:, cb * 512:(cb + 1) * 512], pO[cb], pO[cb][:], g, g[:, cb * 512:(cb + 1) * 512], ALU.mult)
            k.tt(k.pool, yo, yo[:], yo, yo[:], x_, x_[:], ALU.add)
            k.dma(xs, xs.ap()[tok, :], yo, yo[:], q=k.pool)


def layer1(k, c, env):
    e = E(env)
    nc = k.nc
    xs = e.xs
    z_d = k.dram("z_d", [T, D]); gt_d = k.dram("gt_d", [T, 32]); o1_d = k.dram("o1_d", [2, T, D])
    with k.scope():
        qT = k.sb("gqT", [128, 8, T], BF16); kT = k.sb("gkT", [128, 8, T], BF16); vT = k.sb("gvT", [128, 8, T], BF16)
        with k.scope():
            hT = k.sb("hT", [128, 8, T], BF16)
            with k.scope():
                c.xt = [k.sb("xt%d" % i, [128, D]) for i in range(2)]
                c.sq = [k.sb("sq%d" % i, [128, D]) for i in range(2)]; c.ss = [k.sb("ss%d" % i, [128, 4]) for i in range(2)]
                c.pT = [k.ps("pT%d" % i, [128, 4, 128]) for i in range(2)]
                norm_mod(k, c, xs, list(range(NT)), e.ab_fn(0, 1), hT)
            c.wstage = [k.sb("wst%d" % i, [128, 2048]) for i in range(2)]
            c.ltst = [k.sb("ltst%d" % i, [64, 128]) for i in range(2)]
            c.ltps = [k.ps("ltps%d" % i, [128, 64]) for i in range(2)]
            c.lt_i = 0
            cw = k.sb("cw", [128, 24, 9])
            for ci in range(24):
                load_T(k, c, cw, cw[:, ci, :], e.od_conv, e.od_conv.ap()[:, ci * 128:(ci + 1) * 128], 9)
            ones32 = k.sb("ones32", [128, 128]); k.memset(k.dve, ones32, ones32[:], 1.0)
            wch = [k.sb("wch%d" % i, [128, 8, 256], BF16) for i in range(2)]
            P32 = k.sb("P32", [128, T]); Cv = k.sb("Cv", [128, T]); S32 = P32; Q32 = Cv
            rs = k.sb("rs", [128, 512])
            pp = [k.ps("pp%d" % i, [128, 512]) for i in range(3)]
            blocks = [(0, 512), (512, 512), (1024, 512), (1536, 512), (2048, 256)]
            it = 0
            for ci in range(24):
                if ci % 2 == 0:
                    wc = wch[(ci // 2) % 2]
                    load_w_bf16(k, c, wc, wc[:], e.od_w_in, e.od_w_in.ap()[:, ci * 128:(ci + 2) * 128], 256)
                wo = (ci % 2) * 128
                for (t0, tn) in blocks:
                    p = pp[it % 3]; it += 1
                    for kc in range(8):
                        k.mm(p, p[:, 0:tn], wc, wc[:, kc, wo:wo + 128], hT, hT[:, kc, t0:t0 + tn], kc == 0, kc == 7)
                    k.cp(k.act, P32, P32[:, t0:t0 + tn], p, p[:, 0:tn])
                w_ = lambda tap: cw[:, ci, tap:tap + 1]
                k.ts(k.dve, Cv, Cv[:], P32, P32[:], w_(4), None, ALU.mult, extra=[cw])
                k.stt(k.dve, Cv, Cv[:, 1:256], P32, P32[:, 0:255], w_(3), Cv, Cv[:, 1:256], ALU.mult, ALU.add, extra=[cw])
                k.stt(k.dve, Cv, Cv[:, 0:255], P32, P32[:, 1:256], w_(5), Cv, Cv[:, 0:255], ALU.mult, ALU.add, extra=[cw])
                Pl = P32[:, 256:T].rearrange("p (r q) -> p r q", q=64); Cl = Cv[:, 256:T].rearrange("p (r q) -> p r q", q=64)
                for a in range(3):
                    for b in range(3):
                        if a == 1 and b == 1:
                            continue
                        dr, dc = a - 1, b - 1
                        r0, r1 = max(0, -dr), 32 - max(0, dr)
                        c0, c1 = max(0, -dc), 64 - max(0, dc)
                        k.stt(k.dve, Cv, Cl[:, r0:r1, c0:c1], P32, Pl[:, r0 + dr:r1 + dr, c0 + dc:c1 + dc],
                              w_(a * 3 + b), Cv, Cl[:, r0:r1, c0:c1], ALU.mult, ALU.add, extra=[cw])
                h = ci % 8
                if ci >= 16:
                    k.actv(vT, vT[:, h, :], Cv, Cv[:], AF.Silu)
                else:
                    k.actv(S32, S32[:], Cv, Cv[:], AF.Silu)
                    k.tt(k.pool, Q32, Q32[:], S32, S32[:], S32, S32[:], ALU.mult)
                    dst = qT if ci < 8 else kT
                    for (t0, tn) in blocks:
                        p = pp[it % 3]; it += 1
                        k.mm(p, p[:, 0:tn], ones32, ones32[:], Q32, Q32[:, t0:t0 + tn])
                        k.actv(rs, rs[:, 0:tn], p, p[:, 0:tn], AF.Ln, bias=c.epsb[:, 0:1], extra=[c.epsb])
                        k.actv(rs, rs[:, 0:tn], rs, rs[:, 0:tn], AF.Exp, scale=-0.5)
                        k.stt(k.dve, dst, dst[:, h, t0:t0 + tn], S32, S32[:, t0:t0 + tn], (1.0 / math.sqrt(128) if ci < 8 else 1.0),
                              rs, rs[:, 0:tn], ALU.mult, ALU.mult)
            wz = k.sb("wz", [128, 8, 544], BF16)
            zt = [P32, Cv]
            iz = 0
            for (zc0, zn) in ((0, 512), (512, 544)):
                for cb in range(0, zn, 256):
                    n = min(256, zn - cb)
                    load_w_bf16(k, c, wz, wz[:, :, cb:cb + n], e.od_w_in, e.od_w_in.ap()[:, 3072 + zc0 + cb:3072 + zc0 + cb + n], n)
                for t in range(NT):
                    z_ = zt[iz % 2]; iz += 1
                    tok = slice(t * 128, (t + 1) * 128)
                    for bi, (col, n) in enumerate(((0, 512), (512, 32))[:(1 if zc0 == 0 else 2)]):
                        p = pp[it % 3]; it += 1
                        for kc in range(8):
                            k.mm(p, p[:, 0:n], hT, hT[:, kc, tok], wz, wz[:, kc, col:col + n], kc == 0, kc == 7)
                        k.cp(k.act if bi % 2 else k.dve, z_, z_[:, col:col + n], p, p[:, 0:n])
                    k.dma(z_d, z_d.ap()[tok, zc0:zc0 + 512], z_, z_[:, 0:512], q=k.pool)
                    if zc0:
                        k.dma(gt_d, gt_d.ap()[tok, :], z_, z_[:, 512:544], q=k.pool)
        if c.stop == "proj1":
            c.dbg_qkv = (qT, kT, vT)
            return
        gdn_phase(k, c, e, qT, kT, vT, gt_d, o1_d)
    if c.stop == "gdn":
        return
    with k.scope():
        c.wstage = [k.sb("wst%d" % i, [128, 4096]) for i in range(2)]
        wout = k.sb("wout", [128, 8, 1024], BF16)
        for cb in range(2):
            load_w_bf16(k, c, wout, wout[:, :, cb * 512:(cb + 1) * 512], e.od_wout, e.od_wout.ap()[:, cb * 512:(cb + 1) * 512], 512)
        hwb = k.sb("hwb", [128, D])
        k.dma(hwb, hwb[:], e.od_hw, e.od_hw.ap().partition_broadcast(128).rearrange("p o n -> p (o n)"))
        gate = k.sb("gate", [128, D]); e.load_gate(gate, 1, 0, 2)
        ot = [k.sb("ot%d" % i, [128, 2, D]) for i in range(2)]
        zt = [k.sb("zt%d" % i, [128, D]) for i in range(2)]
        xt = [k.sb("xt%d" % i, [128, D]) for i in range(2)]
        w1s = [k.sb("w1_%d" % i, [128, D]) for i in range(2)]; w2s = [k.sb("w2_%d" % i, [128, D]) for i in range(2)]
        sts = [k.sb("st_%d" % i, [128, 16]) for i in range(2)]
        cats = [k.sb("cat%d" % i, [128, D], BF16) for i in range(2)]; catTs = [k.sb("catT%d" % i, [128, 8, 128], BF16) for i in range(2)]
        pTbs = [k.ps("pTb%d" % i, [128, 8, 128], BF16) for i in range(2)]
        pO = [k.ps("pO%d" % i, [128, 512]) for i in range(2)]
        yos = [k.sb("yo%d" % i, [128, D]) for i in range(2)]
        for i, t in enumerate(range(2, NT)):
            tok = slice(t * 128, (t + 1) * 128)
            o_, z_, x_ = ot[i % 2], zt[i % 2], xt[i % 2]
            w1, w2, st, cat, catT, pTb, yo = w1s[i % 2], w2s[i % 2], sts[i % 2], cats[i % 2], catTs[i % 2], pTbs[i % 2], yos[i % 2]
            k.dma(o_, o_[:], o1_d, o1_d.ap()[:, tok, :].rearrange("r t n -> t r n"))
            k.dma(z_, z_[:], z_d, z_d.ap()[tok, :])
            k.dma(x_, x_[:], xs, xs.ap()[tok, :])
            k.tt(k.dve, w1, w1[:], o_, o_[:, 0, :], o_, o_[:, 1, :], ALU.add)
            k.tt(k.pool, w2, w2[:], w1, w1[:], w1, w1[:], ALU.mult)
            k.op(k.dve, lambda: nc.vector.reduce_sum(out=st[:, 0:8], in_=w2[:].rearrange("p (h e) -> p h e", h=8), axis=AX.X), [st], [w2])
            k.actv(st, st[:, 0:8], st, st[:, 0:8], AF.Ln, bias=c.epsb[:, 0:1], scale=1.0 / 128, extra=[c.epsb])
            k.actv(st, st[:, 8:16], st, st[:, 0:8], AF.Exp, scale=-0.5)
            k.tt(k.dve, w1, w1[:].rearrange("p (h e) -> p h e", h=8), w1, w1[:].rearrange("p (h e) -> p h e", h=8),
                 st, bc(st[:, 8:16][:, :, None], [128, 8, 128]), ALU.mult)
            k.tt(k.dve, w1, w1[:], w1, w1[:], hwb, hwb[:], ALU.mult)
            k.actv(z_, z_[:], z_, z_[:], AF.Silu)
            k.tt(k.dve, cat, cat[:], w1, w1[:], z_, z_[:], ALU.mult)
            for j in range(8):
                k.tr(pTb, pTb[:, j, :], cat, cat[:, j * 128:(j + 1) * 128], c.idb, c.idb[:])
            k.cp(k.act, catT, catT[:], pTb, pTb[:])
            for cb in range(2):
                for kc in range(8):
                    k.mm(pO[cb], pO[cb][:], catT, catT[:, kc, :], wout, wout[:, kc, cb * 512:(cb + 1) * 512], kc == 0, kc == 7)
                k.tt(k.dve, yo, yo[:, cb * 512:(cb + 1) * 512], pO[cb], pO[cb][:], gate, gate[:, cb * 512:(cb + 1) * 512], ALU.mult)
            k.tt(k.pool, yo, yo[:], yo, yo[:], x_, x_[:], ALU.add)
            k.dma(xs, xs.ap()[tok, :], yo, yo[:], q=k.pool)


def gdn_phase(k, c, e, qT, kT, vT, gt_d, o1_d):
    nc = k.nc
    with k.scope():
        gt = k.sb("gt", [64, 36, 32])
        for c0 in range(0, 36, 6):
            k.dma(gt, gt[:, c0:c0 + 6, :], gt_d, gt_d.ap()[c0 * 64:(c0 + 6) * 64, :].rearrange("(c l) n -> l c n", l=64))
        ga = k.sb("ga", [64, 16]); dtb = k.sb("dtb", [64, 16])
        k.dma(ga, ga[:], e.od_alog, e.od_alog.ap().partition_broadcast(64).rearrange("p o n -> p (o n)"))
        k.dma(dtb, dtb[:], e.od_dtb, e.od_dtb.ap().partition_broadcast(64).rearrange("p o n -> p (o n)"))
        k.actv(ga, ga[:], ga, ga[:], AF.Exp)
        tri = k.sb("tri", [64, 2, 64]); strict = k.sb("strict", [64, 2, 64])
        k.dma(tri, tri[:], e.ctri, e.ctri.ap().rearrange("r s l -> s r l"))
        k.dma(strict, strict[:], e.cstrict, e.cstrict.ap().rearrange("r s l -> s r l"))
        ones = k.sb("ones", [64, 128]); k.memset(k.dve, ones, ones[:], 1.0)
        ng = k.sb("ng", [64, 2, 36, 8]); beta = k.sb("beta", [64, 2, 36, 8])
        eG = k.sb("eG", [64, 2, 36, 8]); kds = k.sb("kds", [64, 2, 36, 8]); bg = k.sb("bg", [64, 2, 36, 8])
        gl = k.sb("gl", [128, 2, 36, 8])
        for d in range(2):
            k.tt(k.dve, ng, ng[:, d], gt, gt[:, :, 8 * d:8 * d + 8], dtb, bc(dtb[:, None, 8 * d:8 * d + 8], [64, 36, 8]), ALU.add)
            k.actv(beta, beta[:, d], gt, gt[:, :, 16 + 8 * d:24 + 8 * d], AF.Sigmoid)
        k.actv(ng, ng[:], ng, ng[:], AF.Exp)
        k.actv(ng, ng[:], ng, ng[:], AF.Ln, bias=1.0)
        for d in range(2):
            k.tt(k.dve, ng, ng[:, d], ng, ng[:, d], ga, bc(ga[:, None, 8 * d:8 * d + 8], [64, 36, 8]), ALU.mult)
        with k.scope():
            pF = k.ps("pF", [64, 2, 512]); pT_ = k.ps("pTt", [64, 2, 512]); pG = k.ps("pG", [128, 2, 512])
            for d in range(2):
                ngd = ng[:, d].rearrange("p c h -> p (c h)")
                k.mm(pF, pF[:, d, 0:288], tri, tri[:, d, :], ng, ngd)
                k.mm(pT_, pT_[:, d, 0:288], ones, ones[:, 0:64], ng, ngd)
                k.mm(pG, pG[:, d, 0:288], ones, ones[:], ng, ngd)
            fl = lambda t_: t_[:].rearrange("p d c h -> p d (c h)")
            k.actv(eG, fl(eG), pF, pF[:, :, 0:288], AF.Exp, scale=-1.0)
            k.cp(k.dve, kds, fl(kds), pF, pF[:, :, 0:288])
            k.tt(k.dve, kds, fl(kds), kds, fl(kds), pT_, pT_[:, :, 0:288], ALU.subtract)
            k.actv(kds, kds[:], kds, kds[:], AF.Exp)
            k.tt(k.dve, bg, bg[:], beta, beta[:], eG, eG[:], ALU.mult)
            k.actv(gl, fl(gl), pG, pG[:, :, 0:288], AF.Exp, scale=-1.0)
        S32 = [[k.sb("S32_%d_%d" % (d, hp), [128, 2, 128]) for hp in range(4)] for d in range(2)]
        Sb = [[k.sb("Sb_%d_%d" % (d, hp), [128, 2, 128], BF16) for hp in range(4)] for d in range(2)]
        for d in range(2):
            for hp in range(4):
                k.memset(k.dve, S32[d][hp], S32[d][hp][:], 0.0); k.memset(k.pool, Sb[d][hp], Sb[d][hp][:], 0.0)
        idb2 = bc(c.id32[0:64, None, 0:64], [64, 2, 64])

        class G:
            pass
        GS = {}
        for d in range(2):
            for par in range(2):
                g = G()
                sfx = "_%d%d" % (d, par)
                g.X = k.ps("gX" + sfx, [128, 512]); g.Y_ = k.ps("gY" + sfx, [128, 512])
                f3 = lambda h_, p1, c0, a_: h_[0:p1, c0:c0 + 256].rearrange("p (a b) -> p a b", a=a_)
                g.KDv = f3(g.X.h, 64, 0, 4); g.QDv = f3(g.X.h, 64, 256, 4)
                g.Nv = f3(g.X.h, 64, 0, 4); g.Vv = f3(g.X.h, 64, 256, 2); g.O1v = f3(g.X.h, 64, 0, 2)
                g.Tv = g.Y_.h[0:64, 0:256].bitcast(BF16).rearrange("p (a b) -> p a b", a=4)
                g.WTv = g.Y_.h[:, 256:384].rearrange("p (a b) -> p a b", a=2)
                g.O2v = f3(g.Y_.h, 64, 0, 2); g.Sv = f3(g.Y_.h, 128, 256, 2)
                g.kbg = k.sb("kbg" + sfx, [64, 2, 128], BF16); g.kd = k.sb("kd" + sfx, [64, 2, 128], BF16); g.bv = k.sb("bv" + sfx, [64, 2, 128], BF16)
                g.gm = k.sb("gm" + sfx, [64, 4, 64]); g.MBs = k.sb("MBs" + sfx, [64, 4, 64])
                g.gam = k.sb("gam" + sfx, [64, 2, 64]); g.A32 = k.sb("A32" + sfx, [64, 2, 64])
                g.Mb = k.sb("Mb" + sfx, [64, 2, 64], BF16); g.nAb = k.sb("nAb" + sfx, [64, 2, 64], BF16)
                g.Y = k.sb("Y" + sfx, [64, 2, 64], BF16); g.Rt = k.sb("Rt" + sfx, [64, 2, 64], BF16)
                g.nWT = k.sb("nWT" + sfx, [128, 2, 64], BF16); g.vn = k.sb("vn" + sfx, [64, 2, 128], BF16)
                g.gT_ = k.sb("gT_" + sfx, [64, 2, 64]); g.attT = k.sb("attT" + sfx, [64, 2, 64], BF16)
                g.t2 = k.sb("t2" + sfx, [64, 2, 128]); g.otl = [k.sb("otl%d" % i + sfx, [64, 2, 128]) for i in range(2)]
                GS[(d, par)] = g

        def chunk_gen(d, par, ch):
            g = GS[(d, par)]
            tok = slice(ch * 64, (ch + 1) * 64)
            hsel = lambda t_: bc(t_[:, d, ch, :].rearrange("p (a b) -> p a b", b=2)[:, par::2, :].rearrange("p a b -> p (a b)")[:, :, None], [64, 4, 64]) if False else None
            for qi, hp in enumerate((par, par + 2)):
                h0 = 2 * hp
                k.tt(k.pool, g.gm, g.gm[:, 2 * qi:2 * qi + 2, :], tri, bc(tri[:, d:d + 1, :], [64, 2, 64]),
                     ng, bc(ng[:, d, ch, h0:h0 + 2][:, :, None], [64, 2, 64]), ALU.mult)
                k.tt(k.pool, g.MBs, g.MBs[:, 2 * qi:2 * qi + 2, :], strict, bc(strict[:, d:d + 1, :], [64, 2, 64]),
                     beta, bc(beta[:, d, ch, h0:h0 + 2][:, :, None], [64, 2, 64]), ALU.mult)
            for qi, hp in enumerate((par, par + 2)):
                h0 = 2 * hp
                S3, Sb_ = S32[d][hp], Sb[d][hp]
                for hh in range(2):
                    h = h0 + hh
                    k.tr(g.Y_, g.Tv[:, hh, :], kT, kT[:, h, tok], c.idb, c.idb[:])
                    k.tr(g.Y_, g.Tv[:, 2 + hh, :], vT, vT[:, h, tok], c.idb, c.idb[:])
                    k.mm(g.X, g.KDv[:, hh, :], kT, kT[:, h, tok], kT, kT[:, h, tok])
                    k.mm(g.X, g.KDv[:, 2 + hh, :], g.gm, g.gm[:, 2 * qi + hh, :], strict, strict[:, d, :])
                    k.mm(g.X, g.QDv[:, hh, :], kT, kT[:, h, tok], qT, qT[:, h, tok])
                    k.mm(g.X, g.QDv[:, 2 + hh, :], strict, strict[:, d, :], g.gm, g.gm[:, 2 * qi + hh, :])
                yield
                sc = lambda t_: bc(t_[:, d, ch, h0:h0 + 2][:, :, None], [64, 2, 128])
                k.tt(k.dve, g.kbg, g.kbg[:], g.Y_, g.Tv[:, 0:2, :], bg, sc(bg), ALU.mult)
                k.tt(k.dve, g.kd, g.kd[:], g.Y_, g.Tv[:, 0:2, :], kds, sc(kds), ALU.mult)
                k.tt(k.dve, g.bv, g.bv[:], g.Y_, g.Tv[:, 2:4, :], beta, sc(beta), ALU.mult)
                k.actv(g.gam, g.gam[:], g.X, g.KDv[:, 2:4, :], AF.Exp, scale=-1.0)
                k.actv(g.gT_, g.gT_[:], g.X, g.QDv[:, 2:4, :], AF.Exp, scale=-1.0)
                k.tt(k.pool, g.gam, g.gam[:], g.gam, g.gam[:], g.MBs, g.MBs[:, 2 * qi:2 * qi + 2, :], ALU.mult)
                k.tt(k.pool, g.gT_, g.gT_[:], g.gT_, g.gT_[:], tri, bc(tri[:, d:d + 1, :], [64, 2, 64]), ALU.mult)
                k.tt(k.dve, g.A32, g.A32[:], g.X, g.KDv[:, 0:2, :], g.gam, g.gam[:], ALU.mult)
                k.tt(k.dve, g.attT, g.attT[:], g.X, g.QDv[:, 0:2, :], g.gT_, g.gT_[:], ALU.mult)
                k.tt(k.pool, g.Mb, g.Mb[:], g.A32, g.A32[:], c.id32, idb2, ALU.add)
                k.ts(k.dve, g.nAb, g.nAb[:], g.A32, g.A32[:], -1.0, None, ALU.mult)
                for hh in range(2):
                    k.mm(g.X, g.Nv[:, hh, :], g.nAb, g.nAb[:, hh, :], c.idb, c.idb[0:64, 0:64])
                yield
                k.tt(k.dve, g.Y, g.Y[:], g.X, g.Nv[:, 0:2, :], c.id32, idb2, ALU.add)
                for itn in range(5):
                    for hh in range(2):
                        k.mm(g.X, g.Nv[:, hh, :], g.Y, g.Y[:, hh, :], g.Mb, g.Mb[:, hh, :])
                    yield
                    k.tt(k.dve, g.Rt, g.Rt[:], c.id32, idb2, g.X, g.Nv[:, 0:2, :], ALU.subtract)
                    for hh in range(2):
                        k.mm(g.X, g.Nv[:, 2 + hh, :], g.Rt, g.Rt[:, hh, :], g.Y, g.Y[:, hh, :])
                    yield
                    k.tt(k.dve, g.Y, g.Y[:], g.Y, g.Y[:], g.X, g.Nv[:, 2:4, :], ALU.add)
                for hh in range(2):
                    k.mm(g.Y_, g.WTv[:, hh, :], g.kbg, g.kbg[:, hh, :], g.Y, g.Y[:, hh, :])
                yield
                k.actv(g.nWT, g.nWT[:], g.Y_, g.WTv, AF.Copy, scale=-1.0)
                for hh in range(2):
                    k.mm(g.X, g.Vv[:, hh, :], g.Y, g.Y[:, hh, :], g.bv, g.bv[:, hh, :], True, False)
                    k.mm(g.X, g.Vv[:, hh, :], g.nWT, g.nWT[:, hh, :], Sb_, Sb_[:, hh, :], False, True)
                yield
                k.cp(k.act, g.vn, g.vn[:], g.X, g.Vv)
                for hh in range(2):
                    h = h0 + hh
                    k.mm(g.X, g.O1v[:, hh, :], qT, qT[:, h, tok], Sb_, Sb_[:, hh, :])
                    k.mm(g.Y_, g.O2v[:, hh, :], g.attT, g.attT[:, hh, :], g.vn, g.vn[:, hh, :])
                    k.mm(g.Y_, g.Sv[:, hh, :], g.kd, g.kd[:, hh, :], g.vn, g.vn[:, hh, :])
                yield
                ot_ = g.otl[qi]
                k.cp(k.act, g.t2, g.t2[:], g.Y_, g.O2v)
                k.tt(k.dve, ot_, ot_[:], g.X, g.O1v, eG, bc(eG[:, d, ch, h0:h0 + 2][:, :, None], [64, 2, 128]), ALU.mult)
                k.tt(k.pool, ot_, ot_[:], ot_, ot_[:], g.t2, g.t2[:], ALU.add)
                k.tt(k.pool, S3, S3[:], S3, S3[:], gl, bc(gl[:, d, ch, h0:h0 + 2][:, :, None], [128, 2, 128]), ALU.mult)
                k.tt(k.dve, S3, S3[:], S3, S3[:], g.Y_, g.Sv, ALU.add)
                k.cp(k.act, Sb_, Sb_[:], S3, S3[:])
                k.dma(o1_d, o1_d.ap()[d, tok, h0 * 128:(h0 + 2) * 128], ot_, ot_[:].rearrange("p h e -> p (h e)"), q=k.pool)
                yield

        for step in range(36):
            gens = [chunk_gen(0, 0, step), chunk_gen(1, 0, ORDB[step]), chunk_gen(0, 1, step), chunk_gen(1, 1, ORDB[step])]
            alive = [True] * 4
            while any(alive):
                for i in range(4):
                    if alive[i]:
                        try:
                            next(gens[i])
                        except StopIteration:
                            alive[i] = False


def host_consts():
    s = np.arange(64)
    tri = np.stack([(s[:, None] <= s[None, :]), (s[:, None] >= s[None, :])]).astype(np.float32)
    ip = np.arange(128) // 16
    maskM = np.stack([(ip[None, :] >= ip[:, None]), (ip[None, :] <= ip[:, None])]).astype(np.float32)
    return {
        "k_id32": np.eye(128, dtype=np.float32),
        "k_idb": np.eye(128, dtype=np.float32).astype(ml_dtypes.bfloat16),
        "k_tri": tri, "k_maskM": maskM,
        "k_strict": np.stack([(s[:, None] > s[None, :]), (s[:, None] < s[None, :])]).astype(np.float32),
    }


def make_in_maps(inputs, cores):
    f = lambda a: np.ascontiguousarray(np.asarray(a, dtype=np.float32))
    sh = {
        "c_ctx": f(inputs["c_ctx"]).reshape(1, D), "ada_w": f(inputs["ada_w"]), "ada_b": f(inputs["ada_b"]),
        "norm1_w": f(inputs["norm1_w"]), "norm2_w": f(inputs["norm2_w"]),
        "ffn_w1": f(inputs["ffn_w1"]), "ffn_w3": f(inputs["ffn_w3"]), "ffn_w2": f(inputs["ffn_w2"]),
        "final_norm_w": f(inputs["final_norm_w"]).reshape(1, D),
        "ev_w_in": f(inputs["ev_w_in"])[0], "ev_i_bias": f(inputs["ev_i_bias"]).reshape(1, 8),
        "ev_f_bias": f(inputs["ev_f_bias"]).reshape(1, 8), "ev_head_norm_w": f(inputs["ev_head_norm_w"]).reshape(1, 512),
        "ev_lam_re": f(inputs["ev_lam_re"])[0], "ev_lam_im": f(inputs["ev_lam_im"])[0],
        "ev_log_dt": f(inputs["ev_log_dt"]).reshape(1, 64),
        "ev_b_re": f(inputs["ev_b_re"])[0], "ev_b_im": f(inputs["ev_b_im"])[0],
        "ev_c_re": f(inputs["ev_c_re"]).reshape(1024, 64), "ev_c_im": f(inputs["ev_c_im"]).reshape(1024, 64),
        "ev_d": f(inputs["ev_d"]).reshape(1, 512), "ev_w_glu": f(inputs["ev_w_glu"])[0], "ev_w_out": f(inputs["ev_w_out"])[0],
        "od_w_in": f(inputs["od_w_in"])[0], "od_conv_w": f(inputs["od_conv_w"]).reshape(9, 3072),
        "od_a_log": f(inputs["od_a_log"]).reshape(1, 16), "od_dt_bias": f(inputs["od_dt_bias"]).reshape(1, 16),
        "od_head_norm_w": f(inputs["od_head_norm_w"]).reshape(1, D), "od_w_out": f(inputs["od_w_out"])[0],
    }
    sh.update(host_consts())
    x, cc, ctx = f(inputs["x"]), f(inputs["c"]), f(inputs["ctx"])
    maps = []
    for b in cores:
        m = dict(sh)
        m["x"] = x[b]; m["c"] = cc[b:b + 1]; m["ctx"] = ctx[b]
        maps.append(m)
    return maps


def kernel(**inputs):
    nc, _ = build_program()
    maps = make_in_maps(inputs, list(range(8)))
    res = run_bass_kernel_spmd(nc, maps, core_ids=list(range(8)))
    return np.stack([np.asarray(r["out"], dtype=np.float32) for r in res.results], axis=0)
```

```python
import math
import numpy as np
import ml_dtypes
import concourse.bass as bass
import concourse.mybir as mybir
from concourse.bass_types import AP
from concourse.bass_utils import run_bass_kernel_spmd

F32 = mybir.dt.float32
BF16 = mybir.dt.bfloat16
AF = mybir.ActivationFunctionType
ALU = mybir.AluOpType
AX = mybir.AxisListType

D = 1024
T = 2304
NCTX = 256
NLAT = 2048
NT = T // 128
EPS = 1e-6
HID = 2816
PI = math.pi


class Obj:
    def __init__(self, k, name, handle, space):
        self.k, self.name, self.h, self.space = k, name, handle, space
        self.uid = k.uid
        self.w, self.r = {}, {}
        self.sems = {}

    def __getitem__(self, idx):
        return self.h[idx]

    def ap(self):
        return self.h.ap() if self.space == "dram" else self.h[:]

    def dsem(self, kind):
        if kind not in self.sems:
            if not self.sems:
                self.k.dma_objs.append(self)
            pool = self.k.sem_pool[kind]
            if pool:
                self.sems[kind] = pool.pop()
            else:
                self.k.nsem += 1
                self.sems[kind] = [self.k.new_sem("d%s_%d" % (kind, self.k.nsem), keep=True), 0]
        return self.sems[kind]


class Eng:
    def __init__(self, k, name, e):
        self.k, self.name, self.e = k, name, e
        self.sem = k.new_sem("p_" + name)
        self.cnt = 0
        self.seen = {}

    def need(self, tok):
        s, v = tok
        if self.seen.get(id(s), 0) >= v:
            return
        self.e.wait_ge(s, v)
        self.seen[id(s)] = v


class K:
    def __init__(self, nc):
        self.nc = nc
        self._ctx = []
        self.dma_objs = []
        self.sem_pool = {"hw": [], "sw": []}
        self.nsem = 0
        self._perm = []
        self.pe = Eng(self, "pe", nc.tensor)
        self.dve = Eng(self, "dve", nc.vector)
        self.act = Eng(self, "act", nc.scalar)
        self.pool = Eng(self, "pool", nc.gpsimd)
        self.sp = Eng(self, "sp", nc.sync)
        self.engs = [self.pe, self.dve, self.act, self.pool, self.sp]
        self.n_ins = 0
        self.uid = 0

    def new_sem(self, name, keep=False):
        cm = self.nc.semaphore(name)
        s = cm.__enter__()
        self._perm.append((cm, s))
        return s

    def _alloc(self, cm, name, space):
        h = cm.__enter__()
        self._ctx.append(cm)
        return Obj(self, name, h, space)

    def sb(self, name, shape, dt=F32):
        self.uid += 1
        name = "%s_%d" % (name, self.uid)
        return self._alloc(self.nc.sbuf_tensor(name, list(shape), dt), name, "sb")

    def ps(self, name, shape, dt=F32):
        self.uid += 1
        name = "%s_%d" % (name, self.uid)
        return self._alloc(self.nc.psum_tensor(name, list(shape), dt), name, "ps")

    def dram(self, name, shape, dt=F32, kind="Internal"):
        h = self.nc.dram_tensor(name, list(shape), dt, kind=kind)
        return Obj(self, name, h, "dram")

    class _Scope:
        def __init__(self, k):
            self.k = k

        def __enter__(self):
            self.mark = len(self.k._ctx)
            self.uid0 = self.k.uid
            return self

        def __exit__(self, *a):
            k = self.k
            k.barrier()
            while len(k._ctx) > self.mark:
                k._ctx.pop().__exit__(None, None, None)
            keep = []
            for o in k.dma_objs:
                if o.space == "dram" or o.uid <= self.uid0:
                    keep.append(o)
                else:
                    for kind, sc in o.sems.items():
                        k.sem_pool[kind].append(sc)
                    o.sems = {}
            k.dma_objs = keep
            return False

    def scope(self):
        return K._Scope(self)

    def barrier(self):
        toks = [(e.sem, e.cnt) for e in self.engs if e.cnt]
        toks += [(sc[0], sc[1]) for o in self.dma_objs for sc in o.sems.values() if sc[1]]
        for e in self.engs:
            for t in toks:
                if t[0] is e.sem:
                    continue
                e.need(t)

    def _deps(self, eng, outs, ins, same_eng_raw=True):
        toks = []
        for o in ins:
            toks += list(o.w.values())
            if o.space == "ps":
                toks += [t for t in o.r.values() if t[0] is not eng.sem]
        for o in outs:
            toks += list(o.w.values())
            toks += list(o.r.values())
        for t in toks:
            if t[0] is eng.sem and (eng is self.pe or not same_eng_raw):
                continue
            eng.need(t)

    SAME_ENG_WAIT = True

    SKIP_SELF = ()

    def op(self, eng, fn, outs, ins):
        self._deps(eng, outs, ins, same_eng_raw=(K.SAME_ENG_WAIT and eng.name not in K.SKIP_SELF))
        ins_ = fn()
        eng.cnt += 1
        ins_.then_inc(eng.sem, 1)
        tok = (eng.sem, eng.cnt)
        eng.seen[id(eng.sem)] = max(eng.seen.get(id(eng.sem), 0), 0)
        for o in ins:
            o.r[id(tok[0])] = tok
        for o in outs:
            o.w = {id(tok[0]): tok}
            o.r = {}
        self.n_ins += 1
        return ins_

    def dma(self, out_obj, out_ap, in_obj, in_ap, q=None, **kw):
        q = q or self.sp
        self._deps(q, [out_obj], [in_obj], same_eng_raw=True)
        sc = out_obj.dsem("sw" if q is self.pool else "hw")
        s = sc[0]
        ins_ = q.e.dma_start(out=out_ap, in_=in_ap, **kw)
        sc[1] += 16
        ins_.then_inc(s, 16)
        tok = (s, sc[1])
        in_obj.r[id(s)] = tok
        out_obj.w[id(s)] = tok
        out_obj.r = {}
        self.n_ins += 1
        return ins_

    def finish(self, outs):
        self.barrier()

    def close(self):
        while self._ctx:
            self._ctx.pop().__exit__(None, None, None)
        while self._perm:
            self._perm.pop()[0].__exit__(None, None, None)

    def mm(self, out_o, out_ap, l_o, l_ap, r_o, r_ap, start=True, stop=True):
        nc = self.nc
        return self.op(self.pe, lambda: nc.tensor.matmul(out_ap, lhsT=l_ap, rhs=r_ap, start=start, stop=stop),
                       [out_o], [l_o, r_o])

    def tr(self, out_o, out_ap, in_o, in_ap, id_o, id_ap):
        nc = self.nc
        return self.op(self.pe, lambda: nc.tensor.transpose(out_ap, in_ap, id_ap), [out_o], [in_o, id_o])

    def tt(self, eng, out_o, out_ap, a_o, a_ap, b_o, b_ap, op):
        return self.op(eng, lambda: eng.e.tensor_tensor(out=out_ap, in0=a_ap, in1=b_ap, op=op), [out_o], [a_o, b_o])

    def ts(self, eng, out_o, out_ap, a_o, a_ap, s1, s2, op0, op1=None, extra=()):
        if op1 is None:
            return self.op(eng, lambda: eng.e.tensor_scalar(out=out_ap, in0=a_ap, scalar1=s1, scalar2=None, op0=op0),
                           [out_o], [a_o] + list(extra))
        return self.op(eng, lambda: eng.e.tensor_scalar(out=out_ap, in0=a_ap, scalar1=s1, scalar2=s2, op0=op0, op1=op1),
                       [out_o], [a_o] + list(extra))

    def stt(self, eng, out_o, out_ap, a_o, a_ap, sc, b_o, b_ap, op0, op1, extra=()):
        return self.op(eng, lambda: eng.e.scalar_tensor_tensor(out=out_ap, in0=a_ap, scalar=sc, in1=b_ap, op0=op0, op1=op1),
                       [out_o], [a_o, b_o] + list(extra))

    def actv(self, out_o, out_ap, a_o, a_ap, func, bias=0.0, scale=1.0, extra=()):
        nc = self.nc
        return self.op(self.act, lambda: nc.scalar.activation(out=out_ap, in_=a_ap, func=func, bias=bias, scale=scale),
                       [out_o], [a_o] + list(extra))

    def cp(self, eng, out_o, out_ap, a_o, a_ap):
        if eng is self.act:
            nc = self.nc
            return self.op(eng, lambda: nc.scalar.copy(out=out_ap, in_=a_ap), [out_o], [a_o])
        return self.op(eng, lambda: eng.e.tensor_copy(out=out_ap, in_=a_ap), [out_o], [a_o])

    def memset(self, eng, o, ap, val):
        return self.op(eng, lambda: eng.e.memset(ap, val), [o], [])


def bc(ap, shape):
    return ap.broadcast_to(list(shape))


class Ctx:
    pass


def load_w_bf16(k, c, dst, dst_ap_fn, wsrc, w_ap, ncols, kchunks=8, eng=None):
    eng = eng or k.pool
    st = c.wstage[c.wstage_i % 2]
    c.wstage_i += 1
    sv = st[:, 0:kchunks * ncols].rearrange("p (k n) -> p k n", k=kchunks)
    k.dma(st, sv, wsrc, w_ap.rearrange("(kc p) n -> p kc n", p=128))
    k.cp(eng, dst, dst_ap_fn, st, sv)


def norm_mod(k, c, xs, tiles, ab_of_tile, hT, col0=0):
    for i, t in enumerate(tiles):
        xt = c.xt[i % 2]
        k.dma(xt, xt[:], xs, xs.ap()[t * 128:(t + 1) * 128, :])
        sq = c.sq[i % 2] if isinstance(c.sq, list) else c.sq
        k.tt(k.dve, sq, sq[:], xt, xt[:], xt, xt[:], ALU.mult)
        ss = c.ss[i % 2] if isinstance(c.ss, list) else c.ss
        k.op(k.dve, lambda: k.nc.vector.reduce_sum(out=ss[:, 0:1], in_=sq[:], axis=AX.X), [ss], [sq])
        k.actv(ss, ss[:, 1:2], ss, ss[:, 0:1], AF.Ln, bias=c.epsb[:, 0:1], scale=1.0 / D, extra=[c.epsb])
        k.actv(ss, ss[:, 2:3], ss, ss[:, 1:2], AF.Exp, scale=-0.5)
        k.ts(k.dve, sq, sq[:], xt, xt[:], ss[:, 2:3], None, ALU.mult, extra=[ss])
        a, b = ab_of_tile(t)
        for half in range(2):
            pt = c.pT[half]
            for j in range(4):
                kc = half * 4 + j
                k.tr(pt, pt[:, j, :], sq, sq[:, kc * 128:(kc + 1) * 128], c.id32, c.id32[:])
            for j in range(4):
                kc = half * 4 + j
                k.actv(hT, hT[:, kc, col0 + i * 128: col0 + (i + 1) * 128], pt, pt[:, j, :], AF.Identity,
                       bias=b[0][:, b[1] + kc: b[1] + kc + 1], scale=a[0][:, a[1] + kc:a[1] + kc + 1], extra=[a[0], b[0]])


def load_T(k, c, dst_o, dst_ap, src_o, src_ap, nrows, ncols=128):
    st = c.ltst[c.lt_i % 2]; pt = c.ltps[c.lt_i % 2]; c.lt_i += 1
    k.dma(st, st[0:nrows, 0:ncols], src_o, src_ap)
    k.tr(pt, pt[0:ncols, 0:nrows], st, st[0:nrows, 0:ncols], c.id32, c.id32[0:nrows, 0:nrows])
    k.cp(k.dve, dst_o, dst_ap, pt, pt[0:ncols, 0:nrows])


def tok_blocks(tiles_n):
    out, s = [], 0
    while s < tiles_n:
        n = min(4, tiles_n - s)
        out.append((s, n))
        s += n
    return out


def build_program(stop_after=None, debug=False):
    nc = bass.Bass("TRN2", target_bir_lowering=False)
    k = K(nc)
    c = Ctx()
    c.wstage_i = 0
    c.stop = stop_after
    I = {}

    def inp(name, shape, dt=F32):
        I[name] = k.dram(name, shape, dt, kind="ExternalInput")
        return I[name]

    x_in = inp("x", [NLAT, D]); cvec = inp("c", [1, D]); ctx_in = inp("ctx", [NCTX, D]); c_ctx = inp("c_ctx", [1, D])
    ada_w = inp("ada_w", [2, D, 6 * D]); ada_b = inp("ada_b", [2, 6 * D])
    norm1_w = inp("norm1_w", [2, D]); norm2_w = inp("norm2_w", [2, D])
    ffn_w1 = inp("ffn_w1", [2, D, HID]); ffn_w3 = inp("ffn_w3", [2, D, HID]); ffn_w2 = inp("ffn_w2", [2, HID, D])
    final_w = inp("final_norm_w", [1, D])
    ev_w_in = inp("ev_w_in", [D, 2576]); ev_ib = inp("ev_i_bias", [1, 8]); ev_fb = inp("ev_f_bias", [1, 8])
    ev_hw = inp("ev_head_norm_w", [1, 512])
    ev_lre = inp("ev_lam_re", [2, 32, 64]); ev_lim = inp("ev_lam_im", [2, 32, 64]); ev_ldt = inp("ev_log_dt", [1, 64])
    ev_bre = inp("ev_b_re", [2, 32, 64, 16]); ev_bim = inp("ev_b_im", [2, 32, 64, 16])
    ev_cre = inp("ev_c_re", [1024, 64]); ev_cim = inp("ev_c_im", [1024, 64])
    ev_d = inp("ev_d", [1, 512]); ev_wglu = inp("ev_w_glu", [512, 1024]); ev_wout = inp("ev_w_out", [D, D])
    od_w_in = inp("od_w_in", [D, 4128]); od_conv = inp("od_conv_w", [9, 3072])
    od_alog = inp("od_a_log", [1, 16]); od_dtb = inp("od_dt_bias", [1, 16]); od_hw = inp("od_head_norm_w", [1, D])
    od_wout = inp("od_w_out", [D, D])
    cid32 = inp("k_id32", [128, 128]); cidb = inp("k_idb", [128, 128], BF16)
    ctri = inp("k_tri", [2, 64, 64])
    cmaskM = inp("k_maskM", [2, 128, 128])
    cstrict = inp("k_strict", [2, 64, 64])
    out_d = k.dram("out", [NLAT, D], F32, kind="ExternalOutput")

    xs = k.dram("xs", [T, D])
    modv = k.dram("modv", [2, 2, 6 * D])
    dbg = {}

    c.id32 = k.sb("id32", [128, 128]); k.dma(c.id32, c.id32[:], cid32, cid32.ap())
    c.idb = k.sb("idb", [128, 128], BF16); k.dma(c.idb, c.idb[:], cidb, cidb.ap())
    c.epsb = k.sb("epsb", [128, 2]); k.memset(k.dve, c.epsb, c.epsb[:, 0:1], EPS); k.memset(k.dve, c.epsb, c.epsb[:, 1:2], 0.5 * PI)

    k.dma(xs, xs.ap()[0:NCTX, :], ctx_in, ctx_in.ap())
    k.dma(xs, xs.ap()[NCTX:T, :], x_in, x_in.ap())
    with k.scope():
        sT = k.sb("sT", [128, 8, 2])
        c.ltst = [k.sb("ltst%d" % i, [64, 128]) for i in range(2)]
        c.ltps = [k.ps("ltps%d" % i, [128, 64]) for i in range(2)]
        c.lt_i = 0
        load_T(k, c, sT, sT[:, :, 0], cvec, cvec.ap().rearrange("o (kc p) -> (o kc) p", p=128), 8)
        load_T(k, c, sT, sT[:, :, 1], c_ctx, c_ctx.ap().rearrange("o (kc p) -> (o kc) p", p=128), 8)
        sS = k.sb("sS", [128, 8, 2])
        k.actv(sS, sS[:], sT, sT[:], AF.Silu)
        wst = [k.sb("adw%d" % i, [128, 8, 512]) for i in range(2)]
        pm = [k.ps("pm%d" % i, [128, 512]) for i in range(2)]
        brow = k.sb("brow", [2, 6 * D]); mrow = k.sb("mrow", [2, 6 * D])
        for li in range(2):
            k.dma(brow, brow[:], ada_b, ada_b.ap()[li:li + 1, :].partition_broadcast(2).rearrange("p o n -> p (o n)"))
            for j in range(12):
                w = wst[j % 2]
                k.dma(w, w[:], ada_w, ada_w.ap()[li, :, j * 512:(j + 1) * 512].rearrange("(kc p) n -> p kc n", p=128))
                p = pm[j % 2]
                for kc in range(8):
                    k.mm(p, p[0:2, :], sS, sS[:, kc, :], w, w[:, kc, :], start=(kc == 0), stop=(kc == 7))
                k.tt(k.dve, mrow, mrow[:, j * 512:(j + 1) * 512], p, p[0:2, :], brow, brow[:, j * 512:(j + 1) * 512], ALU.add)
            k.dma(modv, modv.ap()[li], mrow, mrow[:], q=k.pool)

    if stop_after == "ada":
        k.finish([])
        k.close()
        return nc, ["modv"]

    modF = k.sb("modF", [128, 2, 2, 48])
    nwF = k.sb("nwF", [128, 2, 2, 8])
    with k.scope():
        c.ltst = [k.sb("ltst%d" % i, [64, 128]) for i in range(2)]
        c.ltps = [k.ps("ltps%d" % i, [128, 64]) for i in range(2)]
        c.lt_i = 0
        for li in range(2):
            for who in range(2):
                load_T(k, c, modF, modF[:, li, who, :], modv, modv.ap()[li, who, :].rearrange("(c p) -> c p", p=128), 48)
        for wi, nw in enumerate((norm1_w, norm2_w)):
            for li in range(2):
                load_T(k, c, nwF, nwF[:, wi, li, :], nw, nw.ap()[li, :].rearrange("(c p) -> c p", p=128), 8)
    aF = k.sb("aF", [128, 2, 2, 2, 8])
    for wi in range(2):
        for li in range(2):
            for who in range(2):
                sc0 = 8 if wi == 0 else 32
                k.stt(k.dve, aF, aF[:, wi, li, who, :], modF, modF[:, li, who, sc0:sc0 + 8], 1.0, nwF, nwF[:, wi, li, :],
                      ALU.add, ALU.mult)

    def ab_fn(wi, li):
        sh0 = 0 if wi == 0 else 24

        def f(t):
            who = 1 if t < 2 else 0
            a_flat = aF.h[:].rearrange("p a b c d -> p (a b c d)")
            b_flat = modF.h[:].rearrange("p a b c -> p (a b c)")
            return ((_View(aF, a_flat), ((wi * 2 + li) * 2 + who) * 8), (_View(modF, b_flat), (li * 2 + who) * 48 + sh0))
        return f

    def load_gate(dst, li, who, part):
        k.dma(dst, dst[:], modv, modv.ap()[li, who:who + 1, part * D:(part + 1) * D].partition_broadcast(128).rearrange("p o n -> p (o n)"))

    def ffn_phase(li, tiles):
        with k.scope():
            c.xt = [k.sb("xt%d" % i, [128, D]) for i in range(2)]
            c.sq = [k.sb("sq%d" % i, [128, D]) for i in range(2)]; c.ss = [k.sb("ss%d" % i, [128, 4]) for i in range(2)]
            c.pT = [k.ps("pT%d" % i, [128, 4, 128]) for i in range(2)]
            c.wstage = [k.sb("wst%d" % i, [128, 2048]) for i in range(2)]
            ntl = len(tiles)
            half_n = (ntl + 1) // 2
            gate = [k.sb("gate%d" % w, [128, D]) for w in range(2)]
            load_gate(gate[0], li, 0, 5); load_gate(gate[1], li, 1, 5)
            hT = k.sb("hT", [128, 8, half_n * 128], BF16)
            gT = k.sb("gT", [128, 22, half_n * 128], BF16)
            w2b = k.sb("w2b", [128, 22, D], BF16)
            w1b = [k.sb("w1b%d" % i, [128, 8, 256], BF16) for i in range(2)]
            w3b = [k.sb("w3b%d" % i, [128, 8, 256], BF16) for i in range(2)]
            p1 = [k.ps("p1_%d" % i, [128, 512]) for i in range(2)]
            p3 = [k.ps("p3_%d" % i, [128, 512]) for i in range(2)]
            py = [k.ps("py%d" % i, [128, 512]) for i in range(2)]
            sg = [k.sb("sg%d" % i, [128, 512]) for i in range(2)]
            yo = [k.sb("yo%d" % i, [128, D]) for i in range(2)]
            for jb in range(0, 22, 4):
                n = min(4, 22 - jb)
                for cb in range(2):
                    st = c.wstage[c.wstage_i % 2]; c.wstage_i += 1
                    sv = st[:, 0:n * 512].rearrange("p (j n) -> p j n", j=n)
                    k.dma(st, sv, ffn_w2, ffn_w2.ap()[li, jb * 128:(jb + n) * 128, cb * 512:(cb + 1) * 512]
                          .rearrange("(j p) n -> p j n", p=128))
                    k.cp(k.pool, w2b, w2b[:, jb:jb + n, cb * 512:(cb + 1) * 512], st, sv)
            it = 0
            for hs in range(0, ntl, half_n):
                ht = tiles[hs:hs + half_n]
                norm_mod(k, c, xs, ht, ab_fn(1, li), hT)
                blocks = tok_blocks(len(ht))
                for jb in range(0, 22, 2):
                    n = 2
                    wa, wb = w1b[(jb // 2) % 2], w3b[(jb // 2) % 2]
                    load_w_bf16(k, c, wa, wa[:, :, 0:n * 128], ffn_w1, ffn_w1.ap()[li, :, jb * 128:(jb + n) * 128], n * 128)
                    load_w_bf16(k, c, wb, wb[:, :, 0:n * 128], ffn_w3, ffn_w3.ap()[li, :, jb * 128:(jb + n) * 128], n * 128, eng=k.dve)
                    for jj in range(n):
                        j = jb + jj
                        for (b0, bn) in blocks:
                            q1, q3, s_ = p1[it % 2], p3[it % 2], sg[it % 2]; it += 1
                            cols = slice(b0 * 128, (b0 + bn) * 128)
                            w_ = bn * 128
                            for kc in range(8):
                                k.mm(q1, q1[:, 0:w_], wa, wa[:, kc, jj * 128:(jj + 1) * 128], hT, hT[:, kc, cols], kc == 0, kc == 7)
                            for kc in range(8):
                                k.mm(q3, q3[:, 0:w_], wb, wb[:, kc, jj * 128:(jj + 1) * 128], hT, hT[:, kc, cols], kc == 0, kc == 7)
                            k.actv(s_, s_[:, 0:w_], q1, q1[:, 0:w_], AF.Silu)
                            k.tt(k.dve, gT, gT[:, j, cols], s_, s_[:, 0:w_], q3, q3[:, 0:w_], ALU.mult)
                for i, t in enumerate(ht):
                    xt = c.xt[i % 2]
                    k.dma(xt, xt[:], xs, xs.ap()[t * 128:(t + 1) * 128, :])
                    g = gate[1 if t < 2 else 0]
                    y = yo[i % 2]
                    for cb in range(2):
                        p = py[cb]
                        for j in range(22):
                            k.mm(p, p[:], gT, gT[:, j, i * 128:(i + 1) * 128], w2b, w2b[:, j, cb * 512:(cb + 1) * 512], j == 0, j == 21)
                        k.tt(k.dve, y, y[:, cb * 512:(cb + 1) * 512], p, p[:], g, g[:, cb * 512:(cb + 1) * 512], ALU.mult)
                    k.tt(k.pool, y, y[:], y, y[:], xt, xt[:], ALU.add)
                    k.dma(xs, xs.ap()[t * 128:(t + 1) * 128, :], y, y[:], q=k.pool)

    def final_phase():
        with k.scope():
            xt = [k.sb("fx%d" % i, [128, D]) for i in range(2)]
            sq = [k.sb("fs%d" % i, [128, D]) for i in range(2)]
            ss = k.sb("fss", [128, 4])
            fw = k.sb("fw", [128, D])
            k.dma(fw, fw[:], final_w, final_w.ap().partition_broadcast(128).rearrange("p o n -> p (o n)"))
            for i in range(16):
                t = i + 2
                x_, s_ = xt[i % 2], sq[i % 2]
                k.dma(x_, x_[:], xs, xs.ap()[t * 128:(t + 1) * 128, :])
                k.tt(k.dve, s_, s_[:], x_, x_[:], x_, x_[:], ALU.mult)
                k.op(k.dve, lambda: nc.vector.reduce_sum(out=ss[:, 0:1], in_=s_[:], axis=AX.X), [ss], [s_])
                k.actv(ss, ss[:, 1:2], ss, ss[:, 0:1], AF.Ln, bias=c.epsb[:, 0:1], scale=1.0 / D, extra=[c.epsb])
                k.actv(ss, ss[:, 2:3], ss, ss[:, 1:2], AF.Exp, scale=-0.5)
                k.stt(k.dve, s_, s_[:], x_, x_[:], ss[:, 2:3], fw, fw[:], ALU.mult, ALU.mult, extra=[ss])
                k.dma(out_d, out_d.ap()[i * 128:(i + 1) * 128, :], s_, s_[:], q=k.pool)

    env = dict(locals())
    layer0(k, c, env)
    if stop_after in ("proj0", "ml_0", "ml_1", "ml_2", "ml_3", "ml_4", "ml_5", "ml_5a", "ml_5b", "ml_a", "ml_b", "mlstm", "s5", "mix0"):
        k.finish([]); k.close(); return nc, ["xs"]
    ffn_phase(0, list(range(NT)))
    if stop_after == "ffn0":
        k.finish([]); k.close(); return nc, ["xs"]
    layer1(k, c, env)
    if stop_after == "mix1":
        k.finish([]); k.close(); return nc, ["xs"]
    ffn_phase(1, list(range(2, NT)))
    final_phase()
    k.finish([out_d])
    k.close()
    return nc, ["out"]


class _View:
    def __init__(self, obj, flat):
        self.obj, self.flat = obj, flat

    @property
    def space(self):
        return self.obj.space

    @property
    def w(self):
        return self.obj.w

    @property
    def r(self):
        return self.obj.r

    def __getitem__(self, idx):
        return self.flat[idx]


LAYER_FUNCS = []


class E:
    def __init__(self, d):
        self.__dict__.update(d)


ORDB = [3, 2, 1, 0] + list(range(35, 3, -1))
ORDB8 = list(range(31, -1, -1)) + list(range(287, 31, -1))


def layer0(k, c, env):
    e = E(env)
    nc = k.nc
    xs = e.xs
    qT_d = k.dram("qT_d", [512, T], BF16); kT_d = k.dram("kT_d", [512, T], BF16)
    ktok_d = k.dram("ktok_d", [T, 512], BF16); v_d = k.dram("v_d", [T, 512], BF16)
    o_d = k.dram("o_d", [T, 512]); g_d = k.dram("g_d", [T, 16]); u_d = k.dram("u_d", [T, 512])
    hA_d = k.dram("hA_d", [2, T, 512]); yS_d = k.dram("yS_d", [T, 512])
    c.l0 = dict(qT=qT_d, kT=kT_d, ktok=ktok_d, v=v_d, o=o_d, g=g_d, u=u_d, hA=hA_d, yS=yS_d)

    with k.scope():
        c.xt = [k.sb("xt%d" % i, [128, D]) for i in range(2)]
        c.sq = [k.sb("sq%d" % i, [128, D]) for i in range(2)]; c.ss = [k.sb("ss%d" % i, [128, 4]) for i in range(2)]
        c.pT = [k.ps("pT%d" % i, [128, 4, 128]) for i in range(2)]
        c.wstage = [k.sb("wst%d" % i, [128, 4096]) for i in range(2)]
        hT = k.sb("hT", [128, 8, T], BF16)
        norm_mod(k, c, xs, list(range(NT)), e.ab_fn(0, 0), hT)
        wb = k.sb("wb", [128, 8, 2576], BF16)
        for cb in range(0, 2576, 512):
            n = min(512, 2576 - cb)
            load_w_bf16(k, c, wb, wb[:, :, cb:cb + n], e.ev_w_in, e.ev_w_in.ap()[:, cb:cb + n], n,
                        eng=(k.pool if (cb // 512) % 2 else k.dve))
        pp = [k.ps("pp%d" % i, [128, 512]) for i in range(4)]
        fst = [k.sb("fst%d" % i, [128, T], BF16) for i in range(2)]
        blocks = [(0, 512), (512, 512), (1024, 512), (1536, 512), (2048, 256)]
        it = 0
        for which, dst in ((0, qT_d), (1, kT_d)):
            for h in range(4):
                st = fst[(which * 4 + h) % 2]
                col = which * 512 + h * 128
                for (t0, tn) in blocks:
                    p = pp[it % 4]; it += 1
                    for kc in range(8):
                        k.mm(p, p[:, 0:tn], wb, wb[:, kc, col:col + 128], hT, hT[:, kc, t0:t0 + tn], kc == 0, kc == 7)
                    k.actv(st, st[:, t0:t0 + tn], p, p[:, 0:tn], AF.Copy, scale=(1.0 if which == 0 else 1.0 / math.sqrt(128)))
                k.dma(dst, dst.ap()[h * 128:(h + 1) * 128, :], st, st[:], q=k.pool)
        tkb = [k.sb("tkb%d" % i, [128, 1024], BF16) for i in range(2)]
        tof = [k.sb("tof%d" % i, [128, 1040]) for i in range(2)]
        for t in range(NT):
            kb, of = tkb[t % 2], tof[t % 2]
            tok = slice(t * 128, (t + 1) * 128)
            for bi, (col, n) in enumerate(((512, 512), (1024, 512), (1536, 512), (2064, 512), (2048, 16))):
                p = pp[it % 4]; it += 1
                for kc in range(8):
                    k.mm(p, p[:, 0:n], hT, hT[:, kc, tok], wb, wb[:, kc, col:col + n], kc == 0, kc == 7)
                if bi == 0:
                    k.actv(kb, kb[:, 0:512], p, p[:, 0:512], AF.Copy, scale=1.0 / math.sqrt(128))
                elif bi == 1:
                    k.cp(k.dve, kb, kb[:, 512:1024], p, p[:, 0:512])
                elif bi == 2:
                    k.cp(k.act, of, of[:, 0:512], p, p[:, 0:512])
                elif bi == 3:
                    k.cp(k.dve, of, of[:, 512:1024], p, p[:, 0:512])
                else:
                    k.cp(k.act, of, of[:, 1024:1040], p, p[:, 0:16])
            k.dma(ktok_d, ktok_d.ap()[tok, :], kb, kb[:, 0:512], q=k.pool)
            k.dma(v_d, v_d.ap()[tok, :], kb, kb[:, 512:1024], q=k.pool)
            k.dma(o_d, o_d.ap()[tok, :], of, of[:, 0:512], q=k.pool)
            k.dma(u_d, u_d.ap()[tok, :], of, of[:, 512:1024], q=k.pool)
            k.dma(g_d, g_d.ap()[tok, :], of, of[:, 1024:1040], q=k.pool)
    if c.stop == "proj0":
        return
    mlstm_phase(k, c, e)
    if c.stop in ("mlstm", "ml_0", "ml_1", "ml_2", "ml_3", "ml_4", "ml_5", "ml_5a", "ml_5b", "ml_a", "ml_b"):
        return
    s5_phase(k, c, e)
    if c.stop == "s5":
        return
    finish0_phase(k, c, e)


def mlstm_phase(k, c, e):
    nc = k.nc
    L = c.l0
    with k.scope():
        qT = k.sb("qT", [128, 4, T], BF16); kT = k.sb("kT", [128, 4, T], BF16)
        for h in range(4):
            k.dma(qT, qT[:, h, :], L['qT'], L['qT'].ap()[h * 128:(h + 1) * 128, :])
            k.dma(kT, kT[:, h, :], L['kT'], L['kT'].ap()[h * 128:(h + 1) * 128, :])
        ktok = k.sb("ktok", [64, 36, 512], BF16)
        v1 = k.sb("v1", [64, 36, 4, 132], BF16)
        for c0 in range(0, 36, 6):
            k.dma(ktok, ktok[:, c0:c0 + 6, :], L['ktok'], L['ktok'].ap()[c0 * 64:(c0 + 6) * 64, :].rearrange("(c l) n -> l c n", l=64))
            for h in range(4):
                k.dma(v1, v1[:, c0:c0 + 6, h, 0:128], L['v'],
                      L['v'].ap()[c0 * 64:(c0 + 6) * 64, h * 128:(h + 1) * 128].rearrange("(c l) n -> l c n", l=64))
        if c.stop == "ml_0":
            return
        k.memset(k.dve, v1, v1[:, :, :, 128:132], 1.0)
        if c.stop == "ml_1":
            return
        g = k.sb("g", [64, 36, 16])
        for c0 in range(0, 36, 6):
            k.dma(g, g[:, c0:c0 + 6, :], L['g'], L['g'].ap()[c0 * 64:(c0 + 6) * 64, :].rearrange("(c l) n -> l c n", l=64))
        fb = k.sb("fb", [64, 8]); ib = k.sb("ib", [64, 8])
        k.dma(fb, fb[:], e.ev_fb, e.ev_fb.ap().partition_broadcast(64).rearrange("p o n -> p (o n)"))
        k.dma(ib, ib[:], e.ev_ib, e.ev_ib.ap().partition_broadcast(64).rearrange("p o n -> p (o n)"))
        tri = k.sb("tri", [64, 2, 64]); ones = k.sb("ones", [64, 128])
        k.dma(tri, tri[:], e.ctri, e.ctri.ap().rearrange("r s l -> s r l"))
        k.memset(k.dve, ones, ones[:], 1.0)
        if c.stop == "ml_2":
            return
        z = k.sb("z", [64, 2, 36, 4]); nlf = k.sb("nlf", [64, 2, 36, 4]); ig = k.sb("ig", [64, 2, 36, 4])
        A = k.sb("A", [64, 2, 36, 4]); Bk = k.sb("Bk", [64, 2, 36, 4]); gdec = k.sb("gdec", [128, 2, 36, 4])
        for d in range(2):
            k.tt(k.dve, z, z[:, d], g, g[:, :, 8 + 4 * d:12 + 4 * d], fb, bc(fb[:, None, 4 * d:4 * d + 4], [64, 36, 4]), ALU.add)
            k.tt(k.dve, ig, ig[:, d], g, g[:, :, 4 * d:4 * d + 4], ib, bc(ib[:, None, 4 * d:4 * d + 4], [64, 36, 4]), ALU.add)
        k.actv(z, z[:], z, z[:], AF.Exp, scale=-1.0)
        k.actv(nlf, nlf[:], z, z[:], AF.Ln, bias=1.0)
        if c.stop == "ml_3":
            return
        with k.scope():
            pF = k.ps("pF", [64, 2, 144]); pG = k.ps("pG", [128, 288])
            for d in range(2):
                k.mm(pF, pF[:, d, :], tri, tri[:, d, :], nlf, nlf[:, d].rearrange("p c h -> p (c h)"))
            k.mm(pG, pG[:], ones, ones[:], nlf, nlf[:].rearrange("p d c h -> p (d c h)"))
            if c.stop == "ml_4":
                k.cp(k.dve, A, A[:].rearrange("p d c h -> p d (c h)"), pF, pF[:])
                k.cp(k.dve, gdec, gdec[:].rearrange("p d c h -> p (d c h)"), pG, pG[:])
            Af = A[:].rearrange("p d c h -> p d (c h)"); Bf = Bk[:].rearrange("p d c h -> p d (c h)")
            if c.stop != "ml_4":
                if c.stop != "ml_5b":
                    k.actv(A, Af, pF, pF[:], AF.Exp, scale=-1.0)
                if c.stop != "ml_5a":
                    k.tt(k.dve, Bk, Bf, ig, ig[:].rearrange("p d c h -> p d (c h)"), pF, pF[:], ALU.add)
                if c.stop not in ("ml_5", "ml_5a", "ml_5b"):
                    k.actv(Bk, Bk[:], Bk, Bk[:], AF.Exp)
                    k.actv(gdec, gdec[:].rearrange("p d c h -> p (d c h)"), pG, pG[:], AF.Exp, scale=-1.0)
        if c.stop in ("ml_a", "ml_4", "ml_5", "ml_5a", "ml_5b"):
            return
        C32 = [k.sb("C32_%d" % d, [128, 4, 132]) for d in range(2)]
        Cb = [k.sb("Cb_%d" % d, [128, 4, 132], BF16) for d in range(2)]
        for d in range(2):
            k.memset(k.dve, C32[d], C32[d][:], 0.0)
            k.memset(k.dve, Cb[d], Cb[d][:], 0.0)
        pS = [k.ps("pS%d" % d, [64, 4, 64]) for d in range(2)]
        pN = [[k.ps("pN%d_%d" % (d, i), [64, 2, 256]) for i in range(2)] for d in range(2)]
        pC = [k.ps("pC%d" % i, [128, 2, 256]) for i in range(2)]
        kt = [k.sb("kt%d" % d, [64, 4, 128], BF16) for d in range(2)]
        MB = [k.sb("MB%d" % d, [64, 4, 64]) for d in range(2)]
        Pt = [k.sb("Pt%d" % d, [64, 4, 64], BF16) for d in range(2)]
        sm = [k.sb("sm%d" % d, [64, 4, 4]) for d in range(2)]
        ho = [k.sb("ho%d" % d, [64, 4, 128]) for d in range(2)]
        for step in range(36):
            for d in range(2):
                ch = step if d == 0 else ORDB[step]
                tok = slice(ch * 64, (ch + 1) * 64)
                Bs = Bk[:, d, ch, :]
                As = A[:, d, ch, :]
                k.tt(k.dve, kt[d], kt[d][:], ktok, ktok[:, ch, :].rearrange("p (h e) -> p h e", h=4),
                     Bk, bc(Bs[:, :, None], [64, 4, 128]), ALU.mult)
                k.tt(k.dve, MB[d], MB[d][:], tri, bc(tri[:, d:d + 1, :], [64, 4, 64]), Bk, bc(Bs[:, :, None], [64, 4, 64]), ALU.mult)
                for h in range(4):
                    k.mm(pS[d], pS[d][:, h, :], kT, kT[:, h, tok], qT, qT[:, h, tok])
                k.tt(k.dve, Pt[d], Pt[d][:], pS[d], pS[d][:], MB[d], MB[d][:], ALU.mult)
                for h in range(4):
                    pn = pN[d][h // 2]
                    k.mm(pn, pn[:, h % 2, 0:132], Pt[d], Pt[d][:, h, :], v1, v1[:, ch, h, :], True, False)
                    k.mm(pn, pn[:, h % 2, 0:132], qT, qT[:, h, tok], Cb[d], Cb[d][:, h, :], False, True)
                s_ = sm[d]
                for i in range(2):
                    pn = pN[d][i]
                    k.tt(k.dve, s_, s_[:, 2 * i:2 * i + 2, 0], pn, pn[:, :, 128], A, As[:, 2 * i:2 * i + 2], ALU.mult)
                k.stt(k.dve, s_, s_[:, :, 1], s_, s_[:, :, 0], -1.0, s_, s_[:, :, 0], ALU.mult, ALU.max)
                k.ts(k.dve, s_, s_[:, :, 1], s_, s_[:, :, 1], 1.0, None, ALU.max)
                k.op(k.dve, lambda: nc.vector.reciprocal(out=s_[:, :, 2], in_=s_[:, :, 1]), [s_], [s_])
                k.tt(k.dve, s_, s_[:, :, 3], s_, s_[:, :, 2], A, As, ALU.mult)
                for i in range(2):
                    pn = pN[d][i]
                    k.tt(k.dve, ho[d], ho[d][:, 2 * i:2 * i + 2, :], pn, pn[:, :, 0:128],
                         s_, bc(s_[:, 2 * i:2 * i + 2, 3:4], [64, 2, 128]), ALU.mult)
                k.dma(L['hA'], L['hA'].ap()[d, tok, :], ho[d], ho[d][:].rearrange("p h e -> p (h e)"), q=k.sp)
                for i in range(2):
                    pc_ = pC[i]
                    for hh in range(2):
                        h = 2 * i + hh
                        k.mm(pc_, pc_[:, hh, 0:132], kt[d], kt[d][:, h, :], v1, v1[:, ch, h, :])
                    k.tt(k.dve, C32[d], C32[d][:, 2 * i:2 * i + 2, :], pc_, pc_[:, :, 0:132], C32[d], C32[d][:, 2 * i:2 * i + 2, :], ALU.add)
                k.tt(k.dve, C32[d], C32[d][:], C32[d], C32[d][:], gdec, bc(gdec[:, d, ch, :][:, :, None], [128, 4, 132]), ALU.mult)
                k.cp(k.act, Cb[d], Cb[d][:], C32[d], C32[d][:])
            if c.stop == "ml_b" and step == 0:
                return


def s5_phase(k, c, e):
    nc = k.nc
    L = c.l0
    TWO_PI = 2 * PI
    with k.scope():
        Mw = k.sb("Mw", [128, 64, 128], BF16)
        WT = [k.sb("WT%d" % i, [128, 64, 64], BF16) for i in range(2)]
        RC = [k.sb("RC%d" % i, [64, 64, 128], BF16) for i in range(2)]
        AR2 = k.sb("AR2", [64, 2, 64]); AI2 = k.sb("AI2", [64, 2, 64])
        with k.scope():
            lre = k.sb("lre", [64, 64]); lim = k.sb("lim", [64, 64]); dt = k.sb("dt", [64, 64])
            c.ltst = [k.sb("ltst%d" % i, [64, 128]) for i in range(2)]
            c.ltps = [k.ps("ltps%d" % i, [128, 64]) for i in range(2)]
            c.lt_i = 0
            load_T(k, c, lre, lre[:], e.ev_lre, e.ev_lre.ap().rearrange("r g n -> (r g) n"), 64, ncols=64)
            load_T(k, c, lim, lim[:], e.ev_lim, e.ev_lim.ap().rearrange("r g n -> (r g) n"), 64, ncols=64)
            k.dma(dt, dt[:], e.ev_ldt, e.ev_ldt.ap().partition_broadcast(64).rearrange("p o n -> p (o n)"))
            k.actv(dt, dt[:], dt, dt[:], AF.Exp)
            ldr = k.sb("ldr", [64, 64]); ang = k.sb("ang", [64, 64])
            k.tt(k.dve, ldr, ldr[:], lre, lre[:], dt, dt[:], ALU.mult)
            k.tt(k.dve, ang, ang[:], lim, lim[:], dt, dt[:], ALU.mult)
            mg = k.sb("mg", [64, 16, 64]); sn = k.sb("sn", [64, 9, 64]); cs = k.sb("cs", [64, 9, 64])
            for ti, tau in enumerate(range(-7, 9)):
                k.actv(mg, mg[:, ti, :], ldr, ldr[:], AF.Exp, scale=float(tau))
            k.memset(k.dve, sn, sn[:, 0, :], 0.0); k.memset(k.dve, cs, cs[:, 0, :], 1.0)
            k.actv(sn, sn[:, 1, :], ang, ang[:], AF.Sin, scale=1.0 / 16)
            k.actv(cs, cs[:, 1, :], ang, ang[:], AF.Sin, bias=c.epsb[0:64, 1:2], scale=1.0 / 16, extra=[c.epsb])
            q1 = k.sb("q1", [64, 64]); q2 = k.sb("q2", [64, 64])
            for _ in range(4):
                k.tt(k.dve, q1, q1[:], sn, sn[:, 1, :], cs, cs[:, 1, :], ALU.mult)
                k.tt(k.dve, q2, q2[:], sn, sn[:, 1, :], sn, sn[:, 1, :], ALU.mult)
                k.ts(k.dve, sn, sn[:, 1, :], q1, q1[:], 2.0, None, ALU.mult)
                k.ts(k.dve, cs, cs[:, 1, :], q2, q2[:], -2.0, 1.0, ALU.mult, ALU.add)
            for tau in range(2, 9):
                k.tt(k.dve, q1, q1[:], cs, cs[:, tau - 1, :], cs, cs[:, 1, :], ALU.mult)
                k.tt(k.dve, q2, q2[:], sn, sn[:, tau - 1, :], sn, sn[:, 1, :], ALU.mult)
                k.tt(k.dve, cs, cs[:, tau, :], q1, q1[:], q2, q2[:], ALU.subtract)
                k.tt(k.dve, q1, q1[:], sn, sn[:, tau - 1, :], cs, cs[:, 1, :], ALU.mult)
                k.tt(k.dve, q2, q2[:], cs, cs[:, tau - 1, :], sn, sn[:, 1, :], ALU.mult)
                k.tt(k.dve, sn, sn[:, tau, :], q1, q1[:], q2, q2[:], ALU.add)
            pwr = k.sb("pwr", [64, 16, 64]); pwi = k.sb("pwi", [64, 16, 64])
            for ti, tau in enumerate(range(-7, 9)):
                at = abs(tau)
                k.tt(k.dve, pwr, pwr[:, ti, :], mg, mg[:, ti, :], cs, cs[:, at, :], ALU.mult)
                if tau >= 0:
                    k.tt(k.dve, pwi, pwi[:, ti, :], mg, mg[:, ti, :], sn, sn[:, at, :], ALU.mult)
                else:
                    k.stt(k.dve, pwi, pwi[:, ti, :], mg, mg[:, ti, :], -1.0, sn, sn[:, at, :], ALU.mult, ALU.mult)
            for s_ in range(2):
                k.cp(k.dve, AR2, AR2[:, s_, :], pwr, pwr[:, 15, :])
            k.ts(k.dve, AI2, AI2[:, 0, :], pwi, pwi[:, 15, :], -1.0, None, ALU.mult)
            k.cp(k.dve, AI2, AI2[:, 1, :], pwi, pwi[:, 15, :])
            nr = k.sb("nr", [64, 64]); den = k.sb("den", [64, 64]); t1 = k.sb("t1", [64, 64]); t2 = k.sb("t2", [64, 64])
            cor = k.sb("cor", [64, 64]); coi = k.sb("coi", [64, 64])
            k.ts(k.dve, nr, nr[:], pwr, pwr[:, 8, :], -1.0, None, ALU.add)
            k.tt(k.dve, den, den[:], lre, lre[:], lre, lre[:], ALU.mult)
            k.tt(k.dve, t1, t1[:], lim, lim[:], lim, lim[:], ALU.mult)
            k.tt(k.dve, den, den[:], den, den[:], t1, t1[:], ALU.add)
            k.op(k.dve, lambda: nc.vector.reciprocal(out=den[:], in_=den[:]), [den], [den])
            k.tt(k.dve, t1, t1[:], nr, nr[:], lre, lre[:], ALU.mult)
            k.tt(k.dve, t2, t2[:], pwi, pwi[:, 8, :], lim, lim[:], ALU.mult)
            k.tt(k.dve, t1, t1[:], t1, t1[:], t2, t2[:], ALU.add)
            k.tt(k.dve, cor, cor[:], t1, t1[:], den, den[:], ALU.mult)
            k.tt(k.dve, t1, t1[:], pwi, pwi[:, 8, :], lre, lre[:], ALU.mult)
            k.tt(k.dve, t2, t2[:], nr, nr[:], lim, lim[:], ALU.mult)
            k.tt(k.dve, t1, t1[:], t1, t1[:], t2, t2[:], ALU.subtract)
            k.tt(k.dve, coi, coi[:], t1, t1[:], den, den[:], ALU.mult)
            bre = k.sb("bre", [64, 64, 16]); bim = k.sb("bim", [64, 64, 16])
            for r in range(2):
                for g0 in range(0, 32, 4):
                    k.dma(bre, bre[:, r * 32 + g0:r * 32 + g0 + 4, :], e.ev_bre, e.ev_bre.ap()[r, g0:g0 + 4].rearrange("g n p -> n g p"))
                    k.dma(bim, bim[:, r * 32 + g0:r * 32 + g0 + 4, :], e.ev_bim, e.ev_bim.ap()[r, g0:g0 + 4].rearrange("g n p -> n g p"))
            bbr = k.sb("bbr", [64, 64, 16]); bbi = k.sb("bbi", [64, 64, 16])
            u1 = k.sb("u1", [64, 64, 16]); u2 = k.sb("u2", [64, 64, 16])
            corb = bc(cor[:, :, None], [64, 64, 16]); coib = bc(coi[:, :, None], [64, 64, 16])
            k.tt(k.dve, u1, u1[:], bre, bre[:], cor, corb, ALU.mult)
            k.tt(k.dve, u2, u2[:], bim, bim[:], coi, coib, ALU.mult)
            k.tt(k.dve, bbr, bbr[:], u1, u1[:], u2, u2[:], ALU.subtract)
            k.tt(k.dve, u1, u1[:], bim, bim[:], cor, corb, ALU.mult)
            k.tt(k.dve, u2, u2[:], bre, bre[:], coi, coib, ALU.mult)
            k.tt(k.dve, bbi, bbi[:], u1, u1[:], u2, u2[:], ALU.add)
            cTr = k.sb("cTr", [64, 64, 16]); cTi = k.sb("cTi", [64, 64, 16])
            cst = k.sb("cst", [128, 8, 64])
            pCt = [k.ps("pCt%d" % i, [64, 4, 128]) for i in range(2)]
            for src, dst in ((e.ev_cre, cTr), (e.ev_cim, cTi)):
                for t0 in range(0, 8, 2):
                    k.dma(cst, cst[:, t0:t0 + 2, :], src, src.ap()[t0 * 128:(t0 + 2) * 128, :].rearrange("(t p) n -> p t n", p=128))
                for t in range(8):
                    p = pCt[t // 4]
                    k.tr(p, p[:, t % 4, :], cst, cst[:, t, :], c.id32, c.id32[:])
                for hf in range(2):
                    k.cp(k.act, dst, dst[:, hf * 32:(hf + 1) * 32, :].rearrange("n g p -> n (g p)"),
                         pCt[hf], pCt[hf][:].rearrange("n t m -> n (t m)"))
            maskM = k.sb("maskM", [128, 2, 128])
            k.dma(maskM, maskM[:], e.cmaskM, e.cmaskM.ap().rearrange("r a b -> a r b"))
            w1_ = k.sb("w1_", [64, 32, 16]); w2_ = k.sb("w2_", [64, 32, 16])

            def cmul(r, powf, sr, si, dr, dr_ap, di, di_ap, neg_im=False):
                rs = slice(r * 32, (r + 1) * 32)
                for i in range(8):
                    ti = powf(i) + 7
                    pr = bc(pwr[:, ti, rs][:, :, None], [64, 32, 16]); pi_ = bc(pwi[:, ti, rs][:, :, None], [64, 32, 16])
                    k.tt(k.dve, w1_, w1_[:], sr, sr[:, rs, :], pwr, pr, ALU.mult)
                    k.tt(k.pool, w2_, w2_[:], si, si[:, rs, :], pwi, pi_, ALU.mult)
                    k.tt(k.dve, dr, dr_ap(i), w1_, w1_[:], w2_, w2_[:], ALU.subtract)
                    k.tt(k.dve, w1_, w1_[:], si, si[:, rs, :], pwr, pr, ALU.mult)
                    k.tt(k.pool, w2_, w2_[:], sr, sr[:, rs, :], pwi, pi_, ALU.mult)
                    if neg_im:
                        k.stt(k.dve, di, di_ap(i), w1_, w1_[:], -1.0, w2_, w2_[:], ALU.mult, ALU.subtract)
                    else:
                        k.tt(k.dve, di, di_ap(i), w1_, w1_[:], w2_, w2_[:], ALU.add)

            EBr = k.sb("EBr", [64, 32, 8, 16]); EBi = k.sb("EBi", [64, 32, 8, 16])
            ECr = k.sb("ECr", [64, 32, 8, 16]); ECi = k.sb("ECi", [64, 32, 8, 16])
            pM = [k.ps("pM%d" % i, [128, 4, 128]) for i in range(2)]
            pW = [k.ps("pW%d" % i, [128, 8, 64]) for i in range(2)]
            for r in range(2):
                sig = (lambda i: i) if r == 0 else (lambda i: 7 - i)
                rs = slice(r * 32, (r + 1) * 32)
                cmul(r, lambda i: -sig(i), bbr, bbi, EBr, lambda i: EBr[:, :, i, :], EBi, lambda i: EBi[:, :, i, :])
                cmul(r, lambda i: sig(i), cTr, cTi, ECr, lambda i: ECr[:, :, i, :], ECi, lambda i: ECi[:, :, i, :], neg_im=True)
                for g0 in range(0, 32, 4):
                    p = pM[(g0 // 4) % 2]
                    for gg in range(4):
                        g = g0 + gg
                        k.mm(p, p[:, gg, :], EBr, EBr[:, g].rearrange("n i p -> n (i p)"), ECr, ECr[:, g].rearrange("n i p -> n (i p)"), True, False)
                        k.mm(p, p[:, gg, :], EBi, EBi[:, g].rearrange("n i p -> n (i p)"), ECi, ECi[:, g].rearrange("n i p -> n (i p)"), False, True)
                    k.tt(k.dve, Mw, Mw[:, r * 32 + g0:r * 32 + g0 + 4, :], p, p[:], maskM, bc(maskM[:, r:r + 1, :], [128, 4, 128]), ALU.mult)
                cmul(r, lambda i: sig(i) + 1, cTr, cTi,
                     RC[0], lambda i: RC[0][:, rs, :].rearrange("n g (j p) -> n g j p", j=8)[:, :, i, :],
                     RC[1], lambda i: RC[1][:, rs, :].rearrange("n g (j p) -> n g j p", j=8)[:, :, i, :], neg_im=True)
                cmul(r, lambda i: 7 - sig(i), bbr, bbi, ECr, lambda i: ECr[:, :, i, :], ECi, lambda i: ECi[:, :, i, :])
                for comp, src in enumerate((ECr, ECi)):
                    for g0 in range(0, 32, 8):
                        p = pW[(g0 // 8) % 2]
                        for gg in range(8):
                            k.tr(p, p[:, gg, :], src, src[:, g0 + gg].rearrange("n i p -> n (i p)"), c.id32, c.id32[0:64, 0:64])
                        k.cp(k.act, WT[comp], WT[comp][:, r * 32 + g0:r * 32 + g0 + 8, :], p, p[:])
        X = k.sb("X", [128, 32, 288], BF16)
        SaL = [k.sb("Sa%d" % r_, [64, 2, 32, 290], BF16) for r_ in range(2)]
        with k.scope():
            u32 = k.sb("u32", [128, 8, 512]); u16 = k.sb("u16", [128, 32, 128], BF16)
            pX = [k.ps("pX%d" % i, [128, 8, 128], BF16) for i in range(2)]
            it = 0
            for ct, (c0, n) in enumerate(((0, 128), (128, 128), (256, 32))):
                k.dma(u32, u32[0:n], L['u'], L['u'].ap()[8 * c0:8 * (c0 + n), :].rearrange("(c i) ch -> c i ch", i=8))
                k.cp(k.dve, u16, u16[0:n].rearrange("c g (i p) -> c g i p", i=8), u32, u32[0:n].rearrange("c i (g p) -> c g i p", g=32))
                for g0 in range(0, 32, 8):
                    p = pX[it % 2]; it += 1
                    for gg in range(8):
                        g = g0 + gg
                        k.tr(p, p[:, gg, 0:n], u16, u16[0:n, g, :], c.idb, c.idb[0:n, 0:n])
                    k.cp(k.act, X, X[:, g0:g0 + 8, c0:c0 + n], p, p[:, :, 0:n])
        with k.scope():
            pB = [k.ps("pB%d" % i, [64, 288]) for i in range(4)]
            it = 0
            for rg in range(64):
                for comp in range(2):
                    p = pB[it % 4]; it += 1
                    k.mm(p, p[:], WT[comp], WT[comp][:, rg, :], X, X[:, rg % 32, :])
                    off = 0 if rg < 32 else 1
                    Sa = SaL[rg // 32]
                    k.cp(k.act if it % 2 else k.dve, Sa, Sa[:, comp, rg % 32, off:off + 288], p, p[:])
        R3 = [k.sb("R3_%d" % r_, [64, 3, 32]) for r_ in range(2)]
        P1 = [k.sb("P1_%d" % r_, [64, 2, 32]) for r_ in range(2)]; P2 = [k.sb("P2_%d" % r_, [64, 2, 32]) for r_ in range(2)]
        for r_ in range(2):
            k.memset(k.dve, R3[r_], R3[r_][:], 0.0)
        for step in range(288):
            cols = (step, ORDB8[step] + 1)
            bvs = [SaL[r_][:, :, :, cols[r_]] for r_ in range(2)]
            rsl = [slice(0, 32), slice(32, 64)]
            for r_ in range(2):
                k.tt(k.dve, P1[r_], P1[r_][:], AR2, AR2[:, :, rsl[r_]], R3[r_], R3[r_][:, 1:3, :], ALU.mult)
            for r_ in range(2):
                k.tt(k.dve, P2[r_], P2[r_][:], AI2, AI2[:, :, rsl[r_]], R3[r_], R3[r_][:, 0:2, :], ALU.mult)
            for r_ in range(2):
                k.tt(k.dve, P1[r_], P1[r_][:], P1[r_], P1[r_][:], P2[r_], P2[r_][:], ALU.add)
            for r_ in range(2):
                k.tt(k.dve, R3[r_], R3[r_][:, 1:3, :], P1[r_], P1[r_][:], SaL[r_], bvs[r_], ALU.add)
            for r_ in range(2):
                k.cp(k.dve, R3[r_], R3[r_][:, 0, :], R3[r_], R3[r_][:, 2, :])
            for r_ in range(2):
                k.cp(k.dve, SaL[r_], bvs[r_], R3[r_], R3[r_][:, 1:3, :])
        k.cp(k.dve, SaL[1], SaL[1][:, :, :, 0:1], SaL[1], SaL[1][:, :, :, 1:2])
        with k.scope():
            pY = [k.ps("pY%d" % i, [128, 288]) for i in range(2)]
            pZ = [k.ps("pZ%d" % i, [128, 4, 128]) for i in range(2)]
            Yq = k.sb("Yq", [128, 8, 288]); Y2q = [k.sb("Y2q%d" % i, [128, 8, 128]) for i in range(2)]
            it = 0; iz = 0; iy = 0
            for q in range(4):
                for gl in range(8):
                    g = q * 8 + gl
                    p = pY[it % 2]; it += 1
                    k.mm(p, p[:], Mw, Mw[:, g, :], X, X[:, g, :], True, False)
                    k.mm(p, p[:], Mw, Mw[:, 32 + g, :], X, X[:, g, :], False, False)
                    for comp in range(2):
                        k.mm(p, p[:, 1:288], RC[comp], RC[comp][:, g, :], SaL[0], SaL[0][:, comp, g, 0:287], False, False)
                    rg = 32 + g
                    for comp in range(2):
                        k.mm(p, p[:, 0:31], RC[comp], RC[comp][:, rg, :], SaL[1], SaL[1][:, comp, g, 2:33], False, False)
                        k.mm(p, p[:, 32:287], RC[comp], RC[comp][:, rg, :], SaL[1], SaL[1][:, comp, g, 34:289], False, False)
                        k.mm(p, p[:, 287:288], RC[comp], RC[comp][:, rg, :], SaL[1], SaL[1][:, comp, g, 0:1], False, comp == 1)
                    k.cp(k.act, Yq, Yq[:, gl, :], p, p[:])
                for ct, (c0, n) in enumerate(((0, 128), (128, 128), (256, 32))):
                    y2 = Y2q[iy % 2]; iy += 1
                    for gl in range(8):
                        if gl % 4 == 0:
                            pz = pZ[iz % 2]; iz += 1
                        k.tr(pz, pz[0:n, gl % 4, :], Yq, Yq[:, gl, c0:c0 + n], c.id32, c.id32[:])
                        k.cp(k.dve if gl % 2 else k.act, y2, y2[0:n, :, gl * 16:(gl + 1) * 16],
                             pz, pz[0:n, gl % 4, :].rearrange("c (j p) -> c j p", j=8))
                    for jh in range(2):
                        k.dma(L['yS'], L['yS'].ap()[8 * c0:8 * (c0 + n), q * 128:(q + 1) * 128].rearrange("(c j) ch -> c j ch", j=8)[:, jh * 4:(jh + 1) * 4, :],
                              y2, y2[0:n, jh * 4:(jh + 1) * 4, :], q=k.pool)


def finish0_phase(k, c, e):
    nc = k.nc
    L = c.l0
    xs = e.xs
    with k.scope():
        c.wstage = [k.sb("wst%d" % i, [128, 4096]) for i in range(2)]
        wglu = k.sb("wglu", [128, 4, 1024], BF16); wout = k.sb("wout", [128, 8, 1024], BF16)
        for cb in range(2):
            load_w_bf16(k, c, wglu, wglu[:, :, cb * 512:(cb + 1) * 512], e.ev_wglu, e.ev_wglu.ap()[:, cb * 512:(cb + 1) * 512], 512, kchunks=4)
            load_w_bf16(k, c, wout, wout[:, :, cb * 512:(cb + 1) * 512], e.ev_wout, e.ev_wout.ap()[:, cb * 512:(cb + 1) * 512], 512)
        hwb = k.sb("hwb", [128, 512]); dsk = k.sb("dsk", [128, 512])
        k.dma(hwb, hwb[:], e.ev_hw, e.ev_hw.ap().partition_broadcast(128).rearrange("p o n -> p (o n)"))
        k.dma(dsk, dsk[:], e.ev_d, e.ev_d.ap().partition_broadcast(128).rearrange("p o n -> p (o n)"))
        gate = [k.sb("gate%d" % w, [128, D]) for w in range(2)]
        e.load_gate(gate[0], 0, 0, 2); e.load_gate(gate[1], 0, 1, 2)
        hA = [k.sb("hA%d" % i, [128, 2, 512]) for i in range(2)]
        ot = [k.sb("ot%d" % i, [128, 512]) for i in range(2)]
        ut = [k.sb("ut%d" % i, [128, 512]) for i in range(2)]
        yt = [k.sb("yt%d" % i, [128, 512]) for i in range(2)]
        xt = [k.sb("xt%d" % i, [128, D]) for i in range(2)]
        w1s = [k.sb("fw1_%d" % i, [128, 512]) for i in range(2)]; w2s = [k.sb("fw2_%d" % i, [128, 512]) for i in range(2)]
        sts = [k.sb("fst_%d" % i, [128, 8]) for i in range(2)]
        cats = [k.sb("cat%d" % i, [128, D], BF16) for i in range(2)]; ybbs = [k.sb("ybb%d" % i, [128, 512], BF16) for i in range(2)]
        ybTs = [k.sb("ybT%d" % i, [128, 4, 128], BF16) for i in range(2)]; catTs = [k.sb("catT%d" % i, [128, 8, 128], BF16) for i in range(2)]
        pTbs = [k.ps("pTb%d" % i, [128, 8, 128], BF16) for i in range(2)]
        pG = [k.ps("pGl%d" % i, [128, 512]) for i in range(2)]
        pO = [k.ps("pO%d" % i, [128, 512]) for i in range(2)]
        yos = [k.sb("yo%d" % i, [128, D]) for i in range(2)]
        for t in range(NT):
            tok = slice(t * 128, (t + 1) * 128)
            h_, o_, u_, y_, x_ = hA[t % 2], ot[t % 2], ut[t % 2], yt[t % 2], xt[t % 2]
            w1, w2, st, cat, ybb, ybT, catT, pTb, yo = w1s[t % 2], w2s[t % 2], sts[t % 2], cats[t % 2], ybbs[t % 2], ybTs[t % 2], catTs[t % 2], pTbs[t % 2], yos[t % 2]
            k.dma(h_, h_[:], L['hA'], L['hA'].ap()[:, tok, :].rearrange("r t n -> t r n"))
            k.dma(o_, o_[:], L['o'], L['o'].ap()[tok, :])
            k.dma(u_, u_[:], L['u'], L['u'].ap()[tok, :])
            k.dma(y_, y_[:], L['yS'], L['yS'].ap()[tok, :])
            k.dma(x_, x_[:], xs, xs.ap()[tok, :])
            k.tt(k.dve, w1, w1[:], h_, h_[:, 0, :], h_, h_[:, 1, :], ALU.add)
            k.tt(k.dve, w2, w2[:], w1, w1[:], w1, w1[:], ALU.mult)
            k.op(k.dve, lambda: nc.vector.reduce_sum(out=st[:, 0:4], in_=w2[:].rearrange("p (h e) -> p h e", h=4), axis=AX.X), [st], [w2])
            k.actv(st, st[:, 0:4], st, st[:, 0:4], AF.Ln, bias=c.epsb[:, 0:1], scale=1.0 / 128, extra=[c.epsb])
            k.actv(st, st[:, 4:8], st, st[:, 0:4], AF.Exp, scale=-0.5)
            k.tt(k.dve, w1, w1[:].rearrange("p (h e) -> p h e", h=4), w1, w1[:].rearrange("p (h e) -> p h e", h=4),
                 st, bc(st[:, 4:8][:, :, None], [128, 4, 128]), ALU.mult)
            k.tt(k.dve, w1, w1[:], w1, w1[:], hwb, hwb[:], ALU.mult)
            k.actv(o_, o_[:], o_, o_[:], AF.Sigmoid)
            k.tt(k.dve, cat, cat[:, 0:512], w1, w1[:], o_, o_[:], ALU.mult)
            k.tt(k.dve, w2, w2[:], u_, u_[:], dsk, dsk[:], ALU.mult)
            k.tt(k.dve, w2, w2[:], w2, w2[:], y_, y_[:], ALU.add)
            k.tt(k.pool, y_, y_[:], w2, w2[:], w2, w2[:], ALU.mult)
            k.ts(k.dve, y_, y_[:], y_, y_[:], 0.044715, 1.0, ALU.mult, ALU.add)
            k.tt(k.dve, y_, y_[:], y_, y_[:], w2, w2[:], ALU.mult)
            k.actv(y_, y_[:], y_, y_[:], AF.Sigmoid, scale=2.0 * math.sqrt(2.0 / PI))
            k.tt(k.dve, ybb, ybb[:], y_, y_[:], w2, w2[:], ALU.mult)
            for j in range(4):
                k.tr(pTb, pTb[:, j, :], ybb, ybb[:, j * 128:(j + 1) * 128], c.idb, c.idb[:])
            k.cp(k.act, ybT, ybT[:], pTb, pTb[:, 0:4, :])
            for cb in range(2):
                for kc in range(4):
                    k.mm(pG[cb], pG[cb][:], ybT, ybT[:, kc, :], wglu, wglu[:, kc, cb * 512:(cb + 1) * 512], kc == 0, kc == 3)
            k.actv(w2, w2[:], pG[1], pG[1][:], AF.Sigmoid)
            k.tt(k.dve, cat, cat[:, 512:1024], pG[0], pG[0][:], w2, w2[:], ALU.mult)
            for j in range(8):
                k.tr(pTb, pTb[:, j, :], cat, cat[:, j * 128:(j + 1) * 128], c.idb, c.idb[:])
            k.cp(k.act, catT, catT[:], pTb, pTb[:])
            g = gate[1 if t < 2 else 0]
            for cb in range(2):
                for kc in range(8):
                    k.mm(pO[cb], pO[cb][:], catT, catT[:, kc, :], wout, wout[:, kc, cb * 512:(cb + 1) * 512], kc == 0, kc == 7)
                k.tt(k.dve, yo, yo[:, cb * 512:(cb + 1) * 512], pO[cb], pO[cb][:], g, g[:, cb * 512:(cb + 1) * 512], ALU.mult)
            k.tt(k.pool, yo, yo[:], yo, yo[:], x_, x_[:], ALU.add)
            k.dma(xs, xs.ap()[tok, :], yo, yo[:], q=k.pool)


def layer1(k, c, env):
    e = E(env)
    nc = k.nc
    xs = e.xs
    z_d = k.dram("z_d", [T, D]); gt_d = k.dram("gt_d", [T, 32]); o1_d = k.dram("o1_d", [2, T, D])
    with k.scope():
        qT = k.sb("gqT", [128, 8, T], BF16); kT = k.sb("gkT", [128, 8, T], BF16); vT = k.sb("gvT", [128, 8, T], BF16)
        with k.scope():
            hT = k.sb("hT", [128, 8, T], BF16)
            with k.scope():
                c.xt = [k.sb("xt%d" % i, [128, D]) for i in range(2)]
                c.sq = [k.sb("sq%d" % i, [128, D]) for i in range(2)]; c.ss = [k.sb("ss%d" % i, [128, 4]) for i in range(2)]
                c.pT = [k.ps("pT%d" % i, [128, 4, 128]) for i in range(2)]
                norm_mod(k, c, xs, list(range(NT)), e.ab_fn(0, 1), hT)
            c.wstage = [k.sb("wst%d" % i, [128, 2048]) for i in range(2)]
            c.ltst = [k.sb("ltst%d" % i, [64, 128]) for i in range(2)]
            c.ltps = [k.ps("ltps%d" % i, [128, 64]) for i in range(2)]
            c.lt_i = 0
            cw = k.sb("cw", [128, 24, 9])
            for ci in range(24):
                load_T(k, c, cw, cw[:, ci, :], e.od_conv, e.od_conv.ap()[:, ci * 128:(ci + 1) * 128], 9)
            ones32 = k.sb("ones32", [128, 128]); k.memset(k.dve, ones32, ones32[:], 1.0)
            wch = [k.sb("wch%d" % i, [128, 8, 256], BF16) for i in range(2)]
            P32 = k.sb("P32", [128, T]); Cv = k.sb("Cv", [128, T]); S32 = P32; Q32 = Cv
            rs = k.sb("rs", [128, 512])
            pp = [k.ps("pp%d" % i, [128, 512]) for i in range(3)]
            blocks = [(0, 512), (512, 512), (1024, 512), (1536, 512), (2048, 256)]
            it = 0
            for ci in range(24):
                if ci % 2 == 0:
                    wc = wch[(ci // 2) % 2]
                    load_w_bf16(k, c, wc, wc[:], e.od_w_in, e.od_w_in.ap()[:, ci * 128:(ci + 2) * 128], 256)
                wo = (ci % 2) * 128
                for (t0, tn) in blocks:
                    p = pp[it % 3]; it += 1
                    for kc in range(8):
                        k.mm(p, p[:, 0:tn], wc, wc[:, kc, wo:wo + 128], hT, hT[:, kc, t0:t0 + tn], kc == 0, kc == 7)
                    k.cp(k.act, P32, P32[:, t0:t0 + tn], p, p[:, 0:tn])
                w_ = lambda tap: cw[:, ci, tap:tap + 1]
                k.ts(k.dve, Cv, Cv[:], P32, P32[:], w_(4), None, ALU.mult, extra=[cw])
                k.stt(k.dve, Cv, Cv[:, 1:256], P32, P32[:, 0:255], w_(3), Cv, Cv[:, 1:256], ALU.mult, ALU.add, extra=[cw])
                k.stt(k.dve, Cv, Cv[:, 0:255], P32, P32[:, 1:256], w_(5), Cv, Cv[:, 0:255], ALU.mult, ALU.add, extra=[cw])
                Pl = P32[:, 256:T].rearrange("p (r q) -> p r q", q=64); Cl = Cv[:, 256:T].rearrange("p (r q) -> p r q", q=64)
                for a in range(3):
                    for b in range(3):
                        if a == 1 and b == 1:
                            continue
                        dr, dc = a - 1, b - 1
                        r0, r1 = max(0, -dr), 32 - max(0, dr)
                        c0, c1 = max(0, -dc), 64 - max(0, dc)
                        k.stt(k.dve, Cv, Cl[:, r0:r1, c0:c1], P32, Pl[:, r0 + dr:r1 + dr, c0 + dc:c1 + dc],
                              w_(a * 3 + b), Cv, Cl[:, r0:r1, c0:c1], ALU.mult, ALU.add, extra=[cw])
                h = ci % 8
                if ci >= 16:
                    k.actv(vT, vT[:, h, :], Cv, Cv[:], AF.Silu)
                else:
                    k.actv(S32, S32[:], Cv, Cv[:], AF.Silu)
                    k.tt(k.pool, Q32, Q32[:], S32, S32[:], S32, S32[:], ALU.mult)
                    dst = qT if ci < 8 else kT
                    for (t0, tn) in blocks:
                        p = pp[it % 3]; it += 1
                        k.mm(p, p[:, 0:tn], ones32, ones32[:], Q32, Q32[:, t0:t0 + tn])
                        k.actv(rs, rs[:, 0:tn], p, p[:, 0:tn], AF.Ln, bias=c.epsb[:, 0:1], extra=[c.epsb])
                        k.actv(rs, rs[:, 0:tn], rs, rs[:, 0:tn], AF.Exp, scale=-0.5)
                        k.stt(k.dve, dst, dst[:, h, t0:t0 + tn], S32, S32[:, t0:t0 + tn], (1.0 / math.sqrt(128) if ci < 8 else 1.0),
                              rs, rs[:, 0:tn], ALU.mult, ALU.mult)
            wz = k.sb("wz", [128, 8, 544], BF16)
            zt = [P32, Cv]
            iz = 0
            for (zc0, zn) in ((0, 512), (512, 544)):
                for cb in range(0, zn, 256):
                    n = min(256, zn - cb)
                    load_w_bf16(k, c, wz, wz[:, :, cb:cb + n], e.od_w_in, e.od_w_in.ap()[:, 3072 + zc0 + cb:3072 + zc0 + cb + n], n)
                for t in range(NT):
                    z_ = zt[iz % 2]; iz += 1
                    tok = slice(t * 128, (t + 1) * 128)
                    for bi, (col, n) in enumerate(((0, 512), (512, 32))[:(1 if zc0 == 0 else 2)]):
                        p = pp[it % 3]; it += 1
                        for kc in range(8):
                            k.mm(p, p[:, 0:n], hT, hT[:, kc, tok], wz, wz[:, kc, col:col + n], kc == 0, kc == 7)
                        k.cp(k.act if bi % 2 else k.dve, z_, z_[:, col:col + n], p, p[:, 0:n])
                    k.dma(z_d, z_d.ap()[tok, zc0:zc0 + 512], z_, z_[:, 0:512], q=k.pool)
                    if zc0:
                        k.dma(gt_d, gt_d.ap()[tok, :], z_, z_[:, 512:544], q=k.pool)
        if c.stop == "proj1":
            c.dbg_qkv = (qT, kT, vT)
            return
        gdn_phase(k, c, e, qT, kT, vT, gt_d, o1_d)
    if c.stop == "gdn":
        return
    with k.scope():
        c.wstage = [k.sb("wst%d" % i, [128, 4096]) for i in range(2)]
        wout = k.sb("wout", [128, 8, 1024], BF16)
        for cb in range(2):
            load_w_bf16(k, c, wout, wout[:, :, cb * 512:(cb + 1) * 512], e.od_wout, e.od_wout.ap()[:, cb * 512:(cb + 1) * 512], 512)
        hwb = k.sb("hwb", [128, D])
        k.dma(hwb, hwb[:], e.od_hw, e.od_hw.ap().partition_broadcast(128).rearrange("p o n -> p (o n)"))
        gate = k.sb("gate", [128, D]); e.load_gate(gate, 1, 0, 2)
        ot = [k.sb("ot%d" % i, [128, 2, D]) for i in range(2)]
        zt = [k.sb("zt%d" % i, [128, D]) for i in range(2)]
        xt = [k.sb("xt%d" % i, [128, D]) for i in range(2)]
        w1s = [k.sb("w1_%d" % i, [128, D]) for i in range(2)]; w2s = [k.sb("w2_%d" % i, [128, D]) for i in range(2)]
        sts = [k.sb("st_%d" % i, [128, 16]) for i in range(2)]
        cats = [k.sb("cat%d" % i, [128, D], BF16) for i in range(2)]; catTs = [k.sb("catT%d" % i, [128, 8, 128], BF16) for i in range(2)]
        pTbs = [k.ps("pTb%d" % i, [128, 8, 128], BF16) for i in range(2)]
        pO = [k.ps("pO%d" % i, [128, 512]) for i in range(2)]
        yos = [k.sb("yo%d" % i, [128, D]) for i in range(2)]
        for i, t in enumerate(range(2, NT)):
            tok = slice(t * 128, (t + 1) * 128)
            o_, z_, x_ = ot[i % 2], zt[i % 2], xt[i % 2]
            w1, w2, st, cat, catT, pTb, yo = w1s[i % 2], w2s[i % 2], sts[i % 2], cats[i % 2], catTs[i % 2], pTbs[i % 2], yos[i % 2]
            k.dma(o_, o_[:], o1_d, o1_d.ap()[:, tok, :].rearrange("r t n -> t r n"))
            k.dma(z_, z_[:], z_d, z_d.ap()[tok, :])
            k.dma(x_, x_[:], xs, xs.ap()[tok, :])
            k.tt(k.dve, w1, w1[:], o_, o_[:, 0, :], o_, o_[:, 1, :], ALU.add)
            k.tt(k.pool, w2, w2[:], w1, w1[:], w1, w1[:], ALU.mult)
            k.op(k.dve, lambda: nc.vector.reduce_sum(out=st[:, 0:8], in_=w2[:].rearrange("p (h e) -> p h e", h=8), axis=AX.X), [st], [w2])
            k.actv(st, st[:, 0:8], st, st[:, 0:8], AF.Ln, bias=c.epsb[:, 0:1], scale=1.0 / 128, extra=[c.epsb])
            k.actv(st, st[:, 8:16], st, st[:, 0:8], AF.Exp, scale=-0.5)
            k.tt(k.dve, w1, w1[:].rearrange("p (h e) -> p h e", h=8), w1, w1[:].rearrange("p (h e) -> p h e", h=8),
                 st, bc(st[:, 8:16][:, :, None], [128, 8, 128]), ALU.mult)
            k.tt(k.dve, w1, w1[:], w1, w1[:], hwb, hwb[:], ALU.mult)
            k.actv(z_, z_[:], z_, z_[:], AF.Silu)
            k.tt(k.dve, cat, cat[:], w1, w1[:], z_, z_[:], ALU.mult)
            for j in range(8):
                k.tr(pTb, pTb[:, j, :], cat, cat[:, j * 128:(j + 1) * 128], c.idb, c.idb[:])
            k.cp(k.act, catT, catT[:], pTb, pTb[:])
            for cb in range(2):
                for kc in range(8):
                    k.mm(pO[cb], pO[cb][:], catT, catT[:, kc, :], wout, wout[:, kc, cb * 512:(cb + 1) * 512], kc == 0, kc == 7)
                k.tt(k.dve, yo, yo[:, cb * 512:(cb + 1) * 512], pO[cb], pO[cb][:], gate, gate[:, cb * 512:(cb + 1) * 512], ALU.mult)
            k.tt(k.pool, yo, yo[:], yo, yo[:], x_, x_[:], ALU.add)
            k.dma(xs, xs.ap()[tok, :], yo, yo[:], q=k.pool)


def gdn_phase(k, c, e, qT, kT, vT, gt_d, o1_d):
    nc = k.nc
    with k.scope():
        gt = k.sb("gt", [64, 36, 32])
        for c0 in range(0, 36, 6):
            k.dma(gt, gt[:, c0:c0 + 6, :], gt_d, gt_d.ap()[c0 * 64:(c0 + 6) * 64, :].rearrange("(c l) n -> l c n", l=64))
        ga = k.sb("ga", [64, 16]); dtb = k.sb("dtb", [64, 16])
        k.dma(ga, ga[:], e.od_alog, e.od_alog.ap().partition_broadcast(64).rearrange("p o n -> p (o n)"))
        k.dma(dtb, dtb[:], e.od_dtb, e.od_dtb.ap().partition_broadcast(64).rearrange("p o n -> p (o n)"))
        k.actv(ga, ga[:], ga, ga[:], AF.Exp)
        tri = k.sb("tri", [64, 2, 64]); strict = k.sb("strict", [64, 2, 64])
        k.dma(tri, tri[:], e.ctri, e.ctri.ap().rearrange("r s l -> s r l"))
        k.dma(strict, strict[:], e.cstrict, e.cstrict.ap().rearrange("r s l -> s r l"))
        ones = k.sb("ones", [64, 128]); k.memset(k.dve, ones, ones[:], 1.0)
        ng = k.sb("ng", [64, 2, 36, 8]); beta = k.sb("beta", [64, 2, 36, 8])
        eG = k.sb("eG", [64, 2, 36, 8]); kds = k.sb("kds", [64, 2, 36, 8]); bg = k.sb("bg", [64, 2, 36, 8])
        gl = k.sb("gl", [128, 2, 36, 8])
        for d in range(2):
            k.tt(k.dve, ng, ng[:, d], gt, gt[:, :, 8 * d:8 * d + 8], dtb, bc(dtb[:, None, 8 * d:8 * d + 8], [64, 36, 8]), ALU.add)
            k.actv(beta, beta[:, d], gt, gt[:, :, 16 + 8 * d:24 + 8 * d], AF.Sigmoid)
        k.actv(ng, ng[:], ng, ng[:], AF.Exp)
        k.actv(ng, ng[:], ng, ng[:], AF.Ln, bias=1.0)
        for d in range(2):
            k.tt(k.dve, ng, ng[:, d], ng, ng[:, d], ga, bc(ga[:, None, 8 * d:8 * d + 8], [64, 36, 8]), ALU.mult)
        with k.scope():
            pF = k.ps("pF", [64, 2, 512]); pT_ = k.ps("pTt", [64, 2, 512]); pG = k.ps("pG", [128, 2, 512])
            for d in range(2):
                ngd = ng[:, d].rearrange("p c h -> p (c h)")
                k.mm(pF, pF[:, d, 0:288], tri, tri[:, d, :], ng, ngd)
                k.mm(pT_, pT_[:, d, 0:288], ones, ones[:, 0:64], ng, ngd)
                k.mm(pG, pG[:, d, 0:288], ones, ones[:], ng, ngd)
            fl = lambda t_: t_[:].rearrange("p d c h -> p d (c h)")
            k.actv(eG, fl(eG), pF, pF[:, :, 0:288], AF.Exp, scale=-1.0)
            k.cp(k.dve, kds, fl(kds), pF, pF[:, :, 0:288])
            k.tt(k.dve, kds, fl(kds), kds, fl(kds), pT_, pT_[:, :, 0:288], ALU.subtract)
            k.actv(kds, kds[:], kds, kds[:], AF.Exp)
            k.tt(k.dve, bg, bg[:], beta, beta[:], eG, eG[:], ALU.mult)
            k.actv(gl, fl(gl), pG, pG[:, :, 0:288], AF.Exp, scale=-1.0)
        S32 = [[k.sb("S32_%d_%d" % (d, hp), [128, 2, 128]) for hp in range(4)] for d in range(2)]
        Sb = [[k.sb("Sb_%d_%d" % (d, hp), [128, 2, 128], BF16) for hp in range(4)] for d in range(2)]
        for d in range(2):
            for hp in range(4):
                k.memset(k.dve, S32[d][hp], S32[d][hp][:], 0.0); k.memset(k.pool, Sb[d][hp], Sb[d][hp][:], 0.0)
        idb2 = bc(c.id32[0:64, None, 0:64], [64, 2, 64])

        class G:
            pass
        GS = {}
        for d in range(2):
            for par in range(2):
                g = G()
                sfx = "_%d%d" % (d, par)
                g.X = k.ps("gX" + sfx, [128, 512]); g.Y_ = k.ps("gY" + sfx, [128, 512])
                f3 = lambda h_, p1, c0, a_: h_[0:p1, c0:c0 + 256].rearrange("p (a b) -> p a b", a=a_)
                g.KDv = f3(g.X.h, 64, 0, 4); g.QDv = f3(g.X.h, 64, 256, 4)
                g.Nv = f3(g.X.h, 64, 0, 4); g.Vv = f3(g.X.h, 64, 256, 2); g.O1v = f3(g.X.h, 64, 0, 2)
                g.Tv = g.Y_.h[0:64, 0:256].bitcast(BF16).rearrange("p (a b) -> p a b", a=4)
                g.WTv = g.Y_.h[:, 256:384].rearrange("p (a b) -> p a b", a=2)
                g.O2v = f3(g.Y_.h, 64, 0, 2); g.Sv = f3(g.Y_.h, 128, 256, 2)
                g.kbg = k.sb("kbg" + sfx, [64, 2, 128], BF16); g.kd = k.sb("kd" + sfx, [64, 2, 128], BF16); g.bv = k.sb("bv" + sfx, [64, 2, 128], BF16)
                g.gm = k.sb("gm" + sfx, [64, 4, 64]); g.MBs = k.sb("MBs" + sfx, [64, 4, 64])
                g.gam = k.sb("gam" + sfx, [64, 2, 64]); g.A32 = k.sb("A32" + sfx, [64, 2, 64])
                g.Mb = k.sb("Mb" + sfx, [64, 2, 64], BF16); g.nAb = k.sb("nAb" + sfx, [64, 2, 64], BF16)
                g.Y = k.sb("Y" + sfx, [64, 2, 64], BF16); g.Rt = k.sb("Rt" + sfx, [64, 2, 64], BF16)
                g.nWT = k.sb("nWT" + sfx, [128, 2, 64], BF16); g.vn = k.sb("vn" + sfx, [64, 2, 128], BF16)
                g.gT_ = k.sb("gT_" + sfx, [64, 2, 64]); g.attT = k.sb("attT" + sfx, [64, 2, 64], BF16)
                g.t2 = k.sb("t2" + sfx, [64, 2, 128]); g.otl = [k.sb("otl%d" % i + sfx, [64, 2, 128]) for i in range(2)]
                GS[(d, par)] = g

        def chunk_gen(d, par, ch):
            g = GS[(d, par)]
            tok = slice(ch * 64, (ch + 1) * 64)
            hsel = lambda t_: bc(t_[:, d, ch, :].rearrange("p (a b) -> p a b", b=2)[:, par::2, :].rearrange("p a b -> p (a b)")[:, :, None], [64, 4, 64]) if False else None
            for qi, hp in enumerate((par, par + 2)):
                h0 = 2 * hp
                k.tt(k.pool, g.gm, g.gm[:, 2 * qi:2 * qi + 2, :], tri, bc(tri[:, d:d + 1, :], [64, 2, 64]),
                     ng, bc(ng[:, d, ch, h0:h0 + 2][:, :, None], [64, 2, 64]), ALU.mult)
                k.tt(k.pool, g.MBs, g.MBs[:, 2 * qi:2 * qi + 2, :], strict, bc(strict[:, d:d + 1, :], [64, 2, 64]),
                     beta, bc(beta[:, d, ch, h0:h0 + 2][:, :, None], [64, 2, 64]), ALU.mult)
            for qi, hp in enumerate((par, par + 2)):
                h0 = 2 * hp
                S3, Sb_ = S32[d][hp], Sb[d][hp]
                for hh in range(2):
                    h = h0 + hh
                    k.tr(g.Y_, g.Tv[:, hh, :], kT, kT[:, h, tok], c.idb, c.idb[:])
                    k.tr(g.Y_, g.Tv[:, 2 + hh, :], vT, vT[:, h, tok], c.idb, c.idb[:])
                    k.mm(g.X, g.KDv[:, hh, :], kT, kT[:, h, tok], kT, kT[:, h, tok])
                    k.mm(g.X, g.KDv[:, 2 + hh, :], g.gm, g.gm[:, 2 * qi + hh, :], strict, strict[:, d, :])
                    k.mm(g.X, g.QDv[:, hh, :], kT, kT[:, h, tok], qT, qT[:, h, tok])
                    k.mm(g.X, g.QDv[:, 2 + hh, :], strict, strict[:, d, :], g.gm, g.gm[:, 2 * qi + hh, :])
                yield
                sc = lambda t_: bc(t_[:, d, ch, h0:h0 + 2][:, :, None], [64, 2, 128])
                k.tt(k.dve, g.kbg, g.kbg[:], g.Y_, g.Tv[:, 0:2, :], bg, sc(bg), ALU.mult)
                k.tt(k.dve, g.kd, g.kd[:], g.Y_, g.Tv[:, 0:2, :], kds, sc(kds), ALU.mult)
                k.tt(k.dve, g.bv, g.bv[:], g.Y_, g.Tv[:, 2:4, :], beta, sc(beta), ALU.mult)
                k.actv(g.gam, g.gam[:], g.X, g.KDv[:, 2:4, :], AF.Exp, scale=-1.0)
                k.actv(g.gT_, g.gT_[:], g.X, g.QDv[:, 2:4, :], AF.Exp, scale=-1.0)
                k.tt(k.pool, g.gam, g.gam[:], g.gam, g.gam[:], g.MBs, g.MBs[:, 2 * qi:2 * qi + 2, :], ALU.mult)
                k.tt(k.pool, g.gT_, g.gT_[:], g.gT_, g.gT_[:], tri, bc(tri[:, d:d + 1, :], [64, 2, 64]), ALU.mult)
                k.tt(k.dve, g.A32, g.A32[:], g.X, g.KDv[:, 0:2, :], g.gam, g.gam[:], ALU.mult)
                k.tt(k.dve, g.attT, g.attT[:], g.X, g.QDv[:, 0:2, :], g.gT_, g.gT_[:], ALU.mult)
                k.tt(k.pool, g.Mb, g.Mb[:], g.A32, g.A32[:], c.id32, idb2, ALU.add)
                k.ts(k.dve, g.nAb, g.nAb[:], g.A32, g.A32[:], -1.0, None, ALU.mult)
                for hh in range(2):
                    k.mm(g.X, g.Nv[:, hh, :], g.nAb, g.nAb[:, hh, :], c.idb, c.idb[0:64, 0:64])
                yield
                k.tt(k.dve, g.Y, g.Y[:], g.X, g.Nv[:, 0:2, :], c.id32, idb2, ALU.add)
                for itn in range(5):
                    for hh in range(2):
                        k.mm(g.X, g.Nv[:, hh, :], g.Y, g.Y[:, hh, :], g.Mb, g.Mb[:, hh, :])
                    yield
                    k.tt(k.dve, g.Rt, g.Rt[:], c.id32, idb2, g.X, g.Nv[:, 0:2, :], ALU.subtract)
                    for hh in range(2):
                        k.mm(g.X, g.Nv[:, 2 + hh, :], g.Rt, g.Rt[:, hh, :], g.Y, g.Y[:, hh, :])
                    yield
                    k.tt(k.dve, g.Y, g.Y[:], g.Y, g.Y[:], g.X, g.Nv[:, 2:4, :], ALU.add)
                for hh in range(2):
                    k.mm(g.Y_, g.WTv[:, hh, :], g.kbg, g.kbg[:, hh, :], g.Y, g.Y[:, hh, :])
                yield
                k.actv(g.nWT, g.nWT[:], g.Y_, g.WTv, AF.Copy, scale=-1.0)
                for hh in range(2):
                    k.mm(g.X, g.Vv[:, hh, :], g.Y, g.Y[:, hh, :], g.bv, g.bv[:, hh, :], True, False)
                    k.mm(g.X, g.Vv[:, hh, :], g.nWT, g.nWT[:, hh, :], Sb_, Sb_[:, hh, :], False, True)
                yield
                k.cp(k.act, g.vn, g.vn[:], g.X, g.Vv)
                for hh in range(2):
                    h = h0 + hh
                    k.mm(g.X, g.O1v[:, hh, :], qT, qT[:, h, tok], Sb_, Sb_[:, hh, :])
                    k.mm(g.Y_, g.O2v[:, hh, :], g.attT, g.attT[:, hh, :], g.vn, g.vn[:, hh, :])
                    k.mm(g.Y_, g.Sv[:, hh, :], g.kd, g.kd[:, hh, :], g.vn, g.vn[:, hh, :])
                yield
                ot_ = g.otl[qi]
                k.cp(k.act, g.t2, g.t2[:], g.Y_, g.O2v)
                k.tt(k.dve, ot_, ot_[:], g.X, g.O1v, eG, bc(eG[:, d, ch, h0:h0 + 2][:, :, None], [64, 2, 128]), ALU.mult)
                k.tt(k.pool, ot_, ot_[:], ot_, ot_[:], g.t2, g.t2[:], ALU.add)
                k.tt(k.pool, S3, S3[:], S3, S3[:], gl, bc(gl[:, d, ch, h0:h0 + 2][:, :, None], [128, 2, 128]), ALU.mult)
                k.tt(k.dve, S3, S3[:], S3, S3[:], g.Y_, g.Sv, ALU.add)
                k.cp(k.act, Sb_, Sb_[:], S3, S3[:])
                k.dma(o1_d, o1_d.ap()[d, tok, h0 * 128:(h0 + 2) * 128], ot_, ot_[:].rearrange("p h e -> p (h e)"), q=k.sp)
                yield

        for step in range(36):
            gens = [chunk_gen(0, 0, step), chunk_gen(1, 0, ORDB[step]), chunk_gen(0, 1, step), chunk_gen(1, 1, ORDB[step])]
            alive = [True] * 4
            while any(alive):
                for i in range(4):
                    if alive[i]:
                        try:
                            next(gens[i])
                        except StopIteration:
                            alive[i] = False


def host_consts():
    s = np.arange(64)
    tri = np.stack([(s[:, None] <= s[None, :]), (s[:, None] >= s[None, :])]).astype(np.float32)
    ip = np.arange(128) // 16
    maskM = np.stack([(ip[None, :] >= ip[:, None]), (ip[None, :] <= ip[:, None])]).astype(np.float32)
    return {
        "k_id32": np.eye(128, dtype=np.float32),
        "k_idb": np.eye(128, dtype=np.float32).astype(ml_dtypes.bfloat16),
        "k_tri": tri, "k_maskM": maskM,
        "k_strict": np.stack([(s[:, None] > s[None, :]), (s[:, None] < s[None, :])]).astype(np.float32),
    }


def make_in_maps(inputs, cores):
    f = lambda a: np.ascontiguousarray(np.asarray(a, dtype=np.float32))
    sh = {
        "c_ctx": f(inputs["c_ctx"]).reshape(1, D), "ada_w": f(inputs["ada_w"]), "ada_b": f(inputs["ada_b"]),
        "norm1_w": f(inputs["norm1_w"]), "norm2_w": f(inputs["norm2_w"]),
        "ffn_w1": f(inputs["ffn_w1"]), "ffn_w3": f(inputs["ffn_w3"]), "ffn_w2": f(inputs["ffn_w2"]),
        "final_norm_w": f(inputs["final_norm_w"]).reshape(1, D),
        "ev_w_in": f(inputs["ev_w_in"])[0], "ev_i_bias": f(inputs["ev_i_bias"]).reshape(1, 8),
        "ev_f_bias": f(inputs["ev_f_bias"]).reshape(1, 8), "ev_head_norm_w": f(inputs["ev_head_norm_w"]).reshape(1, 512),
        "ev_lam_re": f(inputs["ev_lam_re"])[0], "ev_lam_im": f(inputs["ev_lam_im"])[0],
        "ev_log_dt": f(inputs["ev_log_dt"]).reshape(1, 64),
        "ev_b_re": f(inputs["ev_b_re"])[0], "ev_b_im": f(inputs["ev_b_im"])[0],
        "ev_c_re": f(inputs["ev_c_re"]).reshape(1024, 64), "ev_c_im": f(inputs["ev_c_im"]).reshape(1024, 64),
        "ev_d": f(inputs["ev_d"]).reshape(1, 512), "ev_w_glu": f(inputs["ev_w_glu"])[0], "ev_w_out": f(inputs["ev_w_out"])[0],
        "od_w_in": f(inputs["od_w_in"])[0], "od_conv_w": f(inputs["od_conv_w"]).reshape(9, 3072),
        "od_a_log": f(inputs["od_a_log"]).reshape(1, 16), "od_dt_bias": f(inputs["od_dt_bias"]).reshape(1, 16),
        "od_head_norm_w": f(inputs["od_head_norm_w"]).reshape(1, D), "od_w_out": f(inputs["od_w_out"])[0],
    }
    sh.update(host_consts())
    x, cc, ctx = f(inputs["x"]), f(inputs["c"]), f(inputs["ctx"])
    maps = []
    for b in cores:
        m = dict(sh)
        m["x"] = x[b]; m["c"] = cc[b:b + 1]; m["ctx"] = ctx[b]
        maps.append(m)
    return maps


def kernel(**inputs):
    nc, _ = build_program()
    maps = make_in_maps(inputs, list(range(8)))
    res = run_bass_kernel_spmd(nc, maps, core_ids=list(range(8)))
    return np.stack([np.asarray(r["out"], dtype=np.float32) for r in res.results], axis=0)
```

```python
import math
import numpy as np
import ml_dtypes
import concourse.bass as bass
import concourse.mybir as mybir
from concourse.bass_types import AP
from concourse.bass_utils import run_bass_kernel_spmd

F32 = mybir.dt.float32
BF16 = mybir.dt.bfloat16
AF = mybir.ActivationFunctionType
ALU = mybir.AluOpType
AX = mybir.AxisListType

D = 1024
T = 2304
NCTX = 256
NLAT = 2048
NT = T // 128
EPS = 1e-6
HID = 2816
PI = math.pi


class Obj:
    def __init__(self, k, name, handle, space):
        self.k, self.name, self.h, self.space = k, name, handle, space
        self.uid = k.uid
        self.w, self.r = {}, {}
        self.sems = {}

    def __getitem__(self, idx):
        return self.h[idx]

    def ap(self):
        return self.h.ap() if self.space == "dram" else self.h[:]

    def dsem(self, kind):
        if kind not in self.sems:
            if not self.sems:
                self.k.dma_objs.append(self)
            pool = self.k.sem_pool[kind]
            if pool:
                self.sems[kind] = pool.pop()
            else:
                self.k.nsem += 1
                self.sems[kind] = [self.k.new_sem("d%s_%d" % (kind, self.k.nsem), keep=True), 0]
        return self.sems[kind]


class Eng:
    def __init__(self, k, name, e):
        self.k, self.name, self.e = k, name, e
        self.sem = k.new_sem("p_" + name)
        self.cnt = 0
        self.seen = {}

    def need(self, tok):
        s, v = tok
        if self.seen.get(id(s), 0) >= v:
            return
        self.e.wait_ge(s, v)
        self.seen[id(s)] = v


class K:
    def __init__(self, nc):
        self.nc = nc
        self._ctx = []
        self.dma_objs = []
        self.sem_pool = {"hw": [], "sw": []}
        self.nsem = 0
        self._perm = []
        self.pe = Eng(self, "pe", nc.tensor)
        self.dve = Eng(self, "dve", nc.vector)
        self.act = Eng(self, "act", nc.scalar)
        self.pool = Eng(self, "pool", nc.gpsimd)
        self.sp = Eng(self, "sp", nc.sync)
        self.engs = [self.pe, self.dve, self.act, self.pool, self.sp]
        self.n_ins = 0
        self.uid = 0

    def new_sem(self, name, keep=False):
        cm = self.nc.semaphore(name)
        s = cm.__enter__()
        self._perm.append((cm, s))
        return s

    def _alloc(self, cm, name, space):
        h = cm.__enter__()
        self._ctx.append(cm)
        return Obj(self, name, h, space)

    def sb(self, name, shape, dt=F32):
        self.uid += 1
        name = "%s_%d" % (name, self.uid)
        return self._alloc(self.nc.sbuf_tensor(name, list(shape), dt), name, "sb")

    def ps(self, name, shape, dt=F32):
        self.uid += 1
        name = "%s_%d" % (name, self.uid)
        return self._alloc(self.nc.psum_tensor(name, list(shape), dt), name, "ps")

    def dram(self, name, shape, dt=F32, kind="Internal"):
        h = self.nc.dram_tensor(name, list(shape), dt, kind=kind)
        return Obj(self, name, h, "dram")

    class _Scope:
        def __init__(self, k):
            self.k = k

        def __enter__(self):
            self.mark = len(self.k._ctx)
            self.uid0 = self.k.uid
            return self

        def __exit__(self, *a):
            k = self.k
            k.barrier()
            while len(k._ctx) > self.mark:
                k._ctx.pop().__exit__(None, None, None)
            keep = []
            for o in k.dma_objs:
                if o.space == "dram" or o.uid <= self.uid0:
                    keep.append(o)
                else:
                    for kind, sc in o.sems.items():
                        k.sem_pool[kind].append(sc)
                    o.sems = {}
            k.dma_objs = keep
            return False

    def scope(self):
        return K._Scope(self)

    def barrier(self):
        toks = [(e.sem, e.cnt) for e in self.engs if e.cnt]
        toks += [(sc[0], sc[1]) for o in self.dma_objs for sc in o.sems.values() if sc[1]]
        for e in self.engs:
            for t in toks:
                if t[0] is e.sem:
                    continue
                e.need(t)

    def _deps(self, eng, outs, ins, same_eng_raw=True):
        toks = []
        for o in ins:
            toks += list(o.w.values())
            if o.space == "ps":
                toks += [t for t in o.r.values() if t[0] is not eng.sem]
        for o in outs:
            toks += list(o.w.values())
            toks += list(o.r.values())
        for t in toks:
            if t[0] is eng.sem and (eng is self.pe or not same_eng_raw):
                continue
            eng.need(t)

    SAME_ENG_WAIT = True

    SKIP_SELF = ()

    def op(self, eng, fn, outs, ins):
        self._deps(eng, outs, ins, same_eng_raw=(K.SAME_ENG_WAIT and eng.name not in K.SKIP_SELF))
        ins_ = fn()
        eng.cnt += 1
        ins_.then_inc(eng.sem, 1)
        tok = (eng.sem, eng.cnt)
        eng.seen[id(eng.sem)] = max(eng.seen.get(id(eng.sem), 0), 0)
        for o in ins:
            o.r[id(tok[0])] = tok
        for o in outs:
            o.w = {id(tok[0]): tok}
            o.r = {}
        self.n_ins += 1
        return ins_

    def dma(self, out_obj, out_ap, in_obj, in_ap, q=None, **kw):
        q = q or self.sp
        self._deps(q, [out_obj], [in_obj], same_eng_raw=True)
        sc = out_obj.dsem("sw" if q is self.pool else "hw")
        s = sc[0]
        ins_ = q.e.dma_start(out=out_ap, in_=in_ap, **kw)
        sc[1] += 16
        ins_.then_inc(s, 16)
        tok = (s, sc[1])
        in_obj.r[id(s)] = tok
        out_obj.w[id(s)] = tok
        out_obj.r = {}
        self.n_ins += 1
        return ins_

    def finish(self, outs):
        self.barrier()

    def close(self):
        while self._ctx:
            self._ctx.pop().__exit__(None, None, None)
        while self._perm:
            self._perm.pop()[0].__exit__(None, None, None)

    def mm(self, out_o, out_ap, l_o, l_ap, r_o, r_ap, start=True, stop=True):
        nc = self.nc
        return self.op(self.pe, lambda: nc.tensor.matmul(out_ap, lhsT=l_ap, rhs=r_ap, start=start, stop=stop),
                       [out_o], [l_o, r_o])

    def tr(self, out_o, out_ap, in_o, in_ap, id_o, id_ap):
        nc = self.nc
        return self.op(self.pe, lambda: nc.tensor.transpose(out_ap, in_ap, id_ap), [out_o], [in_o, id_o])

    def tt(self, eng, out_o, out_ap, a_o, a_ap, b_o, b_ap, op):
        return self.op(eng, lambda: eng.e.tensor_tensor(out=out_ap, in0=a_ap, in1=b_ap, op=op), [out_o], [a_o, b_o])

    def ts(self, eng, out_o, out_ap, a_o, a_ap, s1, s2, op0, op1=None, extra=()):
        if op1 is None:
            return self.op(eng, lambda: eng.e.tensor_scalar(out=out_ap, in0=a_ap, scalar1=s1, scalar2=None, op0=op0),
                           [out_o], [a_o] + list(extra))
        return self.op(eng, lambda: eng.e.tensor_scalar(out=out_ap, in0=a_ap, scalar1=s1, scalar2=s2, op0=op0, op1=op1),
                       [out_o], [a_o] + list(extra))

    def stt(self, eng, out_o, out_ap, a_o, a_ap, sc, b_o, b_ap, op0, op1, extra=()):
        return self.op(eng, lambda: eng.e.scalar_tensor_tensor(out=out_ap, in0=a_ap, scalar=sc, in1=b_ap, op0=op0, op1=op1),
                       [out_o], [a_o, b_o] + list(extra))

    def actv(self, out_o, out_ap, a_o, a_ap, func, bias=0.0, scale=1.0, extra=()):
        nc = self.nc
        return self.op(self.act, lambda: nc.scalar.activation(out=out_ap, in_=a_ap, func=func, bias=bias, scale=scale),
                       [out_o], [a_o] + list(extra))

    def cp(self, eng, out_o, out_ap, a_o, a_ap):
        if eng is self.act:
            nc = self.nc
            return self.op(eng, lambda: nc.scalar.copy(out=out_ap, in_=a_ap), [out_o], [a_o])
        return self.op(eng, lambda: eng.e.tensor_copy(out=out_ap, in_=a_ap), [out_o], [a_o])

    def memset(self, eng, o, ap, val):
        return self.op(eng, lambda: eng.e.memset(ap, val), [o], [])


def bc(ap, shape):
    return ap.broadcast_to(list(shape))


class Ctx:
    pass


def load_w_bf16(k, c, dst, dst_ap_fn, wsrc, w_ap, ncols, kchunks=8, eng=None):
    eng = eng or k.pool
    st = c.wstage[c.wstage_i % 2]
    c.wstage_i += 1
    sv = st[:, 0:kchunks * ncols].rearrange("p (k n) -> p k n", k=kchunks)
    k.dma(st, sv, wsrc, w_ap.rearrange("(kc p) n -> p kc n", p=128))
    k.cp(eng, dst, dst_ap_fn, st, sv)


def norm_mod(k, c, xs, tiles, ab_of_tile, hT, col0=0):
    for i, t in enumerate(tiles):
        xt = c.xt[i % 2]
        k.dma(xt, xt[:], xs, xs.ap()[t * 128:(t + 1) * 128, :])
        sq = c.sq[i % 2] if isinstance(c.sq, list) else c.sq
        k.tt(k.dve, sq, sq[:], xt, xt[:], xt, xt[:], ALU.mult)
        ss = c.ss[i % 2] if isinstance(c.ss, list) else c.ss
        k.op(k.dve, lambda: k.nc.vector.reduce_sum(out=ss[:, 0:1], in_=sq[:], axis=AX.X), [ss], [sq])
        k.actv(ss, ss[:, 1:2], ss, ss[:, 0:1], AF.Ln, bias=c.epsb[:, 0:1], scale=1.0 / D, extra=[c.epsb])
        k.actv(ss, ss[:, 2:3], ss, ss[:, 1:2], AF.Exp, scale=-0.5)
        k.ts(k.dve, sq, sq[:], xt, xt[:], ss[:, 2:3], None, ALU.mult, extra=[ss])
        a, b = ab_of_tile(t)
        for half in range(2):
            pt = c.pT[half]
            for j in range(4):
                kc = half * 4 + j
                k.tr(pt, pt[:, j, :], sq, sq[:, kc * 128:(kc + 1) * 128], c.id32, c.id32[:])
            for j in range(4):
                kc = half * 4 + j
                k.actv(hT, hT[:, kc, col0 + i * 128: col0 + (i + 1) * 128], pt, pt[:, j, :], AF.Identity,
                       bias=b[0][:, b[1] + kc: b[1] + kc + 1], scale=a[0][:, a[1] + kc:a[1] + kc + 1], extra=[a[0], b[0]])


def load_T(k, c, dst_o, dst_ap, src_o, src_ap, nrows, ncols=128):
    st = c.ltst[c.lt_i % 2]; pt = c.ltps[c.lt_i % 2]; c.lt_i += 1
    k.dma(st, st[0:nrows, 0:ncols], src_o, src_ap)
    k.tr(pt, pt[0:ncols, 0:nrows], st, st[0:nrows, 0:ncols], c.id32, c.id32[0:nrows, 0:nrows])
    k.cp(k.dve, dst_o, dst_ap, pt, pt[0:ncols, 0:nrows])


def tok_blocks(tiles_n):
    out, s = [], 0
    while s < tiles_n:
        n = min(4, tiles_n - s)
        out.append((s, n))
        s += n
    return out


def build_program(stop_after=None, debug=False):
    nc = bass.Bass("TRN2", target_bir_lowering=False)
    k = K(nc)
    c = Ctx()
    c.wstage_i = 0
    c.stop = stop_after
    I = {}

    def inp(name, shape, dt=F32):
        I[name] = k.dram(name, shape, dt, kind="ExternalInput")
        return I[name]

    x_in = inp("x", [NLAT, D]); cvec = inp("c", [1, D]); ctx_in = inp("ctx", [NCTX, D]); c_ctx = inp("c_ctx", [1, D])
    ada_w = inp("ada_w", [2, D, 6 * D]); ada_b = inp("ada_b", [2, 6 * D])
    norm1_w = inp("norm1_w", [2, D]); norm2_w = inp("norm2_w", [2, D])
    ffn_w1 = inp("ffn_w1", [2, D, HID]); ffn_w3 = inp("ffn_w3", [2, D, HID]); ffn_w2 = inp("ffn_w2", [2, HID, D])
    final_w = inp("final_norm_w", [1, D])
    ev_w_in = inp("ev_w_in", [D, 2576]); ev_ib = inp("ev_i_bias", [1, 8]); ev_fb = inp("ev_f_bias", [1, 8])
    ev_hw = inp("ev_head_norm_w", [1, 512])
    ev_lre = inp("ev_lam_re", [2, 32, 64]); ev_lim = inp("ev_lam_im", [2, 32, 64]); ev_ldt = inp("ev_log_dt", [1, 64])
    ev_bre = inp("ev_b_re", [2, 32, 64, 16]); ev_bim = inp("ev_b_im", [2, 32, 64, 16])
    ev_cre = inp("ev_c_re", [1024, 64]); ev_cim = inp("ev_c_im", [1024, 64])
    ev_d = inp("ev_d", [1, 512]); ev_wglu = inp("ev_w_glu", [512, 1024]); ev_wout = inp("ev_w_out", [D, D])
    od_w_in = inp("od_w_in", [D, 4128]); od_conv = inp("od_conv_w", [9, 3072])
    od_alog = inp("od_a_log", [1, 16]); od_dtb = inp("od_dt_bias", [1, 16]); od_hw = inp("od_head_norm_w", [1, D])
    od_wout = inp("od_w_out", [D, D])
    cid32 = inp("k_id32", [128, 128]); cidb = inp("k_idb", [128, 128], BF16)
    ctri = inp("k_tri", [2, 64, 64])
    cmaskM = inp("k_maskM", [2, 128, 128])
    cstrict = inp("k_strict", [2, 64, 64])
    out_d = k.dram("out", [NLAT, D], F32, kind="ExternalOutput")

    xs = k.dram("xs", [T, D])
    modv = k.dram("modv", [2, 2, 6 * D])
    dbg = {}

    c.id32 = k.sb("id32", [128, 128]); k.dma(c.id32, c.id32[:], cid32, cid32.ap())
    c.idb = k.sb("idb", [128, 128], BF16); k.dma(c.idb, c.idb[:], cidb, cidb.ap())
    c.epsb = k.sb("epsb", [128, 2]); k.memset(k.dve, c.epsb, c.epsb[:, 0:1], EPS); k.memset(k.dve, c.epsb, c.epsb[:, 1:2], 0.5 * PI)

    k.dma(xs, xs.ap()[0:NCTX, :], ctx_in, ctx_in.ap())
    k.dma(xs, xs.ap()[NCTX:T, :], x_in, x_in.ap())
    with k.scope():
        sT = k.sb("sT", [128, 8, 2])
        c.ltst = [k.sb("ltst%d" % i, [64, 128]) for i in range(2)]
        c.ltps = [k.ps("ltps%d" % i, [128, 64]) for i in range(2)]
        c.lt_i = 0
        load_T(k, c, sT, sT[:, :, 0], cvec, cvec.ap().rearrange("o (kc p) -> (o kc) p", p=128), 8)
        load_T(k, c, sT, sT[:, :, 1], c_ctx, c_ctx.ap().rearrange("o (kc p) -> (o kc) p", p=128), 8)
        sS = k.sb("sS", [128, 8, 2])
        k.actv(sS, sS[:], sT, sT[:], AF.Silu)
        wst = [k.sb("adw%d" % i, [128, 8, 512]) for i in range(2)]
        pm = [k.ps("pm%d" % i, [128, 512]) for i in range(2)]
        brow = k.sb("brow", [2, 6 * D]); mrow = k.sb("mrow", [2, 6 * D])
        for li in range(2):
            k.dma(brow, brow[:], ada_b, ada_b.ap()[li:li + 1, :].partition_broadcast(2).rearrange("p o n -> p (o n)"))
            for j in range(12):
                w = wst[j % 2]
                k.dma(w, w[:], ada_w, ada_w.ap()[li, :, j * 512:(j + 1) * 512].rearrange("(kc p) n -> p kc n", p=128))
                p = pm[j % 2]
                for kc in range(8):
                    k.mm(p, p[0:2, :], sS, sS[:, kc, :], w, w[:, kc, :], start=(kc == 0), stop=(kc == 7))
                k.tt(k.dve, mrow, mrow[:, j * 512:(j + 1) * 512], p, p[0:2, :], brow, brow[:, j * 512:(j + 1) * 512], ALU.add)
            k.dma(modv, modv.ap()[li], mrow, mrow[:], q=k.pool)

    if stop_after == "ada":
        k.finish([])
        k.close()
        return nc, ["modv"]

    modF = k.sb("modF", [128, 2, 2, 48])
    nwF = k.sb("nwF", [128, 2, 2, 8])
    with k.scope():
        c.ltst = [k.sb("ltst%d" % i, [64, 128]) for i in range(2)]
        c.ltps = [k.ps("ltps%d" % i, [128, 64]) for i in range(2)]
        c.lt_i = 0
        for li in range(2):
            for who in range(2):
                load_T(k, c, modF, modF[:, li, who, :], modv, modv.ap()[li, who, :].rearrange("(c p) -> c p", p=128), 48)
        for wi, nw in enumerate((norm1_w, norm2_w)):
            for li in range(2):
                load_T(k, c, nwF, nwF[:, wi, li, :], nw, nw.ap()[li, :].rearrange("(c p) -> c p", p=128), 8)
    aF = k.sb("aF", [128, 2, 2, 2, 8])
    for wi in range(2):
        for li in range(2):
            for who in range(2):
                sc0 = 8 if wi == 0 else 32
                k.stt(k.dve, aF, aF[:, wi, li, who, :], modF, modF[:, li, who, sc0:sc0 + 8], 1.0, nwF, nwF[:, wi, li, :],
                      ALU.add, ALU.mult)

    def ab_fn(wi, li):
        sh0 = 0 if wi == 0 else 24

        def f(t):
            who = 1 if t < 2 else 0
            a_flat = aF.h[:].rearrange("p a b c d -> p (a b c d)")
            b_flat = modF.h[:].rearrange("p a b c -> p (a b c)")
            return ((_View(aF, a_flat), ((wi * 2 + li) * 2 + who) * 8), (_View(modF, b_flat), (li * 2 + who) * 48 + sh0))
        return f

    def load_gate(dst, li, who, part):
        k.dma(dst, dst[:], modv, modv.ap()[li, who:who + 1, part * D:(part + 1) * D].partition_broadcast(128).rearrange("p o n -> p (o n)"))

    def ffn_phase(li, tiles):
        with k.scope():
            c.xt = [k.sb("xt%d" % i, [128, D]) for i in range(2)]
            c.sq = [k.sb("sq%d" % i, [128, D]) for i in range(2)]; c.ss = [k.sb("ss%d" % i, [128, 4]) for i in range(2)]
            c.pT = [k.ps("pT%d" % i, [128, 4, 128]) for i in range(2)]
            c.wstage = [k.sb("wst%d" % i, [128, 2048]) for i in range(2)]
            ntl = len(tiles)
            half_n = (ntl + 1) // 2
            gate = [k.sb("gate%d" % w, [128, D]) for w in range(2)]
            load_gate(gate[0], li, 0, 5); load_gate(gate[1], li, 1, 5)
            hT = k.sb("hT", [128, 8, half_n * 128], BF16)
            gT = k.sb("gT", [128, 22, half_n * 128], BF16)
            w2b = k.sb("w2b", [128, 22, D], BF16)
            w1b = [k.sb("w1b%d" % i, [128, 8, 256], BF16) for i in range(2)]
            w3b = [k.sb("w3b%d" % i, [128, 8, 256], BF16) for i in range(2)]
            p1 = [k.ps("p1_%d" % i, [128, 512]) for i in range(2)]
            p3 = [k.ps("p3_%d" % i, [128, 512]) for i in range(2)]
            py = [k.ps("py%d" % i, [128, 512]) for i in range(2)]
            sg = [k.sb("sg%d" % i, [128, 512]) for i in range(2)]
            yo = [k.sb("yo%d" % i, [128, D]) for i in range(2)]
            for jb in range(0, 22, 4):
                n = min(4, 22 - jb)
                for cb in range(2):
                    st = c.wstage[c.wstage_i % 2]; c.wstage_i += 1
                    sv = st[:, 0:n * 512].rearrange("p (j n) -> p j n", j=n)
                    k.dma(st, sv, ffn_w2, ffn_w2.ap()[li, jb * 128:(jb + n) * 128, cb * 512:(cb + 1) * 512]
                          .rearrange("(j p) n -> p j n", p=128))
                    k.cp(k.pool, w2b, w2b[:, jb:jb + n, cb * 512:(cb + 1) * 512], st, sv)
            it = 0
            for hs in range(0, ntl, half_n):
                ht = tiles[hs:hs + half_n]
                norm_mod(k, c, xs, ht, ab_fn(1, li), hT)
                blocks = tok_blocks(len(ht))
                for jb in range(0, 22, 2):
                    n = 2
                    wa, wb = w1b[(jb // 2) % 2], w3b[(jb // 2) % 2]
                    load_w_bf16(k, c, wa, wa[:, :, 0:n * 128], ffn_w1, ffn_w1.ap()[li, :, jb * 128:(jb + n) * 128], n * 128)
                    load_w_bf16(k, c, wb, wb[:, :, 0:n * 128], ffn_w3, ffn_w3.ap()[li, :, jb * 128:(jb + n) * 128], n * 128, eng=k.dve)
                    for jj in range(n):
                        j = jb + jj
                        for (b0, bn) in blocks:
                            q1, q3, s_ = p1[it % 2], p3[it % 2], sg[it % 2]; it += 1
                            cols = slice(b0 * 128, (b0 + bn) * 128)
                            w_ = bn * 128
                            for kc in range(8):
                                k.mm(q1, q1[:, 0:w_], wa, wa[:, kc, jj * 128:(jj + 1) * 128], hT, hT[:, kc, cols], kc == 0, kc == 7)
                            for kc in range(8):
                                k.mm(q3, q3[:, 0:w_], wb, wb[:, kc, jj * 128:(jj + 1) * 128], hT, hT[:, kc, cols], kc == 0, kc == 7)
                            k.actv(s_, s_[:, 0:w_], q1, q1[:, 0:w_], AF.Silu)
                            k.tt(k.dve, gT, gT[:, j, cols], s_, s_[:, 0:w_], q3, q3[:, 0:w_], ALU.mult)
                for i, t in enumerate(ht):
                    xt = c.xt[i % 2]
                    k.dma(xt, xt[:], xs, xs.ap()[t * 128:(t + 1) * 128, :])
                    g = gate[1 if t < 2 else 0]
                    y = yo[i % 2]
                    for cb in range(2):
                        p = py[cb]
                        for j in range(22):
                            k.mm(p, p[:], gT, gT[:, j, i * 128:(i + 1) * 128], w2b, w2b[:, j, cb * 512:(cb + 1) * 512], j == 0, j == 21)
                        k.tt(k.dve, y, y[:, cb * 512:(cb + 1) * 512], p, p[:], g, g[:, cb * 512:(cb + 1) * 512], ALU.mult)
                    k.tt(k.pool, y, y[:], y, y[:], xt, xt[:], ALU.add)
                    k.dma(xs, xs.ap()[t * 128:(t + 1) * 128, :], y, y[:], q=k.pool)

    def final_phase():
        with k.scope():
            xt = [k.sb("fx%d" % i, [128, D]) for i in range(2)]
            sq = [k.sb("fs%d" % i, [128, D]) for i in range(2)]
            ss = k.sb("fss", [128, 4])
            fw = k.sb("fw", [128, D])
            k.dma(fw, fw[:], final_w, final_w.ap().partition_broadcast(128).rearrange("p o n -> p (o n)"))
            for i in range(16):
                t = i + 2
                x_, s_ = xt[i % 2], sq[i % 2]
                k.dma(x_, x_[:], xs, xs.ap()[t * 128:(t + 1) * 128, :])
                k.tt(k.dve, s_, s_[:], x_, x_[:], x_, x_[:], ALU.mult)
                k.op(k.dve, lambda: nc.vector.reduce_sum(out=ss[:, 0:1], in_=s_[:], axis=AX.X), [ss], [s_])
                k.actv(ss, ss[:, 1:2], ss, ss[:, 0:1], AF.Ln, bias=c.epsb[:, 0:1], scale=1.0 / D, extra=[c.epsb])
                k.actv(ss, ss[:, 2:3], ss, ss[:, 1:2], AF.Exp, scale=-0.5)
                k.stt(k.dve, s_, s_[:], x_, x_[:], ss[:, 2:3], fw, fw[:], ALU.mult, ALU.mult, extra=[ss])
                k.dma(out_d, out_d.ap()[i * 128:(i + 1) * 128, :], s_, s_[:], q=k.pool)

    env = dict(locals())
    layer0(k, c, env)
    if stop_after in ("proj0", "ml_0", "ml_1", "ml_2", "ml_3", "ml_4", "ml_5", "ml_5a", "ml_5b", "ml_a", "ml_b", "mlstm", "s5", "mix0"):
        k.finish([]); k.close(); return nc, ["xs"]
    ffn_phase(0, list(range(NT)))
    if stop_after == "ffn0":
        k.finish([]); k.close(); return nc, ["xs"]
    layer1(k, c, env)
    if stop_after == "mix1":
        k.finish([]); k.close(); return nc, ["xs"]
    ffn_phase(1, list(range(2, NT)))
    final_phase()
    k.finish([out_d])
    k.close()
    return nc, ["out"]


class _View:
    def __init__(self, obj, flat):
        self.obj, self.flat = obj, flat

    @property
    def space(self):
        return self.obj.space

    @property
    def w(self):
        return self.obj.w

    @property
    def r(self):
        return self.obj.r

    def __getitem__(self, idx):
        return self.flat[idx]


LAYER_FUNCS = []


class E:
    def __init__(self, d):
        self.__dict__.update(d)


ORDB = [3, 2, 1, 0] + list(range(35, 3, -1))
ORDB8 = list(range(31, -1, -1)) + list(range(287, 31, -1))


def layer0(k, c, env):
    e = E(env)
    nc = k.nc
    xs = e.xs
    qT_d = k.dram("qT_d", [512, T], BF16); kT_d = k.dram("kT_d", [512, T], BF16)
    ktok_d = k.dram("ktok_d", [T, 512], BF16); v_d = k.dram("v_d", [T, 512], BF16)
    o_d = k.dram("o_d", [T, 512]); g_d = k.dram("g_d", [T, 16]); u_d = k.dram("u_d", [T, 512])
    hA_d = k.dram("hA_d", [2, T, 512]); yS_d = k.dram("yS_d", [T, 512])
    c.l0 = dict(qT=qT_d, kT=kT_d, ktok=ktok_d, v=v_d, o=o_d, g=g_d, u=u_d, hA=hA_d, yS=yS_d)

    with k.scope():
        c.xt = [k.sb("xt%d" % i, [128, D]) for i in range(2)]
        c.sq = [k.sb("sq%d" % i, [128, D]) for i in range(2)]; c.ss = [k.sb("ss%d" % i, [128, 4]) for i in range(2)]
        c.pT = [k.ps("pT%d" % i, [128, 4, 128]) for i in range(2)]
        c.wstage = [k.sb("wst%d" % i, [128, 4096]) for i in range(2)]
        hT = k.sb("hT", [128, 8, T], BF16)
        norm_mod(k, c, xs, list(range(NT)), e.ab_fn(0, 0), hT)
        wb = k.sb("wb", [128, 8, 2576], BF16)
        for cb in range(0, 2576, 512):
            n = min(512, 2576 - cb)
            load_w_bf16(k, c, wb, wb[:, :, cb:cb + n], e.ev_w_in, e.ev_w_in.ap()[:, cb:cb + n], n,
                        eng=(k.pool if (cb // 512) % 2 else k.dve))
        pp = [k.ps("pp%d" % i, [128, 512]) for i in range(4)]
        fst = [k.sb("fst%d" % i, [128, T], BF16) for i in range(2)]
        blocks = [(0, 512), (512, 512), (1024, 512), (1536, 512), (2048, 256)]
        it = 0
        for which, dst in ((0, qT_d), (1, kT_d)):
            for h in range(4):
                st = fst[(which * 4 + h) % 2]
                col = which * 512 + h * 128
                for (t0, tn) in blocks:
                    p = pp[it % 4]; it += 1
                    for kc in range(8):
                        k.mm(p, p[:, 0:tn], wb, wb[:, kc, col:col + 128], hT, hT[:, kc, t0:t0 + tn], kc == 0, kc == 7)
                    k.actv(st, st[:, t0:t0 + tn], p, p[:, 0:tn], AF.Copy, scale=(1.0 if which == 0 else 1.0 / math.sqrt(128)))
                k.dma(dst, dst.ap()[h * 128:(h + 1) * 128, :], st, st[:], q=k.pool)
        tkb = [k.sb("tkb%d" % i, [128, 1024], BF16) for i in range(2)]
        tof = [k.sb("tof%d" % i, [128, 1040]) for i in range(2)]
        for t in range(NT):
            kb, of = tkb[t % 2], tof[t % 2]
            tok = slice(t * 128, (t + 1) * 128)
            for bi, (col, n) in enumerate(((512, 512), (1024, 512), (1536, 512), (2064, 512), (2048, 16))):
                p = pp[it % 4]; it += 1
                for kc in range(8):
                    k.mm(p, p[:, 0:n], hT, hT[:, kc, tok], wb, wb[:, kc, col:col + n], kc == 0, kc == 7)
                if bi == 0:
                    k.actv(kb, kb[:, 0:512], p, p[:, 0:512], AF.Copy, scale=1.0 / math.sqrt(128))
                elif bi == 1:
                    k.cp(k.dve, kb, kb[:, 512:1024], p, p[:, 0:512])
                elif bi == 2:
                    k.cp(k.act, of, of[:, 0:512], p, p[:, 0:512])
                elif bi == 3:
                    k.cp(k.dve, of, of[:, 512:1024], p, p[:, 0:512])
                else:
                    k.cp(k.act, of, of[:, 1024:1040], p, p[:, 0:16])
            k.dma(ktok_d, ktok_d.ap()[tok, :], kb, kb[:, 0:512], q=k.pool)
            k.dma(v_d, v_d.ap()[tok, :], kb, kb[:, 512:1024], q=k.pool)
            k.dma(o_d, o_d.ap()[tok, :], of, of[:, 0:512], q=k.pool)
            k.dma(u_d, u_d.ap()[tok, :], of, of[:, 512:1024], q=k.pool)
            k.dma(g_d, g_d.ap()[tok, :], of, of[:, 1024:1040], q=k.pool)
    if c.stop == "proj0":
        return
    mlstm_phase(k, c, e)
    if c.stop in ("mlstm", "ml_0", "ml_1", "ml_2", "ml_3", "ml_4", "ml_5", "ml_5a", "ml_5b", "ml_a", "ml_b"):
        return
    s5_phase(k, c, e)
    if c.stop == "s5":
        return
    finish0_phase(k, c, e)


def mlstm_phase(k, c, e):
    nc = k.nc
    L = c.l0
    with k.scope():
        qT = k.sb("qT", [128, 4, T], BF16); kT = k.sb("kT", [128, 4, T], BF16)
        for h in range(4):
            k.dma(qT, qT[:, h, :], L['qT'], L['qT'].ap()[h * 128:(h + 1) * 128, :])
            k.dma(kT, kT[:, h, :], L['kT'], L['kT'].ap()[h * 128:(h + 1) * 128, :])
        ktok = k.sb("ktok", [64, 36, 512], BF16)
        v1 = k.sb("v1", [64, 36, 4, 132], BF16)
        for c0 in range(0, 36, 6):
            k.dma(ktok, ktok[:, c0:c0 + 6, :], L['ktok'], L['ktok'].ap()[c0 * 64:(c0 + 6) * 64, :].rearrange("(c l) n -> l c n", l=64))
            for h in range(4):
                k.dma(v1, v1[:, c0:c0 + 6, h, 0:128], L['v'],
                      L['v'].ap()[c0 * 64:(c0 + 6) * 64, h * 128:(h + 1) * 128].rearrange("(c l) n -> l c n", l=64))
        if c.stop == "ml_0":
            return
        k.memset(k.dve, v1, v1[:, :, :, 128:132], 1.0)
        if c.stop == "ml_1":
            return
        g = k.sb("g", [64, 36, 16])
        for c0 in range(0, 36, 6):
            k.dma(g, g[:, c0:c0 + 6, :], L['g'], L['g'].ap()[c0 * 64:(c0 + 6) * 64, :].rearrange("(c l) n -> l c n", l=64))
        fb = k.sb("fb", [64, 8]); ib = k.sb("ib", [64, 8])
        k.dma(fb, fb[:], e.ev_fb, e.ev_fb.ap().partition_broadcast(64).rearrange("p o n -> p (o n)"))
        k.dma(ib, ib[:], e.ev_ib, e.ev_ib.ap().partition_broadcast(64).rearrange("p o n -> p (o n)"))
        tri = k.sb("tri", [64, 2, 64]); ones = k.sb("ones", [64, 128])
        k.dma(tri, tri[:], e.ctri, e.ctri.ap().rearrange("r s l -> s r l"))
        k.memset(k.dve, ones, ones[:], 1.0)
        if c.stop == "ml_2":
            return
        z = k.sb("z", [64, 2, 36, 4]); nlf = k.sb("nlf", [64, 2, 36, 4]); ig = k.sb("ig", [64, 2, 36, 4])
        A = k.sb("A", [64, 2, 36, 4]); Bk = k.sb("Bk", [64, 2, 36, 4]); gdec = k.sb("gdec", [128, 2, 36, 4])
        for d in range(2):
            k.tt(k.dve, z, z[:, d], g, g[:, :, 8 + 4 * d:12 + 4 * d], fb, bc(fb[:, None, 4 * d:4 * d + 4], [64, 36, 4]), ALU.add)
            k.tt(k.dve, ig, ig[:, d], g, g[:, :, 4 * d:4 * d + 4], ib, bc(ib[:, None, 4 * d:4 * d + 4], [64, 36, 4]), ALU.add)
        k.actv(z, z[:], z, z[:], AF.Exp, scale=-1.0)
        k.actv(nlf, nlf[:], z, z[:], AF.Ln, bias=1.0)
        if c.stop == "ml_3":
            return
        with k.scope():
            pF = k.ps("pF", [64, 2, 144]); pG = k.ps("pG", [128, 288])
            for d in range(2):
                k.mm(pF, pF[:, d, :], tri, tri[:, d, :], nlf, nlf[:, d].rearrange("p c h -> p (c h)"))
            k.mm(pG, pG[:], ones, ones[:], nlf, nlf[:].rearrange("p d c h -> p (d c h)"))
            if c.stop == "ml_4":
                k.cp(k.dve, A, A[:].rearrange("p d c h -> p d (c h)"), pF, pF[:])
                k.cp(k.dve, gdec, gdec[:].rearrange("p d c h -> p (d c h)"), pG, pG[:])
            Af = A[:].rearrange("p d c h -> p d (c h)"); Bf = Bk[:].rearrange("p d c h -> p d (c h)")
            if c.stop != "ml_4":
                if c.stop != "ml_5b":
                    k.actv(A, Af, pF, pF[:], AF.Exp, scale=-1.0)
                if c.stop != "ml_5a":
                    k.tt(k.dve, Bk, Bf, ig, ig[:].rearrange("p d c h -> p d (c h)"), pF, pF[:], ALU.add)
                if c.stop not in ("ml_5", "ml_5a", "ml_5b"):
                    k.actv(Bk, Bk[:], Bk, Bk[:], AF.Exp)
                    k.actv(gdec, gdec[:].rearrange("p d c h -> p (d c h)"), pG, pG[:], AF.Exp, scale=-1.0)
        if c.stop in ("ml_a", "ml_4", "ml_5", "ml_5a", "ml_5b"):
            return
        C32 = [k.sb("C32_%d" % d, [128, 4, 132]) for d in range(2)]
        Cb = [k.sb("Cb_%d" % d, [128, 4, 132], BF16) for d in range(2)]
        for d in range(2):
            k.memset(k.dve, C32[d], C32[d][:], 0.0)
            k.memset(k.dve, Cb[d], Cb[d][:], 0.0)
        pS = [k.ps("pS%d" % d, [64, 4, 64]) for d in range(2)]
        pN = [[k.ps("pN%d_%d" % (d, i), [64, 2, 256]) for i in range(2)] for d in range(2)]
        pC = [k.ps("pC%d" % i, [128, 2, 256]) for i in range(2)]
        kt = [k.sb("kt%d" % d, [64, 4, 128], BF16) for d in range(2)]
        MB = [k.sb("MB%d" % d, [64, 4, 64]) for d in range(2)]
        Pt = [k.sb("Pt%d" % d, [64, 4, 64], BF16) for d in range(2)]
        sm = [k.sb("sm%d" % d, [64, 4, 4]) for d in range(2)]
        ho = [k.sb("ho%d" % d, [64, 4, 128]) for d in range(2)]
        for step in range(36):
            for d in range(2):
                ch = step if d == 0 else ORDB[step]
                tok = slice(ch * 64, (ch + 1) * 64)
                Bs = Bk[:, d, ch, :]
                As = A[:, d, ch, :]
                k.tt(k.dve, kt[d], kt[d][:], ktok, ktok[:, ch, :].rearrange("p (h e) -> p h e", h=4),
                     Bk, bc(Bs[:, :, None], [64, 4, 128]), ALU.mult)
                k.tt(k.dve, MB[d], MB[d][:], tri, bc(tri[:, d:d + 1, :], [64, 4, 64]), Bk, bc(Bs[:, :, None], [64, 4, 64]), ALU.mult)
                for h in range(4):
                    k.mm(pS[d], pS[d][:, h, :], kT, kT[:, h, tok], qT, qT[:, h, tok])
                k.tt(k.dve, Pt[d], Pt[d][:], pS[d], pS[d][:], MB[d], MB[d][:], ALU.mult)
                for h in range(4):
                    pn = pN[d][h // 2]
                    k.mm(pn, pn[:, h % 2, 0:132], Pt[d], Pt[d][:, h, :], v1, v1[:, ch, h, :], True, False)
                    k.mm(pn, pn[:, h % 2, 0:132], qT, qT[:, h, tok], Cb[d], Cb[d][:, h, :], False, True)
                s_ = sm[d]
                for i in range(2):
                    pn = pN[d][i]
                    k.tt(k.dve, s_, s_[:, 2 * i:2 * i + 2, 0], pn, pn[:, :, 128], A, As[:, 2 * i:2 * i + 2], ALU.mult)
                k.stt(k.dve, s_, s_[:, :, 1], s_, s_[:, :, 0], -1.0, s_, s_[:, :, 0], ALU.mult, ALU.max)
                k.ts(k.dve, s_, s_[:, :, 1], s_, s_[:, :, 1], 1.0, None, ALU.max)
                k.op(k.dve, lambda: nc.vector.reciprocal(out=s_[:, :, 2], in_=s_[:, :, 1]), [s_], [s_])
                k.tt(k.dve, s_, s_[:, :, 3], s_, s_[:, :, 2], A, As, ALU.mult)
                for i in range(2):
                    pn = pN[d][i]
                    k.tt(k.dve, ho[d], ho[d][:, 2 * i:2 * i + 2, :], pn, pn[:, :, 0:128],
                         s_, bc(s_[:, 2 * i:2 * i + 2, 3:4], [64, 2, 128]), ALU.mult)
                k.dma(L['hA'], L['hA'].ap()[d, tok, :], ho[d], ho[d][:].rearrange("p h e -> p (h e)"), q=k.sp)
                for i in range(2):
                    pc_ = pC[i]
                    for hh in range(2):
                        h = 2 * i + hh
                        k.mm(pc_, pc_[:, hh, 0:132], kt[d], kt[d][:, h, :], v1, v1[:, ch, h, :])
                    k.tt(k.dve, C32[d], C32[d][:, 2 * i:2 * i + 2, :], pc_, pc_[:, :, 0:132], C32[d], C32[d][:, 2 * i:2 * i + 2, :], ALU.add)
                k.tt(k.dve, C32[d], C32[d][:], C32[d], C32[d][:], gdec, bc(gdec[:, d, ch, :][:, :, None], [128, 4, 132]), ALU.mult)
                k.cp(k.act, Cb[d], Cb[d][:], C32[d], C32[d][:])
            if c.stop == "ml_b" and step == 0:
                return


def s5_phase(k, c, e):
    nc = k.nc
    L = c.l0
    TWO_PI = 2 * PI
    with k.scope():
        Mw = k.sb("Mw", [128, 64, 128], BF16)
        WT = [k.sb("WT%d" % i, [128, 64, 64], BF16) for i in range(2)]
        RC = [k.sb("RC%d" % i, [64, 64, 128], BF16) for i in range(2)]
        AR2 = k.sb("AR2", [64, 2, 64]); AI2 = k.sb("AI2", [64, 2, 64])
        with k.scope():
            lre = k.sb("lre", [64, 64]); lim = k.sb("lim", [64, 64]); dt = k.sb("dt", [64, 64])
            c.ltst = [k.sb("ltst%d" % i, [64, 128]) for i in range(2)]
            c.ltps = [k.ps("ltps%d" % i, [128, 64]) for i in range(2)]
            c.lt_i = 0
            load_T(k, c, lre, lre[:], e.ev_lre, e.ev_lre.ap().rearrange("r g n -> (r g) n"), 64, ncols=64)
            load_T(k, c, lim, lim[:], e.ev_lim, e.ev_lim.ap().rearrange("r g n -> (r g) n"), 64, ncols=64)
            k.dma(dt, dt[:], e.ev_ldt, e.ev_ldt.ap().partition_broadcast(64).rearrange("p o n -> p (o n)"))
            k.actv(dt, dt[:], dt, dt[:], AF.Exp)
            ldr = k.sb("ldr", [64, 64]); ang = k.sb("ang", [64, 64])
            k.tt(k.dve, ldr, ldr[:], lre, lre[:], dt, dt[:], ALU.mult)
            k.tt(k.dve, ang, ang[:], lim, lim[:], dt, dt[:], ALU.mult)
            mg = k.sb("mg", [64, 16, 64]); sn = k.sb("sn", [64, 9, 64]); cs = k.sb("cs", [64, 9, 64])
            for ti, tau in enumerate(range(-7, 9)):
                k.actv(mg, mg[:, ti, :], ldr, ldr[:], AF.Exp, scale=float(tau))
            k.memset(k.dve, sn, sn[:, 0, :], 0.0); k.memset(k.dve, cs, cs[:, 0, :], 1.0)
            k.actv(sn, sn[:, 1, :], ang, ang[:], AF.Sin, scale=1.0 / 16)
            k.actv(cs, cs[:, 1, :], ang, ang[:], AF.Sin, bias=c.epsb[0:64, 1:2], scale=1.0 / 16, extra=[c.epsb])
            q1 = k.sb("q1", [64, 64]); q2 = k.sb("q2", [64, 64])
            for _ in range(4):
                k.tt(k.dve, q1, q1[:], sn, sn[:, 1, :], cs, cs[:, 1, :], ALU.mult)
                k.tt(k.dve, q2, q2[:], sn, sn[:, 1, :], sn, sn[:, 1, :], ALU.mult)
                k.ts(k.dve, sn, sn[:, 1, :], q1, q1[:], 2.0, None, ALU.mult)
                k.ts(k.dve, cs, cs[:, 1, :], q2, q2[:], -2.0, 1.0, ALU.mult, ALU.add)
            for tau in range(2, 9):
                k.tt(k.dve, q1, q1[:], cs, cs[:, tau - 1, :], cs, cs[:, 1, :], ALU.mult)
                k.tt(k.dve, q2, q2[:], sn, sn[:, tau - 1, :], sn, sn[:, 1, :], ALU.mult)
                k.tt(k.dve, cs, cs[:, tau, :], q1, q1[:], q2, q2[:], ALU.subtract)
                k.tt(k.dve, q1, q1[:], sn, sn[:, tau - 1, :], cs, cs[:, 1, :], ALU.mult)
                k.tt(k.dve, q2, q2[:], cs, cs[:, tau - 1, :], sn, sn[:, 1, :], ALU.mult)
                k.tt(k.dve, sn, sn[:, tau, :], q1, q1[:], q2, q2[:], ALU.add)
            pwr = k.sb("pwr", [64, 16, 64]); pwi = k.sb("pwi", [64, 16, 64])
            for ti, tau in enumerate(range(-7, 9)):
                at = abs(tau)
                k.tt(k.dve, pwr, pwr[:, ti, :], mg, mg[:, ti, :], cs, cs[:, at, :], ALU.mult)
                if tau >= 0:
                    k.tt(k.dve, pwi, pwi[:, ti, :], mg, mg[:, ti, :], sn, sn[:, at, :], ALU.mult)
                else:
                    k.stt(k.dve, pwi, pwi[:, ti, :], mg, mg[:, ti, :], -1.0, sn, sn[:, at, :], ALU.mult, ALU.mult)
            for s_ in range(2):
                k.cp(k.dve, AR2, AR2[:, s_, :], pwr, pwr[:, 15, :])
            k.ts(k.dve, AI2, AI2[:, 0, :], pwi, pwi[:, 15, :], -1.0, None, ALU.mult)
            k.cp(k.dve, AI2, AI2[:, 1, :], pwi, pwi[:, 15, :])
            nr = k.sb("nr", [64, 64]); den = k.sb("den", [64, 64]); t1 = k.sb("t1", [64, 64]); t2 = k.sb("t2", [64, 64])
            cor = k.sb("cor", [64, 64]); coi = k.sb("coi", [64, 64])
            k.ts(k.dve, nr, nr[:], pwr, pwr[:, 8, :], -1.0, None, ALU.add)
            k.tt(k.dve, den, den[:], lre, lre[:], lre, lre[:], ALU.mult)
            k.tt(k.dve, t1, t1[:], lim, lim[:], lim, lim[:], ALU.mult)
            k.tt(k.dve, den, den[:], den, den[:], t1, t1[:], ALU.add)
            k.op(k.dve, lambda: nc.vector.reciprocal(out=den[:], in_=den[:]), [den], [den])
            k.tt(k.dve, t1, t1[:], nr, nr[:], lre, lre[:], ALU.mult)
            k.tt(k.dve, t2, t2[:], pwi, pwi[:, 8, :], lim, lim[:], ALU.mult)
            k.tt(k.dve, t1, t1[:], t1, t1[:], t2, t2[:], ALU.add)
            k.tt(k.dve, cor, cor[:], t1, t1[:], den, den[:], ALU.mult)
            k.tt(k.dve, t1, t1[:], pwi, pwi[:, 8, :], lre, lre[:], ALU.mult)
            k.tt(k.dve, t2, t2[:], nr, nr[:], lim, lim[:], ALU.mult)
            k.tt(k.dve, t1, t1[:], t1, t1[:], t2, t2[:], ALU.subtract)
            k.tt(k.dve, coi, coi[:], t1, t1[:], den, den[:], ALU.mult)
            bre = k.sb("bre", [64, 64, 16]); bim = k.sb("bim", [64, 64, 16])
            for r in range(2):
                for g0 in range(0, 32, 4):
                    k.dma(bre, bre[:, r * 32 + g0:r * 32 + g0 + 4, :], e.ev_bre, e.ev_bre.ap()[r, g0:g0 + 4].rearrange("g n p -> n g p"))
                    k.dma(bim, bim[:, r * 32 + g0:r * 32 + g0 + 4, :], e.ev_bim, e.ev_bim.ap()[r, g0:g0 + 4].rearrange("g n p -> n g p"))
            bbr = k.sb("bbr", [64, 64, 16]); bbi = k.sb("bbi", [64, 64, 16])
            u1 = k.sb("u1", [64, 64, 16]); u2 = k.sb("u2", [64, 64, 16])
            corb = bc(cor[:, :, None], [64, 64, 16]); coib = bc(coi[:, :, None], [64, 64, 16])
            k.tt(k.dve, u1, u1[:], bre, bre[:], cor, corb, ALU.mult)
            k.tt(k.dve, u2, u2[:], bim, bim[:], coi, coib, ALU.mult)
            k.tt(k.dve, bbr, bbr[:], u1, u1[:], u2, u2[:], ALU.subtract)
            k.tt(k.dve, u1, u1[:], bim, bim[:], cor, corb, ALU.mult)
            k.tt(k.dve, u2, u2[:], bre, bre[:], coi, coib, ALU.mult)
            k.tt(k.dve, bbi, bbi[:], u1, u1[:], u2, u2[:], ALU.add)
            cTr = k.sb("cTr", [64, 64, 16]); cTi = k.sb("cTi", [64, 64, 16])
            cst = k.sb("cst", [128, 8, 64])
            pCt = [k.ps("pCt%d" % i, [64, 4, 128]) for i in range(2)]
            for src, dst in ((e.ev_cre, cTr), (e.ev_cim, cTi)):
                for t0 in range(0, 8, 2):
                    k.dma(cst, cst[:, t0:t0 + 2, :], src, src.ap()[t0 * 128:(t0 + 2) * 128, :].rearrange("(t p) n -> p t n", p=128))
                for t in range(8):
                    p = pCt[t // 4]
                    k.tr(p, p[:, t % 4, :], cst, cst[:, t, :], c.id32, c.id32[:])
                for hf in range(2):
                    k.cp(k.act, dst, dst[:, hf * 32:(hf + 1) * 32, :].rearrange("n g p -> n (g p)"),
                         pCt[hf], pCt[hf][:].rearrange("n t m -> n (t m)"))
            maskM = k.sb("maskM", [128, 2, 128])
            k.dma(maskM, maskM[:], e.cmaskM, e.cmaskM.ap().rearrange("r a b -> a r b"))
            w1_ = k.sb("w1_", [64, 32, 16]); w2_ = k.sb("w2_", [64, 32, 16])

            def cmul(r, powf, sr, si, dr, dr_ap, di, di_ap, neg_im=False):
                rs = slice(r * 32, (r + 1) * 32)
                for i in range(8):
                    ti = powf(i) + 7
                    pr = bc(pwr[:, ti, rs][:, :, None], [64, 32, 16]); pi_ = bc(pwi[:, ti, rs][:, :, None], [64, 32, 16])
                    k.tt(k.dve, w1_, w1_[:], sr, sr[:, rs, :], pwr, pr, ALU.mult)
                    k.tt(k.pool, w2_, w2_[:], si, si[:, rs, :], pwi, pi_, ALU.mult)
                    k.tt(k.dve, dr, dr_ap(i), w1_, w1_[:], w2_, w2_[:], ALU.subtract)
                    k.tt(k.dve, w1_, w1_[:], si, si[:, rs, :], pwr, pr, ALU.mult)
                    k.tt(k.pool, w2_, w2_[:], sr, sr[:, rs, :], pwi, pi_, ALU.mult)
                    if neg_im:
                        k.stt(k.dve, di, di_ap(i), w1_, w1_[:], -1.0, w2_, w2_[:], ALU.mult, ALU.subtract)
                    else:
                        k.tt(k.dve, di, di_ap(i), w1_, w1_[:], w2_, w2_[:], ALU.add)

            EBr = k.sb("EBr", [64, 32, 8, 16]); EBi = k.sb("EBi", [64, 32, 8, 16])
            ECr = k.sb("ECr", [64, 32, 8, 16]); ECi = k.sb("ECi", [64, 32, 8, 16])
            pM = [k.ps("pM%d" % i, [128, 4, 128]) for i in range(2)]
            pW = [k.ps("pW%d" % i, [128, 8, 64]) for i in range(2)]
            for r in range(2):
                sig = (lambda i: i) if r == 0 else (lambda i: 7 - i)
                rs = slice(r * 32, (r + 1) * 32)
                cmul(r, lambda i: -sig(i), bbr, bbi, EBr, lambda i: EBr[:, :, i, :], EBi, lambda i: EBi[:, :, i, :])
                cmul(r, lambda i: sig(i), cTr, cTi, ECr, lambda i: ECr[:, :, i, :], ECi, lambda i: ECi[:, :, i, :], neg_im=True)
                for g0 in range(0, 32, 4):
                    p = pM[(g0 // 4) % 2]
                    for gg in range(4):
                        g = g0 + gg
                        k.mm(p, p[:, gg, :], EBr, EBr[:, g].rearrange("n i p -> n (i p)"), ECr, ECr[:, g].rearrange("n i p -> n (i p)"), True, False)
                        k.mm(p, p[:, gg, :], EBi, EBi[:, g].rearrange("n i p -> n (i p)"), ECi, ECi[:, g].rearrange("n i p -> n (i p)"), False, True)
                    k.tt(k.dve, Mw, Mw[:, r * 32 + g0:r * 32 + g0 + 4, :], p, p[:], maskM, bc(maskM[:, r:r + 1, :], [128, 4, 128]), ALU.mult)
                cmul(r, lambda i: sig(i) + 1, cTr, cTi,
                     RC[0], lambda i: RC[0][:, rs, :].rearrange("n g (j p) -> n g j p", j=8)[:, :, i, :],
                     RC[1], lambda i: RC[1][:, rs, :].rearrange("n g (j p) -> n g j p", j=8)[:, :, i, :], neg_im=True)
                cmul(r, lambda i: 7 - sig(i), bbr, bbi, ECr, lambda i: ECr[:, :, i, :], ECi, lambda i: ECi[:, :, i, :])
                for comp, src in enumerate((ECr, ECi)):
                    for g0 in range(0, 32, 8):
                        p = pW[(g0 // 8) % 2]
                        for gg in range(8):
                            k.tr(p, p[:, gg, :], src, src[:, g0 + gg].rearrange("n i p -> n (i p)"), c.id32, c.id32[0:64, 0:64])
                        k.cp(k.act, WT[comp], WT[comp][:, r * 32 + g0:r * 32 + g0 + 8, :], p, p[:])
        X = k.sb("X", [128, 32, 288], BF16)
        SaL = [k.sb("Sa%d" % r_, [64, 2, 32, 290], BF16) for r_ in range(2)]
        with k.scope():
            u32 = k.sb("u32", [128, 8, 512]); u16 = k.sb("u16", [128, 32, 128], BF16)
            pX = [k.ps("pX%d" % i, [128, 8, 128], BF16) for i in range(2)]
            it = 0
            for ct, (c0, n) in enumerate(((0, 128), (128, 128), (256, 32))):
                k.dma(u32, u32[0:n], L['u'], L['u'].ap()[8 * c0:8 * (c0 + n), :].rearrange("(c i) ch -> c i ch", i=8))
                k.cp(k.dve, u16, u16[0:n].rearrange("c g (i p) -> c g i p", i=8), u32, u32[0:n].rearrange("c i (g p) -> c g i p", g=32))
                for g0 in range(0, 32, 8):
                    p = pX[it % 2]; it += 1
                    for gg in range(8):
                        g = g0 + gg
                        k.tr(p, p[:, gg, 0:n], u16, u16[0:n, g, :], c.idb, c.idb[0:n, 0:n])
                    k.cp(k.act, X, X[:, g0:g0 + 8, c0:c0 + n], p, p[:, :, 0:n])
        with k.scope():
            pB = [k.ps("pB%d" % i, [64, 288]) for i in range(4)]
            it = 0
            for rg in range(64):
                for comp in range(2):
                    p = pB[it % 4]; it += 1
                    k.mm(p, p[:], WT[comp], WT[comp][:, rg, :], X, X[:, rg % 32, :])
                    off = 0 if rg < 32 else 1
                    Sa = SaL[rg // 32]
                    k.cp(k.act if it % 2 else k.dve, Sa, Sa[:, comp, rg % 32, off:off + 288], p, p[:])
        R3 = [k.sb("R3_%d" % r_, [64, 3, 32]) for r_ in range(2)]
        P1 = [k.sb("P1_%d" % r_, [64, 2, 32]) for r_ in range(2)]; P2 = [k.sb("P2_%d" % r_, [64, 2, 32]) for r_ in range(2)]
        for r_ in range(2):
            k.memset(k.dve, R3[r_], R3[r_][:], 0.0)
        for step in range(288):
            cols = (step, ORDB8[step] + 1)
            bvs = [SaL[r_][:, :, :, cols[r_]] for r_ in range(2)]
            rsl = [slice(0, 32), slice(32, 64)]
            for r_ in range(2):
                k.tt(k.dve, P1[r_], P1[r_][:], AR2, AR2[:, :, rsl[r_]], R3[r_], R3[r_][:, 1:3, :], ALU.mult)
            for r_ in range(2):
                k.tt(k.dve, P2[r_], P2[r_][:], AI2, AI2[:, :, rsl[r_]], R3[r_], R3[r_][:, 0:2, :], ALU.mult)
            for r_ in range(2):
                k.tt(k.dve, P1[r_], P1[r_][:], P1[r_], P1[r_][:], P2[r_], P2[r_][:], ALU.add)
            for r_ in range(2):
                k.tt(k.dve, R3[r_], R3[r_][:, 1:3, :], P1[r_], P1[r_][:], SaL[r_], bvs[r_], ALU.add)
            for r_ in range(2):
                k.cp(k.dve, R3[r_], R3[r_][:, 0, :], R3[r_], R3[r_][:, 2, :])
            for r_ in range(2):
                k.cp(k.dve, SaL[r_], bvs[r_], R3[r_], R3[r_][:, 1:3, :])
        k.cp(k.dve, SaL[1], SaL[1][:, :, :, 0:1], SaL[1], SaL[1][:, :, :, 1:2])
        with k.scope():
            pY = [k.ps("pY%d" % i, [128, 288]) for i in range(2)]
            pZ = [k.ps("pZ%d" % i, [128, 4, 128]) for i in range(2)]
            Yq = k.sb("Yq", [128, 8, 288]); Y2q = [k.sb("Y2q%d" % i, [128, 8, 128]) for i in range(2)]
            it = 0; iz = 0; iy = 0
            for q in range(4):
                for gl in range(8):
                    g = q * 8 + gl
                    p = pY[it % 2]; it += 1
                    k.mm(p, p[:], Mw, Mw[:, g, :], X, X[:, g, :], True, False)
                    k.mm(p, p[:], Mw, Mw[:, 32 + g, :], X, X[:, g, :], False, False)
                    for comp in range(2):
                        k.mm(p, p[:, 1:288], RC[comp], RC[comp][:, g, :], SaL[0], SaL[0][:, comp, g, 0:287], False, False)
                    rg = 32 + g
                    for comp in range(2):
                        k.mm(p, p[:, 0:31], RC[comp], RC[comp][:, rg, :], SaL[1], SaL[1][:, comp, g, 2:33], False, False)
                        k.mm(p, p[:, 32:287], RC[comp], RC[comp][:, rg, :], SaL[1], SaL[1][:, comp, g, 34:289], False, False)
                        k.mm(p, p[:, 287:288], RC[comp], RC[comp][:, rg, :], SaL[1], SaL[1][:, comp, g, 0:1], False, comp == 1)
                    k.cp(k.act, Yq, Yq[:, gl, :], p, p[:])
                for ct, (c0, n) in enumerate(((0, 128), (128, 128), (256, 32))):
                    y2 = Y2q[iy % 2]; iy += 1
                    for gl in range(8):
                        if gl % 4 == 0:
                            pz = pZ[iz % 2]; iz += 1
                        k.tr(pz, pz[0:n, gl % 4, :], Yq, Yq[:, gl, c0:c0 + n], c.id32, c.id32[:])
                        k.cp(k.dve if gl % 2 else k.act, y2, y2[0:n, :, gl * 16:(gl + 1) * 16],
                             pz, pz[0:n, gl % 4, :].rearrange("c (j p) -> c j p", j=8))
                    for jh in range(2):
                        k.dma(L['yS'], L['yS'].ap()[8 * c0:8 * (c0 + n), q * 128:(q + 1) * 128].rearrange("(c j) ch -> c j ch", j=8)[:, jh * 4:(jh + 1) * 4, :],
                              y2, y2[0:n, jh * 4:(jh + 1) * 4, :], q=k.pool)


def finish0_phase(k, c, e):
    nc = k.nc
    L = c.l0
    xs = e.xs
    with k.scope():
        c.wstage = [k.sb("wst%d" % i, [128, 4096]) for i in range(2)]
        wglu = k.sb("wglu", [128, 4, 1024], BF16); wout = k.sb("wout", [128, 8, 1024], BF16)
        for cb in range(2):
            load_w_bf16(k, c, wglu, wglu[:, :, cb * 512:(cb + 1) * 512], e.ev_wglu, e.ev_wglu.ap()[:, cb * 512:(cb + 1) * 512], 512, kchunks=4)
            load_w_bf16(k, c, wout, wout[:, :, cb * 512:(cb + 1) * 512], e.ev_wout, e.ev_wout.ap()[:, cb * 512:(cb + 1) * 512], 512)
        hwb = k.sb("hwb", [128, 512]); dsk = k.sb("dsk", [128, 512])
        k.dma(hwb, hwb[:], e.ev_hw, e.ev_hw.ap().partition_broadcast(128).rearrange("p o n -> p (o n)"))
        k.dma(dsk, dsk[:], e.ev_d, e.ev_d.ap().partition_broadcast(128).rearrange("p o n -> p (o n)"))
        gate = [k.sb("gate%d" % w, [128, D]) for w in range(2)]
        e.load_gate(gate[0], 0, 0, 2); e.load_gate(gate[1], 0, 1, 2)
        hA = [k.sb("hA%d" % i, [128, 2, 512]) for i in range(2)]
        ot = [k.sb("ot%d" % i, [128, 512]) for i in range(2)]
        ut = [k.sb("ut%d" % i, [128, 512]) for i in range(2)]
        yt = [k.sb("yt%d" % i, [128, 512]) for i in range(2)]
        xt = [k.sb("xt%d" % i, [128, D]) for i in range(2)]
        w1s = [k.sb("fw1_%d" % i, [128, 512]) for i in range(2)]; w2s = [k.sb("fw2_%d" % i, [128, 512]) for i in range(2)]
        sts = [k.sb("fst_%d" % i, [128, 8]) for i in range(2)]
        cats = [k.sb("cat%d" % i, [128, D], BF16) for i in range(2)]; ybbs = [k.sb("ybb%d" % i, [128, 512], BF16) for i in range(2)]
        ybTs = [k.sb("ybT%d" % i, [128, 4, 128], BF16) for i in range(2)]; catTs = [k.sb("catT%d" % i, [128, 8, 128], BF16) for i in range(2)]
        pTbs = [k.ps("pTb%d" % i, [128, 8, 128], BF16) for i in range(2)]
        pG = [k.ps("pGl%d" % i, [128, 512]) for i in range(2)]
        pO = [k.ps("pO%d" % i, [128, 512]) for i in range(2)]
        yos = [k.sb("yo%d" % i, [128, D]) for i in range(2)]
        for t in range(NT):
            tok = slice(t * 128, (t + 1) * 128)
            h_, o_, u_, y_, x_ = hA[t % 2], ot[t % 2], ut[t % 2], yt[t % 2], xt[t % 2]
            w1, w2, st, cat, ybb, ybT, catT, pTb, yo = w1s[t % 2], w2s[t % 2], sts[t % 2], cats[t % 2], ybbs[t % 2], ybTs[t % 2], catTs[t % 2], pTbs[t % 2], yos[t % 2]
            k.dma(h_, h_[:], L['hA'], L['hA'].ap()[:, tok, :].rearrange("r t n -> t r n"))
            k.dma(o_, o_[:], L['o'], L['o'].ap()[tok, :])
            k.dma(u_, u_[:], L['u'], L['u'].ap()[tok, :])
            k.dma(y_, y_[:], L['yS'], L['yS'].ap()[tok, :])
            k.dma(x_, x_[:], xs, xs.ap()[tok, :])
            k.tt(k.dve, w1, w1[:], h_, h_[:, 0, :], h_, h_[:, 1, :], ALU.add)
            k.tt(k.dve, w2, w2[:], w1, w1[:], w1, w1[:], ALU.mult)
            k.op(k.dve, lambda: nc.vector.reduce_sum(out=st[:, 0:4], in_=w2[:].rearrange("p (h e) -> p h e", h=4), axis=AX.X), [st], [w2])
            k.actv(st, st[:, 0:4], st, st[:, 0:4], AF.Ln, bias=c.epsb[:, 0:1], scale=1.0 / 128, extra=[c.epsb])
            k.actv(st, st[:, 4:8], st, st[:, 0:4], AF.Exp, scale=-0.5)
            k.tt(k.dve, w1, w1[:].rearrange("p (h e) -> p h e", h=4), w1, w1[:].rearrange("p (h e) -> p h e", h=4),
                 st, bc(st[:, 4:8][:, :, None], [128, 4, 128]), ALU.mult)
            k.tt(k.dve, w1, w1[:], w1, w1[:], hwb, hwb[:], ALU.mult)
            k.actv(o_, o_[:], o_, o_[:], AF.Sigmoid)
            k.tt(k.dve, cat, cat[:, 0:512], w1, w1[:], o_, o_[:], ALU.mult)
            k.tt(k.dve, w2, w2[:], u_, u_[:], dsk, dsk[:], ALU.mult)
            k.tt(k.dve, w2, w2[:], w2, w2[:], y_, y_[:], ALU.add)
            k.tt(k.pool, y_, y_[:], w2, w2[:], w2, w2[:], ALU.mult)
            k.ts(k.dve, y_, y_[:], y_, y_[:], 0.044715, 1.0, ALU.mult, ALU.add)
            k.tt(k.dve, y_, y_[:], y_, y_[:], w2, w2[:], ALU.mult)
            k.actv(y_, y_[:], y_, y_[:], AF.Sigmoid, scale=2.0 * math.sqrt(2.0 / PI))
            k.tt(k.dve, ybb, ybb[:], y_, y_[:], w2, w2[:], ALU.mult)
            for j in range(4):
                k.tr(pTb, pTb[:, j, :], ybb, ybb[:, j * 128:(j + 1) * 128], c.idb, c.idb[:])
            k.cp(k.act, ybT, ybT[:], pTb, pTb[:, 0:4, :])
            for cb in range(2):
                for kc in range(4):
                    k.mm(pG[cb], pG[cb][:], ybT, ybT[:, kc, :], wglu, wglu[:, kc, cb * 512:(cb + 1) * 512], kc == 0, kc == 3)
            k.actv(w2, w2[:], pG[1], pG[1][:], AF.Sigmoid)
            k.tt(k.dve, cat, cat[:, 512:1024], pG[0], pG[0][:], w2, w2[:], ALU.mult)
            for j in range(8):
                k.tr(pTb, pTb[:, j, :], cat, cat[:, j * 128:(j + 1) * 128], c.idb, c.idb[:])
            k.cp(k.act, catT, catT[:], pTb, pTb[:])
            g = gate[1 if t < 2 else 0]
            for cb in range(2):
                for kc in range(8):
                    k.mm(pO[cb], pO[cb][:], catT, catT[:, kc, :], wout, wout[:, kc, cb * 512:(cb + 1) * 512], kc == 0, kc == 7)
                k.tt(k.dve, yo, yo[:, cb * 512:(cb + 1) * 512], pO[cb], pO[cb][:], g, g[:, cb * 512:(cb + 1) * 512], ALU.mult)
            k.tt(k.pool, yo, yo[:], yo, yo[:], x_, x_[:], ALU.add)
            k.dma(xs, xs.ap()[tok, :], yo, yo[:], q=k.pool)


def layer1(k, c, env):
    e = E(env)
    nc = k.nc
    xs = e.xs
    z_d = k.dram("z_d", [T, D]); gt_d = k.dram("gt_d", [T, 32]); o1_d = k.dram("o1_d", [2, T, D])
    with k.scope():
        qT = k.sb("gqT", [128, 8, T], BF16); kT = k.sb("gkT", [128, 8, T], BF16); vT = k.sb("gvT", [128, 8, T], BF16)
        with k.scope():
            hT = k.sb("hT", [128, 8, T], BF16)
            with k.scope():
                c.xt = [k.sb("xt%d" % i, [128, D]) for i in range(2)]
                c.sq = [k.sb("sq%d" % i, [128, D]) for i in range(2)]; c.ss = [k.sb("ss%d" % i, [128, 4]) for i in range(2)]
                c.pT = [k.ps("pT%d" % i, [128, 4, 128]) for i in range(2)]
                norm_mod(k, c, xs, list(range(NT)), e.ab_fn(0, 1), hT)
            c.wstage = [k.sb("wst%d" % i, [128, 2048]) for i in range(2)]
            c.ltst = [k.sb("ltst%d" % i, [64, 128]) for i in range(2)]
            c.ltps = [k.ps("ltps%d" % i, [128, 64]) for i in range(2)]
            c.lt_i = 0
            cw = k.sb("cw", [128, 24, 9])
            for ci in range(24):
                load_T(k, c, cw, cw[:, ci, :], e.od_conv, e.od_conv.ap()[:, ci * 128:(ci + 1) * 128], 9)
            ones32 = k.sb("ones32", [128, 128]); k.memset(k.dve, ones32, ones32[:], 1.0)
            wch = [k.sb("wch%d" % i, [128, 8, 256], BF16) for i in range(2)]
            P32 = k.sb("P32", [128, T]); Cv = k.sb("Cv", [128, T]); S32 = P32; Q32 = Cv
            rs = k.sb("rs", [128, 512])
            pp = [k.ps("pp%d" % i, [128, 512]) for i in range(3)]
            blocks = [(0, 512), (512, 512), (1024, 512), (1536, 512), (2048, 256)]
            it = 0
            for ci in range(24):
                if ci % 2 == 0:
                    wc = wch[(ci // 2) % 2]
                    load_w_bf16(k, c, wc, wc[:], e.od_w_in, e.od_w_in.ap()[:, ci * 128:(ci + 2) * 128], 256)
                wo = (ci % 2) * 128
                for (t0, tn) in blocks:
                    p = pp[it % 3]; it += 1
                    for kc in range(8):
                        k.mm(p, p[:, 0:tn], wc, wc[:, kc, wo:wo + 128], hT, hT[:, kc, t0:t0 + tn], kc == 0, kc == 7)
                    k.cp(k.act, P32, P32[:, t0:t0 + tn], p, p[:, 0:tn])
                w_ = lambda tap: cw[:, ci, tap:tap + 1]
                k.ts(k.dve, Cv, Cv[:], P32, P32[:], w_(4), None, ALU.mult, extra=[cw])
                k.stt(k.dve, Cv, Cv[:, 1:256], P32, P32[:, 0:255], w_(3), Cv, Cv[:, 1:256], ALU.mult, ALU.add, extra=[cw])
                k.stt(k.dve, Cv, Cv[:, 0:255], P32, P32[:, 1:256], w_(5), Cv, Cv[:, 0:255], ALU.mult, ALU.add, extra=[cw])
                Pl = P32[:, 256:T].rearrange("p (r q) -> p r q", q=64); Cl = Cv[:, 256:T].rearrange("p (r q) -> p r q", q=64)
                for a in range(3):
                    for b in range(3):
                        if a == 1 and b == 1:
                            continue
                        dr, dc = a - 1, b - 1
                        r0, r1 = max(0, -dr), 32 - max(0, dr)
                        c0, c1 = max(0, -dc), 64 - max(0, dc)
                        k.stt(k.dve, Cv, Cl[:, r0:r1, c0:c1], P32, Pl[:, r0 + dr:r1 + dr, c0 + dc:c1 + dc],
                              w_(a * 3 + b), Cv, Cl[:, r0:r1, c0:c1], ALU.mult, ALU.add, extra=[cw])
                h = ci % 8
                if ci >= 16:
                    k.actv(vT, vT[:, h, :], Cv, Cv[:], AF.Silu)
                else:
                    k.actv(S32, S32[:], Cv, Cv[:], AF.Silu)
                    k.tt(k.pool, Q32, Q32[:], S32, S32[:], S32, S32[:], ALU.mult)
                    dst = qT if ci < 8 else kT
                    for (t0, tn) in blocks:
                        p = pp[it % 3]; it += 1
                        k.mm(p, p[:, 0:tn], ones32, ones32[:], Q32, Q32[:, t0:t0 + tn])
                        k.actv(rs, rs[:, 0:tn], p, p[:, 0:tn], AF.Ln, bias=c.epsb[:, 0:1], extra=[c.epsb])
                        k.actv(rs, rs[:, 0:tn], rs, rs[:, 0:tn], AF.Exp, scale=-0.5)
                        k.stt(k.dve, dst, dst[:, h, t0:t0 + tn], S32, S32[:, t0:t0 + tn], (1.0 / math.sqrt(128) if ci < 8 else 1.0),
                              rs, rs[:, 0:tn], ALU.mult, ALU.mult)
            wz = k.sb("wz", [128, 8, 544], BF16)
            zt = [P32, Cv]
            iz = 0
            for (zc0, zn) in ((0, 512), (512, 544)):
                for cb in range(0, zn, 256):
                    n = min(256, zn - cb)
                    load_w_bf16(k, c, wz, wz[:, :, cb:cb + n], e.od_w_in, e.od_w_in.ap()[:, 3072 + zc0 + cb:3072 + zc0 + cb + n], n)
                for t in range(NT):
                    z_ = zt[iz % 2]; iz += 1
                    tok = slice(t * 128, (t + 1) * 128)
                    for bi, (col, n) in enumerate(((0, 512), (512, 32))[:(1 if zc0 == 0 else 2)]):
                        p = pp[it % 3]; it += 1
                        for kc in range(8):
                            k.mm(p, p[:, 0:n], hT, hT[:, kc, tok], wz, wz[:, kc, col:col + n], kc == 0, kc == 7)
                        k.cp(k.act if bi % 2 else k.dve, z_, z_[:, col:col + n], p, p[:, 0:n])
                    k.dma(z_d, z_d.ap()[tok, zc0:zc0 + 512], z_, z_[:, 0:512], q=k.pool)
                    if zc0:
                        k.dma(gt_d, gt_d.ap()[tok, :], z_, z_[:, 512:544], q=k.pool)
        if c.stop == "proj1":
            c.dbg_qkv = (qT, kT, vT)
            return
        gdn_phase(k, c, e, qT, kT, vT, gt_d, o1_d)
    if c.stop == "gdn":
        return
    with k.scope():
        c.wstage = [k.sb("wst%d" % i, [128, 4096]) for i in range(2)]
        wout = k.sb("wout", [128, 8, 1024], BF16)
        for cb in range(2):
            load_w_bf16(k, c, wout, wout[:, :, cb * 512:(cb + 1) * 512], e.od_wout, e.od_wout.ap()[:, cb * 512:(cb + 1) * 512], 512)
        hwb = k.sb("hwb", [128, D])
        k.dma(hwb, hwb[:], e.od_hw, e.od_hw.ap().partition_broadcast(128).rearrange("p o n -> p (o n)"))
        gate = k.sb("gate", [128, D]); e.load_gate(gate, 1, 0, 2)
        ot = [k.sb("ot%d" % i, [128, 2, D]) for i in range(2)]
        zt = [k.sb("zt%d" % i, [128, D]) for i in range(2)]
        xt = [k.sb("xt%d" % i, [128, D]) for i in range(2)]
        w1s = [k.sb("w1_%d" % i, [128, D]) for i in range(2)]; w2s = [k.sb("w2_%d" % i, [128, D]) for i in range(2)]
        sts = [k.sb("st_%d" % i, [128, 16]) for i in range(2)]
        cats = [k.sb("cat%d" % i, [128, D], BF16) for i in range(2)]; catTs = [k.sb("catT%d" % i, [128, 8, 128], BF16) for i in range(2)]
        pTbs = [k.ps("pTb%d" % i, [128, 8, 128], BF16) for i in range(2)]
        pO = [k.ps("pO%d" % i, [128, 512]) for i in range(2)]
        yos = [k.sb("yo%d" % i, [128, D]) for i in range(2)]
        for i, t in enumerate(range(2, NT)):
            tok = slice(t * 128, (t + 1) * 128)
            o_, z_, x_ = ot[i % 2], zt[i % 2], xt[i % 2]
            w1, w2, st, cat, catT, pTb, yo = w1s[i % 2], w2s[i % 2], sts[i % 2], cats[i % 2], catTs[i % 2], pTbs[i % 2], yos[i % 2]
            k.dma(o_, o_[:], o1_d, o1_d.ap()[:, tok, :].rearrange("r t n -> t r n"))
            k.dma(z_, z_[:], z_d, z_d.ap()[tok, :])
            k.dma(x_, x_[:], xs, xs.ap()[tok, :])
            k.tt(k.dve, w1, w1[:], o_, o_[:, 0, :], o_, o_[:, 1, :], ALU.add)
            k.tt(k.pool, w2, w2[:], w1, w1[:], w1, w1[:], ALU.mult)
            k.op(k.dve, lambda: nc.vector.reduce_sum(out=st[:, 0:8], in_=w2[:].rearrange("p (h e) -> p h e", h=8), axis=AX.X), [st], [w2])
            k.actv(st, st[:, 0:8], st, st[:, 0:8], AF.Ln, bias=c.epsb[:, 0:1], scale=1.0 / 128, extra=[c.epsb])
            k.actv(st, st[:, 8:16], st, st[:, 0:8], AF.Exp, scale=-0.5)
            k.tt(k.dve, w1, w1[:].rearrange("p (h e) -> p h e", h=8), w1, w1[:].rearrange("p (h e) -> p h e", h=8),
                 st, bc(st[:, 8:16][:, :, None], [128, 8, 128]), ALU.mult)
            k.tt(k.dve, w1, w1[:], w1, w1[:], hwb, hwb[:], ALU.mult)
            k.actv(z_, z_[:], z_, z_[:], AF.Silu)
            k.tt(k.dve, cat, cat[:], w1, w1[:], z_, z_[:], ALU.mult)
            for j in range(8):
                k.tr(pTb, pTb[:, j, :], cat, cat[:, j * 128:(j + 1) * 128], c.idb, c.idb[:])
            k.cp(k.act, catT, catT[:], pTb, pTb[:])
            for cb in range(2):
                for kc in range(8):
                    k.mm(pO[cb], pO[cb][:], catT, catT[:, kc, :], wout, wout[:, kc, cb * 512:(cb + 1) * 512], kc == 0, kc == 7)
                k.tt(k.dve, yo, yo[:, cb * 512:(cb + 1) * 512], pO[cb], pO[cb][:], gate, gate[:, cb * 512:(cb + 1) * 512], ALU.mult)
            k.tt(k.pool, yo, yo[:], yo, yo[:], x_, x_[:], ALU.add)
            k.dma(xs, xs.ap()[tok, :], yo, yo[:], q=k.pool)


def gdn_phase(k, c, e, qT, kT, vT, gt_d, o1_d):
    nc = k.nc
    with k.scope():
        gt = k.sb("gt", [64, 36, 32])
        for c0 in range(0, 36, 6):
            k.dma(gt, gt[:, c0:c0 + 6, :], gt_d, gt_d.ap()[c0 * 64:(c0 + 6) * 64, :].rearrange("(c l) n -> l c n", l=64))
        ga = k.sb("ga", [64, 16]); dtb = k.sb("dtb", [64, 16])
        k.dma(ga, ga[:], e.od_alog, e.od_alog.ap().partition_broadcast(64).rearrange("p o n -> p (o n)"))
        k.dma(dtb, dtb[:], e.od_dtb, e.od_dtb.ap().partition_broadcast(64).rearrange("p o n -> p (o n)"))
        k.actv(ga, ga[:], ga, ga[:], AF.Exp)
        tri = k.sb("tri", [64, 2, 64]); strict = k.sb("strict", [64, 2, 64])
        k.dma(tri, tri[:], e.ctri, e.ctri.ap().rearrange("r s l -> s r l"))
        k.dma(strict, strict[:], e.cstrict, e.cstrict.ap().rearrange("r s l -> s r l"))
        ones = k.sb("ones", [64, 128]); k.memset(k.dve, ones, ones[:], 1.0)
        ng = k.sb("ng", [64, 2, 36, 8]); beta = k.sb("beta", [64, 2, 36, 8])
        eG = k.sb("eG", [64, 2, 36, 8]); kds = k.sb("kds", [64, 2, 36, 8]); bg = k.sb("bg", [64, 2, 36, 8])
        gl = k.sb("gl", [128, 2, 36, 8])
        for d in range(2):
            k.tt(k.dve, ng, ng[:, d], gt, gt[:, :, 8 * d:8 * d + 8], dtb, bc(dtb[:, None, 8 * d:8 * d + 8], [64, 36, 8]), ALU.add)
            k.actv(beta, beta[:, d], gt, gt[:, :, 16 + 8 * d:24 + 8 * d], AF.Sigmoid)
        k.actv(ng, ng[:], ng, ng[:], AF.Exp)
        k.actv(ng, ng[:], ng, ng[:], AF.Ln, bias=1.0)
        for d in range(2):
            k.tt(k.dve, ng, ng[:, d], ng, ng[:, d], ga, bc(ga[:, None, 8 * d:8 * d + 8], [64, 36, 8]), ALU.mult)
        with k.scope():
            pF = k.ps("pF", [64, 2, 512]); pT_ = k.ps("pTt", [64, 2, 512]); pG = k.ps("pG", [128, 2, 512])
            for d in range(2):
                ngd = ng[:, d].rearrange("p c h -> p (c h)")
                k.mm(pF, pF[:, d, 0:288], tri, tri[:, d, :], ng, ngd)
                k.mm(pT_, pT_[:, d, 0:288], ones, ones[:, 0:64], ng, ngd)
                k.mm(pG, pG[:, d, 0:288], ones, ones[:], ng, ngd)
            fl = lambda t_: t_[:].rearrange("p d c h -> p d (c h)")
            k.actv(eG, fl(eG), pF, pF[:, :, 0:288], AF.Exp, scale=-1.0)
            k.cp(k.dve, kds, fl(kds), pF, pF[:, :, 0:288])
            k.tt(k.dve, kds, fl(kds), kds, fl(kds), pT_, pT_[:, :, 0:288], ALU.subtract)
            k.actv(kds, kds[:], kds, kds[:], AF.Exp)
            k.tt(k.dve, bg, bg[:], beta, beta[:], eG, eG[:], ALU.mult)
            k.actv(gl, fl(gl), pG, pG[:, :, 0:288], AF.Exp, scale=-1.0)
        S32 = [[k.sb("S32_%d_%d" % (d, hp), [128, 2, 128]) for hp in range(4)] for d in range(2)]
        Sb = [[k.sb("Sb_%d_%d" % (d, hp), [128, 2, 128], BF16) for hp in range(4)] for d in range(2)]
        for d in range(2):
            for hp in range(4):
                k.memset(k.dve, S32[d][hp], S32[d][hp][:], 0.0); k.memset(k.pool, Sb[d][hp], Sb[d][hp][:], 0.0)
        idb2 = bc(c.id32[0:64, None, 0:64], [64, 2, 64])

        class G:
            pass
        GS = {}
        for d in range(2):
            for par in range(2):
                g = G()
                sfx = "_%d%d" % (d, par)
                g.X = k.ps("gX" + sfx, [128, 512]); g.Y_ = k.ps("gY" + sfx, [128, 512])
                f3 = lambda h_, p1, c0, a_: h_[0:p1, c0:c0 + 256].rearrange("p (a b) -> p a b", a=a_)
                g.KDv = f3(g.X.h, 64, 0, 4); g.QDv = f3(g.X.h, 64, 256, 4)
                g.Nv = f3(g.X.h, 64, 0, 4); g.Vv = f3(g.X.h, 64, 256, 2); g.O1v = f3(g.X.h, 64, 0, 2)
                g.Tv = g.Y_.h[0:64, 0:256].bitcast(BF16).rearrange("p (a b) -> p a b", a=4)
                g.WTv = g.Y_.h[:, 256:384].rearrange("p (a b) -> p a b", a=2)
                g.O2v = f3(g.Y_.h, 64, 0, 2); g.Sv = f3(g.Y_.h, 128, 256, 2)
                g.kbg = k.sb("kbg" + sfx, [64, 2, 128], BF16); g.kd = k.sb("kd" + sfx, [64, 2, 128], BF16); g.bv = k.sb("bv" + sfx, [64, 2, 128], BF16)
                g.gm = k.sb("gm" + sfx, [64, 4, 64]); g.MBs = k.sb("MBs" + sfx, [64, 4, 64])
                g.gam = k.sb("gam" + sfx, [64, 2, 64]); g.A32 = k.sb("A32" + sfx, [64, 2, 64])
                g.Mb = k.sb("Mb" + sfx, [64, 2, 64], BF16); g.nAb = k.sb("nAb" + sfx, [64, 2, 64], BF16)
                g.Y = k.sb("Y" + sfx, [64, 2, 64], BF16); g.Rt = k.sb("Rt" + sfx, [64, 2, 64], BF16)
                g.nWT = k.sb("nWT" + sfx, [128, 2, 64], BF16); g.vn = k.sb("vn" + sfx, [64, 2, 128], BF16)
                g.gT_ = k.sb("gT_" + sfx, [64, 2, 64]); g.attT = k.sb("attT" + sfx, [64, 2, 64], BF16)
                g.t2 = k.sb("t2" + sfx, [64, 2, 128]); g.otl = [k.sb("otl%d" % i + sfx, [64, 2, 128]) for i in range(2)]
                GS[(d, par)] = g

        idb4 = bc(c.id32[0:64, None, 0:64], [64, 4, 64])
        for key, g in GS.items():
            sfx = "_%d%d" % key
            g.Mb4 = k.sb("Mb4" + sfx, [64, 4, 64], BF16); g.nAb4 = k.sb("nAb4" + sfx, [64, 4, 64], BF16)
            g.Y4 = k.sb("Y4" + sfx, [64, 4, 64], BF16); g.Rt4 = k.sb("Rt4" + sfx, [64, 4, 64], BF16)
            g.attT4 = k.sb("attT4" + sfx, [64, 4, 64], BF16)
            g.kbg4 = k.sb("kbg4" + sfx, [64, 4, 128], BF16); g.kd4 = k.sb("kd4" + sfx, [64, 4, 128], BF16); g.bv4 = k.sb("bv4" + sfx, [64, 4, 128], BF16)
            g.N8v = g.X.h[0:64, 0:512].rearrange("p (a b) -> p a b", a=8)

        def chunk_gen(d, par, ch):
            g = GS[(d, par)]
            tok = slice(ch * 64, (ch + 1) * 64)
            pairs = (par, par + 2)
            for qi, hp in enumerate(pairs):
                h0 = 2 * hp
                k.tt(k.pool, g.gm, g.gm[:, 2 * qi:2 * qi + 2, :], tri, bc(tri[:, d:d + 1, :], [64, 2, 64]),
                     ng, bc(ng[:, d, ch, h0:h0 + 2][:, :, None], [64, 2, 64]), ALU.mult)
                k.tt(k.pool, g.MBs, g.MBs[:, 2 * qi:2 * qi + 2, :], strict, bc(strict[:, d:d + 1, :], [64, 2, 64]),
                     beta, bc(beta[:, d, ch, h0:h0 + 2][:, :, None], [64, 2, 64]), ALU.mult)
            for qi, hp in enumerate(pairs):
                h0 = 2 * hp
                qs = slice(2 * qi, 2 * qi + 2)
                for hh in range(2):
                    h = h0 + hh
                    k.tr(g.Y_, g.Tv[:, hh, :], kT, kT[:, h, tok], c.idb, c.idb[:])
                    k.tr(g.Y_, g.Tv[:, 2 + hh, :], vT, vT[:, h, tok], c.idb, c.idb[:])
                    k.mm(g.X, g.KDv[:, hh, :], kT, kT[:, h, tok], kT, kT[:, h, tok])
                    k.mm(g.X, g.KDv[:, 2 + hh, :], g.gm, g.gm[:, 2 * qi + hh, :], strict, strict[:, d, :])
                    k.mm(g.X, g.QDv[:, hh, :], kT, kT[:, h, tok], qT, qT[:, h, tok])
                    k.mm(g.X, g.QDv[:, 2 + hh, :], strict, strict[:, d, :], g.gm, g.gm[:, 2 * qi + hh, :])
                yield
                sc = lambda t_: bc(t_[:, d, ch, h0:h0 + 2][:, :, None], [64, 2, 128])
                k.tt(k.dve, g.kbg4, g.kbg4[:, qs, :], g.Y_, g.Tv[:, 0:2, :], bg, sc(bg), ALU.mult)
                k.tt(k.dve, g.kd4, g.kd4[:, qs, :], g.Y_, g.Tv[:, 0:2, :], kds, sc(kds), ALU.mult)
                k.tt(k.dve, g.bv4, g.bv4[:, qs, :], g.Y_, g.Tv[:, 2:4, :], beta, sc(beta), ALU.mult)
                k.actv(g.gam, g.gam[:], g.X, g.KDv[:, 2:4, :], AF.Exp, scale=-1.0)
                k.actv(g.gT_, g.gT_[:], g.X, g.QDv[:, 2:4, :], AF.Exp, scale=-1.0)
                k.tt(k.pool, g.gam, g.gam[:], g.gam, g.gam[:], g.MBs, g.MBs[:, qs, :], ALU.mult)
                k.tt(k.pool, g.gT_, g.gT_[:], g.gT_, g.gT_[:], tri, bc(tri[:, d:d + 1, :], [64, 2, 64]), ALU.mult)
                k.tt(k.dve, g.A32, g.A32[:], g.X, g.KDv[:, 0:2, :], g.gam, g.gam[:], ALU.mult)
                k.tt(k.dve, g.attT4, g.attT4[:, qs, :], g.X, g.QDv[:, 0:2, :], g.gT_, g.gT_[:], ALU.mult)
                k.tt(k.pool, g.Mb4, g.Mb4[:, qs, :], g.A32, g.A32[:], c.id32, idb2, ALU.add)
                k.ts(k.dve, g.nAb4, g.nAb4[:, qs, :], g.A32, g.A32[:], -1.0, None, ALU.mult)
                yield
            for q in range(4):
                k.mm(g.X, g.N8v[:, q, :], g.nAb4, g.nAb4[:, q, :], c.idb, c.idb[0:64, 0:64])
            yield
            k.tt(k.dve, g.Y4, g.Y4[:], g.X, g.N8v[:, 0:4, :], c.id32, idb4, ALU.add)
            for itn in range(5):
                for q in range(4):
                    k.mm(g.X, g.N8v[:, q, :], g.Y4, g.Y4[:, q, :], g.Mb4, g.Mb4[:, q, :])
                yield
                k.tt(k.dve, g.Rt4, g.Rt4[:], c.id32, idb4, g.X, g.N8v[:, 0:4, :], ALU.subtract)
                for q in range(4):
                    k.mm(g.X, g.N8v[:, 4 + q, :], g.Rt4, g.Rt4[:, q, :], g.Y4, g.Y4[:, q, :])
                yield
                k.tt(k.dve, g.Y4, g.Y4[:], g.Y4, g.Y4[:], g.X, g.N8v[:, 4:8, :], ALU.add)
            for qi, hp in enumerate(pairs):
                h0 = 2 * hp
                S3, Sb_ = S32[d][hp], Sb[d][hp]
                for hh in range(2):
                    k.mm(g.Y_, g.WTv[:, hh, :], g.kbg4, g.kbg4[:, 2 * qi + hh, :], g.Y4, g.Y4[:, 2 * qi + hh, :])
                yield
                k.actv(g.nWT, g.nWT[:], g.Y_, g.WTv, AF.Copy, scale=-1.0)
                for hh in range(2):
                    k.mm(g.X, g.Vv[:, hh, :], g.Y4, g.Y4[:, 2 * qi + hh, :], g.bv4, g.bv4[:, 2 * qi + hh, :], True, False)
                    k.mm(g.X, g.Vv[:, hh, :], g.nWT, g.nWT[:, hh, :], Sb_, Sb_[:, hh, :], False, True)
                yield
                k.cp(k.act, g.vn, g.vn[:], g.X, g.Vv)
                for hh in range(2):
                    h = h0 + hh
                    k.mm(g.X, g.O1v[:, hh, :], qT, qT[:, h, tok], Sb_, Sb_[:, hh, :])
                    k.mm(g.Y_, g.O2v[:, hh, :], g.attT4, g.attT4[:, 2 * qi + hh, :], g.vn, g.vn[:, hh, :])
                    k.mm(g.Y_, g.Sv[:, hh, :], g.kd4, g.kd4[:, 2 * qi + hh, :], g.vn, g.vn[:, hh, :])
                yield
                ot_ = g.otl[qi]
                k.cp(k.act, g.t2, g.t2[:], g.Y_, g.O2v)
                k.tt(k.dve, ot_, ot_[:], g.X, g.O1v, eG, bc(eG[:, d, ch, h0:h0 + 2][:, :, None], [64, 2, 128]), ALU.mult)
                k.tt(k.pool, ot_, ot_[:], ot_, ot_[:], g.t2, g.t2[:], ALU.add)
                k.tt(k.pool, S3, S3[:], S3, S3[:], gl, bc(gl[:, d, ch, h0:h0 + 2][:, :, None], [128, 2, 128]), ALU.mult)
                k.tt(k.dve, S3, S3[:], S3, S3[:], g.Y_, g.Sv, ALU.add)
                k.cp(k.act, Sb_, Sb_[:], S3, S3[:])
                k.dma(o1_d, o1_d.ap()[d, tok, h0 * 128:(h0 + 2) * 128], ot_, ot_[:].rearrange("p h e -> p (h e)"), q=k.sp)
                yield

        for step in range(36):
            gens = [chunk_gen(0, 0, step), chunk_gen(1, 0, ORDB[step]), chunk_gen(0, 1, step), chunk_gen(1, 1, ORDB[step])]
            alive = [True] * 4
            while any(alive):
                for i in range(4):
                    if alive[i]:
                        try:
                            next(gens[i])
                        except StopIteration:
                            alive[i] = False


def host_consts():
    s = np.arange(64)
    tri = np.stack([(s[:, None] <= s[None, :]), (s[:, None] >= s[None, :])]).astype(np.float32)
    ip = np.arange(128) // 16
    maskM = np.stack([(ip[None, :] >= ip[:, None]), (ip[None, :] <= ip[:, None])]).astype(np.float32)
    return {
        "k_id32": np.eye(128, dtype=np.float32),
        "k_idb": np.eye(128, dtype=np.float32).astype(ml_dtypes.bfloat16),
        "k_tri": tri, "k_maskM": maskM,
        "k_strict": np.stack([(s[:, None] > s[None, :]), (s[:, None] < s[None, :])]).astype(np.float32),
    }


def make_in_maps(inputs, cores):
    f = lambda a: np.ascontiguousarray(np.asarray(a, dtype=np.float32))
    sh = {
        "c_ctx": f(inputs["c_ctx"]).reshape(1, D), "ada_w": f(inputs["ada_w"]), "ada_b": f(inputs["ada_b"]),
        "norm1_w": f(inputs["norm1_w"]), "norm2_w": f(inputs["norm2_w"]),
        "ffn_w1": f(inputs["ffn_w1"]), "ffn_w3": f(inputs["ffn_w3"]), "ffn_w2": f(inputs["ffn_w2"]),
        "final_norm_w": f(inputs["final_norm_w"]).reshape(1, D),
        "ev_w_in": f(inputs["ev_w_in"])[0], "ev_i_bias": f(inputs["ev_i_bias"]).reshape(1, 8),
        "ev_f_bias": f(inputs["ev_f_bias"]).reshape(1, 8), "ev_head_norm_w": f(inputs["ev_head_norm_w"]).reshape(1, 512),
        "ev_lam_re": f(inputs["ev_lam_re"])[0], "ev_lam_im": f(inputs["ev_lam_im"])[0],
        "ev_log_dt": f(inputs["ev_log_dt"]).reshape(1, 64),
        "ev_b_re": f(inputs["ev_b_re"])[0], "ev_b_im": f(inputs["ev_b_im"])[0],
        "ev_c_re": f(inputs["ev_c_re"]).reshape(1024, 64), "ev_c_im": f(inputs["ev_c_im"]).reshape(1024, 64),
        "ev_d": f(inputs["ev_d"]).reshape(1, 512), "ev_w_glu": f(inputs["ev_w_glu"])[0], "ev_w_out": f(inputs["ev_w_out"])[0],
        "od_w_in": f(inputs["od_w_in"])[0], "od_conv_w": f(inputs["od_conv_w"]).reshape(9, 3072),
        "od_a_log": f(inputs["od_a_log"]).reshape(1, 16), "od_dt_bias": f(inputs["od_dt_bias"]).reshape(1, 16),
        "od_head_norm_w": f(inputs["od_head_norm_w"]).reshape(1, D), "od_w_out": f(inputs["od_w_out"])[0],
    }
    sh.update(host_consts())
    x, cc, ctx = f(inputs["x"]), f(inputs["c"]), f(inputs["ctx"])
    maps = []
    for b in cores:
        m = dict(sh)
        m["x"] = x[b]; m["c"] = cc[b:b + 1]; m["ctx"] = ctx[b]
        maps.append(m)
    return maps


def kernel(**inputs):
    nc, _ = build_program()
    maps = make_in_maps(inputs, list(range(8)))
    res = run_bass_kernel_spmd(nc, maps, core_ids=list(range(8)))
    return np.stack([np.asarray(r["out"], dtype=np.float32) for r in res.results], axis=0)
```
